# Optimizing a Trainium2 kernel written in Bass

```python
import math
import jax, jax.numpy as jnp
from jax import lax
import numpy as np

D_MODEL = 1024
BATCH = 16
SEQ = 4096
DEPTH = 4

N_HEADS_A = 16
N_KV_A = 4
HEAD_DIM_A = 64
WINDOW = 128
N_HEADS_B = 8
KEY_DIM_B = 128
VAL_DIM_B = 128
CONV_K = 4
CHUNK = 64
D_FF = 2816
NUM_BUCKETS = 32
MAX_DISTANCE = 128
DN_ALPHA = (2 * DEPTH) ** 0.25
DN_BETA = (8 * DEPTH) ** -0.25
LN_EPS = 1e-5
NORM_EPS = 1e-6
NEG_INF = -1e30

Q_A = N_HEADS_A * HEAD_DIM_A
KV_W = N_KV_A * HEAD_DIM_A
QK_B = N_HEADS_B * KEY_DIM_B
V_B = N_HEADS_B * VAL_DIM_B
CONV_CH = 2 * QK_B + V_B
SPLIT_SIZES = (Q_A, KV_W, KV_W, CONV_CH, N_HEADS_B, N_HEADS_B, V_B, 2 * D_MODEL)
N_IN = sum(SPLIT_SIZES)
SPLIT_POINTS = tuple(int(s) for s in np.cumsum(SPLIT_SIZES)[:-1])

kernel_name = "hybrid_swa_sink_gated_deltanet_macaron_deepnorm"


def layernorm(x, g, b):
    xf = x.astype(jnp.float32)
    mu = xf.mean(-1, keepdims=True)
    var = jnp.square(xf - mu).mean(-1, keepdims=True)
    y = (xf - mu) * lax.rsqrt(var + LN_EPS) * g.astype(jnp.float32) + b.astype(jnp.float32)
    return y.astype(x.dtype)


def swiglu(x, w13, w2):
    gate, up = jnp.split(x @ w13, 2, axis=-1)
    return (jax.nn.silu(gate) * up) @ w2


def t5_bucket(rel):
    n = jnp.maximum(rel, 0)
    max_exact = NUM_BUCKETS // 2
    nf = jnp.maximum(n, 1).astype(jnp.float32)
    large = max_exact + (jnp.log(nf / max_exact) / math.log(MAX_DISTANCE / max_exact)
                         * (NUM_BUCKETS - max_exact)).astype(jnp.int32)
    large = jnp.minimum(large, NUM_BUCKETS - 1)
    return jnp.where(n < max_exact, n, large)


def band_rel():
    r = jnp.arange(WINDOW)[:, None]
    j = jnp.arange(2 * WINDOW)[None, :]
    return r + WINDOW - j


def sliding_window_attention(q, k, v, pos_bias, sink):
    b, l = q.shape[:2]
    nb = l // WINDOW
    grp = N_HEADS_A // N_KV_A
    qb = (q.astype(jnp.float32) * HEAD_DIM_A ** -0.5).reshape(b, nb, WINDOW, N_KV_A, grp, HEAD_DIM_A)
    kb = k.astype(jnp.float32).reshape(b, nb, WINDOW, N_KV_A, HEAD_DIM_A)
    vb = v.astype(jnp.float32).reshape(b, nb, WINDOW, N_KV_A, HEAD_DIM_A)

    def with_prev(t):
        prev = jnp.concatenate([jnp.zeros_like(t[:, :1]), t[:, :-1]], axis=1)
        return jnp.concatenate([prev, t], axis=2)

    kk, vv = with_prev(kb), with_prev(vb)
    rel = band_rel()
    in_band = (rel >= 0) & (rel < WINDOW)
    key_in_own_block = jnp.arange(2 * WINDOW) >= WINDOW
    bias = pos_bias.reshape(N_KV_A, grp, WINDOW, 2 * WINDOW)
    sk = sink.astype(jnp.float32).reshape(N_KV_A, grp, 1, 1)

    def one_block(args):
        qi, ki, vi, i = args
        s = jnp.einsum('bqkgd,bskd->bkgqs', qi, ki) + bias
        valid = in_band & ((i > 0) | key_in_own_block)
        s = jnp.where(valid, s, NEG_INF)
        m = jnp.maximum(s.max(-1, keepdims=True), sk)
        p = jnp.exp(s - m)
        p = p / (p.sum(-1, keepdims=True) + jnp.exp(sk - m))
        return jnp.einsum('bkgqs,bskd->bqkgd', p, vi)

    o = lax.map(one_block, (jnp.moveaxis(qb, 1, 0), jnp.moveaxis(kk, 1, 0),
                            jnp.moveaxis(vv, 1, 0), jnp.arange(nb)))
    return jnp.moveaxis(o, 0, 1).reshape(b, l, Q_A)


def causal_depthwise_conv(u, w):
    return lax.conv_general_dilated(u, w[:, None, :].astype(u.dtype), window_strides=(1,),
                                    padding=[(CONV_K - 1, 0)],
                                    dimension_numbers=('NWC', 'WIO', 'NWC'),
                                    feature_group_count=u.shape[-1])


def l2norm(t):
    t = t.astype(jnp.float32)
    return t * lax.rsqrt(jnp.sum(t * t, -1, keepdims=True) + NORM_EPS)


def gated_delta_rule(q, k, v, g, beta):
    b, l, h, dk = q.shape
    dv = v.shape[-1]
    n = l // CHUNK

    def chunked(t):
        return t.reshape(b, n, CHUNK, h, t.shape[-1]).transpose(0, 3, 1, 2, 4)

    q = chunked(q * dk ** -0.5)
    k = chunked(k)
    v = chunked(v)
    g = jnp.cumsum(g.reshape(b, n, CHUNK, h).transpose(0, 3, 1, 2), axis=-1)
    beta = beta.reshape(b, n, CHUNK, h).transpose(0, 3, 1, 2)[..., None]
    kb, vb = k * beta, v * beta

    causal = jnp.tril(jnp.ones((CHUNK, CHUNK), bool))
    strict = jnp.tril(jnp.ones((CHUNK, CHUNK), bool), -1)
    decay = jnp.exp(jnp.where(causal, g[..., :, None] - g[..., None, :], NEG_INF))
    lower = jnp.where(strict, jnp.einsum('bhncd,bhnmd->bhncm', kb, k) * decay, 0.0)
    eye = jnp.eye(CHUNK, dtype=jnp.float32)
    t_inv = lax.linalg.triangular_solve(lower + eye, jnp.broadcast_to(eye, lower.shape),
                                        left_side=True, lower=True, unit_diagonal=True)
    u = t_inv @ vb
    w = t_inv @ (kb * jnp.exp(g)[..., None])
    a_intra = jnp.where(causal, jnp.einsum('bhncd,bhnmd->bhncm', q, k) * decay, 0.0)
    g_last = g[..., -1]
    k_tail = k * jnp.exp(g_last[..., None] - g)[..., None]
    q_dec = q * jnp.exp(g)[..., None]

    def step(s, xs):
        qd, kt, uc, wc, ac, gl = xs
        v_new = uc - jnp.einsum('bhcd,bhde->bhce', wc, s)
        o = jnp.einsum('bhcd,bhde->bhce', qd, s) + jnp.einsum('bhcm,bhme->bhce', ac, v_new)
        s = s * jnp.exp(gl)[..., None, None] + jnp.einsum('bhcd,bhce->bhde', kt, v_new)
        return s, o

    xs = tuple(jnp.moveaxis(t, 2, 0) for t in (q_dec, k_tail, u, w, a_intra, g_last))
    s0 = jnp.zeros((b, h, dk, dv), jnp.float32)
    _, o = lax.scan(step, s0, xs)
    return o.transpose(1, 0, 3, 2, 4).reshape(b, l, h, dv)


def hybrid_mixer(x, w_in, conv_w, a_log, dt_bias, dn_norm_g, sink,
                 w_branch_a, w_branch_b, w_out, pos_bias):
    b, l, _ = x.shape
    hcat = x @ w_in
    qa, ka, va, qkv_b, beta_raw, dt_raw, z, gate_raw = jnp.split(hcat, SPLIT_POINTS, axis=-1)

    ya = sliding_window_attention(qa.reshape(b, l, N_HEADS_A, HEAD_DIM_A),
                                  ka.reshape(b, l, N_KV_A, HEAD_DIM_A),
                                  va.reshape(b, l, N_KV_A, HEAD_DIM_A), pos_bias, sink)
    ya = ya.astype(x.dtype) @ w_branch_a

    qkv_b = jax.nn.silu(causal_depthwise_conv(qkv_b, conv_w))
    qb, kb, vb = jnp.split(qkv_b, (QK_B, 2 * QK_B), axis=-1)
    qb = l2norm(qb.reshape(b, l, N_HEADS_B, KEY_DIM_B))
    kb = l2norm(kb.reshape(b, l, N_HEADS_B, KEY_DIM_B))
    vb = vb.reshape(b, l, N_HEADS_B, VAL_DIM_B).astype(jnp.float32)
    beta = jax.nn.sigmoid(beta_raw.astype(jnp.float32))
    g = -jnp.exp(a_log.astype(jnp.float32)) * jax.nn.softplus(
        dt_raw.astype(jnp.float32) + dt_bias.astype(jnp.float32))
    o = gated_delta_rule(qb, kb, vb, g, beta)
    o = (o * lax.rsqrt(jnp.mean(o * o, -1, keepdims=True) + NORM_EPS)
         * dn_norm_g.astype(jnp.float32)
         * jax.nn.silu(z.reshape(b, l, N_HEADS_B, VAL_DIM_B).astype(jnp.float32)))
    yb = o.reshape(b, l, V_B).astype(x.dtype) @ w_branch_b

    gate_a, gate_b = jnp.split(jax.nn.sigmoid(gate_raw), 2, axis=-1)
    return (gate_a * ya + gate_b * yb) @ w_out


def setup_inputs(seed: int = 0) -> dict:
    key = jax.random.key(seed)
    ks = jax.random.split(key, 16)
    f32 = jnp.float32

    def nrm(k, shape, scale):
        return jax.random.normal(k, shape, f32) * scale

    dt = jnp.exp(jax.random.uniform(ks[9], (DEPTH, N_HEADS_B), f32, math.log(1e-3), math.log(1e-1)))
    return {
        "x": nrm(ks[0], (BATCH, SEQ, D_MODEL), 1.0),
        "rel_bias": nrm(ks[1], (NUM_BUCKETS, N_HEADS_A), 0.5),
        "ln_g": 1.0 + nrm(ks[2], (DEPTH, 3, D_MODEL), 0.05),
        "ln_b": nrm(ks[3], (DEPTH, 3, D_MODEL), 0.02),
        "ffn_w13": nrm(ks[4], (DEPTH, 2, D_MODEL, 2 * D_FF), D_MODEL ** -0.5),
        "ffn_w2": nrm(ks[5], (DEPTH, 2, D_FF, D_MODEL), DN_BETA * D_FF ** -0.5),
        "w_in": nrm(ks[6], (DEPTH, D_MODEL, N_IN), D_MODEL ** -0.5),
        "conv_w": nrm(ks[7], (DEPTH, CONV_K, CONV_CH), CONV_K ** -0.5),
        "a_log": jnp.log(jax.random.uniform(ks[8], (DEPTH, N_HEADS_B), f32, 1.0, 16.0)),
        "dt_bias": dt + jnp.log(-jnp.expm1(-dt)),
        "dn_norm_g": 1.0 + nrm(ks[10], (DEPTH, VAL_DIM_B), 0.05),
        "sinks": nrm(ks[11], (DEPTH, N_HEADS_A), 0.5),
        "w_branch_a": nrm(ks[12], (DEPTH, Q_A, D_MODEL), Q_A ** -0.5),
        "w_branch_b": nrm(ks[13], (DEPTH, V_B, D_MODEL), V_B ** -0.5),
        "w_out": nrm(ks[14], (DEPTH, D_MODEL, D_MODEL), DN_BETA * D_MODEL ** -0.5),
    }


def reference(x, rel_bias, ln_g, ln_b, ffn_w13, ffn_w2, w_in, conv_w, a_log, dt_bias,
              dn_norm_g, sinks, w_branch_a, w_branch_b, w_out):
    pos_bias = jnp.transpose(rel_bias[t5_bucket(band_rel())], (2, 0, 1)).astype(jnp.float32)
    for i in range(DEPTH):
        x = layernorm(DN_ALPHA * x + 0.5 * swiglu(x, ffn_w13[i, 0], ffn_w2[i, 0]), ln_g[i, 0], ln_b[i, 0])
        y = hybrid_mixer(x, w_in[i], conv_w[i], a_log[i], dt_bias[i], dn_norm_g[i], sinks[i],
                         w_branch_a[i], w_branch_b[i], w_out[i], pos_bias)
        x = layernorm(DN_ALPHA * x + y, ln_g[i, 1], ln_b[i, 1])
        x = layernorm(DN_ALPHA * x + 0.5 * swiglu(x, ffn_w13[i, 1], ffn_w2[i, 1]), ln_g[i, 2], ln_b[i, 2])
    return x
```

```python
import contextlib
import numpy as np
import concourse.bass as bass
import concourse.mybir as mybir
from concourse.bass_utils import run_bass_kernel_spmd

F32 = mybir.dt.float32
BF16 = mybir.dt.bfloat16
AF = mybir.ActivationFunctionType
ALU = mybir.AluOpType
AX = mybir.AxisListType

D = 1024
DFF = 2816
NIN = 7696
DEPTH = 4
SEQ = 4096
NCORES = 8
LN_EPS = 1e-5
NORM_EPS = 1e-6
DN_ALPHA = (2 * DEPTH) ** 0.25
C_Q0, C_KA, C_VA, C_QKVB, C_BETA, C_DT, C_Z, C_GATE = 0, 1024, 1280, 1536, 4608, 4616, 4624, 5648
SOLVE_DT = BF16
import os
M3STOP = int(os.environ.get("M3STOP", "99"))
M3SUB = int(os.environ.get("M3SUB", "7"))

NDMASEM = 16
ENGS = ("pe", "act", "dve", "pool", "sp")


class Buf:
    __slots__ = ("lw", "rd")

    def __init__(self):
        self.lw = None
        self.rd = []


def bufs(n):
    return [Buf() for _ in range(n)]


class Prog:
    def __init__(self, nc, stack):
        self.nc = nc
        self.ops = {e: [] for e in ENGS}
        self.cnt = {e: 0 for e in ("pe", "act", "dve", "pool")}
        self.seen = {e: {} for e in ENGS}
        self.dq_n = {"sp": 0, "pool": 0}
        self.esem = {e: stack.enter_context(nc.semaphore("s_" + e)) for e in ("pe", "act", "dve", "pool")}
        self.dsem = {}
        for q in ("sp", "pool"):
            for k in range(NDMASEM):
                self.dsem[(q, k)] = stack.enter_context(nc.semaphore("d_%s%d" % (q, k)))
        self.ninstr = 0

    def _kv(self, tok):
        if tok[0] == "e":
            return ("e", tok[1]), tok[2]
        q, i = tok[1], tok[2]
        return ("d", q, i % NDMASEM), 16 * (i // NDMASEM + 1)

    def _deps(self, eng, reads, writes):
        need = {}

        def add(tok):
            if tok is None:
                return
            if tok[0] == "e" and tok[1] == "pe" and eng == "pe":
                return
            k, v = self._kv(tok)
            if need.get(k, 0) < v:
                need[k] = v

        for b in reads:
            add(b.lw)
        for b in writes:
            add(b.lw)
            for t in b.rd:
                add(t)
        out = []
        s = self.seen[eng]
        for k, v in need.items():
            if s.get(k, 0) < v:
                s[k] = v
                out.append((k, v))
        return out

    def _commit(self, tok, reads, writes):
        for b in reads:
            b.rd.append(tok)
            if len(b.rd) > 32:
                best = {}
                for t in b.rd:
                    k, v = self._kv(t)
                    if k not in best or best[k][0] < v:
                        best[k] = (v, t)
                b.rd = [t for (_, t) in best.values()]
        for b in writes:
            b.lw = tok
            b.rd = []

    def op(self, eng, fn, reads=(), writes=()):
        waits = self._deps(eng, reads, writes)
        self.cnt[eng] += 1
        tok = ("e", eng, self.cnt[eng])
        self.ops[eng].append((waits, fn, None))
        self._commit(tok, reads, writes)

    def dma(self, q, out_ap, in_ap, reads=(), writes=(), slow=False):
        i = self.dq_n[q]
        self.dq_n[q] += 1
        tok = ("d", q, i)
        waits = self._deps(q, reads, writes)
        if i >= NDMASEM:
            k, v = self._kv(("d", q, i - NDMASEM))
            if self.seen[q].get(k, 0) < v:
                self.seen[q][k] = v
                waits.append((k, v))
        self.ops[q].append((waits, (out_ap, in_ap, slow), tok))
        self._commit(tok, reads, writes)

    def barrier(self):
        allk = []
        for e, c in self.cnt.items():
            if c:
                allk.append((("e", e), c))
        for q, n in self.dq_n.items():
            for k in range(min(NDMASEM, n)):
                last = ((n - 1 - k) // NDMASEM) * NDMASEM + k
                allk.append((("d", q, k), 16 * (last // NDMASEM + 1)))
        for e in ENGS:
            s = self.seen[e]
            waits = []
            for k, v in allk:
                if k == ("e", "pe") and e == "pe":
                    continue
                if s.get(k, 0) < v:
                    s[k] = v
                    waits.append((k, v))
            if waits:
                self.ops[e].append((waits, None, None))

    def emit(self):
        nc = self.nc

        def semof(k):
            return self.esem[k[1]] if k[0] == "e" else self.dsem[(k[1], k[2])]

        def run(engname, e):
            for waits, fn, tok in self.ops[engname]:
                for k, v in waits:
                    e.wait_ge(semof(k), v)
                if fn is None:
                    continue
                self.ninstr += 1
                if tok is None:
                    fn(e).then_inc(self.esem[engname], 1)
                else:
                    o, i, slow = fn
                    if slow:
                        ins = e.dma_start(out=o, in_=i, allow_slow_non_contiguous=True)
                    else:
                        ins = e.dma_start(out=o, in_=i)
                    ins.then_inc(self.dsem[(tok[1], tok[2] % NDMASEM)], 16)
            self.ops[engname] = []

        with nc.Block() as block:
            @block.tensor
            def _(e):
                run("pe", e)

            @block.scalar
            def _(e):
                run("act", e)

            @block.vector
            def _(e):
                run("dve", e)

            @block.gpsimd
            def _(e):
                run("pool", e)

            @block.sync
            def _(e):
                run("sp", e)


class Phase:
    def __init__(self, kb):
        self.kb = kb
        self.st = contextlib.ExitStack()
        self.n = 0

    def __enter__(self):
        self.st.__enter__()
        return self

    def __exit__(self, *a):
        self.kb.P.barrier()
        self.kb.P.emit()
        return self.st.__exit__(*a)

    def sb(self, shape, dt):
        self.n += 1
        return self.st.enter_context(self.kb.nc.sbuf_tensor("t%d_%d" % (self.kb.phase_id, self.n), list(shape), dt))

    def ps(self, dt=F32):
        self.n += 1
        cols = 512 if dt == F32 else 1024
        return self.st.enter_context(self.kb.nc.psum_tensor("p%d_%d" % (self.kb.phase_id, self.n), [128, cols], dt))


class Rot:
    def __init__(self, tiles):
        self.t = tiles
        self.b = bufs(len(tiles))
        self.i = 0

    def next(self):
        k = self.i % len(self.t)
        self.i += 1
        return self.t[k], self.b[k]


class KB:
    def __init__(self, nseq, L, depth, debug=False):
        self.nseq, self.L, self.depth = nseq, L, depth
        self.T = nseq * L
        self.debug = debug
        self.nc = bass.Bass("TRN2", target_bir_lowering=False)
        self.top = contextlib.ExitStack()
        self.phase_id = 0
        nc = self.nc
        T = self.T
        ext = lambda n, s: nc.dram_tensor(n, list(s), F32, kind="ExternalInput").ap()
        self.x = ext("x", [T, D])
        self.ln_g = ext("ln_g", [DEPTH, 3, D])
        self.ln_b = ext("ln_b", [DEPTH, 3, D])
        self.w13 = ext("ffn_w13", [DEPTH, 2, D, 2 * DFF])
        self.w2 = ext("ffn_w2", [DEPTH, 2, DFF, D])
        self.w_in = ext("w_in", [DEPTH, D, NIN])
        self.conv_w = ext("conv_w", [DEPTH, 4, 3072])
        self.a_log = ext("a_log", [DEPTH, 8])
        self.dt_bias = ext("dt_bias", [DEPTH, 8])
        self.dn_g = ext("dn_norm_g", [DEPTH, 128])
        self.sinks = ext("sinks", [DEPTH, 16])
        self.w_a = ext("w_branch_a", [DEPTH, D, D])
        self.w_b = ext("w_branch_b", [DEPTH, D, D])
        self.w_o = ext("w_out", [DEPTH, D, D])
        self.pbias = ext("pbias", [128, 2 * 16 * 128])
        self.pmask = ext("pmask", [128, 2 * 16 * 128])
        self.out = nc.dram_tensor("out", [T, D], F32, kind="ExternalOutput").ap()
        kind = "ExternalOutput" if debug else "Internal"
        scr = lambda n, s, dt: nc.dram_tensor(n, list(s), dt, kind=kind).ap()
        self.R = [scr("res%d" % i, [T, D], F32) for i in range(2)]
        self.QT = scr("QT", [1024, T], BF16)
        self.KT = scr("KT", [256, T], BF16)
        self.VA = scr("VA", [T, 256], BF16)
        self.QKVB = scr("QKVB", [3072, T], BF16)
        self.BG = scr("BG", [T, 16], F32)
        self.AT = scr("AT", [1024, T], BF16)
        self.OGT = scr("OGT", [1024, T], BF16)
        self.P = Prog(nc, self.top)

    def phase(self):
        self.phase_id += 1
        return Phase(self)

    def make_ident(self, ph, dt):
        P = self.P
        t = ph.sb([128, 128], dt)
        b = Buf()
        P.op("pool", lambda e: e.memset(t[:], 0.0), writes=[b])
        P.op("pool", lambda e: e.affine_select(out=t[:], in_=t[:], pattern=[[-1, 128]], compare_op=ALU.not_equal,
                                               fill=1.0, base=0, channel_multiplier=1), reads=[b], writes=[b])
        return t, b

    def load_w(self, dst, dbufs, src, kcs, c0, c1, piece=2048):
        v = src.rearrange("(kc p) n -> p kc n", p=128)
        for kc in range(kcs):
            a = c0
            while a < c1:
                b = min(c1, a + piece)
                self.P.dma("pool", dst[:, kc, a - c0:b - c0], v[:, kc, a:b], writes=[dbufs[kc]])
                a = b

    def load_x_T(self, ph, src, t0, XSr, xT, bxT, pT, ident, bid, nsub=4):
        P = self.P
        for s in range(nsub):
            Xs, bXs = XSr.next()
            P.dma("sp", Xs[:], src[t0 + s * 128:t0 + (s + 1) * 128, :], writes=[bXs])
            for half in range(2):
                pt, bpt = pT.next()
                for k4 in range(4):
                    kc = half * 4 + k4
                    P.op("pe", lambda e, pt=pt, k4=k4, kc=kc, Xs=Xs: e.transpose(pt[:, k4 * 128:(k4 + 1) * 128],
                                                                              Xs[:, kc * 128:(kc + 1) * 128], ident[:]),
                         reads=[bXs, bid], writes=[bpt])
                wb = [bxT[half * 4 + k4] for k4 in range(4)]
                dst = xT[:, half * 4:half * 4 + 4, s * 128:(s + 1) * 128]
                srcv = pt[:].rearrange("p (k c) -> p k c", k=4)
                if half == 0:
                    P.op("act", lambda e, dst=dst, srcv=srcv: e.copy(dst, srcv), reads=[bpt], writes=wb)
                else:
                    P.op("dve", lambda e, dst=dst, srcv=srcv: e.tensor_copy(dst, srcv), reads=[bpt], writes=wb)

    def ln_consts(self, ph, layer, idx):
        P = self.P
        G = ph.sb([128, D], F32)
        B = ph.sb([128, D], F32)
        bg, bb = Buf(), Buf()
        P.dma("sp", G[:], self.ln_g[layer, idx, :].partition_broadcast(128), writes=[bg])
        P.dma("sp", B[:], self.ln_b[layer, idx, :].partition_broadcast(128), writes=[bb])
        return (G, bg, B, bb)

    def ln_epilogue(self, py, bpy, src, dst, r0, c, lnc, XRr, small):
        P = self.P
        G, bg, B, bb = lnc
        Rt, bR = XRr.next()
        st, bst = small.next()
        P.dma("sp", Rt[:], src[r0:r0 + 128, :], writes=[bR])
        for hf in range(2):
            P.op("dve", lambda e, hf=hf, Rt=Rt: e.scalar_tensor_tensor(
                out=Rt[:, hf * 512:(hf + 1) * 512], in0=py[hf][:], scalar=float(c),
                in1=Rt[:, hf * 512:(hf + 1) * 512], op0=ALU.mult, op1=ALU.add),
                reads=[bpy[hf], bR], writes=[bR])
        for hf in range(2):
            P.op("dve", lambda e, hf=hf, Rt=Rt, st=st: e.bn_stats(st[:, hf * 6:(hf + 1) * 6], Rt[:, hf * 512:(hf + 1) * 512]),
                 reads=[bR], writes=[bst])
        P.op("dve", lambda e, st=st: e.bn_aggr(st[:, 12:14], st[:, 0:12]), reads=[bst], writes=[bst])
        eps = LN_EPS / (DN_ALPHA ** 2)
        P.op("dve", lambda e, st=st: e.tensor_scalar(st[:, 13:14], st[:, 13:14], float(eps), None, ALU.add),
             reads=[bst], writes=[bst])
        P.op("act", lambda e, st=st: e.activation(out=st[:, 14:15], in_=st[:, 13:14], func=AF.Ln),
             reads=[bst], writes=[bst])
        P.op("act", lambda e, st=st: e.activation(out=st[:, 14:15], in_=st[:, 14:15], func=AF.Exp, scale=-0.5),
             reads=[bst], writes=[bst])
        P.op("dve", lambda e, st=st: e.scalar_tensor_tensor(out=st[:, 15:16], in0=st[:, 12:13], scalar=-1.0,
                                                            in1=st[:, 14:15], op0=ALU.mult, op1=ALU.mult),
             reads=[bst], writes=[bst])
        P.op("act", lambda e, st=st, Rt=Rt: e.activation(out=Rt[:], in_=Rt[:], func=AF.Identity,
                                                        bias=st[:, 15:16], scale=st[:, 14:15]),
             reads=[bst, bR], writes=[bR])
        P.op("pool", lambda e, Rt=Rt: e.tensor_tensor(out=Rt[:], in0=Rt[:], in1=G[:], op=ALU.mult),
             reads=[bR, bg], writes=[bR])
        P.op("pool", lambda e, Rt=Rt: e.tensor_tensor(out=Rt[:], in0=Rt[:], in1=B[:], op=ALU.add),
             reads=[bR, bb], writes=[bR])
        P.dma("sp", dst[r0:r0 + 128, :], Rt[:], reads=[bR])

    def ffn_phase(self, layer, which, src, dst, ln_idx):
        P = self.P
        T = self.T
        with self.phase() as ph:
            W13 = ph.sb([128, 8, 2 * DFF], BF16)
            W2 = ph.sb([128, 22, D], BF16)
            bW13, bW2 = bufs(8), bufs(22)
            self.load_w(W13, bW13, self.w13[layer, which], 8, 0, 2 * DFF, piece=1408)
            self.load_w(W2, bW2, self.w2[layer, which], 22, 0, D, piece=1024)
            ident, bid = self.make_ident(ph, F32)
            lnc = self.ln_consts(ph, layer, ln_idx)
            NB = 2
            XSr = Rot([ph.sb([128, D], F32) for _ in range(2)])
            XRr = Rot([ph.sb([128, D], F32) for _ in range(2)])
            xTr = [ph.sb([128, 8, 512], BF16) for _ in range(NB)]
            bxTr = [bufs(8) for _ in range(NB)]
            HT = ph.sb([128, 22, 512], BF16)
            bHT = bufs(22)
            SG = Rot([ph.sb([128, 512], F32) for _ in range(2)])
            small = Rot([ph.sb([128, 16], F32) for _ in range(2)])
            pT = Rot([ph.ps() for _ in range(2)])
            pGU = Rot([ph.ps() for _ in range(4)])
            pY = [ph.ps() for _ in range(2)]
            bpY = bufs(2)
            ntiles = T // 512
            for ti in range(ntiles):
                t0 = ti * 512
                xT, bxT = xTr[ti % NB], bxTr[ti % NB]
                self.load_x_T(ph, src, t0, XSr, xT, bxT, pT, ident, bid)
                for j in range(22):
                    pg, bpg = pGU.next()
                    pu, bpu = pGU.next()
                    for kc in range(8):
                        P.op("pe", lambda e, pg=pg, kc=kc, j=j, xT=xT: e.matmul(
                            pg[:], W13[:, kc, j * 128:(j + 1) * 128], xT[:, kc, :], start=(kc == 0), stop=(kc == 7)),
                            reads=[bW13[kc], bxT[kc]], writes=[bpg])
                    for kc in range(8):
                        P.op("pe", lambda e, pu=pu, kc=kc, j=j, xT=xT: e.matmul(
                            pu[:], W13[:, kc, DFF + j * 128:DFF + (j + 1) * 128], xT[:, kc, :], start=(kc == 0), stop=(kc == 7)),
                            reads=[bW13[kc], bxT[kc]], writes=[bpu])
                    sg, bsg = SG.next()
                    P.op("act", lambda e, sg=sg, pg=pg: e.activation(out=sg[:], in_=pg[:], func=AF.Silu),
                         reads=[bpg], writes=[bsg])
                    P.op("dve", lambda e, sg=sg, pu=pu, j=j: e.tensor_tensor(out=HT[:, j, :], in0=pu[:], in1=sg[:], op=ALU.mult),
                         reads=[bpu, bsg], writes=[bHT[j]])
                for s in range(4):
                    for hf in range(2):
                        for j in range(22):
                            P.op("pe", lambda e, hf=hf, j=j, s=s: e.matmul(
                                pY[hf][:], HT[:, j, s * 128:(s + 1) * 128], W2[:, j, hf * 512:(hf + 1) * 512],
                                start=(j == 0), stop=(j == 21)),
                                reads=[bHT[j], bW2[j]], writes=[bpY[hf]])
                    self.ln_epilogue(pY, bpY, src, dst, t0 + s * 128, 0.5 / DN_ALPHA, lnc, XRr, small)

    def m1_phase(self, layer, src):
        P = self.P
        T, L = self.T, self.L
        NW = C_Z
        with self.phase() as ph:
            W = ph.sb([128, 8, NW], BF16)
            bW = bufs(8)
            self.load_w(W, bW, self.w_in[layer], 8, 0, NW, piece=1156)
            ident, bid = self.make_ident(ph, F32)
            ones = ph.sb([128, 128], BF16)
            bones = Buf()
            P.op("pool", lambda e: e.memset(ones[:], 1.0), writes=[bones])
            CW = ph.sb([128, 4, 24], F32)
            bCW = Buf()
            for j in range(4):
                P.dma("sp", CW[:, j, :], self.conv_w[layer, j, :].rearrange("(c p) -> p c", p=128), writes=[bCW], slow=True)
            DTB = ph.sb([128, 8], F32)
            NEGA = ph.sb([128, 8], F32)
            bDTB, bNEGA = Buf(), Buf()
            P.dma("sp", DTB[:], self.dt_bias[layer, :].partition_broadcast(128), writes=[bDTB])
            P.dma("sp", NEGA[:], self.a_log[layer, :].partition_broadcast(128), writes=[bNEGA])
            P.op("act", lambda e: e.activation(out=NEGA[:], in_=NEGA[:], func=AF.Exp), reads=[bNEGA], writes=[bNEGA])
            P.op("dve", lambda e: e.tensor_scalar(NEGA[:], NEGA[:], -1.0, None, ALU.mult), reads=[bNEGA], writes=[bNEGA])
            CAR = ph.sb([128, 24, 3], F32)
            bCAR = bufs(24)
            NB = 2
            XSr = Rot([ph.sb([128, D], F32) for _ in range(2)])
            xTr = [ph.sb([128, 8, 512], BF16) for _ in range(NB)]
            bxTr = [bufs(8) for _ in range(NB)]
            QAr = Rot([ph.sb([128, 10, 512], BF16) for _ in range(2)])
            VAr = Rot([ph.sb([128, 4, 256], BF16) for _ in range(2)])
            BGr = Rot([ph.sb([128, 4, 16], F32) for _ in range(2)])
            TMP = Rot([ph.sb([128, 56], F32) for _ in range(2)])
            Ur = Rot([ph.sb([128, 515], F32) for _ in range(3)])
            ACr = Rot([ph.sb([128, 512], F32) for _ in range(3)])
            AC2r = Rot([ph.sb([128, 512], F32) for _ in range(3)])
            Y8 = ph.sb([128, 8, 512], F32)
            SQ8 = ph.sb([128, 8, 512], BF16)
            bY8, bSQ8 = bufs(8), bufs(8)
            RSr = Rot([ph.sb([128, 512], F32) for _ in range(2)])
            OCr = Rot([ph.sb([128, 8, 512], BF16) for _ in range(3)])
            pT = Rot([ph.ps() for _ in range(2)])
            pA = Rot([ph.ps() for _ in range(3)])
            pB = Rot([ph.ps() for _ in range(2)])
            pS = Rot([ph.ps() for _ in range(1)])
            ntiles = T // 512
            for ti in range(ntiles):
                t0 = ti * 512
                seq_start = (t0 % L == 0)
                xT, bxT = xTr[ti % NB], bxTr[ti % NB]
                self.load_x_T(ph, src, t0, XSr, xT, bxT, pT, ident, bid)
                QA, bQA = QAr.next()
                for c in range(10):
                    pa, bpa = pA.next()
                    for kc in range(8):
                        P.op("pe", lambda e, pa=pa, kc=kc, c=c, xT=xT: e.matmul(
                            pa[:], W[:, kc, c * 128:(c + 1) * 128], xT[:, kc, :], start=(kc == 0), stop=(kc == 7)),
                            reads=[bW[kc], bxT[kc]], writes=[bpa])
                    sc = 0.125 if c < 8 else 1.0
                    P.op("act", lambda e, pa=pa, c=c, QA=QA, sc=sc: e.activation(out=QA[:, c, :], in_=pa[:], func=AF.Copy, scale=sc),
                         reads=[bpa], writes=[bQA])
                P.dma("sp", self.QT[:, t0:t0 + 512].rearrange("(c p) t -> p c t", p=128), QA[:, 0:8, :], reads=[bQA])
                P.dma("sp", self.KT[:, t0:t0 + 512].rearrange("(c p) t -> p c t", p=128), QA[:, 8:10, :], reads=[bQA])
                VAt, bVA = VAr.next()
                BGt, bBG = BGr.next()
                for s in range(4):
                    pb, bpb = pB.next()
                    for kc in range(8):
                        P.op("pe", lambda e, pb=pb, kc=kc, s=s, xT=xT: e.matmul(
                            pb[:, 0:256], xT[:, kc, s * 128:(s + 1) * 128], W[:, kc, C_VA:C_VA + 256],
                            start=(kc == 0), stop=(kc == 7)), reads=[bW[kc], bxT[kc]], writes=[bpb])
                    for kc in range(8):
                        P.op("pe", lambda e, pb=pb, kc=kc, s=s, xT=xT: e.matmul(
                            pb[:, 256:272], xT[:, kc, s * 128:(s + 1) * 128], W[:, kc, C_BETA:C_BETA + 16],
                            start=(kc == 0), stop=(kc == 7)), reads=[bW[kc], bxT[kc]], writes=[bpb])
                    P.op("act", lambda e, pb=pb, s=s, VAt=VAt: e.copy(VAt[:, s, :], pb[:, 0:256]), reads=[bpb], writes=[bVA])
                    tm, btm = TMP.next()
                    P.op("act", lambda e, pb=pb, tm=tm: e.copy(tm[:, 40:56], pb[:, 256:272]), reads=[bpb], writes=[btm])
                    P.op("act", lambda e, tm=tm, s=s, BGt=BGt: e.activation(out=BGt[:, s, 0:8], in_=tm[:, 40:48], func=AF.Exp, scale=-1.0),
                         reads=[btm], writes=[bBG])
                    P.op("dve", lambda e, s=s, BGt=BGt: e.tensor_scalar(BGt[:, s, 0:8], BGt[:, s, 0:8], 1.0, None, ALU.add),
                         reads=[bBG], writes=[bBG])
                    P.op("dve", lambda e, s=s, BGt=BGt: e.reciprocal(BGt[:, s, 0:8], BGt[:, s, 0:8]), reads=[bBG], writes=[bBG])
                    P.op("dve", lambda e, tm=tm: e.tensor_tensor(out=tm[:, 0:8], in0=tm[:, 48:56], in1=DTB[:], op=ALU.add),
                         reads=[btm, bDTB], writes=[btm])
                    P.op("dve", lambda e, tm=tm: e.tensor_scalar(tm[:, 8:16], tm[:, 0:8], -1.0, None, ALU.mult),
                         reads=[btm], writes=[btm])
                    P.op("dve", lambda e, tm=tm: e.tensor_tensor(out=tm[:, 8:16], in0=tm[:, 8:16], in1=tm[:, 0:8], op=ALU.max),
                         reads=[btm], writes=[btm])
                    P.op("act", lambda e, tm=tm: e.activation(out=tm[:, 16:24], in_=tm[:, 8:16], func=AF.Exp, scale=-1.0),
                         reads=[btm], writes=[btm])
                    P.op("dve", lambda e, tm=tm: e.tensor_scalar(tm[:, 16:24], tm[:, 16:24], 1.0, None, ALU.add), reads=[btm], writes=[btm])
                    P.op("act", lambda e, tm=tm: e.activation(out=tm[:, 24:32], in_=tm[:, 16:24], func=AF.Ln),
                         reads=[btm], writes=[btm])
                    P.op("dve", lambda e, tm=tm: e.scalar_tensor_tensor(out=tm[:, 32:40], in0=tm[:, 0:8], scalar=0.0,
                                                                        in1=tm[:, 24:32], op0=ALU.max, op1=ALU.add),
                         reads=[btm], writes=[btm])
                    P.op("dve", lambda e, tm=tm, s=s, BGt=BGt: e.tensor_tensor(out=BGt[:, s, 8:16], in0=tm[:, 32:40], in1=NEGA[:], op=ALU.mult),
                         reads=[btm, bNEGA], writes=[bBG])
                P.dma("sp", self.VA[t0:t0 + 512, :].rearrange("(s p) f -> p s f", p=128), VAt[:], reads=[bVA])
                P.dma("sp", self.BG[t0:t0 + 512, :].rearrange("(s p) f -> p s f", p=128), BGt[:], reads=[bBG])
                for grp in range(3):
                    OC, bOC = OCr.next()
                    for cc in range(8):
                        c = grp * 8 + cc
                        pa, bpa = pA.next()
                        col = C_QKVB + c * 128
                        for kc in range(8):
                            P.op("pe", lambda e, pa=pa, kc=kc, col=col, xT=xT: e.matmul(
                                pa[:], W[:, kc, col:col + 128], xT[:, kc, :], start=(kc == 0), stop=(kc == 7)),
                                reads=[bW[kc], bxT[kc]], writes=[bpa])
                        U, bU = Ur.next()
                        if seq_start:
                            P.op("dve", lambda e, U=U: e.memset(U[:, 0:3], 0.0), writes=[bU])
                        else:
                            P.op("dve", lambda e, U=U, c=c: e.tensor_copy(U[:, 0:3], CAR[:, c, :]), reads=[bCAR[c]], writes=[bU])
                        P.op("act", lambda e, U=U, pa=pa: e.copy(U[:, 3:515], pa[:]), reads=[bpa], writes=[bU])
                        P.op("dve", lambda e, U=U, c=c: e.tensor_copy(CAR[:, c, :], U[:, 512:515]), reads=[bU], writes=[bCAR[c]])
                        A1, bA1 = ACr.next()
                        A2, bA2 = AC2r.next()
                        P.op("act", lambda e, U=U, A1=A1, c=c: e.activation(out=A1[:], in_=U[:, 0:512], func=AF.Identity, scale=CW[:, 0, c:c + 1]),
                             reads=[bU, bCW], writes=[bA1])
                        P.op("act", lambda e, U=U, A2=A2, c=c: e.activation(out=A2[:], in_=U[:, 2:514], func=AF.Identity, scale=CW[:, 2, c:c + 1]),
                             reads=[bU, bCW], writes=[bA2])
                        P.op("dve", lambda e, U=U, A1=A1, c=c: e.scalar_tensor_tensor(out=A1[:], in0=U[:, 1:513], scalar=CW[:, 1, c:c + 1],
                                                                                    in1=A1[:], op0=ALU.mult, op1=ALU.add),
                             reads=[bU, bCW, bA1], writes=[bA1])
                        P.op("dve", lambda e, U=U, A2=A2, c=c: e.scalar_tensor_tensor(out=A2[:], in0=U[:, 3:515], scalar=CW[:, 3, c:c + 1],
                                                                                    in1=A2[:], op0=ALU.mult, op1=ALU.add),
                             reads=[bU, bCW, bA2], writes=[bA2])
                        P.op("pool", lambda e, A1=A1, A2=A2: e.tensor_tensor(out=A2[:], in0=A1[:], in1=A2[:], op=ALU.add),
                             reads=[bA1, bA2], writes=[bA2])
                        if grp == 2:
                            P.op("act", lambda e, A2=A2, OC=OC, cc=cc: e.activation(out=OC[:, cc, :], in_=A2[:], func=AF.Silu),
                                 reads=[bA2], writes=[bOC])
                        else:
                            P.op("act", lambda e, A2=A2, cc=cc: e.activation(out=Y8[:, cc, :], in_=A2[:], func=AF.Silu), reads=[bA2], writes=[bY8[cc]])
                            P.op("dve", lambda e, cc=cc: e.tensor_tensor(out=SQ8[:, cc, :], in0=Y8[:, cc, :], in1=Y8[:, cc, :], op=ALU.mult),
                                 reads=[bY8[cc]], writes=[bSQ8[cc]])
                    if grp < 2:
                        for cc in range(8):
                            RS, bRS = RSr.next()
                            ps_, bps = pS.next()
                            P.op("pe", lambda e, ps_=ps_, cc=cc: e.matmul(ps_[:], ones[:], SQ8[:, cc, :], start=True, stop=True),
                                 reads=[bones, bSQ8[cc]], writes=[bps])
                            P.op("dve", lambda e, ps_=ps_, RS=RS: e.tensor_scalar(RS[:], ps_[:], float(NORM_EPS), None, ALU.add),
                                 reads=[bps], writes=[bRS])
                            P.op("act", lambda e, RS=RS: e.activation(out=RS[:], in_=RS[:], func=AF.Ln), reads=[bRS], writes=[bRS])
                            P.op("act", lambda e, RS=RS: e.activation(out=RS[:], in_=RS[:], func=AF.Exp, scale=-0.5), reads=[bRS], writes=[bRS])
                            qs = (128.0 ** -0.5) if grp == 0 else 1.0
                            P.op("dve", lambda e, RS=RS, OC=OC, cc=cc, qs=qs: e.scalar_tensor_tensor(
                                out=OC[:, cc, :], in0=Y8[:, cc, :], scalar=float(qs), in1=RS[:], op0=ALU.mult, op1=ALU.mult),
                                reads=[bY8[cc], bRS], writes=[bOC])
                    P.dma("sp", self.QKVB[grp * 1024:(grp + 1) * 1024, t0:t0 + 512].rearrange("(c p) t -> p c t", p=128),
                          OC[:], reads=[bOC])

    def m2_phase(self, layer):
        P = self.P
        T, L = self.T, self.L
        with self.phase() as ph:
            EB = ph.sb([128, 2 * 16 * 128], F32)
            MK = ph.sb([128, 2 * 16 * 128], F32)
            bEB, bMK = Buf(), Buf()
            for q4 in range(4):
                sl = slice(q4 * 1024, (q4 + 1) * 1024)
                P.dma("sp", EB[:, sl], self.pbias[:, sl], writes=[bEB])
                P.dma("sp", MK[:, sl], self.pmask[:, sl], writes=[bMK])
            P.op("act", lambda e: e.activation(out=EB[:], in_=EB[:], func=AF.Exp), reads=[bEB], writes=[bEB])
            P.op("pool", lambda e: e.tensor_tensor(out=EB[:], in0=EB[:], in1=MK[:], op=ALU.mult), reads=[bEB, bMK], writes=[bEB])
            SK = ph.sb([64, 16], F32)
            SKE = ph.sb([64, 16, 128], F32)
            bSK, bSKE = Buf(), Buf()
            P.dma("sp", SK[:], self.sinks[layer, :].partition_broadcast(64), writes=[bSK])
            P.op("act", lambda e: e.activation(out=SK[:], in_=SK[:], func=AF.Exp), reads=[bSK], writes=[bSK])
            P.op("pool", lambda e: e.memset(SKE[:], 0.0), writes=[bSKE])
            for h in range(16):
                P.op("pool", lambda e, h=h: e.tensor_scalar(SKE[:, h, :], SKE[:, h, :], SK[:, h:h + 1], None, ALU.add),
                     reads=[bSK, bSKE], writes=[bSKE])
            ones = ph.sb([128, 64], BF16)
            bones = Buf()
            P.op("pool", lambda e: e.memset(ones[:], 1.0), writes=[bones])
            Qr = Rot([ph.sb([64, 16, 512], BF16) for _ in range(2)])
            Kr = Rot([ph.sb([64, 4, 640], BF16) for _ in range(2)])
            Vr = Rot([ph.sb([128, 5, 256], BF16) for _ in range(2)])
            ATr = Rot([ph.sb([64, 16, 512], BF16) for _ in range(2)])
            Er = Rot([ph.sb([128, 512], F32) for _ in range(4)])
            Pr = Rot([ph.sb([128, 512], BF16) for _ in range(4)])
            Dr = Rot([ph.sb([64, 512], F32) for _ in range(2)])
            pSr = Rot([ph.ps() for _ in range(4)])
            pOr = Rot([ph.ps() for _ in range(2)])
            pDr = Rot([ph.ps() for _ in range(2)])
            EBv = EB[:].rearrange("p (b h q) -> p b h q", b=2, h=16)
            for ti in range(T // 512):
                t0 = ti * 512
                seq_start = (t0 % L == 0)
                Qt, bQ = Qr.next()
                Kt, bK = Kr.next()
                Vt, bV = Vr.next()
                At, bA = ATr.next()
                P.dma("sp", Qt[:], self.QT[:, t0:t0 + 512].rearrange("(h d) t -> d h t", d=64), writes=[bQ])
                if seq_start:
                    P.dma("sp", Kt[:, :, 128:640], self.KT[:, t0:t0 + 512].rearrange("(g d) t -> d g t", d=64), writes=[bK])
                    P.dma("sp", Vt[:, 1:5, :], self.VA[t0:t0 + 512, :].rearrange("(b p) c -> p b c", p=128), writes=[bV])
                else:
                    P.dma("sp", Kt[:], self.KT[:, t0 - 128:t0 + 512].rearrange("(g d) t -> d g t", d=64), writes=[bK])
                    P.dma("sp", Vt[:], self.VA[t0 - 128:t0 + 512, :].rearrange("(b p) c -> p b c", p=128), writes=[bV])
                for i in range(4):
                    first = seq_start and i == 0
                    sbl = [1] if first else [0, 1]
                    for g in range(4):
                        Pb = {}
                        for sb_ in sbl:
                            ps_, bps = pSr.next()
                            P.op("pe", lambda e, ps_=ps_, g=g, i=i, sb_=sb_, Kt=Kt, Qt=Qt: e.matmul(
                                ps_[:], Kt[:, g, (i + sb_) * 128:(i + sb_ + 1) * 128], Qt[:, 4 * g:4 * g + 4, i * 128:(i + 1) * 128],
                                start=True, stop=True), reads=[bK, bQ], writes=[bps])
                            Et, bE = Er.next()
                            P.op("act", lambda e, Et=Et, ps_=ps_: e.activation(out=Et[:], in_=ps_[:], func=AF.Exp),
                                 reads=[bps], writes=[bE])
                            Pt, bP = Pr.next()
                            P.op("pool", lambda e, Et=Et, Pt=Pt, sb_=sb_, g=g: e.tensor_tensor(
                                out=Pt[:].rearrange("p (h q) -> p h q", h=4), in0=Et[:].rearrange("p (h q) -> p h q", h=4),
                                in1=EBv[:, sb_, 4 * g:4 * g + 4, :], op=ALU.mult), reads=[bE, bEB], writes=[bP])
                            Pb[sb_] = (Pt, bP)
                        po, bpo = pOr.next()
                        pd, bpd = pDr.next()
                        for n_, sb_ in enumerate(sbl):
                            Pt, bP = Pb[sb_]
                            P.op("pe", lambda e, po=po, Pt=Pt, sb_=sb_, g=g, i=i, Vt=Vt, n_=n_: e.matmul(
                                po[0:64, :], Vt[:, i + sb_, g * 64:(g + 1) * 64], Pt[:], start=(n_ == 0), stop=(n_ == len(sbl) - 1)),
                                reads=[bV, bP], writes=[bpo])
                        for n_, sb_ in enumerate(sbl):
                            Pt, bP = Pb[sb_]
                            P.op("pe", lambda e, pd=pd, Pt=Pt, n_=n_: e.matmul(
                                pd[0:64, :], ones[:], Pt[:], start=(n_ == 0), stop=(n_ == len(sbl) - 1)),
                                reads=[bones, bP], writes=[bpd])
                        Dt, bD = Dr.next()
                        P.op("dve", lambda e, Dt=Dt, pd=pd, g=g: e.tensor_tensor(
                            out=Dt[:].rearrange("p (h q) -> p h q", h=4), in0=pd[0:64, :].rearrange("p (h q) -> p h q", h=4),
                            in1=SKE[:, 4 * g:4 * g + 4, :], op=ALU.add), reads=[bpd, bSKE], writes=[bD])
                        P.op("dve", lambda e, Dt=Dt: e.reciprocal(Dt[:], Dt[:]), reads=[bD], writes=[bD])
                        P.op("dve", lambda e, Dt=Dt, po=po, At=At, g=g, i=i: e.tensor_tensor(
                            out=At[:, 4 * g:4 * g + 4, i * 128:(i + 1) * 128], in0=po[0:64, :].rearrange("p (h q) -> p h q", h=4),
                            in1=Dt[:].rearrange("p (h q) -> p h q", h=4), op=ALU.mult), reads=[bpo, bD], writes=[bA])
                P.dma("sp", self.AT[:, t0:t0 + 512].rearrange("(h d) t -> d h t", d=64), At[:], reads=[bA])

    def m3_phase(self, layer, src):
        P = self.P
        T, L = self.T, self.L
        SD = SOLVE_DT
        with self.phase() as ph:
            WZ = ph.sb([128, 8, 1024], BF16)
            bWZ = bufs(8)
            self.load_w(WZ, bWZ, self.w_in[layer], 8, C_Z, C_Z + 1024, piece=1024)
            identF, bidF = self.make_ident(ph, F32)
            identB = ph.sb([128, 128], BF16)
            bidB = Buf()
            P.op("pool", lambda e: e.tensor_copy(identB[:], identF[:]), reads=[bidF], writes=[bidB])
            if SD == BF16:
                identS, bidS = identB, bidB
            else:
                identS, bidS = identF, bidF
            bC = Buf()

            def mk(shape=(128, 128)):
                return ph.sb(list(shape), F32)

            TRI, LGT, MSU, ONES, SELA, SELB, SAME = mk(), mk(), mk(), mk(), mk(), mk(), mk()
            P.op("pool", lambda e: e.memset(TRI[:], 1.0), writes=[bC])
            P.op("pool", lambda e: e.affine_select(out=TRI[:], in_=TRI[:], pattern=[[1, 128]], compare_op=ALU.is_ge,
                                                   fill=0.0, base=0, channel_multiplier=-1), reads=[bC], writes=[bC])
            P.op("pool", lambda e: e.memset(TRI[0:64, 64:128], 0.0), reads=[bC], writes=[bC])
            P.op("pool", lambda e: e.memset(LGT[:], 1.0), reads=[bC], writes=[bC])
            P.op("pool", lambda e: e.affine_select(out=LGT[:], in_=LGT[:], pattern=[[-1, 128]], compare_op=ALU.is_gt,
                                                   fill=0.0, base=0, channel_multiplier=1), reads=[bC], writes=[bC])
            P.op("pool", lambda e: e.memset(LGT[64:128, 0:64], 0.0), reads=[bC], writes=[bC])
            P.op("pool", lambda e: e.memset(MSU[:], 1.0), reads=[bC], writes=[bC])
            P.op("pool", lambda e: e.affine_select(out=MSU[:], in_=MSU[:], pattern=[[1, 128]], compare_op=ALU.is_gt,
                                                   fill=0.0, base=0, channel_multiplier=-1), reads=[bC], writes=[bC])
            P.op("pool", lambda e: e.memset(MSU[0:64, 64:128], 0.0), reads=[bC], writes=[bC])
            P.op("pool", lambda e: e.memset(ONES[:], 1.0), reads=[bC], writes=[bC])
            P.op("pool", lambda e: e.memset(SELA[:], 0.0), reads=[bC], writes=[bC])
            P.op("pool", lambda e: e.memset(SELA[0:64, :], 1.0), reads=[bC], writes=[bC])
            P.op("pool", lambda e: e.memset(SELB[:], 0.0), reads=[bC], writes=[bC])
            P.op("pool", lambda e: e.memset(SELB[64:128, :], 1.0), reads=[bC], writes=[bC])
            P.op("pool", lambda e: e.memset(SAME[:], 0.0), reads=[bC], writes=[bC])
            P.op("pool", lambda e: e.memset(SAME[0:64, 0:64], 1.0), reads=[bC], writes=[bC])
            P.op("pool", lambda e: e.memset(SAME[64:128, 64:128], 1.0), reads=[bC], writes=[bC])
            NEGM8 = mk((128, 8, 128))
            MSU8 = mk((128, 8, 128))
            ID8 = ph.sb([128, 8, 128], SD)
            for h in range(8):
                P.op("pool", lambda e, h=h: e.tensor_scalar(NEGM8[:, h, :], TRI[:], 1e30, -1e30, ALU.mult, ALU.add),
                     reads=[bC], writes=[bC])
                P.op("pool", lambda e, h=h: e.tensor_copy(MSU8[:, h, :], MSU[:]), reads=[bC], writes=[bC])
                P.op("pool", lambda e, h=h: e.tensor_copy(ID8[:, h, :], identF[:]), reads=[bC, bidF], writes=[bC])
            DNG = mk((128, 8, 128))
            bDNG = Buf()
            for h in range(8):
                P.dma("sp", DNG[:, h, :], self.dn_g[layer, :].partition_broadcast(128), writes=[bDNG])
            S = ph.sb([128, 8, 128], F32)
            SB = ph.sb([128, 8, 128], BF16)
            bS, bSB = bufs(8), bufs(8)
            VN = ph.sb([128, 8, 128], BF16)
            bVN = bufs(8)
            P.op("pool", lambda e: e.memset(VN[:], 0.0), writes=bVN)
            NB = 2
            XSr = Rot([ph.sb([128, D], F32) for _ in range(2)])
            xTr = [ph.sb([128, 8, 512], BF16) for _ in range(NB)]
            bxTr = [bufs(8) for _ in range(NB)]
            QKVr = Rot([ph.sb([128, 24, 512], BF16) for _ in range(2)])
            BGr = Rot([ph.sb([128, 4, 16], F32) for _ in range(2)])
            OGTr = Rot([ph.sb([128, 8, 512], BF16) for _ in range(2)])
            NBG = ph.sb([128, 8], F32)
            bNBG = Buf()
            Rt = ph.sb([128, 8, 128], F32)
            bRt = Buf()
            ET = ph.sb([128, 8, 128], F32)
            ETS = ph.sb([128, 8, 128], F32)
            EGB = ph.sb([128, 8, 128], F32)
            bET, bETS, bEGB = Buf(), Buf(), Buf()
            SM = ph.sb([128, 64], F32)
            bSM = Buf()
            YR = [ph.sb([128, 8, 256], SD) for _ in range(2)]
            bYR = [bufs(2), bufs(2)]
            Z = [ph.sb([128, 8, 128], SD) for _ in range(2)]
            bZ = [bufs(2), bufs(2)]
            XS = ph.sb([128, 8, 128], BF16)
            bXS = bufs(2)
            AIT = ph.sb([128, 8, 128], BF16)
            bAIT = Buf()
            KG = ph.sb([128, 8, 128], BF16)
            KT2 = ph.sb([128, 8, 128], BF16)
            KTOK = ph.sb([128, 8, 128], BF16)
            bKTOK = Buf()
            Vt = ph.sb([128, 8, 128], BF16)
            QD = ph.sb([128, 8, 128], BF16)
            bKG, bKT2, bVt, bQD = Buf(), Buf(), Buf(), Buf()
            UB = ph.sb([128, 8, 128], F32)
            WT = ph.sb([128, 8, 128], BF16)
            bUB, bWT = bufs(8), bufs(2)
            O = ph.sb([128, 8, 128], F32)
            bO = Buf()
            SQ = ph.sb([128, 8, 128], F32)
            ZG = ph.sb([128, 8, 128], F32)
            OG = ph.sb([128, 8, 128], BF16)
            bSQ, bZG, bOG = Buf(), Buf(), Buf()
            pF = Rot([ph.ps() for _ in range(6)])
            pH = Rot([ph.ps(BF16) for _ in range(2)])
            YRv = [y[:].rearrange("p h c -> p (h c)") for y in YR]

            def flat(t, g):
                return t[:, 4 * g:4 * g + 4, :]

            for ti in range(T // 512):
                t0 = ti * 512
                seq_start = (t0 % L == 0)
                xT, bxT = xTr[ti % NB], bxTr[ti % NB]
                self.load_x_T(ph, src, t0, XSr, xT, bxT, pF, identF, bidF)
                QKV, bQKV = QKVr.next()
                for grp in range(3):
                    P.dma("sp", QKV[:, grp * 8:(grp + 1) * 8, :],
                          self.QKVB[grp * 1024:(grp + 1) * 1024, t0:t0 + 512].rearrange("(c p) t -> p c t", p=128), writes=[bQKV])
                BGt, bBG = BGr.next()
                P.dma("sp", BGt[:], self.BG[t0:t0 + 512, :].rearrange("(s p) f -> p s f", p=128), writes=[bBG])
                OGT, bOGT = OGTr.next()
                if seq_start:
                    P.op("pool", lambda e: e.memset(S[:], 0.0), writes=bS)
                    P.op("pool", lambda e: e.memset(SB[:], 0.0), writes=bSB)
                for s in range(4):
                    tk = slice(s * 128, (s + 1) * 128)
                    beta = BGt[:, s, 0:8]
                    graw = BGt[:, s, 8:16]
                    P.op("dve", lambda e, beta=beta: e.tensor_scalar(NBG[:], beta, -1.0, None, ALU.mult), reads=[bBG], writes=[bNBG])
                    for h in range(8):
                        P.op("dve", lambda e, h=h, graw=graw: e.tensor_scalar(Rt[:, h, :], TRI[:], graw[:, h:h + 1], None, ALU.mult),
                             reads=[bC, bBG], writes=[bRt])
                    px, bpx = pF.next()
                    P.op("pe", lambda e, px=px, graw=graw: e.matmul(px[:, 0:8], SELA[:], graw, start=True, stop=True),
                         reads=[bC, bBG], writes=[bpx])
                    P.op("pe", lambda e, px=px, graw=graw: e.matmul(px[:, 8:16], SELB[:], graw, start=True, stop=True),
                         reads=[bC, bBG], writes=[bpx])
                    P.op("pe", lambda e, px=px, graw=graw: e.matmul(px[:, 16:24], TRI[:], graw, start=True, stop=True),
                         reads=[bC, bBG], writes=[bpx])
                    P.op("pe", lambda e, px=px, graw=graw: e.matmul(px[:, 24:32], SAME[:], graw, start=True, stop=True),
                         reads=[bC, bBG], writes=[bpx])
                    P.op("act", lambda e, px=px: e.activation(out=SM[:, 0:24], in_=px[:, 0:24], func=AF.Exp), reads=[bpx], writes=[bSM])
                    P.op("act", lambda e, px=px: e.copy(SM[:, 56:64], px[:, 16:24]), reads=[bpx, bSM], writes=[bSM])
                    P.op("dve", lambda e, px=px: e.tensor_tensor(out=SM[:, 32:40], in0=px[:, 24:32], in1=SM[:, 56:64], op=ALU.subtract),
                         reads=[bpx, bSM], writes=[bSM])
                    P.op("act", lambda e: e.activation(out=SM[:, 24:32], in_=SM[:, 32:40], func=AF.Exp), reads=[bSM], writes=[bSM])
                    for g in range(2):
                        pg, bpg = pF.next()
                        rv = flat(Rt, g)
                        P.op("pe", lambda e, pg=pg, rv=rv: e.matmul(pg[:], LGT[:], rv, start=True, stop=False), reads=[bC, bRt], writes=[bpg])
                        P.op("pe", lambda e, pg=pg, g=g: e.matmul(pg[:], identF[:], flat(NEGM8, g), start=False, stop=True),
                             reads=[bC, bidF], writes=[bpg])
                        P.op("act", lambda e, pg=pg, g=g: e.activation(out=flat(ET, g), in_=pg[:].rearrange("p (h c) -> p h c", h=4), func=AF.Exp),
                             reads=[bpg], writes=[bET])
                        pg2, bpg2 = pF.next()
                        P.op("pe", lambda e, pg2=pg2, rv=rv: e.matmul(pg2[:], ONES[:], rv, start=True, stop=True), reads=[bC, bRt], writes=[bpg2])
                        P.op("act", lambda e, pg2=pg2, g=g: e.activation(out=flat(EGB, g), in_=pg2[:].rearrange("p (h c) -> p h c", h=4), func=AF.Exp),
                             reads=[bpg2], writes=[bEGB])
                    P.op("pool", lambda e: e.tensor_tensor(out=ETS[:], in0=ET[:], in1=MSU8[:], op=ALU.mult), reads=[bET, bC], writes=[bETS])
                    if M3STOP <= 1:
                        continue
                    cur = 0
                    for g in range(2):
                        pk, bpk = pF.next()
                        pq, bpq = pF.next()
                        for hh in range(4):
                            h = 4 * g + hh
                            P.op("pe", lambda e, pk=pk, h=h, hh=hh, QKV=QKV, tk=tk: e.matmul(
                                pk[:, hh * 128:(hh + 1) * 128], QKV[:, 8 + h, tk], QKV[:, 8 + h, tk], start=True, stop=True),
                                reads=[bQKV], writes=[bpk])
                            P.op("pe", lambda e, pq=pq, h=h, hh=hh, QKV=QKV, tk=tk: e.matmul(
                                pq[:, hh * 128:(hh + 1) * 128], QKV[:, 8 + h, tk], QKV[:, h, tk], start=True, stop=True),
                                reads=[bQKV], writes=[bpq])
                        for hh in range(4):
                            h = 4 * g + hh
                            P.op("dve", lambda e, pk=pk, h=h, hh=hh: e.scalar_tensor_tensor(
                                out=YR[0][:, h, 0:128], in0=pk[:, hh * 128:(hh + 1) * 128], scalar=NBG[:, h:h + 1],
                                in1=ETS[:, h, :], op0=ALU.mult, op1=ALU.mult), reads=[bpk, bNBG, bETS], writes=[bYR[0][g]])
                        P.op("dve", lambda e, pq=pq, g=g: e.tensor_tensor(out=flat(AIT, g), in0=pq[:].rearrange("p (h c) -> p h c", h=4),
                                                                         in1=flat(ET, g), op=ALU.mult), reads=[bpq, bET], writes=[bAIT])
                    if M3STOP <= 2:
                        continue
                    for g in range(2):
                        P.op("pool", lambda e, g=g: e.tensor_tensor(out=YR[1][:, 4 * g:4 * g + 4, 128:256], in0=YR[0][:, 4 * g:4 * g + 4, 0:128],
                                                                    in1=flat(ID8, g), op=ALU.add), reads=[bYR[0][g], bC], writes=[bYR[1][g]])
                        if SD == BF16:
                            pz, bpz = pH.next()
                        else:
                            pz, bpz = pF.next()
                        for hh in range(4):
                            h = 4 * g + hh
                            P.op("pe", lambda e, pz=pz, h=h, hh=hh: e.transpose(pz[:, hh * 128:(hh + 1) * 128], YR[0][:, h, 0:128], identS[:]),
                                 reads=[bYR[0][g], bidS], writes=[bpz])
                        P.op("act", lambda e, pz=pz, g=g: e.copy(flat(Z[0], g), pz[:, 0:512].rearrange("p (h c) -> p h c", h=4)),
                             reads=[bpz], writes=[bZ[0][g]])
                    for k in range(6):
                        a, b_ = k % 2, (k + 1) % 2
                        for g in range(2):
                            last = (k == 5)
                            if not last:
                                pz, bpz = pF.next()
                                for hh in range(4):
                                    h = 4 * g + hh
                                    P.op("pe", lambda e, pz=pz, h=h, hh=hh, a=a: e.matmul(
                                        pz[:, hh * 128:(hh + 1) * 128], YR[a][:, h, 0:128], Z[a][:, h, :], start=True, stop=True),
                                        reads=[bYR[a][g], bZ[a][g]], writes=[bpz])
                                P.op("act", lambda e, pz=pz, g=g, b_=b_: e.copy(flat(Z[b_], g), pz[:].rearrange("p (h c) -> p h c", h=4)),
                                     reads=[bpz], writes=[bZ[b_][g]])
                            if k == 0:
                                py, bpy = pF.next()
                                for hh in range(4):
                                    h = 4 * g + hh
                                    P.op("pe", lambda e, py=py, h=h, hh=hh: e.matmul(
                                        py[:, hh * 128:(hh + 1) * 128], Z[0][:, h, :], YR[0][:, h, 0:128], start=True, stop=True),
                                        reads=[bYR[0][g], bZ[0][g]], writes=[bpy])
                                P.op("act", lambda e, py=py, g=g: e.copy(YR[1][:, 4 * g:4 * g + 4, 0:128], py[:].rearrange("p (h c) -> p h c", h=4)),
                                     reads=[bpy], writes=[bYR[1][g]])
                            elif not last:
                                for half in range(2):
                                    py, bpy = pF.next()
                                    for hh in range(2):
                                        h = 4 * g + 2 * half + hh
                                        P.op("pe", lambda e, py=py, h=h, hh=hh, a=a: e.matmul(
                                            py[:, hh * 256:(hh + 1) * 256], Z[a][:, h, :], YR[a][:, h, :], start=True, stop=True),
                                            reads=[bYR[a][g], bZ[a][g]], writes=[bpy])
                                    h0 = 4 * g + 2 * half
                                    pv = py[:].rearrange("p (h c) -> p h c", h=2)
                                    P.op("act", lambda e, pv=pv, h0=h0, b_=b_: e.copy(YR[b_][:, h0:h0 + 2, 0:128], pv[:, :, 0:128]),
                                         reads=[bpy], writes=[bYR[b_][g]])
                                    P.op("dve", lambda e, pv=pv, h0=h0, a=a, b_=b_: e.tensor_tensor(
                                        out=YR[b_][:, h0:h0 + 2, 128:256], in0=pv[:, :, 128:256], in1=YR[a][:, h0:h0 + 2, 128:256], op=ALU.add),
                                        reads=[bpy, bYR[a][g]], writes=[bYR[b_][g]])
                            else:
                                py, bpy = pF.next()
                                for hh in range(4):
                                    h = 4 * g + hh
                                    P.op("pe", lambda e, py=py, h=h, hh=hh, a=a: e.matmul(
                                        py[:, hh * 128:(hh + 1) * 128], Z[a][:, h, :], YR[a][:, h, 128:256], start=True, stop=True),
                                        reads=[bYR[a][g], bZ[a][g]], writes=[bpy])
                                P.op("dve", lambda e, py=py, g=g, a=a: e.tensor_tensor(
                                    out=flat(XS, g), in0=py[:].rearrange("p (h c) -> p h c", h=4), in1=YR[a][:, 4 * g:4 * g + 4, 128:256], op=ALU.add),
                                    reads=[bpy, bYR[a][g]], writes=[bXS[g]])
                    if M3STOP <= 3:
                        continue
                    pkt, bpkt = pH.next()
                    for h in range(8 if (M3SUB & 1) else 0):
                        P.op("pe", lambda e, pkt=pkt, h=h, QKV=QKV, tk=tk: e.transpose(pkt[:, h * 128:(h + 1) * 128], QKV[:, 8 + h, tk], identB[:]),
                             reads=[bQKV, bidB], writes=[bpkt])
                    P.op("act", lambda e, pkt=pkt: e.copy(KTOK[:].rearrange("p h c -> p (h c)"), pkt[:]), reads=[bpkt], writes=[bKTOK])
                    for h in range(8):
                        P.op("dve", lambda e, h=h: e.tensor_scalar(KG[:, h, :], KTOK[:, h, :], SM[:, 16 + h:17 + h], None, ALU.mult),
                             reads=[bKTOK, bSM], writes=[bKG])
                        P.op("act", lambda e, h=h: e.activation(out=KT2[:, h, :], in_=KTOK[:, h, :], func=AF.Identity, scale=SM[:, 24 + h:25 + h]),
                             reads=[bKTOK, bSM], writes=[bKT2])
                    pvt, bpvt = pH.next()
                    for h in range(8 if (M3SUB & 2) else 0):
                        P.op("pe", lambda e, pvt=pvt, h=h, QKV=QKV, tk=tk: e.transpose(pvt[:, h * 128:(h + 1) * 128], QKV[:, 16 + h, tk], identB[:]),
                             reads=[bQKV, bidB], writes=[bpvt])
                    if M3SUB & 2:
                        P.op("act", lambda e, pvt=pvt: e.copy(Vt[:].rearrange("p h c -> p (h c)"), pvt[:]), reads=[bpvt], writes=[bVt])
                    if M3SUB & 4:
                        P.op("pool", lambda e, QKV=QKV, tk=tk: e.tensor_tensor(out=QD[:], in0=QKV[:, 0:8, tk], in1=EGB[:], op=ALU.mult),
                             reads=[bQKV, bEGB], writes=[bQD])
                    if M3STOP <= 4:
                        continue
                    for g in range(2):
                        pu, bpu = pF.next()
                        pw, bpw = pF.next()
                        for hh in range(4):
                            h = 4 * g + hh
                            P.op("pe", lambda e, pu=pu, h=h, hh=hh: e.matmul(pu[:, hh * 128:(hh + 1) * 128], XS[:, h, :], Vt[:, h, :], start=True, stop=True),
                                 reads=[bXS[g], bVt], writes=[bpu])
                            P.op("pe", lambda e, pw=pw, h=h, hh=hh: e.matmul(pw[:, hh * 128:(hh + 1) * 128], KG[:, h, :], XS[:, h, :], start=True, stop=True),
                                 reads=[bXS[g], bKG], writes=[bpw])
                        for hh in range(4):
                            h = 4 * g + hh
                            P.op("act", lambda e, pu=pu, h=h, hh=hh, beta=beta: e.activation(out=UB[:, h, :], in_=pu[:, hh * 128:(hh + 1) * 128], func=AF.Identity,
                                                                                          scale=beta[:, h:h + 1]), reads=[bpu, bBG], writes=[bUB[h]])
                        P.op("act", lambda e, pw=pw, g=g: e.copy(flat(WT, g), pw[:].rearrange("p (h c) -> p h c", h=4)), reads=[bpw], writes=[bWT[g]])
                    if M3STOP <= 5:
                        continue
                    for ck in range(2):
                        rows = slice(ck * 64, (ck + 1) * 64)
                        for g in range(2):
                            pv_, bpv = pF.next()
                            for hh in range(4):
                                h = 4 * g + hh
                                P.op("pe", lambda e, pv_=pv_, h=h, hh=hh: e.matmul(pv_[:, hh * 128:(hh + 1) * 128], WT[:, h, :], SB[:, h, :], start=True, stop=True),
                                     reads=[bWT[g], bSB[h]], writes=[bpv])
                            for hh in range(4):
                                h = 4 * g + hh
                                P.op("dve", lambda e, pv_=pv_, h=h, hh=hh, rows=rows: e.scalar_tensor_tensor(
                                    out=VN[rows, h, :], in0=pv_[rows, hh * 128:(hh + 1) * 128], scalar=NBG[rows, h:h + 1], in1=UB[rows, h, :],
                                    op0=ALU.mult, op1=ALU.add), reads=[bpv, bNBG, bUB[h]], writes=[bVN[h]])
                            po, bpo = pF.next()
                            for hh in range(4):
                                h = 4 * g + hh
                                P.op("pe", lambda e, po=po, h=h, hh=hh: e.matmul(po[:, hh * 128:(hh + 1) * 128], QD[:, h, :], SB[:, h, :], start=True, stop=False),
                                     reads=[bQD, bSB[h]], writes=[bpo])
                                P.op("pe", lambda e, po=po, h=h, hh=hh: e.matmul(po[:, hh * 128:(hh + 1) * 128], AIT[:, h, :], VN[:, h, :], start=False, stop=True),
                                     reads=[bAIT, bVN[h]], writes=[bpo])
                            P.op("act", lambda e, po=po, g=g, rows=rows: e.copy(O[rows, 4 * g:4 * g + 4, :], po[rows, :].rearrange("p (h c) -> p h c", h=4)),
                                 reads=[bpo], writes=[bO])
                            ps_, bps = pF.next()
                            for hh in range(4):
                                h = 4 * g + hh
                                P.op("pe", lambda e, ps_=ps_, h=h, hh=hh, rows=rows: e.matmul(ps_[:, hh * 128:(hh + 1) * 128], KT2[rows, h, :], VN[rows, h, :], start=True, stop=True),
                                     reads=[bKT2, bVN[h]], writes=[bps])
                            for hh in range(4):
                                h = 4 * g + hh
                                P.op("dve", lambda e, ps_=ps_, h=h, hh=hh, ck=ck: e.scalar_tensor_tensor(
                                    out=S[:, h, :], in0=S[:, h, :], scalar=SM[:, ck * 8 + h:ck * 8 + h + 1], in1=ps_[:, hh * 128:(hh + 1) * 128],
                                    op0=ALU.mult, op1=ALU.add), reads=[bps, bSM, bS[h]], writes=[bS[h]])
                                P.op("act", lambda e, h=h: e.copy(SB[:, h, :], S[:, h, :]), reads=[bS[h]], writes=[bSB[h]])
                    if M3STOP <= 6:
                        continue
                    pz0, bpz0 = pF.next()
                    pz1, bpz1 = pF.next()
                    for hf, (pz_, bpz_) in enumerate(((pz0, bpz0), (pz1, bpz1))):
                        for kc in range(8):
                            P.op("pe", lambda e, pz_=pz_, kc=kc, hf=hf, xT=xT, tk=tk: e.matmul(pz_[:], xT[:, kc, tk], WZ[:, kc, hf * 512:(hf + 1) * 512],
                                                                                           start=(kc == 0), stop=(kc == 7)),
                                 reads=[bxT[kc], bWZ[kc]], writes=[bpz_])
                        P.op("act", lambda e, pz_=pz_, hf=hf: e.activation(out=ZG[:, 4 * hf:4 * hf + 4, :], in_=pz_[:].rearrange("p (h c) -> p h c", h=4), func=AF.Silu),
                             reads=[bpz_], writes=[bZG])
                    P.op("pool", lambda e: e.tensor_tensor(out=ZG[:], in0=ZG[:], in1=DNG[:], op=ALU.mult), reads=[bZG, bDNG], writes=[bZG])
                    P.op("pool", lambda e: e.tensor_tensor(out=SQ[:], in0=O[:], in1=O[:], op=ALU.mult), reads=[bO], writes=[bSQ])
                    P.op("dve", lambda e: e.tensor_reduce(out=SM[:, 40:48], in_=SQ[:], axis=AX.X, op=ALU.add), reads=[bSQ, bSM], writes=[bSM])
                    P.op("dve", lambda e: e.tensor_scalar(SM[:, 48:56], SM[:, 40:48], 1.0 / 128.0, float(NORM_EPS), ALU.mult, ALU.add), reads=[bSM], writes=[bSM])
                    P.op("act", lambda e: e.activation(out=SM[:, 48:56], in_=SM[:, 48:56], func=AF.Ln), reads=[bSM], writes=[bSM])
                    P.op("act", lambda e: e.activation(out=SM[:, 48:56], in_=SM[:, 48:56], func=AF.Exp, scale=-0.5), reads=[bSM], writes=[bSM])
                    for h in range(8):
                        P.op("dve", lambda e, h=h: e.scalar_tensor_tensor(out=OG[:, h, :], in0=O[:, h, :], scalar=SM[:, 48 + h:49 + h], in1=ZG[:, h, :],
                                                                          op0=ALU.mult, op1=ALU.mult), reads=[bO, bSM, bZG], writes=[bOG])
                    pt_, bpt = pH.next()
                    for h in range(8):
                        P.op("pe", lambda e, pt_=pt_, h=h: e.transpose(pt_[:, h * 128:(h + 1) * 128], OG[:, h, :], identB[:]),
                             reads=[bOG, bidB], writes=[bpt])
                    P.op("act", lambda e, pt_=pt_, OGT=OGT, tk=tk: e.copy(OGT[:, :, tk], pt_[:].rearrange("p (h c) -> p h c", h=8)),
                         reads=[bpt], writes=[bOGT])
                P.dma("sp", self.OGT[:, t0:t0 + 512].rearrange("(c p) t -> p c t", p=128), OGT[:], reads=[bOGT])

    def m4_phase(self, layer, src, dst):
        P = self.P
        T = self.T
        with self.phase() as ph:
            WA = ph.sb([128, 8, D], BF16)
            WB = ph.sb([128, 8, D], BF16)
            WO = ph.sb([128, 8, D], BF16)
            WG = ph.sb([128, 8, 2 * D], BF16)
            bWA, bWB, bWO, bWG = bufs(8), bufs(8), bufs(8), bufs(8)
            self.load_w(WA, bWA, self.w_a[layer], 8, 0, D, piece=1024)
            self.load_w(WB, bWB, self.w_b[layer], 8, 0, D, piece=1024)
            self.load_w(WG, bWG, self.w_in[layer], 8, C_GATE, C_GATE + 2 * D, piece=1024)
            self.load_w(WO, bWO, self.w_o[layer], 8, 0, D, piece=1024)
            ident, bid = self.make_ident(ph, F32)
            lnc = self.ln_consts(ph, layer, 1)
            NB = 2
            XSr = Rot([ph.sb([128, D], F32) for _ in range(2)])
            XRr = Rot([ph.sb([128, D], F32) for _ in range(2)])
            xTr = [ph.sb([128, 8, 512], BF16) for _ in range(NB)]
            bxTr = [bufs(8) for _ in range(NB)]
            ATr = Rot([ph.sb([128, 8, 512], BF16) for _ in range(2)])
            OGr = Rot([ph.sb([128, 8, 512], BF16) for _ in range(2)])
            MT = ph.sb([128, 8, 512], BF16)
            bMT = bufs(8)
            SGr = Rot([ph.sb([128, 512], F32) for _ in range(4)])
            T1r = Rot([ph.sb([128, 512], F32) for _ in range(2)])
            T2r = Rot([ph.sb([128, 512], F32) for _ in range(2)])
            small = Rot([ph.sb([128, 16], F32) for _ in range(2)])
            pT = Rot([ph.ps() for _ in range(2)])
            pM = Rot([ph.ps() for _ in range(4)])
            pY = [ph.ps() for _ in range(2)]
            bpY = bufs(2)
            for ti in range(T // 512):
                t0 = ti * 512
                xT, bxT = xTr[ti % NB], bxTr[ti % NB]
                self.load_x_T(ph, src, t0, XSr, xT, bxT, pT, ident, bid)
                At, bA = ATr.next()
                Og, bOg = OGr.next()
                P.dma("sp", At[:], self.AT[:, t0:t0 + 512].rearrange("(c p) t -> p c t", p=128), writes=[bA])
                P.dma("sp", Og[:], self.OGT[:, t0:t0 + 512].rearrange("(c p) t -> p c t", p=128), writes=[bOg])
                for n in range(8):
                    ns = slice(n * 128, (n + 1) * 128)
                    pa, bpa = pM.next()
                    pga, bpga = pM.next()
                    for kc in range(8):
                        P.op("pe", lambda e, pa=pa, kc=kc, ns=ns, At=At: e.matmul(pa[:], WA[:, kc, ns], At[:, kc, :], start=(kc == 0), stop=(kc == 7)),
                             reads=[bWA[kc], bA], writes=[bpa])
                    for kc in range(8):
                        P.op("pe", lambda e, pga=pga, kc=kc, ns=ns, xT=xT: e.matmul(pga[:], WG[:, kc, ns], xT[:, kc, :], start=(kc == 0), stop=(kc == 7)),
                             reads=[bWG[kc], bxT[kc]], writes=[bpga])
                    sga, bsga = SGr.next()
                    P.op("act", lambda e, sga=sga, pga=pga: e.activation(out=sga[:], in_=pga[:], func=AF.Sigmoid), reads=[bpga], writes=[bsga])
                    t1, bt1 = T1r.next()
                    P.op("dve", lambda e, t1=t1, pa=pa, sga=sga: e.tensor_tensor(out=t1[:], in0=pa[:], in1=sga[:], op=ALU.mult),
                         reads=[bpa, bsga], writes=[bt1])
                    pb, bpb = pM.next()
                    pgb, bpgb = pM.next()
                    for kc in range(8):
                        P.op("pe", lambda e, pb=pb, kc=kc, ns=ns, Og=Og: e.matmul(pb[:], WB[:, kc, ns], Og[:, kc, :], start=(kc == 0), stop=(kc == 7)),
                             reads=[bWB[kc], bOg], writes=[bpb])
                    for kc in range(8):
                        P.op("pe", lambda e, pgb=pgb, kc=kc, n=n, xT=xT: e.matmul(pgb[:], WG[:, kc, D + n * 128:D + (n + 1) * 128], xT[:, kc, :],
                                                                               start=(kc == 0), stop=(kc == 7)),
                             reads=[bWG[kc], bxT[kc]], writes=[bpgb])
                    sgb, bsgb = SGr.next()
                    P.op("act", lambda e, sgb=sgb, pgb=pgb: e.activation(out=sgb[:], in_=pgb[:], func=AF.Sigmoid), reads=[bpgb], writes=[bsgb])
                    t2, bt2 = T2r.next()
                    P.op("dve", lambda e, t2=t2, pb=pb, sgb=sgb: e.tensor_tensor(out=t2[:], in0=pb[:], in1=sgb[:], op=ALU.mult),
                         reads=[bpb, bsgb], writes=[bt2])
                    P.op("pool", lambda e, t1=t1, t2=t2, n=n: e.tensor_tensor(out=MT[:, n, :], in0=t1[:], in1=t2[:], op=ALU.add),
                         reads=[bt1, bt2], writes=[bMT[n]])
                for s in range(4):
                    for hf in range(2):
                        for kc in range(8):
                            P.op("pe", lambda e, hf=hf, kc=kc, s=s: e.matmul(pY[hf][:], MT[:, kc, s * 128:(s + 1) * 128], WO[:, kc, hf * 512:(hf + 1) * 512],
                                                                            start=(kc == 0), stop=(kc == 7)),
                                 reads=[bMT[kc], bWO[kc]], writes=[bpY[hf]])
                    self.ln_epilogue(pY, bpY, src, dst, t0 + s * 128, 1.0 / DN_ALPHA, lnc, XRr, small)

    def build(self, upto=99):
        cur = self.x
        n = 0
        for layer in range(self.depth):
            last = (layer == self.depth - 1)
            steps = [
                lambda: self.ffn_phase(layer, 0, cur, self.R[0], 0),
                lambda: self.m1_phase(layer, self.R[0]),
                lambda: self.m2_phase(layer),
                lambda: self.m3_phase(layer, self.R[0]),
                lambda: self.m4_phase(layer, self.R[0], self.R[1]),
                lambda: self.ffn_phase(layer, 1, self.R[1], self.out if last else self.R[0], 2),
            ]
            for st in steps:
                if n < upto:
                    st()
                n += 1
            cur = self.R[0]
        self.top.close()
        return self.nc


def host_pos_tables(rel_bias):
    s = np.arange(128)[:, None]
    q = np.arange(128)[None, :]
    out_b = np.zeros((128, 2, 16, 128), np.float32)
    out_m = np.zeros((128, 2, 16, 128), np.float32)
    for blk in range(2):
        j = s + 128 * blk
        rel = q + 128 - j
        valid = (rel >= 0) & (rel < 128)
        n = np.maximum(rel, 0)
        nf = np.maximum(n, 1).astype(np.float32)
        large = 16 + (np.log(nf / np.float32(16)) / np.float32(np.log(128 / 16)) * np.float32(16)).astype(np.int32)
        large = np.minimum(large, 31)
        bucket = np.where(n < 16, n, large)
        bucket = np.where(valid, bucket, 0)
        g = rel_bias[bucket]
        out_b[:, blk] = np.transpose(g, (0, 2, 1))
        out_m[:, blk] = np.broadcast_to(valid[:, None, :], (128, 16, 128))
    return out_b.reshape(128, -1), out_m.reshape(128, -1)


_NC_CACHE = {}


def kernel(x, rel_bias, ln_g, ln_b, ffn_w13, ffn_w2, w_in, conv_w, a_log, dt_bias,
           dn_norm_g, sinks, w_branch_a, w_branch_b, w_out):
    x = np.asarray(x, np.float32)
    B, L, _ = x.shape
    nseq = B // NCORES
    key = (nseq, L)
    if key not in _NC_CACHE:
        _NC_CACHE[key] = KB(nseq, L, DEPTH).build()
    nc = _NC_CACHE[key]
    pb, pm = host_pos_tables(np.asarray(rel_bias, np.float32))
    f = lambda a: np.ascontiguousarray(np.asarray(a, np.float32))
    shared = dict(ln_g=f(ln_g), ln_b=f(ln_b), ffn_w13=f(ffn_w13), ffn_w2=f(ffn_w2), w_in=f(w_in), conv_w=f(conv_w),
                  a_log=f(a_log), dt_bias=f(dt_bias), dn_norm_g=f(dn_norm_g), sinks=f(sinks),
                  w_branch_a=f(w_branch_a), w_branch_b=f(w_branch_b), w_out=f(w_out), pbias=pb, pmask=pm)
    in_maps = []
    for c in range(NCORES):
        m = dict(shared)
        m["x"] = np.ascontiguousarray(x[c * nseq:(c + 1) * nseq].reshape(nseq * L, D))
        in_maps.append(m)
    res = run_bass_kernel_spmd(nc, in_maps, core_ids=list(range(NCORES)))
    outs = [np.asarray(r["out"], np.float32).reshape(nseq, L, D) for r in res.results]
    return np.concatenate(outs, axis=0)
```

```python
import contextlib
import numpy as np
import concourse.bass as bass
import concourse.mybir as mybir
from concourse.bass_utils import run_bass_kernel_spmd

F32 = mybir.dt.float32
BF16 = mybir.dt.bfloat16
AF = mybir.ActivationFunctionType
ALU = mybir.AluOpType
AX = mybir.AxisListType

D = 1024
DFF = 2816
NIN = 7696
DEPTH = 4
SEQ = 4096
NCORES = 8
LN_EPS = 1e-5
NORM_EPS = 1e-6
DN_ALPHA = (2 * DEPTH) ** 0.25
C_Q0, C_KA, C_VA, C_QKVB, C_BETA, C_DT, C_Z, C_GATE = 0, 1024, 1280, 1536, 4608, 4616, 4624, 5648
SOLVE_DT = BF16
import os
M3STOP = int(os.environ.get("M3STOP", "99"))
M3SUB = int(os.environ.get("M3SUB", "7"))

NDMASEM = 16
ENGS = ("pe", "act", "dve", "pool", "sp")


class Buf:
    __slots__ = ("lw", "rd")

    def __init__(self):
        self.lw = None
        self.rd = []


def bufs(n):
    return [Buf() for _ in range(n)]


class Prog:
    def __init__(self, nc, stack):
        self.nc = nc
        self.ops = {e: [] for e in ENGS}
        self.cnt = {e: 0 for e in ("pe", "act", "dve", "pool")}
        self.seen = {e: {} for e in ENGS}
        self.dq_n = {"sp": 0, "pool": 0}
        self.esem = {e: stack.enter_context(nc.semaphore("s_" + e)) for e in ("pe", "act", "dve", "pool")}
        self.dsem = {}
        for q in ("sp", "pool"):
            for k in range(NDMASEM):
                self.dsem[(q, k)] = stack.enter_context(nc.semaphore("d_%s%d" % (q, k)))
        self.ninstr = 0

    def _kv(self, tok):
        if tok[0] == "e":
            return ("e", tok[1]), tok[2]
        q, i = tok[1], tok[2]
        return ("d", q, i % NDMASEM), 16 * (i // NDMASEM + 1)

    def _deps(self, eng, reads, writes):
        need = {}

        def add(tok):
            if tok is None:
                return
            if tok[0] == "e" and tok[1] == "pe" and eng == "pe":
                return
            k, v = self._kv(tok)
            if need.get(k, 0) < v:
                need[k] = v

        for b in reads:
            add(b.lw)
        for b in writes:
            add(b.lw)
            for t in b.rd:
                add(t)
        out = []
        s = self.seen[eng]
        for k, v in need.items():
            if s.get(k, 0) < v:
                s[k] = v
                out.append((k, v))
        return out

    def _commit(self, tok, reads, writes):
        for b in reads:
            b.rd.append(tok)
            if len(b.rd) > 32:
                best = {}
                for t in b.rd:
                    k, v = self._kv(t)
                    if k not in best or best[k][0] < v:
                        best[k] = (v, t)
                b.rd = [t for (_, t) in best.values()]
        for b in writes:
            b.lw = tok
            b.rd = []

    def op(self, eng, fn, reads=(), writes=()):
        waits = self._deps(eng, reads, writes)
        self.cnt[eng] += 1
        tok = ("e", eng, self.cnt[eng])
        self.ops[eng].append((waits, fn, None))
        self._commit(tok, reads, writes)

    def dma(self, q, out_ap, in_ap, reads=(), writes=(), slow=False):
        i = self.dq_n[q]
        self.dq_n[q] += 1
        tok = ("d", q, i)
        waits = self._deps(q, reads, writes)
        if i >= NDMASEM:
            k, v = self._kv(("d", q, i - NDMASEM))
            if self.seen[q].get(k, 0) < v:
                self.seen[q][k] = v
                waits.append((k, v))
        self.ops[q].append((waits, (out_ap, in_ap, slow), tok))
        self._commit(tok, reads, writes)

    def barrier(self):
        allk = []
        for e, c in self.cnt.items():
            if c:
                allk.append((("e", e), c))
        for q, n in self.dq_n.items():
            for k in range(min(NDMASEM, n)):
                last = ((n - 1 - k) // NDMASEM) * NDMASEM + k
                allk.append((("d", q, k), 16 * (last // NDMASEM + 1)))
        for e in ENGS:
            s = self.seen[e]
            waits = []
            for k, v in allk:
                if k == ("e", "pe") and e == "pe":
                    continue
                if s.get(k, 0) < v:
                    s[k] = v
                    waits.append((k, v))
            if waits:
                self.ops[e].append((waits, None, None))

    def emit(self):
        nc = self.nc

        def semof(k):
            return self.esem[k[1]] if k[0] == "e" else self.dsem[(k[1], k[2])]

        def run(engname, e):
            for waits, fn, tok in self.ops[engname]:
                for k, v in waits:
                    e.wait_ge(semof(k), v)
                if fn is None:
                    continue
                self.ninstr += 1
                if tok is None:
                    fn(e).then_inc(self.esem[engname], 1)
                else:
                    o, i, slow = fn
                    if slow:
                        ins = e.dma_start(out=o, in_=i, allow_slow_non_contiguous=True)
                    else:
                        ins = e.dma_start(out=o, in_=i)
                    ins.then_inc(self.dsem[(tok[1], tok[2] % NDMASEM)], 16)
            self.ops[engname] = []

        with nc.Block() as block:
            @block.tensor
            def _(e):
                run("pe", e)

            @block.scalar
            def _(e):
                run("act", e)

            @block.vector
            def _(e):
                run("dve", e)

            @block.gpsimd
            def _(e):
                run("pool", e)

            @block.sync
            def _(e):
                run("sp", e)


class Phase:
    def __init__(self, kb):
        self.kb = kb
        self.st = contextlib.ExitStack()
        self.n = 0

    def __enter__(self):
        self.st.__enter__()
        return self

    def __exit__(self, *a):
        self.kb.P.barrier()
        self.kb.P.emit()
        return self.st.__exit__(*a)

    def sb(self, shape, dt):
        self.n += 1
        return self.st.enter_context(self.kb.nc.sbuf_tensor("t%d_%d" % (self.kb.phase_id, self.n), list(shape), dt))

    def ps(self, dt=F32):
        self.n += 1
        cols = 512 if dt == F32 else 1024
        return self.st.enter_context(self.kb.nc.psum_tensor("p%d_%d" % (self.kb.phase_id, self.n), [128, cols], dt))


class Rot:
    def __init__(self, tiles):
        self.t = tiles
        self.b = bufs(len(tiles))
        self.i = 0

    def next(self):
        k = self.i % len(self.t)
        self.i += 1
        return self.t[k], self.b[k]


class KB:
    def __init__(self, nseq, L, depth, debug=False):
        self.nseq, self.L, self.depth = nseq, L, depth
        self.T = nseq * L
        self.debug = debug
        self.nc = bass.Bass("TRN2", target_bir_lowering=False)
        self.top = contextlib.ExitStack()
        self.phase_id = 0
        nc = self.nc
        T = self.T
        ext = lambda n, s: nc.dram_tensor(n, list(s), F32, kind="ExternalInput").ap()
        self.x = ext("x", [T, D])
        self.ln_g = ext("ln_g", [DEPTH, 3, D])
        self.ln_b = ext("ln_b", [DEPTH, 3, D])
        self.w13 = ext("ffn_w13", [DEPTH, 2, D, 2 * DFF])
        self.w2 = ext("ffn_w2", [DEPTH, 2, DFF, D])
        self.w_in = ext("w_in", [DEPTH, D, NIN])
        self.conv_w = ext("conv_w", [DEPTH, 4, 3072])
        self.a_log = ext("a_log", [DEPTH, 8])
        self.dt_bias = ext("dt_bias", [DEPTH, 8])
        self.dn_g = ext("dn_norm_g", [DEPTH, 128])
        self.sinks = ext("sinks", [DEPTH, 16])
        self.w_a = ext("w_branch_a", [DEPTH, D, D])
        self.w_b = ext("w_branch_b", [DEPTH, D, D])
        self.w_o = ext("w_out", [DEPTH, D, D])
        self.pbias = ext("pbias", [128, 2 * 16 * 128])
        self.pmask = ext("pmask", [128, 2 * 16 * 128])
        self.out = nc.dram_tensor("out", [T, D], F32, kind="ExternalOutput").ap()
        kind = "ExternalOutput" if debug else "Internal"
        scr = lambda n, s, dt: nc.dram_tensor(n, list(s), dt, kind=kind).ap()
        self.R = [scr("res%d" % i, [T, D], F32) for i in range(2)]
        self.QT = scr("QT", [1024, T], BF16)
        self.KT = scr("KT", [256, T], BF16)
        self.VA = scr("VA", [T, 256], BF16)
        self.QKVB = scr("QKVB", [3072, T], BF16)
        self.BG = scr("BG", [T, 16], F32)
        self.AT = scr("AT", [1024, T], BF16)
        self.OGT = scr("OGT", [1024, T], BF16)
        self.P = Prog(nc, self.top)

    def phase(self):
        self.phase_id += 1
        return Phase(self)

    def make_ident(self, ph, dt):
        P = self.P
        t = ph.sb([128, 128], dt)
        b = Buf()
        P.op("pool", lambda e: e.memset(t[:], 0.0), writes=[b])
        P.op("pool", lambda e: e.affine_select(out=t[:], in_=t[:], pattern=[[-1, 128]], compare_op=ALU.not_equal,
                                               fill=1.0, base=0, channel_multiplier=1), reads=[b], writes=[b])
        return t, b

    def load_w(self, dst, dbufs, src, kcs, c0, c1, piece=2048):
        v = src.rearrange("(kc p) n -> p kc n", p=128)
        for kc in range(kcs):
            a = c0
            while a < c1:
                b = min(c1, a + piece)
                self.P.dma("pool", dst[:, kc, a - c0:b - c0], v[:, kc, a:b], writes=[dbufs[kc]])
                a = b

    def issue_x(self, src, t0, XS4, nsub=4):
        out = []
        for s in range(nsub):
            Xs, bXs = XS4.next()
            self.P.dma("sp", Xs[:], src[t0 + s * 128:t0 + (s + 1) * 128, :], writes=[bXs])
            out.append((Xs, bXs))
        return out

    def transpose_x(self, xs, xT, bxT, pT, ident, bid):
        P = self.P
        for s, (Xs, bXs) in enumerate(xs):
            for half in range(2):
                pt, bpt = pT.next()
                for k4 in range(4):
                    kc = half * 4 + k4
                    P.op("pe", lambda e, pt=pt, k4=k4, kc=kc, Xs=Xs: e.transpose(pt[:, k4 * 128:(k4 + 1) * 128],
                                                                              Xs[:, kc * 128:(kc + 1) * 128], ident[:]),
                         reads=[bXs, bid], writes=[bpt])
                wb = [bxT[half * 4 + k4] for k4 in range(4)]
                dst = xT[:, half * 4:half * 4 + 4, s * 128:(s + 1) * 128]
                srcv = pt[:].rearrange("p (k c) -> p k c", k=4)
                if half == 0:
                    P.op("act", lambda e, dst=dst, srcv=srcv: e.copy(dst, srcv), reads=[bpt], writes=wb)
                else:
                    P.op("dve", lambda e, dst=dst, srcv=srcv: e.tensor_copy(dst, srcv), reads=[bpt], writes=wb)

    def ln_consts(self, ph, layer, idx):
        P = self.P
        G = ph.sb([128, D], F32)
        B = ph.sb([128, D], F32)
        bg, bb = Buf(), Buf()
        P.dma("sp", G[:], self.ln_g[layer, idx, :].partition_broadcast(128), writes=[bg])
        P.dma("sp", B[:], self.ln_b[layer, idx, :].partition_broadcast(128), writes=[bb])
        return (G, bg, B, bb)

    def ln_epilogue(self, py, bpy, src, dst, r0, c, lnc, XRr, small):
        P = self.P
        G, bg, B, bb = lnc
        Rt, bR = XRr.next()
        st, bst = small.next()
        P.dma("sp", Rt[:], src[r0:r0 + 128, :], writes=[bR])
        for hf in range(2):
            P.op("dve", lambda e, hf=hf, Rt=Rt: e.scalar_tensor_tensor(
                out=Rt[:, hf * 512:(hf + 1) * 512], in0=py[hf][:], scalar=float(c),
                in1=Rt[:, hf * 512:(hf + 1) * 512], op0=ALU.mult, op1=ALU.add),
                reads=[bpy[hf], bR], writes=[bR])
        for hf in range(2):
            P.op("dve", lambda e, hf=hf, Rt=Rt, st=st: e.bn_stats(st[:, hf * 6:(hf + 1) * 6], Rt[:, hf * 512:(hf + 1) * 512]),
                 reads=[bR], writes=[bst])
        P.op("dve", lambda e, st=st: e.bn_aggr(st[:, 12:14], st[:, 0:12]), reads=[bst], writes=[bst])
        eps = LN_EPS / (DN_ALPHA ** 2)
        P.op("dve", lambda e, st=st: e.tensor_scalar(st[:, 13:14], st[:, 13:14], float(eps), None, ALU.add),
             reads=[bst], writes=[bst])
        P.op("act", lambda e, st=st: e.activation(out=st[:, 14:15], in_=st[:, 13:14], func=AF.Ln),
             reads=[bst], writes=[bst])
        P.op("act", lambda e, st=st: e.activation(out=st[:, 14:15], in_=st[:, 14:15], func=AF.Exp, scale=-0.5),
             reads=[bst], writes=[bst])
        P.op("dve", lambda e, st=st: e.scalar_tensor_tensor(out=st[:, 15:16], in0=st[:, 12:13], scalar=-1.0,
                                                            in1=st[:, 14:15], op0=ALU.mult, op1=ALU.mult),
             reads=[bst], writes=[bst])
        P.op("act", lambda e, st=st, Rt=Rt: e.activation(out=Rt[:], in_=Rt[:], func=AF.Identity,
                                                        bias=st[:, 15:16], scale=st[:, 14:15]),
             reads=[bst, bR], writes=[bR])
        P.op("pool", lambda e, Rt=Rt: e.tensor_tensor(out=Rt[:], in0=Rt[:], in1=G[:], op=ALU.mult),
             reads=[bR, bg], writes=[bR])
        P.op("pool", lambda e, Rt=Rt: e.tensor_tensor(out=Rt[:], in0=Rt[:], in1=B[:], op=ALU.add),
             reads=[bR, bb], writes=[bR])
        P.dma("pool", dst[r0:r0 + 128, :], Rt[:], reads=[bR])

    def ffn_phase(self, layer, which, src, dst, ln_idx):
        P = self.P
        T = self.T
        with self.phase() as ph:
            W13 = ph.sb([128, 8, 2 * DFF], BF16)
            W2 = ph.sb([128, 22, D], BF16)
            bW13, bW2 = bufs(8), bufs(22)
            self.load_w(W13, bW13, self.w13[layer, which], 8, 0, 2 * DFF, piece=1408)
            self.load_w(W2, bW2, self.w2[layer, which], 22, 0, D, piece=1024)
            ident, bid = self.make_ident(ph, F32)
            lnc = self.ln_consts(ph, layer, ln_idx)
            NB = 2
            XSr = Rot([ph.sb([128, D], F32) for _ in range(4)])
            XRr = Rot([ph.sb([128, D], F32) for _ in range(2)])
            xTr = [ph.sb([128, 8, 512], BF16) for _ in range(NB)]
            bxTr = [bufs(8) for _ in range(NB)]
            HT = ph.sb([128, 22, 512], BF16)
            bHT = bufs(22)
            SG = Rot([ph.sb([128, 512], F32) for _ in range(2)])
            small = Rot([ph.sb([128, 16], F32) for _ in range(2)])
            pT = Rot([ph.ps() for _ in range(2)])
            pGU = Rot([ph.ps() for _ in range(4)])
            pY = [ph.ps() for _ in range(2)]
            bpY = bufs(2)
            ntiles = T // 512
            xs_next = self.issue_x(src, 0, XSr)
            self.transpose_x(xs_next, xTr[0], bxTr[0], pT, ident, bid)
            for ti in range(ntiles):
                t0 = ti * 512
                xT, bxT = xTr[ti % NB], bxTr[ti % NB]
                if ti + 1 < ntiles:
                    xs_next = self.issue_x(src, t0 + 512, XSr)
                for j in range(22):
                    pg, bpg = pGU.next()
                    pu, bpu = pGU.next()
                    for kc in range(8):
                        P.op("pe", lambda e, pg=pg, kc=kc, j=j, xT=xT: e.matmul(
                            pg[:], W13[:, kc, j * 128:(j + 1) * 128], xT[:, kc, :], start=(kc == 0), stop=(kc == 7)),
                            reads=[bW13[kc], bxT[kc]], writes=[bpg])
                    for kc in range(8):
                        P.op("pe", lambda e, pu=pu, kc=kc, j=j, xT=xT: e.matmul(
                            pu[:], W13[:, kc, DFF + j * 128:DFF + (j + 1) * 128], xT[:, kc, :], start=(kc == 0), stop=(kc == 7)),
                            reads=[bW13[kc], bxT[kc]], writes=[bpu])
                    sg, bsg = SG.next()
                    P.op("act", lambda e, sg=sg, pg=pg: e.activation(out=sg[:], in_=pg[:], func=AF.Silu),
                         reads=[bpg], writes=[bsg])
                    P.op("dve", lambda e, sg=sg, pu=pu, j=j: e.tensor_tensor(out=HT[:, j, :], in0=pu[:], in1=sg[:], op=ALU.mult),
                         reads=[bpu, bsg], writes=[bHT[j]])
                if ti + 1 < ntiles:
                    self.transpose_x(xs_next, xTr[(ti + 1) % NB], bxTr[(ti + 1) % NB], pT, ident, bid)
                for s in range(4):
                    for hf in range(2):
                        for j in range(22):
                            P.op("pe", lambda e, hf=hf, j=j, s=s: e.matmul(
                                pY[hf][:], HT[:, j, s * 128:(s + 1) * 128], W2[:, j, hf * 512:(hf + 1) * 512],
                                start=(j == 0), stop=(j == 21)),
                                reads=[bHT[j], bW2[j]], writes=[bpY[hf]])
                    self.ln_epilogue(pY, bpY, src, dst, t0 + s * 128, 0.5 / DN_ALPHA, lnc, XRr, small)

    def m1_phase(self, layer, src):
        P = self.P
        T, L = self.T, self.L
        NW = C_Z
        with self.phase() as ph:
            W = ph.sb([128, 8, NW], BF16)
            bW = bufs(8)
            self.load_w(W, bW, self.w_in[layer], 8, 0, NW, piece=1156)
            ident, bid = self.make_ident(ph, F32)
            ones = ph.sb([128, 128], BF16)
            bones = Buf()
            P.op("pool", lambda e: e.memset(ones[:], 1.0), writes=[bones])
            CW = ph.sb([128, 4, 24], F32)
            bCW = Buf()
            for j in range(4):
                P.dma("sp", CW[:, j, :], self.conv_w[layer, j, :].rearrange("(c p) -> p c", p=128), writes=[bCW], slow=True)
            DTB = ph.sb([128, 8], F32)
            NEGA = ph.sb([128, 8], F32)
            bDTB, bNEGA = Buf(), Buf()
            P.dma("sp", DTB[:], self.dt_bias[layer, :].partition_broadcast(128), writes=[bDTB])
            P.dma("sp", NEGA[:], self.a_log[layer, :].partition_broadcast(128), writes=[bNEGA])
            P.op("act", lambda e: e.activation(out=NEGA[:], in_=NEGA[:], func=AF.Exp), reads=[bNEGA], writes=[bNEGA])
            P.op("dve", lambda e: e.tensor_scalar(NEGA[:], NEGA[:], -1.0, None, ALU.mult), reads=[bNEGA], writes=[bNEGA])
            CAR = ph.sb([128, 24, 3], F32)
            bCAR = bufs(24)
            NB = 2
            XSr = Rot([ph.sb([128, D], F32) for _ in range(4)])
            xTr = [ph.sb([128, 8, 512], BF16) for _ in range(NB)]
            bxTr = [bufs(8) for _ in range(NB)]
            QAr = Rot([ph.sb([128, 10, 512], BF16) for _ in range(2)])
            VAr = Rot([ph.sb([128, 4, 256], BF16) for _ in range(2)])
            BGr = Rot([ph.sb([128, 4, 16], F32) for _ in range(2)])
            TMP = Rot([ph.sb([128, 56], F32) for _ in range(2)])
            Ur = Rot([ph.sb([128, 515], F32) for _ in range(3)])
            ACr = Rot([ph.sb([128, 512], F32) for _ in range(3)])
            AC2r = Rot([ph.sb([128, 512], F32) for _ in range(3)])
            Y8 = ph.sb([128, 8, 512], F32)
            SQ8 = ph.sb([128, 8, 512], BF16)
            bY8, bSQ8 = bufs(8), bufs(8)
            RSr = Rot([ph.sb([128, 512], F32) for _ in range(2)])
            OCr = Rot([ph.sb([128, 8, 512], BF16) for _ in range(3)])
            pT = Rot([ph.ps() for _ in range(2)])
            pA = Rot([ph.ps() for _ in range(3)])
            pB = Rot([ph.ps() for _ in range(2)])
            pS = Rot([ph.ps() for _ in range(1)])
            ntiles = T // 512
            xs_next = self.issue_x(src, 0, XSr)
            self.transpose_x(xs_next, xTr[0], bxTr[0], pT, ident, bid)
            for ti in range(ntiles):
                t0 = ti * 512
                seq_start = (t0 % L == 0)
                xT, bxT = xTr[ti % NB], bxTr[ti % NB]
                if ti + 1 < ntiles:
                    xs_next = self.issue_x(src, t0 + 512, XSr)
                QA, bQA = QAr.next()
                for c in range(10):
                    pa, bpa = pA.next()
                    for kc in range(8):
                        P.op("pe", lambda e, pa=pa, kc=kc, c=c, xT=xT: e.matmul(
                            pa[:], W[:, kc, c * 128:(c + 1) * 128], xT[:, kc, :], start=(kc == 0), stop=(kc == 7)),
                            reads=[bW[kc], bxT[kc]], writes=[bpa])
                    sc = 0.125 if c < 8 else 1.0
                    P.op("act", lambda e, pa=pa, c=c, QA=QA, sc=sc: e.activation(out=QA[:, c, :], in_=pa[:], func=AF.Copy, scale=sc),
                         reads=[bpa], writes=[bQA])
                P.dma("pool", self.QT[:, t0:t0 + 512].rearrange("(c p) t -> p c t", p=128), QA[:, 0:8, :], reads=[bQA])
                P.dma("pool", self.KT[:, t0:t0 + 512].rearrange("(c p) t -> p c t", p=128), QA[:, 8:10, :], reads=[bQA])
                VAt, bVA = VAr.next()
                BGt, bBG = BGr.next()
                for s in range(4):
                    pb, bpb = pB.next()
                    for kc in range(8):
                        P.op("pe", lambda e, pb=pb, kc=kc, s=s, xT=xT: e.matmul(
                            pb[:, 0:256], xT[:, kc, s * 128:(s + 1) * 128], W[:, kc, C_VA:C_VA + 256],
                            start=(kc == 0), stop=(kc == 7)), reads=[bW[kc], bxT[kc]], writes=[bpb])
                    for kc in range(8):
                        P.op("pe", lambda e, pb=pb, kc=kc, s=s, xT=xT: e.matmul(
                            pb[:, 256:272], xT[:, kc, s * 128:(s + 1) * 128], W[:, kc, C_BETA:C_BETA + 16],
                            start=(kc == 0), stop=(kc == 7)), reads=[bW[kc], bxT[kc]], writes=[bpb])
                    P.op("act", lambda e, pb=pb, s=s, VAt=VAt: e.copy(VAt[:, s, :], pb[:, 0:256]), reads=[bpb], writes=[bVA])
                    tm, btm = TMP.next()
                    P.op("act", lambda e, pb=pb, tm=tm: e.copy(tm[:, 40:56], pb[:, 256:272]), reads=[bpb], writes=[btm])
                    P.op("act", lambda e, tm=tm, s=s, BGt=BGt: e.activation(out=BGt[:, s, 0:8], in_=tm[:, 40:48], func=AF.Exp, scale=-1.0),
                         reads=[btm], writes=[bBG])
                    P.op("dve", lambda e, s=s, BGt=BGt: e.tensor_scalar(BGt[:, s, 0:8], BGt[:, s, 0:8], 1.0, None, ALU.add),
                         reads=[bBG], writes=[bBG])
                    P.op("dve", lambda e, s=s, BGt=BGt: e.reciprocal(BGt[:, s, 0:8], BGt[:, s, 0:8]), reads=[bBG], writes=[bBG])
                    P.op("dve", lambda e, tm=tm: e.tensor_tensor(out=tm[:, 0:8], in0=tm[:, 48:56], in1=DTB[:], op=ALU.add),
                         reads=[btm, bDTB], writes=[btm])
                    P.op("dve", lambda e, tm=tm: e.tensor_scalar(tm[:, 8:16], tm[:, 0:8], -1.0, None, ALU.mult),
                         reads=[btm], writes=[btm])
                    P.op("dve", lambda e, tm=tm: e.tensor_tensor(out=tm[:, 8:16], in0=tm[:, 8:16], in1=tm[:, 0:8], op=ALU.max),
                         reads=[btm], writes=[btm])
                    P.op("act", lambda e, tm=tm: e.activation(out=tm[:, 16:24], in_=tm[:, 8:16], func=AF.Exp, scale=-1.0),
                         reads=[btm], writes=[btm])
                    P.op("dve", lambda e, tm=tm: e.tensor_scalar(tm[:, 16:24], tm[:, 16:24], 1.0, None, ALU.add), reads=[btm], writes=[btm])
                    P.op("act", lambda e, tm=tm: e.activation(out=tm[:, 24:32], in_=tm[:, 16:24], func=AF.Ln),
                         reads=[btm], writes=[btm])
                    P.op("dve", lambda e, tm=tm: e.scalar_tensor_tensor(out=tm[:, 32:40], in0=tm[:, 0:8], scalar=0.0,
                                                                        in1=tm[:, 24:32], op0=ALU.max, op1=ALU.add),
                         reads=[btm], writes=[btm])
                    P.op("dve", lambda e, tm=tm, s=s, BGt=BGt: e.tensor_tensor(out=BGt[:, s, 8:16], in0=tm[:, 32:40], in1=NEGA[:], op=ALU.mult),
                         reads=[btm, bNEGA], writes=[bBG])
                P.dma("pool", self.VA[t0:t0 + 512, :].rearrange("(s p) f -> p s f", p=128), VAt[:], reads=[bVA])
                P.dma("pool", self.BG[t0:t0 + 512, :].rearrange("(s p) f -> p s f", p=128), BGt[:], reads=[bBG])
                for grp in range(3):
                    OC, bOC = OCr.next()
                    for cc in range(8):
                        c = grp * 8 + cc
                        pa, bpa = pA.next()
                        col = C_QKVB + c * 128
                        for kc in range(8):
                            P.op("pe", lambda e, pa=pa, kc=kc, col=col, xT=xT: e.matmul(
                                pa[:], W[:, kc, col:col + 128], xT[:, kc, :], start=(kc == 0), stop=(kc == 7)),
                                reads=[bW[kc], bxT[kc]], writes=[bpa])
                        U, bU = Ur.next()
                        if seq_start:
                            P.op("dve", lambda e, U=U: e.memset(U[:, 0:3], 0.0), writes=[bU])
                        else:
                            P.op("dve", lambda e, U=U, c=c: e.tensor_copy(U[:, 0:3], CAR[:, c, :]), reads=[bCAR[c]], writes=[bU])
                        P.op("act", lambda e, U=U, pa=pa: e.copy(U[:, 3:515], pa[:]), reads=[bpa], writes=[bU])
                        P.op("dve", lambda e, U=U, c=c: e.tensor_copy(CAR[:, c, :], U[:, 512:515]), reads=[bU], writes=[bCAR[c]])
                        A1, bA1 = ACr.next()
                        A2, bA2 = AC2r.next()
                        P.op("act", lambda e, U=U, A1=A1, c=c: e.activation(out=A1[:], in_=U[:, 0:512], func=AF.Identity, scale=CW[:, 0, c:c + 1]),
                             reads=[bU, bCW], writes=[bA1])
                        P.op("act", lambda e, U=U, A2=A2, c=c: e.activation(out=A2[:], in_=U[:, 2:514], func=AF.Identity, scale=CW[:, 2, c:c + 1]),
                             reads=[bU, bCW], writes=[bA2])
                        P.op("dve", lambda e, U=U, A1=A1, c=c: e.scalar_tensor_tensor(out=A1[:], in0=U[:, 1:513], scalar=CW[:, 1, c:c + 1],
                                                                                    in1=A1[:], op0=ALU.mult, op1=ALU.add),
                             reads=[bU, bCW, bA1], writes=[bA1])
                        P.op("dve", lambda e, U=U, A2=A2, c=c: e.scalar_tensor_tensor(out=A2[:], in0=U[:, 3:515], scalar=CW[:, 3, c:c + 1],
                                                                                    in1=A2[:], op0=ALU.mult, op1=ALU.add),
                             reads=[bU, bCW, bA2], writes=[bA2])
                        P.op("pool", lambda e, A1=A1, A2=A2: e.tensor_tensor(out=A2[:], in0=A1[:], in1=A2[:], op=ALU.add),
                             reads=[bA1, bA2], writes=[bA2])
                        if grp == 2:
                            P.op("act", lambda e, A2=A2, OC=OC, cc=cc: e.activation(out=OC[:, cc, :], in_=A2[:], func=AF.Silu),
                                 reads=[bA2], writes=[bOC])
                        else:
                            P.op("act", lambda e, A2=A2, cc=cc: e.activation(out=Y8[:, cc, :], in_=A2[:], func=AF.Silu), reads=[bA2], writes=[bY8[cc]])
                            P.op("dve", lambda e, cc=cc: e.tensor_tensor(out=SQ8[:, cc, :], in0=Y8[:, cc, :], in1=Y8[:, cc, :], op=ALU.mult),
                                 reads=[bY8[cc]], writes=[bSQ8[cc]])
                    if grp < 2:
                        for cc in range(8):
                            RS, bRS = RSr.next()
                            ps_, bps = pS.next()
                            P.op("pe", lambda e, ps_=ps_, cc=cc: e.matmul(ps_[:], ones[:], SQ8[:, cc, :], start=True, stop=True),
                                 reads=[bones, bSQ8[cc]], writes=[bps])
                            P.op("dve", lambda e, ps_=ps_, RS=RS: e.tensor_scalar(RS[:], ps_[:], float(NORM_EPS), None, ALU.add),
                                 reads=[bps], writes=[bRS])
                            P.op("act", lambda e, RS=RS: e.activation(out=RS[:], in_=RS[:], func=AF.Ln), reads=[bRS], writes=[bRS])
                            P.op("act", lambda e, RS=RS: e.activation(out=RS[:], in_=RS[:], func=AF.Exp, scale=-0.5), reads=[bRS], writes=[bRS])
                            qs = (128.0 ** -0.5) if grp == 0 else 1.0
                            P.op("dve", lambda e, RS=RS, OC=OC, cc=cc, qs=qs: e.scalar_tensor_tensor(
                                out=OC[:, cc, :], in0=Y8[:, cc, :], scalar=float(qs), in1=RS[:], op0=ALU.mult, op1=ALU.mult),
                                reads=[bY8[cc], bRS], writes=[bOC])
                    P.dma("pool", self.QKVB[grp * 1024:(grp + 1) * 1024, t0:t0 + 512].rearrange("(c p) t -> p c t", p=128),
                          OC[:], reads=[bOC])
                if ti + 1 < ntiles:
                    self.transpose_x(xs_next, xTr[(ti + 1) % NB], bxTr[(ti + 1) % NB], pT, ident, bid)

    def m2_phase(self, layer):
        P = self.P
        T, L = self.T, self.L
        with self.phase() as ph:
            EB = ph.sb([128, 2 * 16 * 128], F32)
            MK = ph.sb([128, 2 * 16 * 128], F32)
            bEB, bMK = Buf(), Buf()
            for q4 in range(4):
                sl = slice(q4 * 1024, (q4 + 1) * 1024)
                P.dma("sp", EB[:, sl], self.pbias[:, sl], writes=[bEB])
                P.dma("sp", MK[:, sl], self.pmask[:, sl], writes=[bMK])
            P.op("act", lambda e: e.activation(out=EB[:], in_=EB[:], func=AF.Exp), reads=[bEB], writes=[bEB])
            P.op("pool", lambda e: e.tensor_tensor(out=EB[:], in0=EB[:], in1=MK[:], op=ALU.mult), reads=[bEB, bMK], writes=[bEB])
            SK = ph.sb([64, 16], F32)
            SKE = ph.sb([64, 16, 128], F32)
            bSK, bSKE = Buf(), Buf()
            P.dma("sp", SK[:], self.sinks[layer, :].partition_broadcast(64), writes=[bSK])
            P.op("act", lambda e: e.activation(out=SK[:], in_=SK[:], func=AF.Exp), reads=[bSK], writes=[bSK])
            P.op("pool", lambda e: e.memset(SKE[:], 0.0), writes=[bSKE])
            for h in range(16):
                P.op("pool", lambda e, h=h: e.tensor_scalar(SKE[:, h, :], SKE[:, h, :], SK[:, h:h + 1], None, ALU.add),
                     reads=[bSK, bSKE], writes=[bSKE])
            ones = ph.sb([128, 64], BF16)
            bones = Buf()
            P.op("pool", lambda e: e.memset(ones[:], 1.0), writes=[bones])
            Qr = Rot([ph.sb([64, 16, 512], BF16) for _ in range(2)])
            Kr = Rot([ph.sb([64, 4, 640], BF16) for _ in range(2)])
            Vr = Rot([ph.sb([128, 5, 256], BF16) for _ in range(2)])
            ATr = Rot([ph.sb([64, 16, 512], BF16) for _ in range(2)])
            Er = Rot([ph.sb([128, 512], F32) for _ in range(4)])
            Pr = Rot([ph.sb([128, 512], BF16) for _ in range(4)])
            Dr = Rot([ph.sb([64, 512], F32) for _ in range(2)])
            pSr = Rot([ph.ps() for _ in range(4)])
            pOr = Rot([ph.ps() for _ in range(2)])
            pDr = Rot([ph.ps() for _ in range(2)])
            EBv = EB[:].rearrange("p (b h q) -> p b h q", b=2, h=16)
            def m2_loads(t0):
                Qt, bQ = Qr.next()
                Kt, bK = Kr.next()
                Vt, bV = Vr.next()
                P.dma("sp", Qt[:], self.QT[:, t0:t0 + 512].rearrange("(h d) t -> d h t", d=64), writes=[bQ])
                if t0 % L == 0:
                    P.dma("sp", Kt[:, :, 128:640], self.KT[:, t0:t0 + 512].rearrange("(g d) t -> d g t", d=64), writes=[bK])
                    P.dma("sp", Vt[:, 1:5, :], self.VA[t0:t0 + 512, :].rearrange("(b p) c -> p b c", p=128), writes=[bV])
                else:
                    P.dma("sp", Kt[:], self.KT[:, t0 - 128:t0 + 512].rearrange("(g d) t -> d g t", d=64), writes=[bK])
                    P.dma("sp", Vt[:], self.VA[t0 - 128:t0 + 512, :].rearrange("(b p) c -> p b c", p=128), writes=[bV])
                return Qt, bQ, Kt, bK, Vt, bV

            nxt = m2_loads(0)
            for ti in range(T // 512):
                t0 = ti * 512
                seq_start = (t0 % L == 0)
                Qt, bQ, Kt, bK, Vt, bV = nxt
                if ti + 1 < T // 512:
                    nxt = m2_loads(t0 + 512)
                At, bA = ATr.next()
                for i in range(4):
                    first = seq_start and i == 0
                    sbl = [1] if first else [0, 1]
                    for g in range(4):
                        Pb = {}
                        for sb_ in sbl:
                            ps_, bps = pSr.next()
                            P.op("pe", lambda e, ps_=ps_, g=g, i=i, sb_=sb_, Kt=Kt, Qt=Qt: e.matmul(
                                ps_[:], Kt[:, g, (i + sb_) * 128:(i + sb_ + 1) * 128], Qt[:, 4 * g:4 * g + 4, i * 128:(i + 1) * 128],
                                start=True, stop=True), reads=[bK, bQ], writes=[bps])
                            Et, bE = Er.next()
                            P.op("act", lambda e, Et=Et, ps_=ps_: e.activation(out=Et[:], in_=ps_[:], func=AF.Exp),
                                 reads=[bps], writes=[bE])
                            Pt, bP = Pr.next()
                            P.op("pool", lambda e, Et=Et, Pt=Pt, sb_=sb_, g=g: e.tensor_tensor(
                                out=Pt[:].rearrange("p (h q) -> p h q", h=4), in0=Et[:].rearrange("p (h q) -> p h q", h=4),
                                in1=EBv[:, sb_, 4 * g:4 * g + 4, :], op=ALU.mult), reads=[bE, bEB], writes=[bP])
                            Pb[sb_] = (Pt, bP)
                        po, bpo = pOr.next()
                        pd, bpd = pDr.next()
                        for n_, sb_ in enumerate(sbl):
                            Pt, bP = Pb[sb_]
                            P.op("pe", lambda e, po=po, Pt=Pt, sb_=sb_, g=g, i=i, Vt=Vt, n_=n_: e.matmul(
                                po[0:64, :], Vt[:, i + sb_, g * 64:(g + 1) * 64], Pt[:], start=(n_ == 0), stop=(n_ == len(sbl) - 1)),
                                reads=[bV, bP], writes=[bpo])
                        for n_, sb_ in enumerate(sbl):
                            Pt, bP = Pb[sb_]
                            P.op("pe", lambda e, pd=pd, Pt=Pt, n_=n_: e.matmul(
                                pd[0:64, :], ones[:], Pt[:], start=(n_ == 0), stop=(n_ == len(sbl) - 1)),
                                reads=[bones, bP], writes=[bpd])
                        Dt, bD = Dr.next()
                        P.op("dve", lambda e, Dt=Dt, pd=pd, g=g: e.tensor_tensor(
                            out=Dt[:].rearrange("p (h q) -> p h q", h=4), in0=pd[0:64, :].rearrange("p (h q) -> p h q", h=4),
                            in1=SKE[:, 4 * g:4 * g + 4, :], op=ALU.add), reads=[bpd, bSKE], writes=[bD])
                        P.op("dve", lambda e, Dt=Dt: e.reciprocal(Dt[:], Dt[:]), reads=[bD], writes=[bD])
                        P.op("dve", lambda e, Dt=Dt, po=po, At=At, g=g, i=i: e.tensor_tensor(
                            out=At[:, 4 * g:4 * g + 4, i * 128:(i + 1) * 128], in0=po[0:64, :].rearrange("p (h q) -> p h q", h=4),
                            in1=Dt[:].rearrange("p (h q) -> p h q", h=4), op=ALU.mult), reads=[bpo, bD], writes=[bA])
                P.dma("pool", self.AT[:, t0:t0 + 512].rearrange("(h d) t -> d h t", d=64), At[:], reads=[bA])

    def m3_phase(self, layer, src):
        P = self.P
        T, L = self.T, self.L
        SD = SOLVE_DT
        with self.phase() as ph:
            WZ = ph.sb([128, 8, 1024], BF16)
            bWZ = bufs(8)
            self.load_w(WZ, bWZ, self.w_in[layer], 8, C_Z, C_Z + 1024, piece=1024)
            identF, bidF = self.make_ident(ph, F32)
            identB = ph.sb([128, 128], BF16)
            bidB = Buf()
            P.op("pool", lambda e: e.tensor_copy(identB[:], identF[:]), reads=[bidF], writes=[bidB])
            if SD == BF16:
                identS, bidS = identB, bidB
            else:
                identS, bidS = identF, bidF
            bC = Buf()

            def mk(shape=(128, 128)):
                return ph.sb(list(shape), F32)

            TRI, LGT, MSU, ONES, SELA, SELB, SAME = mk(), mk(), mk(), mk(), mk(), mk(), mk()
            P.op("pool", lambda e: e.memset(TRI[:], 1.0), writes=[bC])
            P.op("pool", lambda e: e.affine_select(out=TRI[:], in_=TRI[:], pattern=[[1, 128]], compare_op=ALU.is_ge,
                                                   fill=0.0, base=0, channel_multiplier=-1), reads=[bC], writes=[bC])
            P.op("pool", lambda e: e.memset(TRI[0:64, 64:128], 0.0), reads=[bC], writes=[bC])
            P.op("pool", lambda e: e.memset(LGT[:], 1.0), reads=[bC], writes=[bC])
            P.op("pool", lambda e: e.affine_select(out=LGT[:], in_=LGT[:], pattern=[[-1, 128]], compare_op=ALU.is_gt,
                                                   fill=0.0, base=0, channel_multiplier=1), reads=[bC], writes=[bC])
            P.op("pool", lambda e: e.memset(LGT[64:128, 0:64], 0.0), reads=[bC], writes=[bC])
            P.op("pool", lambda e: e.memset(MSU[:], 1.0), reads=[bC], writes=[bC])
            P.op("pool", lambda e: e.affine_select(out=MSU[:], in_=MSU[:], pattern=[[1, 128]], compare_op=ALU.is_gt,
                                                   fill=0.0, base=0, channel_multiplier=-1), reads=[bC], writes=[bC])
            P.op("pool", lambda e: e.memset(MSU[0:64, 64:128], 0.0), reads=[bC], writes=[bC])
            P.op("pool", lambda e: e.memset(ONES[:], 1.0), reads=[bC], writes=[bC])
            P.op("pool", lambda e: e.memset(SELA[:], 0.0), reads=[bC], writes=[bC])
            P.op("pool", lambda e: e.memset(SELA[0:64, :], 1.0), reads=[bC], writes=[bC])
            P.op("pool", lambda e: e.memset(SELB[:], 0.0), reads=[bC], writes=[bC])
            P.op("pool", lambda e: e.memset(SELB[64:128, :], 1.0), reads=[bC], writes=[bC])
            P.op("pool", lambda e: e.memset(SAME[:], 0.0), reads=[bC], writes=[bC])
            P.op("pool", lambda e: e.memset(SAME[0:64, 0:64], 1.0), reads=[bC], writes=[bC])
            P.op("pool", lambda e: e.memset(SAME[64:128, 64:128], 1.0), reads=[bC], writes=[bC])
            NEGM8 = mk((128, 8, 128))
            MSU8 = mk((128, 8, 128))
            ID8 = ph.sb([128, 8, 128], SD)
            for h in range(8):
                P.op("pool", lambda e, h=h: e.tensor_scalar(NEGM8[:, h, :], TRI[:], 1e30, -1e30, ALU.mult, ALU.add),
                     reads=[bC], writes=[bC])
                P.op("pool", lambda e, h=h: e.tensor_copy(MSU8[:, h, :], MSU[:]), reads=[bC], writes=[bC])
                P.op("pool", lambda e, h=h: e.tensor_copy(ID8[:, h, :], identF[:]), reads=[bC, bidF], writes=[bC])
            DNG = mk((128, 8, 128))
            bDNG = Buf()
            for h in range(8):
                P.dma("sp", DNG[:, h, :], self.dn_g[layer, :].partition_broadcast(128), writes=[bDNG])
            S = ph.sb([128, 8, 128], F32)
            SB = ph.sb([128, 8, 128], BF16)
            bS, bSB = bufs(8), bufs(8)
            VN = ph.sb([128, 8, 128], BF16)
            bVN = bufs(8)
            P.op("pool", lambda e: e.memset(VN[:], 0.0), writes=bVN)
            NB = 2
            XSr = Rot([ph.sb([128, D], F32) for _ in range(4)])
            xTr = [ph.sb([128, 8, 512], BF16) for _ in range(NB)]
            bxTr = [bufs(8) for _ in range(NB)]
            QKVr = Rot([ph.sb([128, 24, 512], BF16) for _ in range(2)])
            BGr = Rot([ph.sb([128, 4, 16], F32) for _ in range(2)])
            OGTr = Rot([ph.sb([128, 8, 512], BF16) for _ in range(2)])
            NBG = ph.sb([128, 8], F32)
            bNBG = Buf()
            Rt = ph.sb([128, 8, 128], F32)
            bRt = Buf()
            ET = ph.sb([128, 8, 128], F32)
            ETS = ph.sb([128, 8, 128], F32)
            EGB = ph.sb([128, 8, 128], F32)
            bET, bETS, bEGB = Buf(), Buf(), Buf()
            SM = ph.sb([128, 64], F32)
            bSM = Buf()
            YR = [ph.sb([128, 8, 256], SD) for _ in range(2)]
            bYR = [bufs(2), bufs(2)]
            Z = [ph.sb([128, 8, 128], SD) for _ in range(2)]
            bZ = [bufs(2), bufs(2)]
            XS = ph.sb([128, 8, 128], BF16)
            bXS = bufs(2)
            AIT = ph.sb([128, 8, 128], BF16)
            bAIT = Buf()
            KG = ph.sb([128, 8, 128], BF16)
            KT2 = ph.sb([128, 8, 128], BF16)
            KTOK = ph.sb([128, 8, 128], BF16)
            bKTOK = Buf()
            Vt = ph.sb([128, 8, 128], BF16)
            QD = ph.sb([128, 8, 128], BF16)
            bKG, bKT2, bVt, bQD = Buf(), Buf(), Buf(), Buf()
            UB = ph.sb([128, 8, 128], F32)
            WT = ph.sb([128, 8, 128], BF16)
            bUB, bWT = bufs(8), bufs(2)
            O = ph.sb([128, 8, 128], F32)
            bO = Buf()
            SQ = ph.sb([128, 8, 128], F32)
            ZG = ph.sb([128, 8, 128], F32)
            OG = ph.sb([128, 8, 128], BF16)
            bSQ, bZG, bOG = Buf(), Buf(), Buf()
            pF = Rot([ph.ps() for _ in range(6)])
            pH = Rot([ph.ps(BF16) for _ in range(2)])
            YRv = [y[:].rearrange("p h c -> p (h c)") for y in YR]

            def flat(t, g):
                return t[:, 4 * g:4 * g + 4, :]

            def m3_loads(t0):
                xs = self.issue_x(src, t0, XSr)
                QKV, bQKV = QKVr.next()
                for grp in range(3):
                    P.dma("sp", QKV[:, grp * 8:(grp + 1) * 8, :],
                          self.QKVB[grp * 1024:(grp + 1) * 1024, t0:t0 + 512].rearrange("(c p) t -> p c t", p=128), writes=[bQKV])
                BGt, bBG = BGr.next()
                P.dma("sp", BGt[:], self.BG[t0:t0 + 512, :].rearrange("(s p) f -> p s f", p=128), writes=[bBG])
                return xs, QKV, bQKV, BGt, bBG

            nxt = m3_loads(0)
            self.transpose_x(nxt[0], xTr[0], bxTr[0], pF, identF, bidF)
            for ti in range(T // 512):
                t0 = ti * 512
                seq_start = (t0 % L == 0)
                xT, bxT = xTr[ti % NB], bxTr[ti % NB]
                _, QKV, bQKV, BGt, bBG = nxt
                if ti + 1 < T // 512:
                    nxt = m3_loads(t0 + 512)
                OGT, bOGT = OGTr.next()
                if seq_start:
                    P.op("pool", lambda e: e.memset(S[:], 0.0), writes=bS)
                    P.op("pool", lambda e: e.memset(SB[:], 0.0), writes=bSB)
                for s in range(4):
                    tk = slice(s * 128, (s + 1) * 128)
                    beta = BGt[:, s, 0:8]
                    graw = BGt[:, s, 8:16]
                    P.op("dve", lambda e, beta=beta: e.tensor_scalar(NBG[:], beta, -1.0, None, ALU.mult), reads=[bBG], writes=[bNBG])
                    for h in range(8):
                        P.op("dve", lambda e, h=h, graw=graw: e.tensor_scalar(Rt[:, h, :], TRI[:], graw[:, h:h + 1], None, ALU.mult),
                             reads=[bC, bBG], writes=[bRt])
                    px, bpx = pF.next()
                    P.op("pe", lambda e, px=px, graw=graw: e.matmul(px[:, 0:8], SELA[:], graw, start=True, stop=True),
                         reads=[bC, bBG], writes=[bpx])
                    P.op("pe", lambda e, px=px, graw=graw: e.matmul(px[:, 8:16], SELB[:], graw, start=True, stop=True),
                         reads=[bC, bBG], writes=[bpx])
                    P.op("pe", lambda e, px=px, graw=graw: e.matmul(px[:, 16:24], TRI[:], graw, start=True, stop=True),
                         reads=[bC, bBG], writes=[bpx])
                    P.op("pe", lambda e, px=px, graw=graw: e.matmul(px[:, 24:32], SAME[:], graw, start=True, stop=True),
                         reads=[bC, bBG], writes=[bpx])
                    P.op("act", lambda e, px=px: e.activation(out=SM[:, 0:24], in_=px[:, 0:24], func=AF.Exp), reads=[bpx], writes=[bSM])
                    P.op("act", lambda e, px=px: e.copy(SM[:, 56:64], px[:, 16:24]), reads=[bpx, bSM], writes=[bSM])
                    P.op("dve", lambda e, px=px: e.tensor_tensor(out=SM[:, 32:40], in0=px[:, 24:32], in1=SM[:, 56:64], op=ALU.subtract),
                         reads=[bpx, bSM], writes=[bSM])
                    P.op("act", lambda e: e.activation(out=SM[:, 24:32], in_=SM[:, 32:40], func=AF.Exp), reads=[bSM], writes=[bSM])
                    for g in range(2):
                        pg, bpg = pF.next()
                        rv = flat(Rt, g)
                        P.op("pe", lambda e, pg=pg, rv=rv: e.matmul(pg[:], LGT[:], rv, start=True, stop=False), reads=[bC, bRt], writes=[bpg])
                        P.op("pe", lambda e, pg=pg, g=g: e.matmul(pg[:], identF[:], flat(NEGM8, g), start=False, stop=True),
                             reads=[bC, bidF], writes=[bpg])
                        P.op("act", lambda e, pg=pg, g=g: e.activation(out=flat(ET, g), in_=pg[:].rearrange("p (h c) -> p h c", h=4), func=AF.Exp),
                             reads=[bpg], writes=[bET])
                        pg2, bpg2 = pF.next()
                        P.op("pe", lambda e, pg2=pg2, rv=rv: e.matmul(pg2[:], ONES[:], rv, start=True, stop=True), reads=[bC, bRt], writes=[bpg2])
                        P.op("act", lambda e, pg2=pg2, g=g: e.activation(out=flat(EGB, g), in_=pg2[:].rearrange("p (h c) -> p h c", h=4), func=AF.Exp),
                             reads=[bpg2], writes=[bEGB])
                    P.op("pool", lambda e: e.tensor_tensor(out=ETS[:], in0=ET[:], in1=MSU8[:], op=ALU.mult), reads=[bET, bC], writes=[bETS])
                    if M3STOP <= 1:
                        continue
                    cur = 0
                    for g in range(2):
                        pk, bpk = pF.next()
                        pq, bpq = pF.next()
                        for hh in range(4):
                            h = 4 * g + hh
                            P.op("pe", lambda e, pk=pk, h=h, hh=hh, QKV=QKV, tk=tk: e.matmul(
                                pk[:, hh * 128:(hh + 1) * 128], QKV[:, 8 + h, tk], QKV[:, 8 + h, tk], start=True, stop=True),
                                reads=[bQKV], writes=[bpk])
                            P.op("pe", lambda e, pq=pq, h=h, hh=hh, QKV=QKV, tk=tk: e.matmul(
                                pq[:, hh * 128:(hh + 1) * 128], QKV[:, 8 + h, tk], QKV[:, h, tk], start=True, stop=True),
                                reads=[bQKV], writes=[bpq])
                        for hh in range(4):
                            h = 4 * g + hh
                            P.op("dve", lambda e, pk=pk, h=h, hh=hh: e.scalar_tensor_tensor(
                                out=YR[0][:, h, 0:128], in0=pk[:, hh * 128:(hh + 1) * 128], scalar=NBG[:, h:h + 1],
                                in1=ETS[:, h, :], op0=ALU.mult, op1=ALU.mult), reads=[bpk, bNBG, bETS], writes=[bYR[0][g]])
                        P.op("dve", lambda e, pq=pq, g=g: e.tensor_tensor(out=flat(AIT, g), in0=pq[:].rearrange("p (h c) -> p h c", h=4),
                                                                         in1=flat(ET, g), op=ALU.mult), reads=[bpq, bET], writes=[bAIT])
                    if M3STOP <= 2:
                        continue
                    for g in range(2):
                        P.op("pool", lambda e, g=g: e.tensor_tensor(out=YR[1][:, 4 * g:4 * g + 4, 128:256], in0=YR[0][:, 4 * g:4 * g + 4, 0:128],
                                                                    in1=flat(ID8, g), op=ALU.add), reads=[bYR[0][g], bC], writes=[bYR[1][g]])
                        if SD == BF16:
                            pz, bpz = pH.next()
                        else:
                            pz, bpz = pF.next()
                        for hh in range(4):
                            h = 4 * g + hh
                            P.op("pe", lambda e, pz=pz, h=h, hh=hh: e.transpose(pz[:, hh * 128:(hh + 1) * 128], YR[0][:, h, 0:128], identS[:]),
                                 reads=[bYR[0][g], bidS], writes=[bpz])
                        P.op("act", lambda e, pz=pz, g=g: e.copy(flat(Z[0], g), pz[:, 0:512].rearrange("p (h c) -> p h c", h=4)),
                             reads=[bpz], writes=[bZ[0][g]])
                    for k in range(6):
                        a, b_ = k % 2, (k + 1) % 2
                        for g in range(2):
                            last = (k == 5)
                            if not last:
                                pz, bpz = pF.next()
                                for hh in range(4):
                                    h = 4 * g + hh
                                    P.op("pe", lambda e, pz=pz, h=h, hh=hh, a=a: e.matmul(
                                        pz[:, hh * 128:(hh + 1) * 128], YR[a][:, h, 0:128], Z[a][:, h, :], start=True, stop=True),
                                        reads=[bYR[a][g], bZ[a][g]], writes=[bpz])
                                P.op("act", lambda e, pz=pz, g=g, b_=b_: e.copy(flat(Z[b_], g), pz[:].rearrange("p (h c) -> p h c", h=4)),
                                     reads=[bpz], writes=[bZ[b_][g]])
                            if k == 0:
                                py, bpy = pF.next()
                                for hh in range(4):
                                    h = 4 * g + hh
                                    P.op("pe", lambda e, py=py, h=h, hh=hh: e.matmul(
                                        py[:, hh * 128:(hh + 1) * 128], Z[0][:, h, :], YR[0][:, h, 0:128], start=True, stop=True),
                                        reads=[bYR[0][g], bZ[0][g]], writes=[bpy])
                                P.op("act", lambda e, py=py, g=g: e.copy(YR[1][:, 4 * g:4 * g + 4, 0:128], py[:].rearrange("p (h c) -> p h c", h=4)),
                                     reads=[bpy], writes=[bYR[1][g]])
                            elif not last:
                                for half in range(2):
                                    py, bpy = pF.next()
                                    for hh in range(2):
                                        h = 4 * g + 2 * half + hh
                                        P.op("pe", lambda e, py=py, h=h, hh=hh, a=a: e.matmul(
                                            py[:, hh * 256:(hh + 1) * 256], Z[a][:, h, :], YR[a][:, h, :], start=True, stop=True),
                                            reads=[bYR[a][g], bZ[a][g]], writes=[bpy])
                                    h0 = 4 * g + 2 * half
                                    pv = py[:].rearrange("p (h c) -> p h c", h=2)
                                    P.op("act", lambda e, pv=pv, h0=h0, b_=b_: e.copy(YR[b_][:, h0:h0 + 2, 0:128], pv[:, :, 0:128]),
                                         reads=[bpy], writes=[bYR[b_][g]])
                                    P.op("dve", lambda e, pv=pv, h0=h0, a=a, b_=b_: e.tensor_tensor(
                                        out=YR[b_][:, h0:h0 + 2, 128:256], in0=pv[:, :, 128:256], in1=YR[a][:, h0:h0 + 2, 128:256], op=ALU.add),
                                        reads=[bpy, bYR[a][g]], writes=[bYR[b_][g]])
                            else:
                                py, bpy = pF.next()
                                for hh in range(4):
                                    h = 4 * g + hh
                                    P.op("pe", lambda e, py=py, h=h, hh=hh, a=a: e.matmul(
                                        py[:, hh * 128:(hh + 1) * 128], Z[a][:, h, :], YR[a][:, h, 128:256], start=True, stop=True),
                                        reads=[bYR[a][g], bZ[a][g]], writes=[bpy])
                                P.op("dve", lambda e, py=py, g=g, a=a: e.tensor_tensor(
                                    out=flat(XS, g), in0=py[:].rearrange("p (h c) -> p h c", h=4), in1=YR[a][:, 4 * g:4 * g + 4, 128:256], op=ALU.add),
                                    reads=[bpy, bYR[a][g]], writes=[bXS[g]])
                    if M3STOP <= 3:
                        continue
                    pkt, bpkt = pH.next()
                    for h in range(8 if (M3SUB & 1) else 0):
                        P.op("pe", lambda e, pkt=pkt, h=h, QKV=QKV, tk=tk: e.transpose(pkt[:, h * 128:(h + 1) * 128], QKV[:, 8 + h, tk], identB[:]),
                             reads=[bQKV, bidB], writes=[bpkt])
                    P.op("act", lambda e, pkt=pkt: e.copy(KTOK[:].rearrange("p h c -> p (h c)"), pkt[:]), reads=[bpkt], writes=[bKTOK])
                    for h in range(8):
                        P.op("dve", lambda e, h=h: e.tensor_scalar(KG[:, h, :], KTOK[:, h, :], SM[:, 16 + h:17 + h], None, ALU.mult),
                             reads=[bKTOK, bSM], writes=[bKG])
                        P.op("act", lambda e, h=h: e.activation(out=KT2[:, h, :], in_=KTOK[:, h, :], func=AF.Identity, scale=SM[:, 24 + h:25 + h]),
                             reads=[bKTOK, bSM], writes=[bKT2])
                    pvt, bpvt = pH.next()
                    for h in range(8 if (M3SUB & 2) else 0):
                        P.op("pe", lambda e, pvt=pvt, h=h, QKV=QKV, tk=tk: e.transpose(pvt[:, h * 128:(h + 1) * 128], QKV[:, 16 + h, tk], identB[:]),
                             reads=[bQKV, bidB], writes=[bpvt])
                    if M3SUB & 2:
                        P.op("act", lambda e, pvt=pvt: e.copy(Vt[:].rearrange("p h c -> p (h c)"), pvt[:]), reads=[bpvt], writes=[bVt])
                    if M3SUB & 4:
                        P.op("pool", lambda e, QKV=QKV, tk=tk: e.tensor_tensor(out=QD[:], in0=QKV[:, 0:8, tk], in1=EGB[:], op=ALU.mult),
                             reads=[bQKV, bEGB], writes=[bQD])
                    if M3STOP <= 4:
                        continue
                    for g in range(2):
                        pu, bpu = pF.next()
                        pw, bpw = pF.next()
                        for hh in range(4):
                            h = 4 * g + hh
                            P.op("pe", lambda e, pu=pu, h=h, hh=hh: e.matmul(pu[:, hh * 128:(hh + 1) * 128], XS[:, h, :], Vt[:, h, :], start=True, stop=True),
                                 reads=[bXS[g], bVt], writes=[bpu])
                            P.op("pe", lambda e, pw=pw, h=h, hh=hh: e.matmul(pw[:, hh * 128:(hh + 1) * 128], KG[:, h, :], XS[:, h, :], start=True, stop=True),
                                 reads=[bXS[g], bKG], writes=[bpw])
                        for hh in range(4):
                            h = 4 * g + hh
                            P.op("act", lambda e, pu=pu, h=h, hh=hh, beta=beta: e.activation(out=UB[:, h, :], in_=pu[:, hh * 128:(hh + 1) * 128], func=AF.Identity,
                                                                                          scale=beta[:, h:h + 1]), reads=[bpu, bBG], writes=[bUB[h]])
                        P.op("act", lambda e, pw=pw, g=g: e.copy(flat(WT, g), pw[:].rearrange("p (h c) -> p h c", h=4)), reads=[bpw], writes=[bWT[g]])
                    if M3STOP <= 5:
                        continue
                    for ck in range(2):
                        rows = slice(ck * 64, (ck + 1) * 64)
                        for g in range(2):
                            pv_, bpv = pF.next()
                            for hh in range(4):
                                h = 4 * g + hh
                                P.op("pe", lambda e, pv_=pv_, h=h, hh=hh: e.matmul(pv_[:, hh * 128:(hh + 1) * 128], WT[:, h, :], SB[:, h, :], start=True, stop=True),
                                     reads=[bWT[g], bSB[h]], writes=[bpv])
                            for hh in range(4):
                                h = 4 * g + hh
                                P.op("dve", lambda e, pv_=pv_, h=h, hh=hh, rows=rows: e.scalar_tensor_tensor(
                                    out=VN[rows, h, :], in0=pv_[rows, hh * 128:(hh + 1) * 128], scalar=NBG[rows, h:h + 1], in1=UB[rows, h, :],
                                    op0=ALU.mult, op1=ALU.add), reads=[bpv, bNBG, bUB[h]], writes=[bVN[h]])
                            po, bpo = pF.next()
                            for hh in range(4):
                                h = 4 * g + hh
                                P.op("pe", lambda e, po=po, h=h, hh=hh: e.matmul(po[:, hh * 128:(hh + 1) * 128], QD[:, h, :], SB[:, h, :], start=True, stop=False),
                                     reads=[bQD, bSB[h]], writes=[bpo])
                                P.op("pe", lambda e, po=po, h=h, hh=hh: e.matmul(po[:, hh * 128:(hh + 1) * 128], AIT[:, h, :], VN[:, h, :], start=False, stop=True),
                                     reads=[bAIT, bVN[h]], writes=[bpo])
                            P.op("act", lambda e, po=po, g=g, rows=rows: e.copy(O[rows, 4 * g:4 * g + 4, :], po[rows, :].rearrange("p (h c) -> p h c", h=4)),
                                 reads=[bpo], writes=[bO])
                            ps_, bps = pF.next()
                            for hh in range(4):
                                h = 4 * g + hh
                                P.op("pe", lambda e, ps_=ps_, h=h, hh=hh, rows=rows: e.matmul(ps_[:, hh * 128:(hh + 1) * 128], KT2[rows, h, :], VN[rows, h, :], start=True, stop=True),
                                     reads=[bKT2, bVN[h]], writes=[bps])
                            for hh in range(4):
                                h = 4 * g + hh
                                P.op("dve", lambda e, ps_=ps_, h=h, hh=hh, ck=ck: e.scalar_tensor_tensor(
                                    out=S[:, h, :], in0=S[:, h, :], scalar=SM[:, ck * 8 + h:ck * 8 + h + 1], in1=ps_[:, hh * 128:(hh + 1) * 128],
                                    op0=ALU.mult, op1=ALU.add), reads=[bps, bSM, bS[h]], writes=[bS[h]])
                                P.op("act", lambda e, h=h: e.copy(SB[:, h, :], S[:, h, :]), reads=[bS[h]], writes=[bSB[h]])
                    if M3STOP <= 6:
                        continue
                    pz0, bpz0 = pF.next()
                    pz1, bpz1 = pF.next()
                    for hf, (pz_, bpz_) in enumerate(((pz0, bpz0), (pz1, bpz1))):
                        for kc in range(8):
                            P.op("pe", lambda e, pz_=pz_, kc=kc, hf=hf, xT=xT, tk=tk: e.matmul(pz_[:], xT[:, kc, tk], WZ[:, kc, hf * 512:(hf + 1) * 512],
                                                                                           start=(kc == 0), stop=(kc == 7)),
                                 reads=[bxT[kc], bWZ[kc]], writes=[bpz_])
                        P.op("act", lambda e, pz_=pz_, hf=hf: e.activation(out=ZG[:, 4 * hf:4 * hf + 4, :], in_=pz_[:].rearrange("p (h c) -> p h c", h=4), func=AF.Silu),
                             reads=[bpz_], writes=[bZG])
                    P.op("pool", lambda e: e.tensor_tensor(out=ZG[:], in0=ZG[:], in1=DNG[:], op=ALU.mult), reads=[bZG, bDNG], writes=[bZG])
                    P.op("pool", lambda e: e.tensor_tensor(out=SQ[:], in0=O[:], in1=O[:], op=ALU.mult), reads=[bO], writes=[bSQ])
                    P.op("dve", lambda e: e.tensor_reduce(out=SM[:, 40:48], in_=SQ[:], axis=AX.X, op=ALU.add), reads=[bSQ, bSM], writes=[bSM])
                    P.op("dve", lambda e: e.tensor_scalar(SM[:, 48:56], SM[:, 40:48], 1.0 / 128.0, float(NORM_EPS), ALU.mult, ALU.add), reads=[bSM], writes=[bSM])
                    P.op("act", lambda e: e.activation(out=SM[:, 48:56], in_=SM[:, 48:56], func=AF.Ln), reads=[bSM], writes=[bSM])
                    P.op("act", lambda e: e.activation(out=SM[:, 48:56], in_=SM[:, 48:56], func=AF.Exp, scale=-0.5), reads=[bSM], writes=[bSM])
                    for h in range(8):
                        P.op("dve", lambda e, h=h: e.scalar_tensor_tensor(out=OG[:, h, :], in0=O[:, h, :], scalar=SM[:, 48 + h:49 + h], in1=ZG[:, h, :],
                                                                          op0=ALU.mult, op1=ALU.mult), reads=[bO, bSM, bZG], writes=[bOG])
                    pt_, bpt = pH.next()
                    for h in range(8):
                        P.op("pe", lambda e, pt_=pt_, h=h: e.transpose(pt_[:, h * 128:(h + 1) * 128], OG[:, h, :], identB[:]),
                             reads=[bOG, bidB], writes=[bpt])
                    P.op("act", lambda e, pt_=pt_, OGT=OGT, tk=tk: e.copy(OGT[:, :, tk], pt_[:].rearrange("p (h c) -> p h c", h=8)),
                         reads=[bpt], writes=[bOGT])
                if ti + 1 < T // 512:
                    self.transpose_x(nxt[0], xTr[(ti + 1) % NB], bxTr[(ti + 1) % NB], pF, identF, bidF)
                P.dma("pool", self.OGT[:, t0:t0 + 512].rearrange("(c p) t -> p c t", p=128), OGT[:], reads=[bOGT])

    def m4_phase(self, layer, src, dst):
        P = self.P
        T = self.T
        with self.phase() as ph:
            WA = ph.sb([128, 8, D], BF16)
            WB = ph.sb([128, 8, D], BF16)
            WO = ph.sb([128, 8, D], BF16)
            WG = ph.sb([128, 8, 2 * D], BF16)
            bWA, bWB, bWO, bWG = bufs(8), bufs(8), bufs(8), bufs(8)
            self.load_w(WA, bWA, self.w_a[layer], 8, 0, D, piece=1024)
            self.load_w(WB, bWB, self.w_b[layer], 8, 0, D, piece=1024)
            self.load_w(WG, bWG, self.w_in[layer], 8, C_GATE, C_GATE + 2 * D, piece=1024)
            self.load_w(WO, bWO, self.w_o[layer], 8, 0, D, piece=1024)
            ident, bid = self.make_ident(ph, F32)
            lnc = self.ln_consts(ph, layer, 1)
            NB = 2
            XSr = Rot([ph.sb([128, D], F32) for _ in range(4)])
            XRr = Rot([ph.sb([128, D], F32) for _ in range(2)])
            xTr = [ph.sb([128, 8, 512], BF16) for _ in range(NB)]
            bxTr = [bufs(8) for _ in range(NB)]
            ATr = Rot([ph.sb([128, 8, 512], BF16) for _ in range(2)])
            OGr = Rot([ph.sb([128, 8, 512], BF16) for _ in range(2)])
            MT = ph.sb([128, 8, 512], BF16)
            bMT = bufs(8)
            SGr = Rot([ph.sb([128, 512], F32) for _ in range(4)])
            T1r = Rot([ph.sb([128, 512], F32) for _ in range(2)])
            T2r = Rot([ph.sb([128, 512], F32) for _ in range(2)])
            small = Rot([ph.sb([128, 16], F32) for _ in range(2)])
            pT = Rot([ph.ps() for _ in range(2)])
            pM = Rot([ph.ps() for _ in range(4)])
            pY = [ph.ps() for _ in range(2)]
            bpY = bufs(2)
            def m4_loads(t0):
                xs = self.issue_x(src, t0, XSr)
                At, bA = ATr.next()
                Og, bOg = OGr.next()
                P.dma("sp", At[:], self.AT[:, t0:t0 + 512].rearrange("(c p) t -> p c t", p=128), writes=[bA])
                P.dma("sp", Og[:], self.OGT[:, t0:t0 + 512].rearrange("(c p) t -> p c t", p=128), writes=[bOg])
                return xs, At, bA, Og, bOg

            nxt = m4_loads(0)
            self.transpose_x(nxt[0], xTr[0], bxTr[0], pT, ident, bid)
            for ti in range(T // 512):
                t0 = ti * 512
                xT, bxT = xTr[ti % NB], bxTr[ti % NB]
                _, At, bA, Og, bOg = nxt
                if ti + 1 < T // 512:
                    nxt = m4_loads(t0 + 512)
                for n in range(8):
                    ns = slice(n * 128, (n + 1) * 128)
                    pa, bpa = pM.next()
                    pga, bpga = pM.next()
                    for kc in range(8):
                        P.op("pe", lambda e, pa=pa, kc=kc, ns=ns, At=At: e.matmul(pa[:], WA[:, kc, ns], At[:, kc, :], start=(kc == 0), stop=(kc == 7)),
                             reads=[bWA[kc], bA], writes=[bpa])
                    for kc in range(8):
                        P.op("pe", lambda e, pga=pga, kc=kc, ns=ns, xT=xT: e.matmul(pga[:], WG[:, kc, ns], xT[:, kc, :], start=(kc == 0), stop=(kc == 7)),
                             reads=[bWG[kc], bxT[kc]], writes=[bpga])
                    sga, bsga = SGr.next()
                    P.op("act", lambda e, sga=sga, pga=pga: e.activation(out=sga[:], in_=pga[:], func=AF.Sigmoid), reads=[bpga], writes=[bsga])
                    t1, bt1 = T1r.next()
                    P.op("dve", lambda e, t1=t1, pa=pa, sga=sga: e.tensor_tensor(out=t1[:], in0=pa[:], in1=sga[:], op=ALU.mult),
                         reads=[bpa, bsga], writes=[bt1])
                    pb, bpb = pM.next()
                    pgb, bpgb = pM.next()
                    for kc in range(8):
                        P.op("pe", lambda e, pb=pb, kc=kc, ns=ns, Og=Og: e.matmul(pb[:], WB[:, kc, ns], Og[:, kc, :], start=(kc == 0), stop=(kc == 7)),
                             reads=[bWB[kc], bOg], writes=[bpb])
                    for kc in range(8):
                        P.op("pe", lambda e, pgb=pgb, kc=kc, n=n, xT=xT: e.matmul(pgb[:], WG[:, kc, D + n * 128:D + (n + 1) * 128], xT[:, kc, :],
                                                                               start=(kc == 0), stop=(kc == 7)),
                             reads=[bWG[kc], bxT[kc]], writes=[bpgb])
                    sgb, bsgb = SGr.next()
                    P.op("act", lambda e, sgb=sgb, pgb=pgb: e.activation(out=sgb[:], in_=pgb[:], func=AF.Sigmoid), reads=[bpgb], writes=[bsgb])
                    t2, bt2 = T2r.next()
                    P.op("dve", lambda e, t2=t2, pb=pb, sgb=sgb: e.tensor_tensor(out=t2[:], in0=pb[:], in1=sgb[:], op=ALU.mult),
                         reads=[bpb, bsgb], writes=[bt2])
                    P.op("pool", lambda e, t1=t1, t2=t2, n=n: e.tensor_tensor(out=MT[:, n, :], in0=t1[:], in1=t2[:], op=ALU.add),
                         reads=[bt1, bt2], writes=[bMT[n]])
                if ti + 1 < T // 512:
                    self.transpose_x(nxt[0], xTr[(ti + 1) % NB], bxTr[(ti + 1) % NB], pT, ident, bid)
                for s in range(4):
                    for hf in range(2):
                        for kc in range(8):
                            P.op("pe", lambda e, hf=hf, kc=kc, s=s: e.matmul(pY[hf][:], MT[:, kc, s * 128:(s + 1) * 128], WO[:, kc, hf * 512:(hf + 1) * 512],
                                                                            start=(kc == 0), stop=(kc == 7)),
                                 reads=[bMT[kc], bWO[kc]], writes=[bpY[hf]])
                    self.ln_epilogue(pY, bpY, src, dst, t0 + s * 128, 1.0 / DN_ALPHA, lnc, XRr, small)

    def build(self, upto=99):
        cur = self.x
        n = 0
        for layer in range(self.depth):
            last = (layer == self.depth - 1)
            steps = [
                lambda: self.ffn_phase(layer, 0, cur, self.R[0], 0),
                lambda: self.m1_phase(layer, self.R[0]),
                lambda: self.m2_phase(layer),
                lambda: self.m3_phase(layer, self.R[0]),
                lambda: self.m4_phase(layer, self.R[0], self.R[1]),
                lambda: self.ffn_phase(layer, 1, self.R[1], self.out if last else self.R[0], 2),
            ]
            for st in steps:
                if n < upto:
                    st()
                n += 1
            cur = self.R[0]
        self.top.close()
        return self.nc


def host_pos_tables(rel_bias):
    s = np.arange(128)[:, None]
    q = np.arange(128)[None, :]
    out_b = np.zeros((128, 2, 16, 128), np.float32)
    out_m = np.zeros((128, 2, 16, 128), np.float32)
    for blk in range(2):
        j = s + 128 * blk
        rel = q + 128 - j
        valid = (rel >= 0) & (rel < 128)
        n = np.maximum(rel, 0)
        nf = np.maximum(n, 1).astype(np.float32)
        large = 16 + (np.log(nf / np.float32(16)) / np.float32(np.log(128 / 16)) * np.float32(16)).astype(np.int32)
        large = np.minimum(large, 31)
        bucket = np.where(n < 16, n, large)
        bucket = np.where(valid, bucket, 0)
        g = rel_bias[bucket]
        out_b[:, blk] = np.transpose(g, (0, 2, 1))
        out_m[:, blk] = np.broadcast_to(valid[:, None, :], (128, 16, 128))
    return out_b.reshape(128, -1), out_m.reshape(128, -1)


_NC_CACHE = {}


def kernel(x, rel_bias, ln_g, ln_b, ffn_w13, ffn_w2, w_in, conv_w, a_log, dt_bias,
           dn_norm_g, sinks, w_branch_a, w_branch_b, w_out):
    x = np.asarray(x, np.float32)
    B, L, _ = x.shape
    nseq = B // NCORES
    key = (nseq, L)
    if key not in _NC_CACHE:
        _NC_CACHE[key] = KB(nseq, L, DEPTH).build()
    nc = _NC_CACHE[key]
    pb, pm = host_pos_tables(np.asarray(rel_bias, np.float32))
    f = lambda a: np.ascontiguousarray(np.asarray(a, np.float32))
    shared = dict(ln_g=f(ln_g), ln_b=f(ln_b), ffn_w13=f(ffn_w13), ffn_w2=f(ffn_w2), w_in=f(w_in), conv_w=f(conv_w),
                  a_log=f(a_log), dt_bias=f(dt_bias), dn_norm_g=f(dn_norm_g), sinks=f(sinks),
                  w_branch_a=f(w_branch_a), w_branch_b=f(w_branch_b), w_out=f(w_out), pbias=pb, pmask=pm)
    in_maps = []
    for c in range(NCORES):
        m = dict(shared)
        m["x"] = np.ascontiguousarray(x[c * nseq:(c + 1) * nseq].reshape(nseq * L, D))
        in_maps.append(m)
    res = run_bass_kernel_spmd(nc, in_maps, core_ids=list(range(NCORES)))
    outs = [np.asarray(r["out"], np.float32).reshape(nseq, L, D) for r in res.results]
    return np.concatenate(outs, axis=0)
```

```python
import contextlib
import numpy as np
import concourse.bass as bass
import concourse.mybir as mybir
from concourse.bass_utils import run_bass_kernel_spmd

F32 = mybir.dt.float32
BF16 = mybir.dt.bfloat16
AF = mybir.ActivationFunctionType
ALU = mybir.AluOpType
AX = mybir.AxisListType

D = 1024
DFF = 2816
NIN = 7696
DEPTH = 4
SEQ = 4096
NCORES = 8
LN_EPS = 1e-5
NORM_EPS = 1e-6
DN_ALPHA = (2 * DEPTH) ** 0.25
C_Q0, C_KA, C_VA, C_QKVB, C_BETA, C_DT, C_Z, C_GATE = 0, 1024, 1280, 1536, 4608, 4616, 4624, 5648
SOLVE_DT = BF16
import os
M3STOP = int(os.environ.get("M3STOP", "99"))
M3SUB = int(os.environ.get("M3SUB", "7"))

NDMASEM = 16
ENGS = ("pe", "act", "dve", "pool", "sp")


class Buf:
    __slots__ = ("lw", "rd")

    def __init__(self):
        self.lw = None
        self.rd = []


def bufs(n):
    return [Buf() for _ in range(n)]


class Prog:
    def __init__(self, nc, stack):
        self.nc = nc
        self.ops = {e: [] for e in ENGS}
        self.cnt = {e: 0 for e in ("pe", "act", "dve", "pool")}
        self.seen = {e: {} for e in ENGS}
        self.dq_n = {"sp": 0, "pool": 0}
        self.esem = {e: stack.enter_context(nc.semaphore("s_" + e)) for e in ("pe", "act", "dve", "pool")}
        self.dsem = {}
        for q in ("sp", "pool"):
            for k in range(NDMASEM):
                self.dsem[(q, k)] = stack.enter_context(nc.semaphore("d_%s%d" % (q, k)))
        self.ninstr = 0

    def _kv(self, tok):
        if tok[0] == "e":
            return ("e", tok[1]), tok[2]
        q, i = tok[1], tok[2]
        return ("d", q, i % NDMASEM), 16 * (i // NDMASEM + 1)

    def _deps(self, eng, reads, writes):
        need = {}

        def add(tok):
            if tok is None:
                return
            if tok[0] == "e" and tok[1] == "pe" and eng == "pe":
                return
            k, v = self._kv(tok)
            if need.get(k, 0) < v:
                need[k] = v

        for b in reads:
            add(b.lw)
        for b in writes:
            add(b.lw)
            for t in b.rd:
                add(t)
        out = []
        s = self.seen[eng]
        for k, v in need.items():
            if s.get(k, 0) < v:
                s[k] = v
                out.append((k, v))
        return out

    def _commit(self, tok, reads, writes):
        for b in reads:
            b.rd.append(tok)
            if len(b.rd) > 32:
                best = {}
                for t in b.rd:
                    k, v = self._kv(t)
                    if k not in best or best[k][0] < v:
                        best[k] = (v, t)
                b.rd = [t for (_, t) in best.values()]
        for b in writes:
            b.lw = tok
            b.rd = []

    def op(self, eng, fn, reads=(), writes=()):
        waits = self._deps(eng, reads, writes)
        self.cnt[eng] += 1
        tok = ("e", eng, self.cnt[eng])
        self.ops[eng].append((waits, fn, None))
        self._commit(tok, reads, writes)

    def dma(self, q, out_ap, in_ap, reads=(), writes=(), slow=False):
        i = self.dq_n[q]
        self.dq_n[q] += 1
        tok = ("d", q, i)
        waits = self._deps(q, reads, writes)
        if i >= NDMASEM:
            k, v = self._kv(("d", q, i - NDMASEM))
            if self.seen[q].get(k, 0) < v:
                self.seen[q][k] = v
                waits.append((k, v))
        self.ops[q].append((waits, (out_ap, in_ap, slow), tok))
        self._commit(tok, reads, writes)

    def barrier(self):
        allk = []
        for e, c in self.cnt.items():
            if c:
                allk.append((("e", e), c))
        for q, n in self.dq_n.items():
            for k in range(min(NDMASEM, n)):
                last = ((n - 1 - k) // NDMASEM) * NDMASEM + k
                allk.append((("d", q, k), 16 * (last // NDMASEM + 1)))
        for e in ENGS:
            s = self.seen[e]
            waits = []
            for k, v in allk:
                if k == ("e", "pe") and e == "pe":
                    continue
                if s.get(k, 0) < v:
                    s[k] = v
                    waits.append((k, v))
            if waits:
                self.ops[e].append((waits, None, None))

    def emit(self):
        nc = self.nc

        def semof(k):
            return self.esem[k[1]] if k[0] == "e" else self.dsem[(k[1], k[2])]

        def run(engname, e):
            for waits, fn, tok in self.ops[engname]:
                for k, v in waits:
                    e.wait_ge(semof(k), v)
                if fn is None:
                    continue
                self.ninstr += 1
                if tok is None:
                    fn(e).then_inc(self.esem[engname], 1)
                else:
                    o, i, slow = fn
                    if slow:
                        ins = e.dma_start(out=o, in_=i, allow_slow_non_contiguous=True)
                    else:
                        ins = e.dma_start(out=o, in_=i)
                    ins.then_inc(self.dsem[(tok[1], tok[2] % NDMASEM)], 16)
            self.ops[engname] = []

        with nc.Block() as block:
            @block.tensor
            def _(e):
                run("pe", e)

            @block.scalar
            def _(e):
                run("act", e)

            @block.vector
            def _(e):
                run("dve", e)

            @block.gpsimd
            def _(e):
                run("pool", e)

            @block.sync
            def _(e):
                run("sp", e)


class Phase:
    def __init__(self, kb):
        self.kb = kb
        self.st = contextlib.ExitStack()
        self.n = 0

    def __enter__(self):
        self.st.__enter__()
        return self

    def __exit__(self, *a):
        self.kb.P.barrier()
        self.kb.P.emit()
        return self.st.__exit__(*a)

    def sb(self, shape, dt):
        self.n += 1
        return self.st.enter_context(self.kb.nc.sbuf_tensor("t%d_%d" % (self.kb.phase_id, self.n), list(shape), dt))

    def ps(self, dt=F32):
        self.n += 1
        cols = 512 if dt == F32 else 1024
        return self.st.enter_context(self.kb.nc.psum_tensor("p%d_%d" % (self.kb.phase_id, self.n), [128, cols], dt))


class Rot:
    def __init__(self, tiles):
        self.t = tiles
        self.b = bufs(len(tiles))
        self.i = 0

    def next(self):
        k = self.i % len(self.t)
        self.i += 1
        return self.t[k], self.b[k]


class KB:
    def __init__(self, nseq, L, depth, debug=False):
        self.nseq, self.L, self.depth = nseq, L, depth
        self.T = nseq * L
        self.debug = debug
        self.nc = bass.Bass("TRN2", target_bir_lowering=False)
        self.top = contextlib.ExitStack()
        self.phase_id = 0
        nc = self.nc
        T = self.T
        ext = lambda n, s: nc.dram_tensor(n, list(s), F32, kind="ExternalInput").ap()
        self.x = ext("x", [T, D])
        self.ln_g = ext("ln_g", [DEPTH, 3, D])
        self.ln_b = ext("ln_b", [DEPTH, 3, D])
        self.w13 = ext("ffn_w13", [DEPTH, 2, D, 2 * DFF])
        self.w2 = ext("ffn_w2", [DEPTH, 2, DFF, D])
        self.w_in = ext("w_in", [DEPTH, D, NIN])
        self.conv_w = ext("conv_w", [DEPTH, 4, 3072])
        self.a_log = ext("a_log", [DEPTH, 8])
        self.dt_bias = ext("dt_bias", [DEPTH, 8])
        self.dn_g = ext("dn_norm_g", [DEPTH, 128])
        self.sinks = ext("sinks", [DEPTH, 16])
        self.w_a = ext("w_branch_a", [DEPTH, D, D])
        self.w_b = ext("w_branch_b", [DEPTH, D, D])
        self.w_o = ext("w_out", [DEPTH, D, D])
        self.pbias = ext("pbias", [128, 2 * 16 * 128])
        self.pmask = ext("pmask", [128, 2 * 16 * 128])
        self.out = nc.dram_tensor("out", [T, D], F32, kind="ExternalOutput").ap()
        kind = "ExternalOutput" if debug else "Internal"
        scr = lambda n, s, dt: nc.dram_tensor(n, list(s), dt, kind=kind).ap()
        self.R = [scr("res%d" % i, [T, D], F32) for i in range(2)]
        self.QT = scr("QT", [1024, T], BF16)
        self.KT = scr("KT", [256, T], BF16)
        self.VA = scr("VA", [T, 256], BF16)
        self.QKVB = scr("QKVB", [3072, T], BF16)
        self.BG = scr("BG", [T, 16], F32)
        self.AT = scr("AT", [1024, T], BF16)
        self.OGT = scr("OGT", [1024, T], BF16)
        self.P = Prog(nc, self.top)

    def phase(self):
        self.phase_id += 1
        return Phase(self)

    def make_ident(self, ph, dt):
        P = self.P
        t = ph.sb([128, 128], dt)
        b = Buf()
        P.op("pool", lambda e: e.memset(t[:], 0.0), writes=[b])
        P.op("pool", lambda e: e.affine_select(out=t[:], in_=t[:], pattern=[[-1, 128]], compare_op=ALU.not_equal,
                                               fill=1.0, base=0, channel_multiplier=1), reads=[b], writes=[b])
        return t, b

    def load_w(self, dst, dbufs, src, kcs, c0, c1, piece=2048):
        v = src.rearrange("(kc p) n -> p kc n", p=128)
        for kc in range(kcs):
            a = c0
            while a < c1:
                b = min(c1, a + piece)
                self.P.dma("pool", dst[:, kc, a - c0:b - c0], v[:, kc, a:b], writes=[dbufs[kc]])
                a = b

    def issue_x(self, src, t0, XS4, nsub=4):
        out = []
        for s in range(nsub):
            Xs, bXs = XS4.next()
            self.P.dma("sp", Xs[:], src[t0 + s * 128:t0 + (s + 1) * 128, :], writes=[bXs])
            out.append((Xs, bXs))
        return out

    def transpose_x(self, xs, xT, bxT, pT, ident, bid):
        P = self.P
        for s, (Xs, bXs) in enumerate(xs):
            for half in range(2):
                pt, bpt = pT.next()
                for k4 in range(4):
                    kc = half * 4 + k4
                    P.op("pe", lambda e, pt=pt, k4=k4, kc=kc, Xs=Xs: e.transpose(pt[:, k4 * 128:(k4 + 1) * 128],
                                                                              Xs[:, kc * 128:(kc + 1) * 128], ident[:]),
                         reads=[bXs, bid], writes=[bpt])
                wb = [bxT[half * 4 + k4] for k4 in range(4)]
                dst = xT[:, half * 4:half * 4 + 4, s * 128:(s + 1) * 128]
                srcv = pt[:].rearrange("p (k c) -> p k c", k=4)
                if half == 0:
                    P.op("act", lambda e, dst=dst, srcv=srcv: e.copy(dst, srcv), reads=[bpt], writes=wb)
                else:
                    P.op("dve", lambda e, dst=dst, srcv=srcv: e.tensor_copy(dst, srcv), reads=[bpt], writes=wb)

    def ln_consts(self, ph, layer, idx):
        P = self.P
        G = ph.sb([128, D], F32)
        B = ph.sb([128, D], F32)
        bg, bb = Buf(), Buf()
        P.dma("sp", G[:], self.ln_g[layer, idx, :].partition_broadcast(128), writes=[bg])
        P.dma("sp", B[:], self.ln_b[layer, idx, :].partition_broadcast(128), writes=[bb])
        return (G, bg, B, bb)

    def ln_epilogue(self, py, bpy, src, dst, r0, c, lnc, XRr, small):
        P = self.P
        G, bg, B, bb = lnc
        Rt, bR = XRr.next()
        st, bst = small.next()
        P.dma("sp", Rt[:], src[r0:r0 + 128, :], writes=[bR])
        for hf in range(2):
            P.op("dve", lambda e, hf=hf, Rt=Rt: e.scalar_tensor_tensor(
                out=Rt[:, hf * 512:(hf + 1) * 512], in0=py[hf][:], scalar=float(c),
                in1=Rt[:, hf * 512:(hf + 1) * 512], op0=ALU.mult, op1=ALU.add),
                reads=[bpy[hf], bR], writes=[bR])
        for hf in range(2):
            P.op("dve", lambda e, hf=hf, Rt=Rt, st=st: e.bn_stats(st[:, hf * 6:(hf + 1) * 6], Rt[:, hf * 512:(hf + 1) * 512]),
                 reads=[bR], writes=[bst])
        P.op("dve", lambda e, st=st: e.bn_aggr(st[:, 12:14], st[:, 0:12]), reads=[bst], writes=[bst])
        eps = LN_EPS / (DN_ALPHA ** 2)
        P.op("dve", lambda e, st=st: e.tensor_scalar(st[:, 13:14], st[:, 13:14], float(eps), None, ALU.add),
             reads=[bst], writes=[bst])
        P.op("act", lambda e, st=st: e.activation(out=st[:, 14:15], in_=st[:, 13:14], func=AF.Ln),
             reads=[bst], writes=[bst])
        P.op("act", lambda e, st=st: e.activation(out=st[:, 14:15], in_=st[:, 14:15], func=AF.Exp, scale=-0.5),
             reads=[bst], writes=[bst])
        P.op("dve", lambda e, st=st: e.scalar_tensor_tensor(out=st[:, 15:16], in0=st[:, 12:13], scalar=-1.0,
                                                            in1=st[:, 14:15], op0=ALU.mult, op1=ALU.mult),
             reads=[bst], writes=[bst])
        P.op("act", lambda e, st=st, Rt=Rt: e.activation(out=Rt[:], in_=Rt[:], func=AF.Identity,
                                                        bias=st[:, 15:16], scale=st[:, 14:15]),
             reads=[bst, bR], writes=[bR])
        P.op("pool", lambda e, Rt=Rt: e.tensor_tensor(out=Rt[:], in0=Rt[:], in1=G[:], op=ALU.mult),
             reads=[bR, bg], writes=[bR])
        P.op("pool", lambda e, Rt=Rt: e.tensor_tensor(out=Rt[:], in0=Rt[:], in1=B[:], op=ALU.add),
             reads=[bR, bb], writes=[bR])
        P.dma("pool", dst[r0:r0 + 128, :], Rt[:], reads=[bR])

    def ffn_phase(self, layer, which, src, dst, ln_idx):
        P = self.P
        T = self.T
        with self.phase() as ph:
            W13 = ph.sb([128, 8, 2 * DFF], BF16)
            W2 = ph.sb([128, 22, D], BF16)
            bW13, bW2 = bufs(8), bufs(22)
            self.load_w(W13, bW13, self.w13[layer, which], 8, 0, 2 * DFF, piece=1408)
            self.load_w(W2, bW2, self.w2[layer, which], 22, 0, D, piece=1024)
            ident, bid = self.make_ident(ph, F32)
            lnc = self.ln_consts(ph, layer, ln_idx)
            NB = 2
            XSr = Rot([ph.sb([128, D], F32) for _ in range(4)])
            XRr = Rot([ph.sb([128, D], F32) for _ in range(2)])
            xTr = [ph.sb([128, 8, 512], BF16) for _ in range(NB)]
            bxTr = [bufs(8) for _ in range(NB)]
            HT = ph.sb([128, 22, 512], BF16)
            bHT = bufs(22)
            SG = Rot([ph.sb([128, 512], F32) for _ in range(2)])
            small = Rot([ph.sb([128, 16], F32) for _ in range(2)])
            pT = Rot([ph.ps() for _ in range(2)])
            pGU = Rot([ph.ps() for _ in range(4)])
            pY = [ph.ps() for _ in range(2)]
            bpY = bufs(2)
            ntiles = T // 512
            xs_next = self.issue_x(src, 0, XSr)
            self.transpose_x(xs_next, xTr[0], bxTr[0], pT, ident, bid)
            for ti in range(ntiles):
                t0 = ti * 512
                xT, bxT = xTr[ti % NB], bxTr[ti % NB]
                if ti + 1 < ntiles:
                    xs_next = self.issue_x(src, t0 + 512, XSr)
                for j in range(22):
                    pg, bpg = pGU.next()
                    pu, bpu = pGU.next()
                    for kc in range(8):
                        P.op("pe", lambda e, pg=pg, kc=kc, j=j, xT=xT: e.matmul(
                            pg[:], W13[:, kc, j * 128:(j + 1) * 128], xT[:, kc, :], start=(kc == 0), stop=(kc == 7)),
                            reads=[bW13[kc], bxT[kc]], writes=[bpg])
                    for kc in range(8):
                        P.op("pe", lambda e, pu=pu, kc=kc, j=j, xT=xT: e.matmul(
                            pu[:], W13[:, kc, DFF + j * 128:DFF + (j + 1) * 128], xT[:, kc, :], start=(kc == 0), stop=(kc == 7)),
                            reads=[bW13[kc], bxT[kc]], writes=[bpu])
                    sg, bsg = SG.next()
                    P.op("act", lambda e, sg=sg, pg=pg: e.activation(out=sg[:], in_=pg[:], func=AF.Silu),
                         reads=[bpg], writes=[bsg])
                    P.op("dve", lambda e, sg=sg, pu=pu, j=j: e.tensor_tensor(out=HT[:, j, :], in0=pu[:], in1=sg[:], op=ALU.mult),
                         reads=[bpu, bsg], writes=[bHT[j]])
                if ti + 1 < ntiles:
                    self.transpose_x(xs_next, xTr[(ti + 1) % NB], bxTr[(ti + 1) % NB], pT, ident, bid)
                for s in range(4):
                    for hf in range(2):
                        for j in range(22):
                            P.op("pe", lambda e, hf=hf, j=j, s=s: e.matmul(
                                pY[hf][:], HT[:, j, s * 128:(s + 1) * 128], W2[:, j, hf * 512:(hf + 1) * 512],
                                start=(j == 0), stop=(j == 21)),
                                reads=[bHT[j], bW2[j]], writes=[bpY[hf]])
                    self.ln_epilogue(pY, bpY, src, dst, t0 + s * 128, 0.5 / DN_ALPHA, lnc, XRr, small)

    def m1_phase(self, layer, src):
        P = self.P
        T, L = self.T, self.L
        NW = C_Z
        with self.phase() as ph:
            W = ph.sb([128, 8, NW], BF16)
            bW = bufs(8)
            self.load_w(W, bW, self.w_in[layer], 8, 0, NW, piece=1156)
            ident, bid = self.make_ident(ph, F32)
            ones = ph.sb([128, 128], BF16)
            bones = Buf()
            P.op("pool", lambda e: e.memset(ones[:], 1.0), writes=[bones])
            CW = ph.sb([128, 4, 24], F32)
            bCW = Buf()
            for j in range(4):
                P.dma("sp", CW[:, j, :], self.conv_w[layer, j, :].rearrange("(c p) -> p c", p=128), writes=[bCW], slow=True)
            DTB = ph.sb([128, 8], F32)
            NEGA = ph.sb([128, 8], F32)
            bDTB, bNEGA = Buf(), Buf()
            P.dma("sp", DTB[:], self.dt_bias[layer, :].partition_broadcast(128), writes=[bDTB])
            P.dma("sp", NEGA[:], self.a_log[layer, :].partition_broadcast(128), writes=[bNEGA])
            P.op("act", lambda e: e.activation(out=NEGA[:], in_=NEGA[:], func=AF.Exp), reads=[bNEGA], writes=[bNEGA])
            P.op("dve", lambda e: e.tensor_scalar(NEGA[:], NEGA[:], -1.0, None, ALU.mult), reads=[bNEGA], writes=[bNEGA])
            CAR = ph.sb([128, 24, 3], F32)
            ZERO3 = ph.sb([128, 3], F32)
            bZ3 = Buf()
            P.op("pool", lambda e: e.memset(ZERO3[:], 0.0), writes=[bZ3])
            bCAR = bufs(24)
            NB = 2
            XSr = Rot([ph.sb([128, D], F32) for _ in range(4)])
            xTr = [ph.sb([128, 8, 512], BF16) for _ in range(NB)]
            bxTr = [bufs(8) for _ in range(NB)]
            QAr = Rot([ph.sb([128, 10, 512], BF16) for _ in range(2)])
            VAr = Rot([ph.sb([128, 4, 256], BF16) for _ in range(2)])
            BGr = Rot([ph.sb([128, 4, 16], F32) for _ in range(2)])
            TMP = Rot([ph.sb([128, 56], F32) for _ in range(2)])
            Ur = Rot([ph.sb([128, 515], F32) for _ in range(3)])
            ACr = Rot([ph.sb([128, 512], F32) for _ in range(3)])
            Y8 = ph.sb([128, 8, 512], F32)
            SQ8 = ph.sb([128, 8, 512], BF16)
            bY8, bSQ8 = bufs(8), bufs(8)
            RS8 = ph.sb([128, 8, 512], F32)
            bRS8 = bufs(8)
            OCr = Rot([ph.sb([128, 8, 512], BF16) for _ in range(2)])
            pT = Rot([ph.ps() for _ in range(2)])
            pA = Rot([ph.ps() for _ in range(3)])
            pB = Rot([ph.ps() for _ in range(1)])
            pS = Rot([ph.ps() for _ in range(2)])
            ntiles = T // 512
            xs_next = self.issue_x(src, 0, XSr)
            self.transpose_x(xs_next, xTr[0], bxTr[0], pT, ident, bid)
            for ti in range(ntiles):
                t0 = ti * 512
                seq_start = (t0 % L == 0)
                xT, bxT = xTr[ti % NB], bxTr[ti % NB]
                if ti + 1 < ntiles:
                    xs_next = self.issue_x(src, t0 + 512, XSr)
                QA, bQA = QAr.next()
                for c in range(10):
                    pa, bpa = pA.next()
                    for kc in range(8):
                        P.op("pe", lambda e, pa=pa, kc=kc, c=c, xT=xT: e.matmul(
                            pa[:], W[:, kc, c * 128:(c + 1) * 128], xT[:, kc, :], start=(kc == 0), stop=(kc == 7)),
                            reads=[bW[kc], bxT[kc]], writes=[bpa])
                    sc = 0.125 if c < 8 else 1.0
                    P.op("act", lambda e, pa=pa, c=c, QA=QA, sc=sc: e.activation(out=QA[:, c, :], in_=pa[:], func=AF.Copy, scale=sc),
                         reads=[bpa], writes=[bQA])
                P.dma("pool", self.QT[:, t0:t0 + 512].rearrange("(c p) t -> p c t", p=128), QA[:, 0:8, :], reads=[bQA])
                P.dma("pool", self.KT[:, t0:t0 + 512].rearrange("(c p) t -> p c t", p=128), QA[:, 8:10, :], reads=[bQA])
                VAt, bVA = VAr.next()
                BGt, bBG = BGr.next()
                for s in range(4):
                    pb, bpb = pB.next()
                    for kc in range(8):
                        P.op("pe", lambda e, pb=pb, kc=kc, s=s, xT=xT: e.matmul(
                            pb[:, 0:256], xT[:, kc, s * 128:(s + 1) * 128], W[:, kc, C_VA:C_VA + 256],
                            start=(kc == 0), stop=(kc == 7)), reads=[bW[kc], bxT[kc]], writes=[bpb])
                    for kc in range(8):
                        P.op("pe", lambda e, pb=pb, kc=kc, s=s, xT=xT: e.matmul(
                            pb[:, 256:272], xT[:, kc, s * 128:(s + 1) * 128], W[:, kc, C_BETA:C_BETA + 16],
                            start=(kc == 0), stop=(kc == 7)), reads=[bW[kc], bxT[kc]], writes=[bpb])
                    P.op("act", lambda e, pb=pb, s=s, VAt=VAt: e.copy(VAt[:, s, :], pb[:, 0:256]), reads=[bpb], writes=[bVA])
                    tm, btm = TMP.next()
                    P.op("act", lambda e, pb=pb, tm=tm: e.copy(tm[:, 40:56], pb[:, 256:272]), reads=[bpb], writes=[btm])
                    P.op("act", lambda e, tm=tm, s=s, BGt=BGt: e.activation(out=BGt[:, s, 0:8], in_=tm[:, 40:48], func=AF.Exp, scale=-1.0),
                         reads=[btm], writes=[bBG])
                    P.op("dve", lambda e, s=s, BGt=BGt: e.tensor_scalar(BGt[:, s, 0:8], BGt[:, s, 0:8], 1.0, None, ALU.add),
                         reads=[bBG], writes=[bBG])
                    P.op("dve", lambda e, s=s, BGt=BGt: e.reciprocal(BGt[:, s, 0:8], BGt[:, s, 0:8]), reads=[bBG], writes=[bBG])
                    P.op("dve", lambda e, tm=tm: e.tensor_tensor(out=tm[:, 0:8], in0=tm[:, 48:56], in1=DTB[:], op=ALU.add),
                         reads=[btm, bDTB], writes=[btm])
                    P.op("dve", lambda e, tm=tm: e.tensor_scalar(tm[:, 8:16], tm[:, 0:8], -1.0, None, ALU.mult),
                         reads=[btm], writes=[btm])
                    P.op("dve", lambda e, tm=tm: e.tensor_tensor(out=tm[:, 8:16], in0=tm[:, 8:16], in1=tm[:, 0:8], op=ALU.max),
                         reads=[btm], writes=[btm])
                    P.op("act", lambda e, tm=tm: e.activation(out=tm[:, 16:24], in_=tm[:, 8:16], func=AF.Exp, scale=-1.0),
                         reads=[btm], writes=[btm])
                    P.op("dve", lambda e, tm=tm: e.tensor_scalar(tm[:, 16:24], tm[:, 16:24], 1.0, None, ALU.add), reads=[btm], writes=[btm])
                    P.op("act", lambda e, tm=tm: e.activation(out=tm[:, 24:32], in_=tm[:, 16:24], func=AF.Ln),
                         reads=[btm], writes=[btm])
                    P.op("dve", lambda e, tm=tm: e.scalar_tensor_tensor(out=tm[:, 32:40], in0=tm[:, 0:8], scalar=0.0,
                                                                        in1=tm[:, 24:32], op0=ALU.max, op1=ALU.add),
                         reads=[btm], writes=[btm])
                    P.op("dve", lambda e, tm=tm, s=s, BGt=BGt: e.tensor_tensor(out=BGt[:, s, 8:16], in0=tm[:, 32:40], in1=NEGA[:], op=ALU.mult),
                         reads=[btm, bNEGA], writes=[bBG])
                P.dma("pool", self.VA[t0:t0 + 512, :].rearrange("(s p) f -> p s f", p=128), VAt[:], reads=[bVA])
                P.dma("pool", self.BG[t0:t0 + 512, :].rearrange("(s p) f -> p s f", p=128), BGt[:], reads=[bBG])
                for grp in range(3):
                    OC, bOC = OCr.next()
                    for cc in range(8):
                        c = grp * 8 + cc
                        pa, bpa = pA.next()
                        col = C_QKVB + c * 128
                        for kc in range(8):
                            P.op("pe", lambda e, pa=pa, kc=kc, col=col, xT=xT: e.matmul(
                                pa[:], W[:, kc, col:col + 128], xT[:, kc, :], start=(kc == 0), stop=(kc == 7)),
                                reads=[bW[kc], bxT[kc]], writes=[bpa])
                        U, bU = Ur.next()
                        if seq_start:
                            P.op("act", lambda e, U=U: e.copy(U[:, 0:3], ZERO3[:]), reads=[bZ3], writes=[bU])
                        else:
                            P.op("act", lambda e, U=U, c=c: e.copy(U[:, 0:3], CAR[:, c, :]), reads=[bCAR[c]], writes=[bU])
                        P.op("act", lambda e, U=U, pa=pa: e.copy(U[:, 3:515], pa[:]), reads=[bpa], writes=[bU])
                        P.op("act", lambda e, U=U, c=c: e.copy(CAR[:, c, :], U[:, 512:515]), reads=[bU], writes=[bCAR[c]])
                        A1, bA1 = ACr.next()
                        P.op("act", lambda e, U=U, A1=A1, c=c: e.activation(out=A1[:], in_=U[:, 0:512], func=AF.Identity, scale=CW[:, 0, c:c + 1]),
                             reads=[bU, bCW], writes=[bA1])
                        P.op("act", lambda e, U=U, cc=cc, c=c: e.activation(out=Y8[:, cc, :], in_=U[:, 2:514], func=AF.Identity, scale=CW[:, 2, c:c + 1]),
                             reads=[bU, bCW], writes=[bY8[cc]])
                        P.op("dve", lambda e, U=U, A1=A1, c=c: e.scalar_tensor_tensor(out=A1[:], in0=U[:, 1:513], scalar=CW[:, 1, c:c + 1],
                                                                                    in1=A1[:], op0=ALU.mult, op1=ALU.add),
                             reads=[bU, bCW, bA1], writes=[bA1])
                        P.op("dve", lambda e, U=U, cc=cc, c=c: e.scalar_tensor_tensor(out=Y8[:, cc, :], in0=U[:, 3:515], scalar=CW[:, 3, c:c + 1],
                                                                                    in1=Y8[:, cc, :], op0=ALU.mult, op1=ALU.add),
                             reads=[bU, bCW, bY8[cc]], writes=[bY8[cc]])
                        P.op("pool", lambda e, A1=A1, cc=cc: e.tensor_tensor(out=Y8[:, cc, :], in0=A1[:], in1=Y8[:, cc, :], op=ALU.add),
                             reads=[bA1, bY8[cc]], writes=[bY8[cc]])
                    for cc in range(8):
                        if grp == 2:
                            P.op("act", lambda e, cc=cc, OC=OC: e.activation(out=OC[:, cc, :], in_=Y8[:, cc, :], func=AF.Silu), reads=[bY8[cc]], writes=[bOC])
                        else:
                            P.op("act", lambda e, cc=cc: e.activation(out=Y8[:, cc, :], in_=Y8[:, cc, :], func=AF.Silu), reads=[bY8[cc]], writes=[bY8[cc]])
                    if grp < 2:
                        for cc in range(8):
                            P.op("dve", lambda e, cc=cc: e.tensor_tensor(out=SQ8[:, cc, :], in0=Y8[:, cc, :], in1=Y8[:, cc, :], op=ALU.mult),
                                 reads=[bY8[cc]], writes=[bSQ8[cc]])
                    if grp < 2:
                        for cc in range(8):
                            ps_, bps = pS.next()
                            P.op("pe", lambda e, ps_=ps_, cc=cc: e.matmul(ps_[:], ones[:], SQ8[:, cc, :], start=True, stop=True),
                                 reads=[bones, bSQ8[cc]], writes=[bps])
                            P.op("dve", lambda e, ps_=ps_, cc=cc: e.tensor_scalar(RS8[:, cc, :], ps_[:], float(NORM_EPS), None, ALU.add),
                                 reads=[bps], writes=[bRS8[cc]])
                        for cc in range(8):
                            P.op("act", lambda e, cc=cc: e.activation(out=RS8[:, cc, :], in_=RS8[:, cc, :], func=AF.Ln), reads=[bRS8[cc]], writes=[bRS8[cc]])
                        for cc in range(8):
                            P.op("act", lambda e, cc=cc: e.activation(out=RS8[:, cc, :], in_=RS8[:, cc, :], func=AF.Exp, scale=-0.5),
                                 reads=[bRS8[cc]], writes=[bRS8[cc]])
                        qs = (128.0 ** -0.5) if grp == 0 else 1.0
                        for cc in range(8):
                            P.op("dve", lambda e, OC=OC, cc=cc, qs=qs: e.scalar_tensor_tensor(
                                out=OC[:, cc, :], in0=Y8[:, cc, :], scalar=float(qs), in1=RS8[:, cc, :], op0=ALU.mult, op1=ALU.mult),
                                reads=[bY8[cc], bRS8[cc]], writes=[bOC])
                    P.dma("pool", self.QKVB[grp * 1024:(grp + 1) * 1024, t0:t0 + 512].rearrange("(c p) t -> p c t", p=128),
                          OC[:], reads=[bOC])
                if ti + 1 < ntiles:
                    self.transpose_x(xs_next, xTr[(ti + 1) % NB], bxTr[(ti + 1) % NB], pT, ident, bid)

    def m2_phase(self, layer):
        P = self.P
        T, L = self.T, self.L
        with self.phase() as ph:
            EB = ph.sb([128, 2 * 16 * 128], F32)
            MK = ph.sb([128, 2 * 16 * 128], F32)
            bEB, bMK = Buf(), Buf()
            for q4 in range(4):
                sl = slice(q4 * 1024, (q4 + 1) * 1024)
                P.dma("sp", EB[:, sl], self.pbias[:, sl], writes=[bEB])
                P.dma("sp", MK[:, sl], self.pmask[:, sl], writes=[bMK])
            P.op("act", lambda e: e.activation(out=EB[:], in_=EB[:], func=AF.Exp), reads=[bEB], writes=[bEB])
            P.op("pool", lambda e: e.tensor_tensor(out=EB[:], in0=EB[:], in1=MK[:], op=ALU.mult), reads=[bEB, bMK], writes=[bEB])
            SK = ph.sb([1, 16], F32)
            SKR = ph.sb([1, 16, 128], BF16)
            ONE1 = ph.sb([1, 128], F32)
            bSK, bSKR = Buf(), Buf()
            P.dma("sp", SK[:], self.sinks[layer:layer + 1, :], writes=[bSK])
            P.op("act", lambda e: e.activation(out=SK[:], in_=SK[:], func=AF.Exp), reads=[bSK], writes=[bSK])
            P.op("dve", lambda e: e.memset(ONE1[:], 1.0), writes=[bSKR])
            for h in range(16):
                P.op("dve", lambda e, h=h: e.tensor_scalar(SKR[0:1, h, :], ONE1[0:1, :], SK[0:1, h:h + 1], None, ALU.mult),
                     reads=[bSK, bSKR], writes=[bSKR])
            ones = ph.sb([128, 64], BF16)
            bones = Buf()
            P.op("pool", lambda e: e.memset(ones[:], 1.0), writes=[bones])
            Qr = Rot([ph.sb([64, 16, 512], BF16) for _ in range(2)])
            Kr = Rot([ph.sb([64, 4, 640], BF16) for _ in range(2)])
            Vr = Rot([ph.sb([128, 5, 256], BF16) for _ in range(2)])
            ATr = Rot([ph.sb([64, 16, 512], BF16) for _ in range(2)])
            Er = Rot([ph.sb([128, 512], F32) for _ in range(4)])
            Pr = Rot([ph.sb([128, 512], BF16) for _ in range(4)])
            Dr = Rot([ph.sb([64, 512], F32) for _ in range(2)])
            pSr = Rot([ph.ps() for _ in range(4)])
            pOr = Rot([ph.ps() for _ in range(2)])
            pDr = Rot([ph.ps() for _ in range(2)])
            EBv = EB[:].rearrange("p (b h q) -> p b h q", b=2, h=16)
            def m2_loads(t0):
                Qt, bQ = Qr.next()
                Kt, bK = Kr.next()
                Vt, bV = Vr.next()
                P.dma("sp", Qt[:], self.QT[:, t0:t0 + 512].rearrange("(h d) t -> d h t", d=64), writes=[bQ])
                if t0 % L == 0:
                    P.dma("sp", Kt[:, :, 128:640], self.KT[:, t0:t0 + 512].rearrange("(g d) t -> d g t", d=64), writes=[bK])
                    P.dma("sp", Vt[:, 1:5, :], self.VA[t0:t0 + 512, :].rearrange("(b p) c -> p b c", p=128), writes=[bV])
                else:
                    P.dma("sp", Kt[:], self.KT[:, t0 - 128:t0 + 512].rearrange("(g d) t -> d g t", d=64), writes=[bK])
                    P.dma("sp", Vt[:], self.VA[t0 - 128:t0 + 512, :].rearrange("(b p) c -> p b c", p=128), writes=[bV])
                return Qt, bQ, Kt, bK, Vt, bV

            nxt = m2_loads(0)
            for ti in range(T // 512):
                t0 = ti * 512
                seq_start = (t0 % L == 0)
                Qt, bQ, Kt, bK, Vt, bV = nxt
                if ti + 1 < T // 512:
                    nxt = m2_loads(t0 + 512)
                At, bA = ATr.next()
                for i in range(4):
                    first = seq_start and i == 0
                    sbl = [1] if first else [0, 1]
                    for g in range(4):
                        Pb = {}
                        for sb_ in sbl:
                            ps_, bps = pSr.next()
                            P.op("pe", lambda e, ps_=ps_, g=g, i=i, sb_=sb_, Kt=Kt, Qt=Qt: e.matmul(
                                ps_[:], Kt[:, g, (i + sb_) * 128:(i + sb_ + 1) * 128], Qt[:, 4 * g:4 * g + 4, i * 128:(i + 1) * 128],
                                start=True, stop=True), reads=[bK, bQ], writes=[bps])
                            Et, bE = Er.next()
                            P.op("act", lambda e, Et=Et, ps_=ps_: e.activation(out=Et[:], in_=ps_[:], func=AF.Exp),
                                 reads=[bps], writes=[bE])
                            Pt, bP = Pr.next()
                            P.op("pool", lambda e, Et=Et, Pt=Pt, sb_=sb_, g=g: e.tensor_tensor(
                                out=Pt[:].rearrange("p (h q) -> p h q", h=4), in0=Et[:].rearrange("p (h q) -> p h q", h=4),
                                in1=EBv[:, sb_, 4 * g:4 * g + 4, :], op=ALU.mult), reads=[bE, bEB], writes=[bP])
                            Pb[sb_] = (Pt, bP)
                        po, bpo = pOr.next()
                        pd, bpd = pDr.next()
                        for n_, sb_ in enumerate(sbl):
                            Pt, bP = Pb[sb_]
                            P.op("pe", lambda e, po=po, Pt=Pt, sb_=sb_, g=g, i=i, Vt=Vt, n_=n_: e.matmul(
                                po[0:64, :], Vt[:, i + sb_, g * 64:(g + 1) * 64], Pt[:], start=(n_ == 0), stop=(n_ == len(sbl) - 1)),
                                reads=[bV, bP], writes=[bpo])
                        for n_, sb_ in enumerate(sbl):
                            Pt, bP = Pb[sb_]
                            P.op("pe", lambda e, pd=pd, Pt=Pt, n_=n_: e.matmul(
                                pd[0:64, :], ones[:], Pt[:], start=(n_ == 0), stop=False),
                                reads=[bones, bP], writes=[bpd])
                        P.op("pe", lambda e, pd=pd, g=g: e.matmul(
                            pd[0:64, :], ones[0:1, :], SKR[0:1, 4 * g:4 * g + 4, :], start=False, stop=True),
                            reads=[bones, bSKR], writes=[bpd])
                        Dt, bD = Dr.next()
                        P.op("dve", lambda e, Dt=Dt, pd=pd: e.reciprocal(Dt[:], pd[0:64, :]), reads=[bpd], writes=[bD])
                        P.op("dve", lambda e, Dt=Dt, po=po, At=At, g=g, i=i: e.tensor_tensor(
                            out=At[:, 4 * g:4 * g + 4, i * 128:(i + 1) * 128], in0=po[0:64, :].rearrange("p (h q) -> p h q", h=4),
                            in1=Dt[:].rearrange("p (h q) -> p h q", h=4), op=ALU.mult), reads=[bpo, bD], writes=[bA])
                P.dma("pool", self.AT[:, t0:t0 + 512].rearrange("(h d) t -> d h t", d=64), At[:], reads=[bA])

    def m3_phase(self, layer, src):
        P = self.P
        T, L = self.T, self.L
        SD = SOLVE_DT
        with self.phase() as ph:
            WZ = ph.sb([128, 8, 1024], BF16)
            bWZ = bufs(8)
            self.load_w(WZ, bWZ, self.w_in[layer], 8, C_Z, C_Z + 1024, piece=1024)
            identF, bidF = self.make_ident(ph, F32)
            identB = ph.sb([128, 128], BF16)
            bidB = Buf()
            P.op("pool", lambda e: e.tensor_copy(identB[:], identF[:]), reads=[bidF], writes=[bidB])
            if SD == BF16:
                identS, bidS = identB, bidB
            else:
                identS, bidS = identF, bidF
            bC = Buf()

            def mk(shape=(128, 128)):
                return ph.sb(list(shape), F32)

            TRI, LGT, MSU, ONES, SELA, SELB, SAME = mk(), mk(), mk(), mk(), mk(), mk(), mk()
            P.op("pool", lambda e: e.memset(TRI[:], 1.0), writes=[bC])
            P.op("pool", lambda e: e.affine_select(out=TRI[:], in_=TRI[:], pattern=[[1, 128]], compare_op=ALU.is_ge,
                                                   fill=0.0, base=0, channel_multiplier=-1), reads=[bC], writes=[bC])
            P.op("pool", lambda e: e.memset(TRI[0:64, 64:128], 0.0), reads=[bC], writes=[bC])
            P.op("pool", lambda e: e.memset(LGT[:], 1.0), reads=[bC], writes=[bC])
            P.op("pool", lambda e: e.affine_select(out=LGT[:], in_=LGT[:], pattern=[[-1, 128]], compare_op=ALU.is_gt,
                                                   fill=0.0, base=0, channel_multiplier=1), reads=[bC], writes=[bC])
            P.op("pool", lambda e: e.memset(LGT[64:128, 0:64], 0.0), reads=[bC], writes=[bC])
            P.op("pool", lambda e: e.memset(MSU[:], 1.0), reads=[bC], writes=[bC])
            P.op("pool", lambda e: e.affine_select(out=MSU[:], in_=MSU[:], pattern=[[1, 128]], compare_op=ALU.is_gt,
                                                   fill=0.0, base=0, channel_multiplier=-1), reads=[bC], writes=[bC])
            P.op("pool", lambda e: e.memset(MSU[0:64, 64:128], 0.0), reads=[bC], writes=[bC])
            P.op("pool", lambda e: e.memset(ONES[:], 1.0), reads=[bC], writes=[bC])
            P.op("pool", lambda e: e.memset(SELA[:], 0.0), reads=[bC], writes=[bC])
            P.op("pool", lambda e: e.memset(SELA[0:64, :], 1.0), reads=[bC], writes=[bC])
            P.op("pool", lambda e: e.memset(SELB[:], 0.0), reads=[bC], writes=[bC])
            P.op("pool", lambda e: e.memset(SELB[64:128, :], 1.0), reads=[bC], writes=[bC])
            P.op("pool", lambda e: e.memset(SAME[:], 0.0), reads=[bC], writes=[bC])
            P.op("pool", lambda e: e.memset(SAME[0:64, 0:64], 1.0), reads=[bC], writes=[bC])
            P.op("pool", lambda e: e.memset(SAME[64:128, 64:128], 1.0), reads=[bC], writes=[bC])
            NEGM8 = mk((128, 8, 128))
            MSU8 = mk((128, 8, 128))
            ID8 = ph.sb([128, 8, 128], SD)
            for h in range(8):
                P.op("pool", lambda e, h=h: e.tensor_scalar(NEGM8[:, h, :], TRI[:], 1e30, -1e30, ALU.mult, ALU.add),
                     reads=[bC], writes=[bC])
                P.op("pool", lambda e, h=h: e.tensor_copy(MSU8[:, h, :], MSU[:]), reads=[bC], writes=[bC])
                P.op("pool", lambda e, h=h: e.tensor_copy(ID8[:, h, :], identF[:]), reads=[bC, bidF], writes=[bC])
            DNG = mk((128, 8, 128))
            bDNG = Buf()
            for h in range(8):
                P.dma("sp", DNG[:, h, :], self.dn_g[layer, :].partition_broadcast(128), writes=[bDNG])
            S = ph.sb([128, 8, 128], F32)
            SB = ph.sb([128, 8, 128], BF16)
            bS, bSB = bufs(8), bufs(8)
            VN = ph.sb([128, 8, 128], BF16)
            bVN = bufs(8)
            P.op("pool", lambda e: e.memset(VN[:], 0.0), writes=bVN)
            NB = 2
            XSr = Rot([ph.sb([128, D], F32) for _ in range(4)])
            xTr = [ph.sb([128, 8, 512], BF16) for _ in range(NB)]
            bxTr = [bufs(8) for _ in range(NB)]
            QKVr = Rot([ph.sb([128, 24, 512], BF16) for _ in range(2)])
            BGr = Rot([ph.sb([128, 4, 16], F32) for _ in range(2)])
            OGTr = Rot([ph.sb([128, 8, 512], BF16) for _ in range(2)])
            NBG = ph.sb([128, 8], F32)
            bNBG = Buf()
            Rt = ph.sb([128, 8, 128], F32)
            bRt = Buf()
            ET = ph.sb([128, 8, 128], F32)
            ETS = ph.sb([128, 8, 128], F32)
            EGB = ph.sb([128, 8, 128], F32)
            bET, bETS, bEGB = Buf(), Buf(), Buf()
            SM = ph.sb([128, 64], F32)
            bSM = Buf()
            YR = [ph.sb([128, 8, 256], SD) for _ in range(2)]
            bYR = [bufs(2), bufs(2)]
            Z = [ph.sb([128, 8, 128], SD) for _ in range(2)]
            bZ = [bufs(2), bufs(2)]
            XS = ph.sb([128, 8, 128], BF16)
            bXS = bufs(2)
            AIT = ph.sb([128, 8, 128], BF16)
            bAIT = Buf()
            KG = ph.sb([128, 8, 128], BF16)
            KT2 = ph.sb([128, 8, 128], BF16)
            KTOK = ph.sb([128, 8, 128], BF16)
            bKTOK = Buf()
            Vt = ph.sb([128, 8, 128], BF16)
            QD = ph.sb([128, 8, 128], BF16)
            bKG, bKT2, bVt, bQD = Buf(), Buf(), Buf(), Buf()
            UB = ph.sb([128, 8, 128], F32)
            WT = ph.sb([128, 8, 128], BF16)
            bUB, bWT = bufs(8), bufs(2)
            O = ph.sb([128, 8, 128], F32)
            bO = Buf()
            SQ = ph.sb([128, 8, 128], F32)
            ZG = ph.sb([128, 8, 128], F32)
            OG = ph.sb([128, 8, 128], BF16)
            bSQ, bZG, bOG = Buf(), Buf(), Buf()
            pF = Rot([ph.ps() for _ in range(6)])
            pH = Rot([ph.ps(BF16) for _ in range(2)])
            YRv = [y[:].rearrange("p h c -> p (h c)") for y in YR]

            def flat(t, g):
                return t[:, 4 * g:4 * g + 4, :]

            def m3_loads(t0):
                xs = self.issue_x(src, t0, XSr)
                QKV, bQKV = QKVr.next()
                for grp in range(3):
                    P.dma("sp", QKV[:, grp * 8:(grp + 1) * 8, :],
                          self.QKVB[grp * 1024:(grp + 1) * 1024, t0:t0 + 512].rearrange("(c p) t -> p c t", p=128), writes=[bQKV])
                BGt, bBG = BGr.next()
                P.dma("sp", BGt[:], self.BG[t0:t0 + 512, :].rearrange("(s p) f -> p s f", p=128), writes=[bBG])
                return xs, QKV, bQKV, BGt, bBG

            nxt = m3_loads(0)
            self.transpose_x(nxt[0], xTr[0], bxTr[0], pF, identF, bidF)
            for ti in range(T // 512):
                t0 = ti * 512
                seq_start = (t0 % L == 0)
                xT, bxT = xTr[ti % NB], bxTr[ti % NB]
                _, QKV, bQKV, BGt, bBG = nxt
                if ti + 1 < T // 512:
                    nxt = m3_loads(t0 + 512)
                OGT, bOGT = OGTr.next()
                if seq_start:
                    P.op("pool", lambda e: e.memset(S[:], 0.0), writes=bS)
                    P.op("pool", lambda e: e.memset(SB[:], 0.0), writes=bSB)
                for s in range(4):
                    tk = slice(s * 128, (s + 1) * 128)
                    beta = BGt[:, s, 0:8]
                    graw = BGt[:, s, 8:16]
                    P.op("dve", lambda e, beta=beta: e.tensor_scalar(NBG[:], beta, -1.0, None, ALU.mult), reads=[bBG], writes=[bNBG])
                    for h in range(8):
                        P.op("dve", lambda e, h=h, graw=graw: e.tensor_scalar(Rt[:, h, :], TRI[:], graw[:, h:h + 1], None, ALU.mult),
                             reads=[bC, bBG], writes=[bRt])
                    px, bpx = pF.next()
                    P.op("pe", lambda e, px=px, graw=graw: e.matmul(px[:, 0:8], SELA[:], graw, start=True, stop=True),
                         reads=[bC, bBG], writes=[bpx])
                    P.op("pe", lambda e, px=px, graw=graw: e.matmul(px[:, 8:16], SELB[:], graw, start=True, stop=True),
                         reads=[bC, bBG], writes=[bpx])
                    P.op("pe", lambda e, px=px, graw=graw: e.matmul(px[:, 16:24], TRI[:], graw, start=True, stop=True),
                         reads=[bC, bBG], writes=[bpx])
                    P.op("pe", lambda e, px=px, graw=graw: e.matmul(px[:, 24:32], SAME[:], graw, start=True, stop=True),
                         reads=[bC, bBG], writes=[bpx])
                    P.op("act", lambda e, px=px: e.activation(out=SM[:, 0:24], in_=px[:, 0:24], func=AF.Exp), reads=[bpx], writes=[bSM])
                    P.op("act", lambda e, px=px: e.copy(SM[:, 56:64], px[:, 16:24]), reads=[bpx, bSM], writes=[bSM])
                    P.op("dve", lambda e, px=px: e.tensor_tensor(out=SM[:, 32:40], in0=px[:, 24:32], in1=SM[:, 56:64], op=ALU.subtract),
                         reads=[bpx, bSM], writes=[bSM])
                    P.op("act", lambda e: e.activation(out=SM[:, 24:32], in_=SM[:, 32:40], func=AF.Exp), reads=[bSM], writes=[bSM])
                    for g in range(2):
                        pg, bpg = pF.next()
                        rv = flat(Rt, g)
                        P.op("pe", lambda e, pg=pg, rv=rv: e.matmul(pg[:], LGT[:], rv, start=True, stop=False), reads=[bC, bRt], writes=[bpg])
                        P.op("pe", lambda e, pg=pg, g=g: e.matmul(pg[:], identF[:], flat(NEGM8, g), start=False, stop=True),
                             reads=[bC, bidF], writes=[bpg])
                        P.op("act", lambda e, pg=pg, g=g: e.activation(out=flat(ET, g), in_=pg[:].rearrange("p (h c) -> p h c", h=4), func=AF.Exp),
                             reads=[bpg], writes=[bET])
                        pg2, bpg2 = pF.next()
                        P.op("pe", lambda e, pg2=pg2, rv=rv: e.matmul(pg2[:], ONES[:], rv, start=True, stop=True), reads=[bC, bRt], writes=[bpg2])
                        P.op("act", lambda e, pg2=pg2, g=g: e.activation(out=flat(EGB, g), in_=pg2[:].rearrange("p (h c) -> p h c", h=4), func=AF.Exp),
                             reads=[bpg2], writes=[bEGB])
                    P.op("pool", lambda e: e.tensor_tensor(out=ETS[:], in0=ET[:], in1=MSU8[:], op=ALU.mult), reads=[bET, bC], writes=[bETS])
                    if M3STOP <= 1:
                        continue
                    cur = 0
                    for g in range(2):
                        pk, bpk = pF.next()
                        pq, bpq = pF.next()
                        for hh in range(4):
                            h = 4 * g + hh
                            P.op("pe", lambda e, pk=pk, h=h, hh=hh, QKV=QKV, tk=tk: e.matmul(
                                pk[:, hh * 128:(hh + 1) * 128], QKV[:, 8 + h, tk], QKV[:, 8 + h, tk], start=True, stop=True),
                                reads=[bQKV], writes=[bpk])
                            P.op("pe", lambda e, pq=pq, h=h, hh=hh, QKV=QKV, tk=tk: e.matmul(
                                pq[:, hh * 128:(hh + 1) * 128], QKV[:, 8 + h, tk], QKV[:, h, tk], start=True, stop=True),
                                reads=[bQKV], writes=[bpq])
                        for hh in range(4):
                            h = 4 * g + hh
                            P.op("dve", lambda e, pk=pk, h=h, hh=hh: e.scalar_tensor_tensor(
                                out=YR[0][:, h, 0:128], in0=pk[:, hh * 128:(hh + 1) * 128], scalar=NBG[:, h:h + 1],
                                in1=ETS[:, h, :], op0=ALU.mult, op1=ALU.mult), reads=[bpk, bNBG, bETS], writes=[bYR[0][g]])
                        P.op("dve", lambda e, pq=pq, g=g: e.tensor_tensor(out=flat(AIT, g), in0=pq[:].rearrange("p (h c) -> p h c", h=4),
                                                                         in1=flat(ET, g), op=ALU.mult), reads=[bpq, bET], writes=[bAIT])
                    if M3STOP <= 2:
                        continue
                    for g in range(2):
                        P.op("pool", lambda e, g=g: e.tensor_tensor(out=YR[1][:, 4 * g:4 * g + 4, 128:256], in0=YR[0][:, 4 * g:4 * g + 4, 0:128],
                                                                    in1=flat(ID8, g), op=ALU.add), reads=[bYR[0][g], bC], writes=[bYR[1][g]])
                        if SD == BF16:
                            pz, bpz = pH.next()
                        else:
                            pz, bpz = pF.next()
                        for hh in range(4):
                            h = 4 * g + hh
                            P.op("pe", lambda e, pz=pz, h=h, hh=hh: e.transpose(pz[:, hh * 128:(hh + 1) * 128], YR[0][:, h, 0:128], identS[:]),
                                 reads=[bYR[0][g], bidS], writes=[bpz])
                        P.op("act", lambda e, pz=pz, g=g: e.copy(flat(Z[0], g), pz[:, 0:512].rearrange("p (h c) -> p h c", h=4)),
                             reads=[bpz], writes=[bZ[0][g]])
                    for k in range(6):
                        a, b_ = k % 2, (k + 1) % 2
                        for g in range(2):
                            last = (k == 5)
                            if not last:
                                pz, bpz = pF.next()
                                for hh in range(4):
                                    h = 4 * g + hh
                                    P.op("pe", lambda e, pz=pz, h=h, hh=hh, a=a: e.matmul(
                                        pz[:, hh * 128:(hh + 1) * 128], YR[a][:, h, 0:128], Z[a][:, h, :], start=True, stop=True),
                                        reads=[bYR[a][g], bZ[a][g]], writes=[bpz])
                                P.op("act", lambda e, pz=pz, g=g, b_=b_: e.copy(flat(Z[b_], g), pz[:].rearrange("p (h c) -> p h c", h=4)),
                                     reads=[bpz], writes=[bZ[b_][g]])
                            if k == 0:
                                py, bpy = pF.next()
                                for hh in range(4):
                                    h = 4 * g + hh
                                    P.op("pe", lambda e, py=py, h=h, hh=hh: e.matmul(
                                        py[:, hh * 128:(hh + 1) * 128], Z[0][:, h, :], YR[0][:, h, 0:128], start=True, stop=True),
                                        reads=[bYR[0][g], bZ[0][g]], writes=[bpy])
                                P.op("act", lambda e, py=py, g=g: e.copy(YR[1][:, 4 * g:4 * g + 4, 0:128], py[:].rearrange("p (h c) -> p h c", h=4)),
                                     reads=[bpy], writes=[bYR[1][g]])
                            elif not last:
                                for half in range(2):
                                    py, bpy = pF.next()
                                    for hh in range(2):
                                        h = 4 * g + 2 * half + hh
                                        P.op("pe", lambda e, py=py, h=h, hh=hh, a=a: e.matmul(
                                            py[:, hh * 256:(hh + 1) * 256], Z[a][:, h, :], YR[a][:, h, :], start=True, stop=True),
                                            reads=[bYR[a][g], bZ[a][g]], writes=[bpy])
                                    h0 = 4 * g + 2 * half
                                    pv = py[:].rearrange("p (h c) -> p h c", h=2)
                                    P.op("act", lambda e, pv=pv, h0=h0, b_=b_: e.copy(YR[b_][:, h0:h0 + 2, 0:128], pv[:, :, 0:128]),
                                         reads=[bpy], writes=[bYR[b_][g]])
                                    P.op("dve", lambda e, pv=pv, h0=h0, a=a, b_=b_: e.tensor_tensor(
                                        out=YR[b_][:, h0:h0 + 2, 128:256], in0=pv[:, :, 128:256], in1=YR[a][:, h0:h0 + 2, 128:256], op=ALU.add),
                                        reads=[bpy, bYR[a][g]], writes=[bYR[b_][g]])
                            else:
                                py, bpy = pF.next()
                                for hh in range(4):
                                    h = 4 * g + hh
                                    P.op("pe", lambda e, py=py, h=h, hh=hh, a=a: e.matmul(
                                        py[:, hh * 128:(hh + 1) * 128], Z[a][:, h, :], YR[a][:, h, 128:256], start=True, stop=True),
                                        reads=[bYR[a][g], bZ[a][g]], writes=[bpy])
                                P.op("dve", lambda e, py=py, g=g, a=a: e.tensor_tensor(
                                    out=flat(XS, g), in0=py[:].rearrange("p (h c) -> p h c", h=4), in1=YR[a][:, 4 * g:4 * g + 4, 128:256], op=ALU.add),
                                    reads=[bpy, bYR[a][g]], writes=[bXS[g]])
                    if M3STOP <= 3:
                        continue
                    pkt, bpkt = pH.next()
                    for h in range(8 if (M3SUB & 1) else 0):
                        P.op("pe", lambda e, pkt=pkt, h=h, QKV=QKV, tk=tk: e.transpose(pkt[:, h * 128:(h + 1) * 128], QKV[:, 8 + h, tk], identB[:]),
                             reads=[bQKV, bidB], writes=[bpkt])
                    P.op("act", lambda e, pkt=pkt: e.copy(KTOK[:].rearrange("p h c -> p (h c)"), pkt[:]), reads=[bpkt], writes=[bKTOK])
                    for h in range(8):
                        P.op("dve", lambda e, h=h: e.tensor_scalar(KG[:, h, :], KTOK[:, h, :], SM[:, 16 + h:17 + h], None, ALU.mult),
                             reads=[bKTOK, bSM], writes=[bKG])
                        P.op("act", lambda e, h=h: e.activation(out=KT2[:, h, :], in_=KTOK[:, h, :], func=AF.Identity, scale=SM[:, 24 + h:25 + h]),
                             reads=[bKTOK, bSM], writes=[bKT2])
                    pvt, bpvt = pH.next()
                    for h in range(8 if (M3SUB & 2) else 0):
                        P.op("pe", lambda e, pvt=pvt, h=h, QKV=QKV, tk=tk: e.transpose(pvt[:, h * 128:(h + 1) * 128], QKV[:, 16 + h, tk], identB[:]),
                             reads=[bQKV, bidB], writes=[bpvt])
                    if M3SUB & 2:
                        P.op("act", lambda e, pvt=pvt: e.copy(Vt[:].rearrange("p h c -> p (h c)"), pvt[:]), reads=[bpvt], writes=[bVt])
                    if M3SUB & 4:
                        P.op("pool", lambda e, QKV=QKV, tk=tk: e.tensor_tensor(out=QD[:], in0=QKV[:, 0:8, tk], in1=EGB[:], op=ALU.mult),
                             reads=[bQKV, bEGB], writes=[bQD])
                    if M3STOP <= 4:
                        continue
                    for g in range(2):
                        pu, bpu = pF.next()
                        pw, bpw = pF.next()
                        for hh in range(4):
                            h = 4 * g + hh
                            P.op("pe", lambda e, pu=pu, h=h, hh=hh: e.matmul(pu[:, hh * 128:(hh + 1) * 128], XS[:, h, :], Vt[:, h, :], start=True, stop=True),
                                 reads=[bXS[g], bVt], writes=[bpu])
                            P.op("pe", lambda e, pw=pw, h=h, hh=hh: e.matmul(pw[:, hh * 128:(hh + 1) * 128], KG[:, h, :], XS[:, h, :], start=True, stop=True),
                                 reads=[bXS[g], bKG], writes=[bpw])
                        for hh in range(4):
                            h = 4 * g + hh
                            P.op("act", lambda e, pu=pu, h=h, hh=hh, beta=beta: e.activation(out=UB[:, h, :], in_=pu[:, hh * 128:(hh + 1) * 128], func=AF.Identity,
                                                                                          scale=beta[:, h:h + 1]), reads=[bpu, bBG], writes=[bUB[h]])
                        P.op("act", lambda e, pw=pw, g=g: e.copy(flat(WT, g), pw[:].rearrange("p (h c) -> p h c", h=4)), reads=[bpw], writes=[bWT[g]])
                    if M3STOP <= 5:
                        continue
                    for ck in range(2):
                        rows = slice(ck * 64, (ck + 1) * 64)
                        for g in range(2):
                            pv_, bpv = pF.next()
                            for hh in range(4):
                                h = 4 * g + hh
                                P.op("pe", lambda e, pv_=pv_, h=h, hh=hh: e.matmul(pv_[:, hh * 128:(hh + 1) * 128], WT[:, h, :], SB[:, h, :], start=True, stop=True),
                                     reads=[bWT[g], bSB[h]], writes=[bpv])
                            for hh in range(4):
                                h = 4 * g + hh
                                P.op("dve", lambda e, pv_=pv_, h=h, hh=hh, rows=rows: e.scalar_tensor_tensor(
                                    out=VN[rows, h, :], in0=pv_[rows, hh * 128:(hh + 1) * 128], scalar=NBG[rows, h:h + 1], in1=UB[rows, h, :],
                                    op0=ALU.mult, op1=ALU.add), reads=[bpv, bNBG, bUB[h]], writes=[bVN[h]])
                            po, bpo = pF.next()
                            for hh in range(4):
                                h = 4 * g + hh
                                P.op("pe", lambda e, po=po, h=h, hh=hh: e.matmul(po[:, hh * 128:(hh + 1) * 128], QD[:, h, :], SB[:, h, :], start=True, stop=False),
                                     reads=[bQD, bSB[h]], writes=[bpo])
                                P.op("pe", lambda e, po=po, h=h, hh=hh: e.matmul(po[:, hh * 128:(hh + 1) * 128], AIT[:, h, :], VN[:, h, :], start=False, stop=True),
                                     reads=[bAIT, bVN[h]], writes=[bpo])
                            P.op("act", lambda e, po=po, g=g, rows=rows: e.copy(O[rows, 4 * g:4 * g + 4, :], po[rows, :].rearrange("p (h c) -> p h c", h=4)),
                                 reads=[bpo], writes=[bO])
                            ps_, bps = pF.next()
                            for hh in range(4):
                                h = 4 * g + hh
                                P.op("pe", lambda e, ps_=ps_, h=h, hh=hh, rows=rows: e.matmul(ps_[:, hh * 128:(hh + 1) * 128], KT2[rows, h, :], VN[rows, h, :], start=True, stop=True),
                                     reads=[bKT2, bVN[h]], writes=[bps])
                            for hh in range(4):
                                h = 4 * g + hh
                                P.op("dve", lambda e, ps_=ps_, h=h, hh=hh, ck=ck: e.scalar_tensor_tensor(
                                    out=S[:, h, :], in0=S[:, h, :], scalar=SM[:, ck * 8 + h:ck * 8 + h + 1], in1=ps_[:, hh * 128:(hh + 1) * 128],
                                    op0=ALU.mult, op1=ALU.add), reads=[bps, bSM, bS[h]], writes=[bS[h]])
                                P.op("act", lambda e, h=h: e.copy(SB[:, h, :], S[:, h, :]), reads=[bS[h]], writes=[bSB[h]])
                    if M3STOP <= 6:
                        continue
                    pz0, bpz0 = pF.next()
                    pz1, bpz1 = pF.next()
                    for hf, (pz_, bpz_) in enumerate(((pz0, bpz0), (pz1, bpz1))):
                        for kc in range(8):
                            P.op("pe", lambda e, pz_=pz_, kc=kc, hf=hf, xT=xT, tk=tk: e.matmul(pz_[:], xT[:, kc, tk], WZ[:, kc, hf * 512:(hf + 1) * 512],
                                                                                           start=(kc == 0), stop=(kc == 7)),
                                 reads=[bxT[kc], bWZ[kc]], writes=[bpz_])
                        P.op("act", lambda e, pz_=pz_, hf=hf: e.activation(out=ZG[:, 4 * hf:4 * hf + 4, :], in_=pz_[:].rearrange("p (h c) -> p h c", h=4), func=AF.Silu),
                             reads=[bpz_], writes=[bZG])
                    P.op("pool", lambda e: e.tensor_tensor(out=ZG[:], in0=ZG[:], in1=DNG[:], op=ALU.mult), reads=[bZG, bDNG], writes=[bZG])
                    P.op("pool", lambda e: e.tensor_tensor(out=SQ[:], in0=O[:], in1=O[:], op=ALU.mult), reads=[bO], writes=[bSQ])
                    P.op("dve", lambda e: e.tensor_reduce(out=SM[:, 40:48], in_=SQ[:], axis=AX.X, op=ALU.add), reads=[bSQ, bSM], writes=[bSM])
                    P.op("dve", lambda e: e.tensor_scalar(SM[:, 48:56], SM[:, 40:48], 1.0 / 128.0, float(NORM_EPS), ALU.mult, ALU.add), reads=[bSM], writes=[bSM])
                    P.op("act", lambda e: e.activation(out=SM[:, 48:56], in_=SM[:, 48:56], func=AF.Ln), reads=[bSM], writes=[bSM])
                    P.op("act", lambda e: e.activation(out=SM[:, 48:56], in_=SM[:, 48:56], func=AF.Exp, scale=-0.5), reads=[bSM], writes=[bSM])
                    for h in range(8):
                        P.op("dve", lambda e, h=h: e.scalar_tensor_tensor(out=OG[:, h, :], in0=O[:, h, :], scalar=SM[:, 48 + h:49 + h], in1=ZG[:, h, :],
                                                                          op0=ALU.mult, op1=ALU.mult), reads=[bO, bSM, bZG], writes=[bOG])
                    pt_, bpt = pH.next()
                    for h in range(8):
                        P.op("pe", lambda e, pt_=pt_, h=h: e.transpose(pt_[:, h * 128:(h + 1) * 128], OG[:, h, :], identB[:]),
                             reads=[bOG, bidB], writes=[bpt])
                    P.op("act", lambda e, pt_=pt_, OGT=OGT, tk=tk: e.copy(OGT[:, :, tk], pt_[:].rearrange("p (h c) -> p h c", h=8)),
                         reads=[bpt], writes=[bOGT])
                if ti + 1 < T // 512:
                    self.transpose_x(nxt[0], xTr[(ti + 1) % NB], bxTr[(ti + 1) % NB], pF, identF, bidF)
                P.dma("pool", self.OGT[:, t0:t0 + 512].rearrange("(c p) t -> p c t", p=128), OGT[:], reads=[bOGT])

    def m4_phase(self, layer, src, dst):
        P = self.P
        T = self.T
        with self.phase() as ph:
            WA = ph.sb([128, 8, D], BF16)
            WB = ph.sb([128, 8, D], BF16)
            WO = ph.sb([128, 8, D], BF16)
            WG = ph.sb([128, 8, 2 * D], BF16)
            bWA, bWB, bWO, bWG = bufs(8), bufs(8), bufs(8), bufs(8)
            self.load_w(WA, bWA, self.w_a[layer], 8, 0, D, piece=1024)
            self.load_w(WB, bWB, self.w_b[layer], 8, 0, D, piece=1024)
            self.load_w(WG, bWG, self.w_in[layer], 8, C_GATE, C_GATE + 2 * D, piece=1024)
            self.load_w(WO, bWO, self.w_o[layer], 8, 0, D, piece=1024)
            ident, bid = self.make_ident(ph, F32)
            lnc = self.ln_consts(ph, layer, 1)
            NB = 2
            XSr = Rot([ph.sb([128, D], F32) for _ in range(4)])
            XRr = Rot([ph.sb([128, D], F32) for _ in range(2)])
            xTr = [ph.sb([128, 8, 512], BF16) for _ in range(NB)]
            bxTr = [bufs(8) for _ in range(NB)]
            ATr = Rot([ph.sb([128, 8, 512], BF16) for _ in range(2)])
            OGr = Rot([ph.sb([128, 8, 512], BF16) for _ in range(2)])
            MT = ph.sb([128, 8, 512], BF16)
            bMT = bufs(8)
            SGr = Rot([ph.sb([128, 512], F32) for _ in range(4)])
            T1r = Rot([ph.sb([128, 512], F32) for _ in range(2)])
            T2r = Rot([ph.sb([128, 512], F32) for _ in range(2)])
            small = Rot([ph.sb([128, 16], F32) for _ in range(2)])
            pT = Rot([ph.ps() for _ in range(2)])
            pM = Rot([ph.ps() for _ in range(4)])
            pY = [ph.ps() for _ in range(2)]
            bpY = bufs(2)
            def m4_loads(t0):
                xs = self.issue_x(src, t0, XSr)
                At, bA = ATr.next()
                Og, bOg = OGr.next()
                P.dma("sp", At[:], self.AT[:, t0:t0 + 512].rearrange("(c p) t -> p c t", p=128), writes=[bA])
                P.dma("sp", Og[:], self.OGT[:, t0:t0 + 512].rearrange("(c p) t -> p c t", p=128), writes=[bOg])
                return xs, At, bA, Og, bOg

            nxt = m4_loads(0)
            self.transpose_x(nxt[0], xTr[0], bxTr[0], pT, ident, bid)
            for ti in range(T // 512):
                t0 = ti * 512
                xT, bxT = xTr[ti % NB], bxTr[ti % NB]
                _, At, bA, Og, bOg = nxt
                if ti + 1 < T // 512:
                    nxt = m4_loads(t0 + 512)
                for n in range(8):
                    ns = slice(n * 128, (n + 1) * 128)
                    pa, bpa = pM.next()
                    pga, bpga = pM.next()
                    for kc in range(8):
                        P.op("pe", lambda e, pa=pa, kc=kc, ns=ns, At=At: e.matmul(pa[:], WA[:, kc, ns], At[:, kc, :], start=(kc == 0), stop=(kc == 7)),
                             reads=[bWA[kc], bA], writes=[bpa])
                    for kc in range(8):
                        P.op("pe", lambda e, pga=pga, kc=kc, ns=ns, xT=xT: e.matmul(pga[:], WG[:, kc, ns], xT[:, kc, :], start=(kc == 0), stop=(kc == 7)),
                             reads=[bWG[kc], bxT[kc]], writes=[bpga])
                    sga, bsga = SGr.next()
                    P.op("act", lambda e, sga=sga, pga=pga: e.activation(out=sga[:], in_=pga[:], func=AF.Sigmoid), reads=[bpga], writes=[bsga])
                    t1, bt1 = T1r.next()
                    P.op("dve", lambda e, t1=t1, pa=pa, sga=sga: e.tensor_tensor(out=t1[:], in0=pa[:], in1=sga[:], op=ALU.mult),
                         reads=[bpa, bsga], writes=[bt1])
                    pb, bpb = pM.next()
                    pgb, bpgb = pM.next()
                    for kc in range(8):
                        P.op("pe", lambda e, pb=pb, kc=kc, ns=ns, Og=Og: e.matmul(pb[:], WB[:, kc, ns], Og[:, kc, :], start=(kc == 0), stop=(kc == 7)),
                             reads=[bWB[kc], bOg], writes=[bpb])
                    for kc in range(8):
                        P.op("pe", lambda e, pgb=pgb, kc=kc, n=n, xT=xT: e.matmul(pgb[:], WG[:, kc, D + n * 128:D + (n + 1) * 128], xT[:, kc, :],
                                                                               start=(kc == 0), stop=(kc == 7)),
                             reads=[bWG[kc], bxT[kc]], writes=[bpgb])
                    sgb, bsgb = SGr.next()
                    P.op("act", lambda e, sgb=sgb, pgb=pgb: e.activation(out=sgb[:], in_=pgb[:], func=AF.Sigmoid), reads=[bpgb], writes=[bsgb])
                    t2, bt2 = T2r.next()
                    P.op("dve", lambda e, t2=t2, pb=pb, sgb=sgb: e.tensor_tensor(out=t2[:], in0=pb[:], in1=sgb[:], op=ALU.mult),
                         reads=[bpb, bsgb], writes=[bt2])
                    P.op("pool", lambda e, t1=t1, t2=t2, n=n: e.tensor_tensor(out=MT[:, n, :], in0=t1[:], in1=t2[:], op=ALU.add),
                         reads=[bt1, bt2], writes=[bMT[n]])
                if ti + 1 < T // 512:
                    self.transpose_x(nxt[0], xTr[(ti + 1) % NB], bxTr[(ti + 1) % NB], pT, ident, bid)
                for s in range(4):
                    for hf in range(2):
                        for kc in range(8):
                            P.op("pe", lambda e, hf=hf, kc=kc, s=s: e.matmul(pY[hf][:], MT[:, kc, s * 128:(s + 1) * 128], WO[:, kc, hf * 512:(hf + 1) * 512],
                                                                            start=(kc == 0), stop=(kc == 7)),
                                 reads=[bMT[kc], bWO[kc]], writes=[bpY[hf]])
                    self.ln_epilogue(pY, bpY, src, dst, t0 + s * 128, 1.0 / DN_ALPHA, lnc, XRr, small)

    def build(self, upto=99):
        cur = self.x
        n = 0
        for layer in range(self.depth):
            last = (layer == self.depth - 1)
            steps = [
                lambda: self.ffn_phase(layer, 0, cur, self.R[0], 0),
                lambda: self.m1_phase(layer, self.R[0]),
                lambda: self.m2_phase(layer),
                lambda: self.m3_phase(layer, self.R[0]),
                lambda: self.m4_phase(layer, self.R[0], self.R[1]),
                lambda: self.ffn_phase(layer, 1, self.R[1], self.out if last else self.R[0], 2),
            ]
            for st in steps:
                if n < upto:
                    st()
                n += 1
            cur = self.R[0]
        self.top.close()
        return self.nc


def host_pos_tables(rel_bias):
    s = np.arange(128)[:, None]
    q = np.arange(128)[None, :]
    out_b = np.zeros((128, 2, 16, 128), np.float32)
    out_m = np.zeros((128, 2, 16, 128), np.float32)
    for blk in range(2):
        j = s + 128 * blk
        rel = q + 128 - j
        valid = (rel >= 0) & (rel < 128)
        n = np.maximum(rel, 0)
        nf = np.maximum(n, 1).astype(np.float32)
        large = 16 + (np.log(nf / np.float32(16)) / np.float32(np.log(128 / 16)) * np.float32(16)).astype(np.int32)
        large = np.minimum(large, 31)
        bucket = np.where(n < 16, n, large)
        bucket = np.where(valid, bucket, 0)
        g = rel_bias[bucket]
        out_b[:, blk] = np.transpose(g, (0, 2, 1))
        out_m[:, blk] = np.broadcast_to(valid[:, None, :], (128, 16, 128))
    return out_b.reshape(128, -1), out_m.reshape(128, -1)


_NC_CACHE = {}


def kernel(x, rel_bias, ln_g, ln_b, ffn_w13, ffn_w2, w_in, conv_w, a_log, dt_bias,
           dn_norm_g, sinks, w_branch_a, w_branch_b, w_out):
    x = np.asarray(x, np.float32)
    B, L, _ = x.shape
    nseq = B // NCORES
    key = (nseq, L)
    if key not in _NC_CACHE:
        _NC_CACHE[key] = KB(nseq, L, DEPTH).build()
    nc = _NC_CACHE[key]
    pb, pm = host_pos_tables(np.asarray(rel_bias, np.float32))
    f = lambda a: np.ascontiguousarray(np.asarray(a, np.float32))
    shared = dict(ln_g=f(ln_g), ln_b=f(ln_b), ffn_w13=f(ffn_w13), ffn_w2=f(ffn_w2), w_in=f(w_in), conv_w=f(conv_w),
                  a_log=f(a_log), dt_bias=f(dt_bias), dn_norm_g=f(dn_norm_g), sinks=f(sinks),
                  w_branch_a=f(w_branch_a), w_branch_b=f(w_branch_b), w_out=f(w_out), pbias=pb, pmask=pm)
    in_maps = []
    for c in range(NCORES):
        m = dict(shared)
        m["x"] = np.ascontiguousarray(x[c * nseq:(c + 1) * nseq].reshape(nseq * L, D))
        in_maps.append(m)
    res = run_bass_kernel_spmd(nc, in_maps, core_ids=list(range(NCORES)))
    outs = [np.asarray(r["out"], np.float32).reshape(nseq, L, D) for r in res.results]
    return np.concatenate(outs, axis=0)
```

```python
import contextlib
import numpy as np
import concourse.bass as bass
import concourse.mybir as mybir
from concourse.bass_utils import run_bass_kernel_spmd

F32 = mybir.dt.float32
BF16 = mybir.dt.bfloat16
AF = mybir.ActivationFunctionType
ALU = mybir.AluOpType
AX = mybir.AxisListType

D = 1024
DFF = 2816
NIN = 7696
DEPTH = 4
SEQ = 4096
NCORES = 8
LN_EPS = 1e-5
NORM_EPS = 1e-6
DN_ALPHA = (2 * DEPTH) ** 0.25
C_Q0, C_KA, C_VA, C_QKVB, C_BETA, C_DT, C_Z, C_GATE = 0, 1024, 1280, 1536, 4608, 4616, 4624, 5648
SOLVE_DT = BF16
import os
M3STOP = int(os.environ.get("M3STOP", "99"))
M3SUB = int(os.environ.get("M3SUB", "7"))

NDMASEM = 16
ENGS = ("pe", "act", "dve", "pool", "sp")


class Buf:
    __slots__ = ("lw", "rd")

    def __init__(self):
        self.lw = None
        self.rd = []


def bufs(n):
    return [Buf() for _ in range(n)]


class Prog:
    def __init__(self, nc, stack):
        self.nc = nc
        self.ops = {e: [] for e in ENGS}
        self.cnt = {e: 0 for e in ("pe", "act", "dve", "pool")}
        self.seen = {e: {} for e in ENGS}
        self.dq_n = {"sp": 0, "pool": 0}
        self.esem = {e: stack.enter_context(nc.semaphore("s_" + e)) for e in ("pe", "act", "dve", "pool")}
        self.dsem = {}
        for q in ("sp", "pool"):
            for k in range(NDMASEM):
                self.dsem[(q, k)] = stack.enter_context(nc.semaphore("d_%s%d" % (q, k)))
        self.ninstr = 0

    def _kv(self, tok):
        if tok[0] == "e":
            return ("e", tok[1]), tok[2]
        q, i = tok[1], tok[2]
        return ("d", q, i % NDMASEM), 16 * (i // NDMASEM + 1)

    def _deps(self, eng, reads, writes):
        need = {}

        def add(tok):
            if tok is None:
                return
            if tok[0] == "e" and tok[1] == "pe" and eng == "pe":
                return
            k, v = self._kv(tok)
            if need.get(k, 0) < v:
                need[k] = v

        for b in reads:
            add(b.lw)
        for b in writes:
            add(b.lw)
            for t in b.rd:
                add(t)
        out = []
        s = self.seen[eng]
        for k, v in need.items():
            if s.get(k, 0) < v:
                s[k] = v
                out.append((k, v))
        return out

    def _commit(self, tok, reads, writes):
        for b in reads:
            b.rd.append(tok)
            if len(b.rd) > 32:
                best = {}
                for t in b.rd:
                    k, v = self._kv(t)
                    if k not in best or best[k][0] < v:
                        best[k] = (v, t)
                b.rd = [t for (_, t) in best.values()]
        for b in writes:
            b.lw = tok
            b.rd = []

    def op(self, eng, fn, reads=(), writes=()):
        waits = self._deps(eng, reads, writes)
        self.cnt[eng] += 1
        tok = ("e", eng, self.cnt[eng])
        self.ops[eng].append((waits, fn, None))
        self._commit(tok, reads, writes)

    def dma(self, q, out_ap, in_ap, reads=(), writes=(), slow=False):
        i = self.dq_n[q]
        self.dq_n[q] += 1
        tok = ("d", q, i)
        waits = self._deps(q, reads, writes)
        if i >= NDMASEM:
            k, v = self._kv(("d", q, i - NDMASEM))
            if self.seen[q].get(k, 0) < v:
                self.seen[q][k] = v
                waits.append((k, v))
        self.ops[q].append((waits, (out_ap, in_ap, slow), tok))
        self._commit(tok, reads, writes)

    def barrier(self):
        allk = []
        for e, c in self.cnt.items():
            if c:
                allk.append((("e", e), c))
        for q, n in self.dq_n.items():
            for k in range(min(NDMASEM, n)):
                last = ((n - 1 - k) // NDMASEM) * NDMASEM + k
                allk.append((("d", q, k), 16 * (last // NDMASEM + 1)))
        for e in ENGS:
            s = self.seen[e]
            waits = []
            for k, v in allk:
                if k == ("e", "pe") and e == "pe":
                    continue
                if s.get(k, 0) < v:
                    s[k] = v
                    waits.append((k, v))
            if waits:
                self.ops[e].append((waits, None, None))

    def emit(self):
        nc = self.nc

        def semof(k):
            return self.esem[k[1]] if k[0] == "e" else self.dsem[(k[1], k[2])]

        def run(engname, e):
            for waits, fn, tok in self.ops[engname]:
                for k, v in waits:
                    e.wait_ge(semof(k), v)
                if fn is None:
                    continue
                self.ninstr += 1
                if tok is None:
                    fn(e).then_inc(self.esem[engname], 1)
                else:
                    o, i, slow = fn
                    if slow:
                        ins = e.dma_start(out=o, in_=i, allow_slow_non_contiguous=True)
                    else:
                        ins = e.dma_start(out=o, in_=i)
                    ins.then_inc(self.dsem[(tok[1], tok[2] % NDMASEM)], 16)
            self.ops[engname] = []

        with nc.Block() as block:
            @block.tensor
            def _(e):
                run("pe", e)

            @block.scalar
            def _(e):
                run("act", e)

            @block.vector
            def _(e):
                run("dve", e)

            @block.gpsimd
            def _(e):
                run("pool", e)

            @block.sync
            def _(e):
                run("sp", e)


class Phase:
    def __init__(self, kb):
        self.kb = kb
        self.st = contextlib.ExitStack()
        self.n = 0

    def __enter__(self):
        self.st.__enter__()
        return self

    def __exit__(self, *a):
        self.kb.P.barrier()
        self.kb.P.emit()
        return self.st.__exit__(*a)

    def sb(self, shape, dt):
        self.n += 1
        return self.st.enter_context(self.kb.nc.sbuf_tensor("t%d_%d" % (self.kb.phase_id, self.n), list(shape), dt))

    def ps(self, dt=F32):
        self.n += 1
        cols = 512 if dt == F32 else 1024
        return self.st.enter_context(self.kb.nc.psum_tensor("p%d_%d" % (self.kb.phase_id, self.n), [128, cols], dt))


class Rot:
    def __init__(self, tiles):
        self.t = tiles
        self.b = bufs(len(tiles))
        self.i = 0

    def next(self):
        k = self.i % len(self.t)
        self.i += 1
        return self.t[k], self.b[k]


class KB:
    def __init__(self, nseq, L, depth, debug=False):
        self.nseq, self.L, self.depth = nseq, L, depth
        self.T = nseq * L
        self.debug = debug
        self.nc = bass.Bass("TRN2", target_bir_lowering=False)
        self.top = contextlib.ExitStack()
        self.phase_id = 0
        nc = self.nc
        T = self.T
        ext = lambda n, s: nc.dram_tensor(n, list(s), F32, kind="ExternalInput").ap()
        self.x = ext("x", [T, D])
        self.ln_g = ext("ln_g", [DEPTH, 3, D])
        self.ln_b = ext("ln_b", [DEPTH, 3, D])
        self.w13 = ext("ffn_w13", [DEPTH, 2, D, 2 * DFF])
        self.w2 = ext("ffn_w2", [DEPTH, 2, DFF, D])
        self.w_in = ext("w_in", [DEPTH, D, NIN])
        self.conv_w = ext("conv_w", [DEPTH, 4, 3072])
        self.a_log = ext("a_log", [DEPTH, 8])
        self.dt_bias = ext("dt_bias", [DEPTH, 8])
        self.dn_g = ext("dn_norm_g", [DEPTH, 128])
        self.sinks = ext("sinks", [DEPTH, 16])
        self.w_a = ext("w_branch_a", [DEPTH, D, D])
        self.w_b = ext("w_branch_b", [DEPTH, D, D])
        self.w_o = ext("w_out", [DEPTH, D, D])
        self.pbias = ext("pbias", [128, 2 * 16 * 128])
        self.pmask = ext("pmask", [128, 2 * 16 * 128])
        self.out = nc.dram_tensor("out", [T, D], F32, kind="ExternalOutput").ap()
        kind = "ExternalOutput" if debug else "Internal"
        scr = lambda n, s, dt: nc.dram_tensor(n, list(s), dt, kind=kind).ap()
        self.R = [scr("res%d" % i, [T, D], F32) for i in range(2)]
        self.QT = scr("QT", [1024, T], BF16)
        self.KT = scr("KT", [256, T], BF16)
        self.VA = scr("VA", [T, 256], BF16)
        self.QKVB = scr("QKVB", [3072, T], BF16)
        self.BG = scr("BG", [T, 16], F32)
        self.AT = scr("AT", [1024, T], BF16)
        self.OGT = scr("OGT", [1024, T], BF16)
        self.P = Prog(nc, self.top)

    def phase(self):
        self.phase_id += 1
        return Phase(self)

    def make_ident(self, ph, dt):
        P = self.P
        t = ph.sb([128, 128], dt)
        b = Buf()
        P.op("pool", lambda e: e.memset(t[:], 0.0), writes=[b])
        P.op("pool", lambda e: e.affine_select(out=t[:], in_=t[:], pattern=[[-1, 128]], compare_op=ALU.not_equal,
                                               fill=1.0, base=0, channel_multiplier=1), reads=[b], writes=[b])
        return t, b

    def load_w(self, dst, dbufs, src, kcs, c0, c1, piece=2048):
        v = src.rearrange("(kc p) n -> p kc n", p=128)
        for kc in range(kcs):
            a = c0
            while a < c1:
                b = min(c1, a + piece)
                self.P.dma("pool", dst[:, kc, a - c0:b - c0], v[:, kc, a:b], writes=[dbufs[kc]])
                a = b

    def issue_x(self, src, t0, XS4, nsub=4):
        out = []
        for s in range(nsub):
            Xs, bXs = XS4.next()
            self.P.dma("sp", Xs[:], src[t0 + s * 128:t0 + (s + 1) * 128, :], writes=[bXs])
            out.append((Xs, bXs))
        return out

    def transpose_x(self, xs, xT, bxT, pT, ident, bid):
        P = self.P
        for s, (Xs, bXs) in enumerate(xs):
            for half in range(2):
                pt, bpt = pT.next()
                for k4 in range(4):
                    kc = half * 4 + k4
                    P.op("pe", lambda e, pt=pt, k4=k4, kc=kc, Xs=Xs: e.transpose(pt[:, k4 * 128:(k4 + 1) * 128],
                                                                              Xs[:, kc * 128:(kc + 1) * 128], ident[:]),
                         reads=[bXs, bid], writes=[bpt])
                wb = [bxT[half * 4 + k4] for k4 in range(4)]
                dst = xT[:, half * 4:half * 4 + 4, s * 128:(s + 1) * 128]
                srcv = pt[:].rearrange("p (k c) -> p k c", k=4)
                if half == 0:
                    P.op("act", lambda e, dst=dst, srcv=srcv: e.copy(dst, srcv), reads=[bpt], writes=wb)
                else:
                    P.op("dve", lambda e, dst=dst, srcv=srcv: e.tensor_copy(dst, srcv), reads=[bpt], writes=wb)

    def ln_consts(self, ph, layer, idx):
        P = self.P
        G = ph.sb([128, D], F32)
        B = ph.sb([128, D], F32)
        bg, bb = Buf(), Buf()
        P.dma("sp", G[:], self.ln_g[layer, idx, :].partition_broadcast(128), writes=[bg])
        P.dma("sp", B[:], self.ln_b[layer, idx, :].partition_broadcast(128), writes=[bb])
        return (G, bg, B, bb)

    def ln_epilogue(self, py, bpy, src, dst, r0, c, lnc, XRr, small):
        P = self.P
        G, bg, B, bb = lnc
        Rt, bR = XRr.next()
        st, bst = small.next()
        P.dma("sp", Rt[:], src[r0:r0 + 128, :], writes=[bR])
        for hf in range(2):
            P.op("dve", lambda e, hf=hf, Rt=Rt: e.scalar_tensor_tensor(
                out=Rt[:, hf * 512:(hf + 1) * 512], in0=py[hf][:], scalar=float(c),
                in1=Rt[:, hf * 512:(hf + 1) * 512], op0=ALU.mult, op1=ALU.add),
                reads=[bpy[hf], bR], writes=[bR])
        for hf in range(2):
            P.op("dve", lambda e, hf=hf, Rt=Rt, st=st: e.bn_stats(st[:, hf * 6:(hf + 1) * 6], Rt[:, hf * 512:(hf + 1) * 512]),
                 reads=[bR], writes=[bst])
        P.op("dve", lambda e, st=st: e.bn_aggr(st[:, 12:14], st[:, 0:12]), reads=[bst], writes=[bst])
        eps = LN_EPS / (DN_ALPHA ** 2)
        P.op("dve", lambda e, st=st: e.tensor_scalar(st[:, 13:14], st[:, 13:14], float(eps), None, ALU.add),
             reads=[bst], writes=[bst])
        P.op("act", lambda e, st=st: e.activation(out=st[:, 14:15], in_=st[:, 13:14], func=AF.Ln),
             reads=[bst], writes=[bst])
        P.op("act", lambda e, st=st: e.activation(out=st[:, 14:15], in_=st[:, 14:15], func=AF.Exp, scale=-0.5),
             reads=[bst], writes=[bst])
        P.op("dve", lambda e, st=st: e.scalar_tensor_tensor(out=st[:, 15:16], in0=st[:, 12:13], scalar=-1.0,
                                                            in1=st[:, 14:15], op0=ALU.mult, op1=ALU.mult),
             reads=[bst], writes=[bst])
        P.op("act", lambda e, st=st, Rt=Rt: e.activation(out=Rt[:], in_=Rt[:], func=AF.Identity,
                                                        bias=st[:, 15:16], scale=st[:, 14:15]),
             reads=[bst, bR], writes=[bR])
        P.op("pool", lambda e, Rt=Rt: e.tensor_tensor(out=Rt[:], in0=Rt[:], in1=G[:], op=ALU.mult),
             reads=[bR, bg], writes=[bR])
        P.op("pool", lambda e, Rt=Rt: e.tensor_tensor(out=Rt[:], in0=Rt[:], in1=B[:], op=ALU.add),
             reads=[bR, bb], writes=[bR])
        P.dma("pool", dst[r0:r0 + 128, :], Rt[:], reads=[bR])

    def ffn_phase(self, layer, which, src, dst, ln_idx):
        P = self.P
        T = self.T
        with self.phase() as ph:
            W13 = ph.sb([128, 8, 2 * DFF], BF16)
            W2 = ph.sb([128, 22, D], BF16)
            bW13, bW2 = bufs(8), bufs(22)
            self.load_w(W13, bW13, self.w13[layer, which], 8, 0, 2 * DFF, piece=1408)
            self.load_w(W2, bW2, self.w2[layer, which], 22, 0, D, piece=1024)
            ident, bid = self.make_ident(ph, F32)
            lnc = self.ln_consts(ph, layer, ln_idx)
            NB = 2
            XSr = Rot([ph.sb([128, D], F32) for _ in range(4)])
            XRr = Rot([ph.sb([128, D], F32) for _ in range(2)])
            xTr = [ph.sb([128, 8, 512], BF16) for _ in range(NB)]
            bxTr = [bufs(8) for _ in range(NB)]
            HT = ph.sb([128, 22, 512], BF16)
            bHT = bufs(22)
            SG = Rot([ph.sb([128, 512], F32) for _ in range(2)])
            small = Rot([ph.sb([128, 16], F32) for _ in range(2)])
            pT = Rot([ph.ps() for _ in range(2)])
            pGU = Rot([ph.ps() for _ in range(4)])
            pY = [ph.ps() for _ in range(2)]
            bpY = bufs(2)
            ntiles = T // 512
            xs_next = self.issue_x(src, 0, XSr)
            self.transpose_x(xs_next, xTr[0], bxTr[0], pT, ident, bid)
            for ti in range(ntiles):
                t0 = ti * 512
                xT, bxT = xTr[ti % NB], bxTr[ti % NB]
                if ti + 1 < ntiles:
                    xs_next = self.issue_x(src, t0 + 512, XSr)
                for j in range(22):
                    pg, bpg = pGU.next()
                    pu, bpu = pGU.next()
                    for kc in range(8):
                        P.op("pe", lambda e, pg=pg, kc=kc, j=j, xT=xT: e.matmul(
                            pg[:], W13[:, kc, j * 128:(j + 1) * 128], xT[:, kc, :], start=(kc == 0), stop=(kc == 7)),
                            reads=[bW13[kc], bxT[kc]], writes=[bpg])
                    for kc in range(8):
                        P.op("pe", lambda e, pu=pu, kc=kc, j=j, xT=xT: e.matmul(
                            pu[:], W13[:, kc, DFF + j * 128:DFF + (j + 1) * 128], xT[:, kc, :], start=(kc == 0), stop=(kc == 7)),
                            reads=[bW13[kc], bxT[kc]], writes=[bpu])
                    sg, bsg = SG.next()
                    P.op("act", lambda e, sg=sg, pg=pg: e.activation(out=sg[:], in_=pg[:], func=AF.Silu),
                         reads=[bpg], writes=[bsg])
                    P.op("dve", lambda e, sg=sg, pu=pu, j=j: e.tensor_tensor(out=HT[:, j, :], in0=pu[:], in1=sg[:], op=ALU.mult),
                         reads=[bpu, bsg], writes=[bHT[j]])
                if ti + 1 < ntiles:
                    self.transpose_x(xs_next, xTr[(ti + 1) % NB], bxTr[(ti + 1) % NB], pT, ident, bid)
                for s in range(4):
                    for hf in range(2):
                        for j in range(22):
                            P.op("pe", lambda e, hf=hf, j=j, s=s: e.matmul(
                                pY[hf][:], HT[:, j, s * 128:(s + 1) * 128], W2[:, j, hf * 512:(hf + 1) * 512],
                                start=(j == 0), stop=(j == 21)),
                                reads=[bHT[j], bW2[j]], writes=[bpY[hf]])
                    self.ln_epilogue(pY, bpY, src, dst, t0 + s * 128, 0.5 / DN_ALPHA, lnc, XRr, small)

    def m1_phase(self, layer, src):
        P = self.P
        T, L = self.T, self.L
        NW = C_Z
        with self.phase() as ph:
            W = ph.sb([128, 8, NW], BF16)
            bW = bufs(8)
            self.load_w(W, bW, self.w_in[layer], 8, 0, NW, piece=1156)
            ident, bid = self.make_ident(ph, F32)
            ones = ph.sb([128, 128], BF16)
            bones = Buf()
            P.op("pool", lambda e: e.memset(ones[:], 1.0), writes=[bones])
            CW = ph.sb([128, 4, 24], F32)
            bCW = Buf()
            for j in range(4):
                P.dma("sp", CW[:, j, :], self.conv_w[layer, j, :].rearrange("(c p) -> p c", p=128), writes=[bCW], slow=True)
            DTB = ph.sb([128, 8], F32)
            NEGA = ph.sb([128, 8], F32)
            bDTB, bNEGA = Buf(), Buf()
            P.dma("sp", DTB[:], self.dt_bias[layer, :].partition_broadcast(128), writes=[bDTB])
            P.dma("sp", NEGA[:], self.a_log[layer, :].partition_broadcast(128), writes=[bNEGA])
            P.op("act", lambda e: e.activation(out=NEGA[:], in_=NEGA[:], func=AF.Exp), reads=[bNEGA], writes=[bNEGA])
            P.op("dve", lambda e: e.tensor_scalar(NEGA[:], NEGA[:], -1.0, None, ALU.mult), reads=[bNEGA], writes=[bNEGA])
            CAR = ph.sb([128, 24, 3], F32)
            ZERO3 = ph.sb([128, 3], F32)
            bZ3 = Buf()
            P.op("pool", lambda e: e.memset(ZERO3[:], 0.0), writes=[bZ3])
            bCAR = bufs(24)
            NB = 2
            XSr = Rot([ph.sb([128, D], F32) for _ in range(4)])
            xTr = [ph.sb([128, 8, 512], BF16) for _ in range(NB)]
            bxTr = [bufs(8) for _ in range(NB)]
            QAr = Rot([ph.sb([128, 10, 512], BF16) for _ in range(2)])
            VAr = Rot([ph.sb([128, 4, 256], BF16) for _ in range(2)])
            BGr = Rot([ph.sb([128, 4, 16], F32) for _ in range(2)])
            TMP = Rot([ph.sb([128, 56], F32) for _ in range(2)])
            Ur = Rot([ph.sb([128, 515], F32) for _ in range(3)])
            ACr = Rot([ph.sb([128, 512], F32) for _ in range(3)])
            Y8 = ph.sb([128, 8, 512], F32)
            SQ8 = ph.sb([128, 8, 512], BF16)
            bY8, bSQ8 = bufs(8), bufs(8)
            RS8 = ph.sb([128, 8, 512], F32)
            bRS8 = bufs(8)
            OCr = Rot([ph.sb([128, 8, 512], BF16) for _ in range(2)])
            pT = Rot([ph.ps() for _ in range(2)])
            pA = Rot([ph.ps() for _ in range(3)])
            pB = Rot([ph.ps() for _ in range(1)])
            pS = Rot([ph.ps() for _ in range(2)])
            ntiles = T // 512
            xs_next = self.issue_x(src, 0, XSr)
            self.transpose_x(xs_next, xTr[0], bxTr[0], pT, ident, bid)
            for ti in range(ntiles):
                t0 = ti * 512
                seq_start = (t0 % L == 0)
                xT, bxT = xTr[ti % NB], bxTr[ti % NB]
                if ti + 1 < ntiles:
                    xs_next = self.issue_x(src, t0 + 512, XSr)
                QA, bQA = QAr.next()
                for c in range(10):
                    pa, bpa = pA.next()
                    for kc in range(8):
                        P.op("pe", lambda e, pa=pa, kc=kc, c=c, xT=xT: e.matmul(
                            pa[:], W[:, kc, c * 128:(c + 1) * 128], xT[:, kc, :], start=(kc == 0), stop=(kc == 7)),
                            reads=[bW[kc], bxT[kc]], writes=[bpa])
                    sc = 0.125 if c < 8 else 1.0
                    P.op("act", lambda e, pa=pa, c=c, QA=QA, sc=sc: e.activation(out=QA[:, c, :], in_=pa[:], func=AF.Copy, scale=sc),
                         reads=[bpa], writes=[bQA])
                P.dma("pool", self.QT[:, t0:t0 + 512].rearrange("(c p) t -> p c t", p=128), QA[:, 0:8, :], reads=[bQA])
                P.dma("pool", self.KT[:, t0:t0 + 512].rearrange("(c p) t -> p c t", p=128), QA[:, 8:10, :], reads=[bQA])
                VAt, bVA = VAr.next()
                BGt, bBG = BGr.next()
                for s in range(4):
                    pb, bpb = pB.next()
                    for kc in range(8):
                        P.op("pe", lambda e, pb=pb, kc=kc, s=s, xT=xT: e.matmul(
                            pb[:, 0:256], xT[:, kc, s * 128:(s + 1) * 128], W[:, kc, C_VA:C_VA + 256],
                            start=(kc == 0), stop=(kc == 7)), reads=[bW[kc], bxT[kc]], writes=[bpb])
                    for kc in range(8):
                        P.op("pe", lambda e, pb=pb, kc=kc, s=s, xT=xT: e.matmul(
                            pb[:, 256:272], xT[:, kc, s * 128:(s + 1) * 128], W[:, kc, C_BETA:C_BETA + 16],
                            start=(kc == 0), stop=(kc == 7)), reads=[bW[kc], bxT[kc]], writes=[bpb])
                    P.op("act", lambda e, pb=pb, s=s, VAt=VAt: e.copy(VAt[:, s, :], pb[:, 0:256]), reads=[bpb], writes=[bVA])
                    tm, btm = TMP.next()
                    P.op("act", lambda e, pb=pb, tm=tm: e.copy(tm[:, 40:56], pb[:, 256:272]), reads=[bpb], writes=[btm])
                    P.op("act", lambda e, tm=tm, s=s, BGt=BGt: e.activation(out=BGt[:, s, 0:8], in_=tm[:, 40:48], func=AF.Exp, scale=-1.0),
                         reads=[btm], writes=[bBG])
                    P.op("dve", lambda e, s=s, BGt=BGt: e.tensor_scalar(BGt[:, s, 0:8], BGt[:, s, 0:8], 1.0, None, ALU.add),
                         reads=[bBG], writes=[bBG])
                    P.op("dve", lambda e, s=s, BGt=BGt: e.reciprocal(BGt[:, s, 0:8], BGt[:, s, 0:8]), reads=[bBG], writes=[bBG])
                    P.op("dve", lambda e, tm=tm: e.tensor_tensor(out=tm[:, 0:8], in0=tm[:, 48:56], in1=DTB[:], op=ALU.add),
                         reads=[btm, bDTB], writes=[btm])
                    P.op("dve", lambda e, tm=tm: e.tensor_scalar(tm[:, 8:16], tm[:, 0:8], -1.0, None, ALU.mult),
                         reads=[btm], writes=[btm])
                    P.op("dve", lambda e, tm=tm: e.tensor_tensor(out=tm[:, 8:16], in0=tm[:, 8:16], in1=tm[:, 0:8], op=ALU.max),
                         reads=[btm], writes=[btm])
                    P.op("act", lambda e, tm=tm: e.activation(out=tm[:, 16:24], in_=tm[:, 8:16], func=AF.Exp, scale=-1.0),
                         reads=[btm], writes=[btm])
                    P.op("dve", lambda e, tm=tm: e.tensor_scalar(tm[:, 16:24], tm[:, 16:24], 1.0, None, ALU.add), reads=[btm], writes=[btm])
                    P.op("act", lambda e, tm=tm: e.activation(out=tm[:, 24:32], in_=tm[:, 16:24], func=AF.Ln),
                         reads=[btm], writes=[btm])
                    P.op("dve", lambda e, tm=tm: e.scalar_tensor_tensor(out=tm[:, 32:40], in0=tm[:, 0:8], scalar=0.0,
                                                                        in1=tm[:, 24:32], op0=ALU.max, op1=ALU.add),
                         reads=[btm], writes=[btm])
                    P.op("dve", lambda e, tm=tm, s=s, BGt=BGt: e.tensor_tensor(out=BGt[:, s, 8:16], in0=tm[:, 32:40], in1=NEGA[:], op=ALU.mult),
                         reads=[btm, bNEGA], writes=[bBG])
                P.dma("pool", self.VA[t0:t0 + 512, :].rearrange("(s p) f -> p s f", p=128), VAt[:], reads=[bVA])
                P.dma("pool", self.BG[t0:t0 + 512, :].rearrange("(s p) f -> p s f", p=128), BGt[:], reads=[bBG])
                for grp in range(3):
                    OC, bOC = OCr.next()
                    for cc in range(8):
                        c = grp * 8 + cc
                        pa, bpa = pA.next()
                        col = C_QKVB + c * 128
                        for kc in range(8):
                            P.op("pe", lambda e, pa=pa, kc=kc, col=col, xT=xT: e.matmul(
                                pa[:], W[:, kc, col:col + 128], xT[:, kc, :], start=(kc == 0), stop=(kc == 7)),
                                reads=[bW[kc], bxT[kc]], writes=[bpa])
                        U, bU = Ur.next()
                        if seq_start:
                            P.op("act", lambda e, U=U: e.copy(U[:, 0:3], ZERO3[:]), reads=[bZ3], writes=[bU])
                        else:
                            P.op("act", lambda e, U=U, c=c: e.copy(U[:, 0:3], CAR[:, c, :]), reads=[bCAR[c]], writes=[bU])
                        P.op("act", lambda e, U=U, pa=pa: e.copy(U[:, 3:515], pa[:]), reads=[bpa], writes=[bU])
                        P.op("act", lambda e, U=U, c=c: e.copy(CAR[:, c, :], U[:, 512:515]), reads=[bU], writes=[bCAR[c]])
                        A1, bA1 = ACr.next()
                        P.op("act", lambda e, U=U, A1=A1, c=c: e.activation(out=A1[:], in_=U[:, 0:512], func=AF.Identity, scale=CW[:, 0, c:c + 1]),
                             reads=[bU, bCW], writes=[bA1])
                        P.op("act", lambda e, U=U, cc=cc, c=c: e.activation(out=Y8[:, cc, :], in_=U[:, 2:514], func=AF.Identity, scale=CW[:, 2, c:c + 1]),
                             reads=[bU, bCW], writes=[bY8[cc]])
                        P.op("dve", lambda e, U=U, A1=A1, c=c: e.scalar_tensor_tensor(out=A1[:], in0=U[:, 1:513], scalar=CW[:, 1, c:c + 1],
                                                                                    in1=A1[:], op0=ALU.mult, op1=ALU.add),
                             reads=[bU, bCW, bA1], writes=[bA1])
                        P.op("dve", lambda e, U=U, cc=cc, c=c: e.scalar_tensor_tensor(out=Y8[:, cc, :], in0=U[:, 3:515], scalar=CW[:, 3, c:c + 1],
                                                                                    in1=Y8[:, cc, :], op0=ALU.mult, op1=ALU.add),
                             reads=[bU, bCW, bY8[cc]], writes=[bY8[cc]])
                        P.op("pool", lambda e, A1=A1, cc=cc: e.tensor_tensor(out=Y8[:, cc, :], in0=A1[:], in1=Y8[:, cc, :], op=ALU.add),
                             reads=[bA1, bY8[cc]], writes=[bY8[cc]])
                    for cc in range(8):
                        if grp == 2:
                            P.op("act", lambda e, cc=cc, OC=OC: e.activation(out=OC[:, cc, :], in_=Y8[:, cc, :], func=AF.Silu), reads=[bY8[cc]], writes=[bOC])
                        else:
                            P.op("act", lambda e, cc=cc: e.activation(out=Y8[:, cc, :], in_=Y8[:, cc, :], func=AF.Silu), reads=[bY8[cc]], writes=[bY8[cc]])
                    if grp < 2:
                        for cc in range(8):
                            P.op("dve", lambda e, cc=cc: e.tensor_tensor(out=SQ8[:, cc, :], in0=Y8[:, cc, :], in1=Y8[:, cc, :], op=ALU.mult),
                                 reads=[bY8[cc]], writes=[bSQ8[cc]])
                    if grp < 2:
                        for cc in range(8):
                            ps_, bps = pS.next()
                            P.op("pe", lambda e, ps_=ps_, cc=cc: e.matmul(ps_[:], ones[:], SQ8[:, cc, :], start=True, stop=True),
                                 reads=[bones, bSQ8[cc]], writes=[bps])
                            P.op("dve", lambda e, ps_=ps_, cc=cc: e.tensor_scalar(RS8[:, cc, :], ps_[:], float(NORM_EPS), None, ALU.add),
                                 reads=[bps], writes=[bRS8[cc]])
                        for cc in range(8):
                            P.op("act", lambda e, cc=cc: e.activation(out=RS8[:, cc, :], in_=RS8[:, cc, :], func=AF.Ln), reads=[bRS8[cc]], writes=[bRS8[cc]])
                        for cc in range(8):
                            P.op("act", lambda e, cc=cc: e.activation(out=RS8[:, cc, :], in_=RS8[:, cc, :], func=AF.Exp, scale=-0.5),
                                 reads=[bRS8[cc]], writes=[bRS8[cc]])
                        qs = (128.0 ** -0.5) if grp == 0 else 1.0
                        for cc in range(8):
                            P.op("dve", lambda e, OC=OC, cc=cc, qs=qs: e.scalar_tensor_tensor(
                                out=OC[:, cc, :], in0=Y8[:, cc, :], scalar=float(qs), in1=RS8[:, cc, :], op0=ALU.mult, op1=ALU.mult),
                                reads=[bY8[cc], bRS8[cc]], writes=[bOC])
                    P.dma("pool", self.QKVB[grp * 1024:(grp + 1) * 1024, t0:t0 + 512].rearrange("(c p) t -> p c t", p=128),
                          OC[:], reads=[bOC])
                if ti + 1 < ntiles:
                    self.transpose_x(xs_next, xTr[(ti + 1) % NB], bxTr[(ti + 1) % NB], pT, ident, bid)

    def m2_phase(self, layer):
        P = self.P
        T, L = self.T, self.L
        with self.phase() as ph:
            EB = ph.sb([128, 2 * 16 * 128], F32)
            MK = ph.sb([128, 2 * 16 * 128], F32)
            bEB, bMK = Buf(), Buf()
            for q4 in range(4):
                sl = slice(q4 * 1024, (q4 + 1) * 1024)
                P.dma("sp", EB[:, sl], self.pbias[:, sl], writes=[bEB])
                P.dma("sp", MK[:, sl], self.pmask[:, sl], writes=[bMK])
            P.op("act", lambda e: e.activation(out=EB[:], in_=EB[:], func=AF.Exp), reads=[bEB], writes=[bEB])
            P.op("pool", lambda e: e.tensor_tensor(out=EB[:], in0=EB[:], in1=MK[:], op=ALU.mult), reads=[bEB, bMK], writes=[bEB])
            SK = ph.sb([1, 16], F32)
            SKR = ph.sb([1, 16, 128], BF16)
            ONE1 = ph.sb([1, 128], F32)
            bSK, bSKR = Buf(), Buf()
            P.dma("sp", SK[:], self.sinks[layer:layer + 1, :], writes=[bSK])
            P.op("act", lambda e: e.activation(out=SK[:], in_=SK[:], func=AF.Exp), reads=[bSK], writes=[bSK])
            P.op("dve", lambda e: e.memset(ONE1[:], 1.0), writes=[bSKR])
            for h in range(16):
                P.op("dve", lambda e, h=h: e.tensor_scalar(SKR[0:1, h, :], ONE1[0:1, :], SK[0:1, h:h + 1], None, ALU.mult),
                     reads=[bSK, bSKR], writes=[bSKR])
            ones = ph.sb([128, 64], BF16)
            bones = Buf()
            P.op("pool", lambda e: e.memset(ones[:], 1.0), writes=[bones])
            Qr = Rot([ph.sb([64, 16, 512], BF16) for _ in range(2)])
            Kr = Rot([ph.sb([64, 4, 640], BF16) for _ in range(2)])
            Vr = Rot([ph.sb([128, 5, 256], BF16) for _ in range(2)])
            ATr = Rot([ph.sb([64, 16, 512], BF16) for _ in range(2)])
            Er = Rot([ph.sb([128, 512], F32) for _ in range(4)])
            Pr = Rot([ph.sb([128, 512], BF16) for _ in range(4)])
            Dr = Rot([ph.sb([64, 512], F32) for _ in range(2)])
            pSr = Rot([ph.ps() for _ in range(4)])
            pOr = Rot([ph.ps() for _ in range(2)])
            pDr = Rot([ph.ps() for _ in range(2)])
            EBv = EB[:].rearrange("p (b h q) -> p b h q", b=2, h=16)
            def m2_loads(t0):
                Qt, bQ = Qr.next()
                Kt, bK = Kr.next()
                Vt, bV = Vr.next()
                P.dma("sp", Qt[:], self.QT[:, t0:t0 + 512].rearrange("(h d) t -> d h t", d=64), writes=[bQ])
                if t0 % L == 0:
                    P.dma("sp", Kt[:, :, 128:640], self.KT[:, t0:t0 + 512].rearrange("(g d) t -> d g t", d=64), writes=[bK])
                    P.dma("sp", Vt[:, 1:5, :], self.VA[t0:t0 + 512, :].rearrange("(b p) c -> p b c", p=128), writes=[bV])
                else:
                    P.dma("sp", Kt[:], self.KT[:, t0 - 128:t0 + 512].rearrange("(g d) t -> d g t", d=64), writes=[bK])
                    P.dma("sp", Vt[:], self.VA[t0 - 128:t0 + 512, :].rearrange("(b p) c -> p b c", p=128), writes=[bV])
                return Qt, bQ, Kt, bK, Vt, bV

            nxt = m2_loads(0)
            for ti in range(T // 512):
                t0 = ti * 512
                seq_start = (t0 % L == 0)
                Qt, bQ, Kt, bK, Vt, bV = nxt
                if ti + 1 < T // 512:
                    nxt = m2_loads(t0 + 512)
                At, bA = ATr.next()
                for i in range(4):
                    first = seq_start and i == 0
                    sbl = [1] if first else [0, 1]
                    for g in range(4):
                        Pb = {}
                        for sb_ in sbl:
                            ps_, bps = pSr.next()
                            P.op("pe", lambda e, ps_=ps_, g=g, i=i, sb_=sb_, Kt=Kt, Qt=Qt: e.matmul(
                                ps_[:], Kt[:, g, (i + sb_) * 128:(i + sb_ + 1) * 128], Qt[:, 4 * g:4 * g + 4, i * 128:(i + 1) * 128],
                                start=True, stop=True), reads=[bK, bQ], writes=[bps])
                            Et, bE = Er.next()
                            P.op("act", lambda e, Et=Et, ps_=ps_: e.activation(out=Et[:], in_=ps_[:], func=AF.Exp),
                                 reads=[bps], writes=[bE])
                            Pt, bP = Pr.next()
                            P.op("pool", lambda e, Et=Et, Pt=Pt, sb_=sb_, g=g: e.tensor_tensor(
                                out=Pt[:].rearrange("p (h q) -> p h q", h=4), in0=Et[:].rearrange("p (h q) -> p h q", h=4),
                                in1=EBv[:, sb_, 4 * g:4 * g + 4, :], op=ALU.mult), reads=[bE, bEB], writes=[bP])
                            Pb[sb_] = (Pt, bP)
                        po, bpo = pOr.next()
                        pd, bpd = pDr.next()
                        for n_, sb_ in enumerate(sbl):
                            Pt, bP = Pb[sb_]
                            P.op("pe", lambda e, po=po, Pt=Pt, sb_=sb_, g=g, i=i, Vt=Vt, n_=n_: e.matmul(
                                po[0:64, :], Vt[:, i + sb_, g * 64:(g + 1) * 64], Pt[:], start=(n_ == 0), stop=(n_ == len(sbl) - 1)),
                                reads=[bV, bP], writes=[bpo])
                        for n_, sb_ in enumerate(sbl):
                            Pt, bP = Pb[sb_]
                            P.op("pe", lambda e, pd=pd, Pt=Pt, n_=n_: e.matmul(
                                pd[0:64, :], ones[:], Pt[:], start=(n_ == 0), stop=False),
                                reads=[bones, bP], writes=[bpd])
                        P.op("pe", lambda e, pd=pd, g=g: e.matmul(
                            pd[0:64, :], ones[0:1, :], SKR[0:1, 4 * g:4 * g + 4, :], start=False, stop=True),
                            reads=[bones, bSKR], writes=[bpd])
                        Dt, bD = Dr.next()
                        P.op("dve", lambda e, Dt=Dt, pd=pd: e.reciprocal(Dt[:], pd[0:64, :]), reads=[bpd], writes=[bD])
                        P.op("dve", lambda e, Dt=Dt, po=po, At=At, g=g, i=i: e.tensor_tensor(
                            out=At[:, 4 * g:4 * g + 4, i * 128:(i + 1) * 128], in0=po[0:64, :].rearrange("p (h q) -> p h q", h=4),
                            in1=Dt[:].rearrange("p (h q) -> p h q", h=4), op=ALU.mult), reads=[bpo, bD], writes=[bA])
                P.dma("pool", self.AT[:, t0:t0 + 512].rearrange("(h d) t -> d h t", d=64), At[:], reads=[bA])

    def m3_phase(self, layer, src):
        P = self.P
        T, L = self.T, self.L
        SD = SOLVE_DT
        with self.phase() as ph:
            WZ = ph.sb([128, 8, 1024], BF16)
            bWZ = bufs(8)
            self.load_w(WZ, bWZ, self.w_in[layer], 8, C_Z, C_Z + 1024, piece=1024)
            identF, bidF = self.make_ident(ph, F32)
            identB = ph.sb([128, 128], BF16)
            bidB = Buf()
            P.op("pool", lambda e: e.tensor_copy(identB[:], identF[:]), reads=[bidF], writes=[bidB])
            if SD == BF16:
                identS, bidS = identB, bidB
            else:
                identS, bidS = identF, bidF
            bC = Buf()

            def mk(shape=(128, 128)):
                return ph.sb(list(shape), F32)

            TRI, LGT, MSU, ONES, SELA, SELB, SAME = mk(), mk(), mk(), mk(), mk(), mk(), mk()
            P.op("pool", lambda e: e.memset(TRI[:], 1.0), writes=[bC])
            P.op("pool", lambda e: e.affine_select(out=TRI[:], in_=TRI[:], pattern=[[1, 128]], compare_op=ALU.is_ge,
                                                   fill=0.0, base=0, channel_multiplier=-1), reads=[bC], writes=[bC])
            P.op("pool", lambda e: e.memset(TRI[0:64, 64:128], 0.0), reads=[bC], writes=[bC])
            P.op("pool", lambda e: e.memset(LGT[:], 1.0), reads=[bC], writes=[bC])
            P.op("pool", lambda e: e.affine_select(out=LGT[:], in_=LGT[:], pattern=[[-1, 128]], compare_op=ALU.is_gt,
                                                   fill=0.0, base=0, channel_multiplier=1), reads=[bC], writes=[bC])
            P.op("pool", lambda e: e.memset(LGT[64:128, 0:64], 0.0), reads=[bC], writes=[bC])
            P.op("pool", lambda e: e.memset(MSU[:], 1.0), reads=[bC], writes=[bC])
            P.op("pool", lambda e: e.affine_select(out=MSU[:], in_=MSU[:], pattern=[[1, 128]], compare_op=ALU.is_gt,
                                                   fill=0.0, base=0, channel_multiplier=-1), reads=[bC], writes=[bC])
            P.op("pool", lambda e: e.memset(MSU[0:64, 64:128], 0.0), reads=[bC], writes=[bC])
            P.op("pool", lambda e: e.memset(ONES[:], 1.0), reads=[bC], writes=[bC])
            P.op("pool", lambda e: e.memset(SELA[:], 0.0), reads=[bC], writes=[bC])
            P.op("pool", lambda e: e.memset(SELA[0:64, :], 1.0), reads=[bC], writes=[bC])
            P.op("pool", lambda e: e.memset(SELB[:], 0.0), reads=[bC], writes=[bC])
            P.op("pool", lambda e: e.memset(SELB[64:128, :], 1.0), reads=[bC], writes=[bC])
            P.op("pool", lambda e: e.memset(SAME[:], 0.0), reads=[bC], writes=[bC])
            P.op("pool", lambda e: e.memset(SAME[0:64, 0:64], 1.0), reads=[bC], writes=[bC])
            P.op("pool", lambda e: e.memset(SAME[64:128, 64:128], 1.0), reads=[bC], writes=[bC])
            NEGM8 = mk((128, 8, 128))
            MSU8 = mk((128, 8, 128))
            ID8 = ph.sb([128, 8, 128], SD)
            for h in range(8):
                P.op("pool", lambda e, h=h: e.tensor_scalar(NEGM8[:, h, :], TRI[:], 1e30, -1e30, ALU.mult, ALU.add),
                     reads=[bC], writes=[bC])
                P.op("pool", lambda e, h=h: e.tensor_copy(MSU8[:, h, :], MSU[:]), reads=[bC], writes=[bC])
                P.op("pool", lambda e, h=h: e.tensor_copy(ID8[:, h, :], identF[:]), reads=[bC, bidF], writes=[bC])
            DNG = mk((128, 8, 128))
            bDNG = Buf()
            for h in range(8):
                P.dma("sp", DNG[:, h, :], self.dn_g[layer, :].partition_broadcast(128), writes=[bDNG])
            S = ph.sb([128, 8, 128], F32)
            SB = ph.sb([128, 8, 128], BF16)
            bS, bSB = bufs(8), bufs(8)
            VN = ph.sb([128, 8, 128], BF16)
            bVN = bufs(8)
            P.op("pool", lambda e: e.memset(VN[:], 0.0), writes=bVN)
            XSr = Rot([ph.sb([128, D], F32) for _ in range(2)])
            xT1 = ph.sb([128, 8, 512], BF16)
            bxT1 = bufs(8)
            QKVr = Rot([ph.sb([128, 24, 512], BF16) for _ in range(2)])
            BGr = Rot([ph.sb([128, 4, 16], F32) for _ in range(2)])
            OGTr = Rot([ph.sb([128, 8, 512], BF16) for _ in range(2)])
            Rt = ph.sb([128, 8, 128], F32)
            bRt = Buf()
            ET = ph.sb([128, 8, 128], F32)
            ETS = ph.sb([128, 8, 128], F32)
            EGB = ph.sb([128, 8, 128], F32)
            bET, bETS, bEGB = Buf(), Buf(), Buf()
            YR = [ph.sb([128, 8, 256], SD) for _ in range(2)]
            bYR = [bufs(2), bufs(2)]
            Z = [ph.sb([128, 8, 128], SD) for _ in range(2)]
            bZ = [bufs(2), bufs(2)]
            XS = ph.sb([128, 8, 128], BF16)
            bXS = bufs(2)
            KG = ph.sb([128, 8, 128], BF16)
            KTOK = ph.sb([128, 8, 128], BF16)
            bKTOK = Buf()
            Vt = ph.sb([128, 8, 128], BF16)
            bKG, bVt = Buf(), Buf()
            O = ph.sb([128, 8, 128], F32)
            bO = Buf()
            SQ = ph.sb([128, 8, 128], F32)
            OG = ph.sb([128, 8, 128], BF16)
            bSQ, bZG, bOG = Buf(), Buf(), Buf()
            HH = []
            for _ in range(2):
                HH.append((ph.sb([128, 8], F32), Buf(), ph.sb([128, 64], F32), Buf(), ph.sb([128, 8, 128], BF16), Buf(),
                           ph.sb([128, 8, 128], BF16), Buf(), ph.sb([128, 8, 128], BF16), Buf(),
                           ph.sb([128, 8, 128], F32), bufs(8), ph.sb([128, 8, 128], BF16), bufs(2)))
            ZG4 = ph.sb([128, 4, 8, 128], BF16)
            DNGb = DNG
            pF = Rot([ph.ps() for _ in range(6)])
            pH = Rot([ph.ps(BF16) for _ in range(2)])
            YRv = [y[:].rearrange("p h c -> p (h c)") for y in YR]

            def flat(t, g):
                return t[:, 4 * g:4 * g + 4, :]

            def m3_loads(t0):
                QKV, bQKV = QKVr.next()
                for grp in range(3):
                    P.dma("sp", QKV[:, grp * 8:(grp + 1) * 8, :],
                          self.QKVB[grp * 1024:(grp + 1) * 1024, t0:t0 + 512].rearrange("(c p) t -> p c t", p=128), writes=[bQKV])
                BGt, bBG = BGr.next()
                P.dma("sp", BGt[:], self.BG[t0:t0 + 512, :].rearrange("(s p) f -> p s f", p=128), writes=[bBG])
                return QKV, bQKV, BGt, bBG

            def prep(cx):
                tk, s, QKV, bQKV, BGt, bBG, OGT, bOGT = cx["tk"], cx["s"], cx["QKV"], cx["bQKV"], cx["BGt"], cx["bBG"], cx["OGT"], cx["bOGT"]
                NBG, bNBG, SM, bSM, AIT, bAIT, KT2, bKT2, QD, bQD, UB, bUB, WT, bWT = cx["H"]
                beta = BGt[:, s, 0:8]
                graw = BGt[:, s, 8:16]
                P.op("dve", lambda e, beta=beta: e.tensor_scalar(NBG[:], beta, -1.0, None, ALU.mult), reads=[bBG], writes=[bNBG])
                for h in range(8):
                    P.op("dve", lambda e, h=h, graw=graw: e.tensor_scalar(Rt[:, h, :], TRI[:], graw[:, h:h + 1], None, ALU.mult),
                         reads=[bC, bBG], writes=[bRt])
                px, bpx = pF.next()
                P.op("pe", lambda e, px=px, graw=graw: e.matmul(px[:, 0:8], SELA[:], graw, start=True, stop=True),
                     reads=[bC, bBG], writes=[bpx])
                P.op("pe", lambda e, px=px, graw=graw: e.matmul(px[:, 8:16], SELB[:], graw, start=True, stop=True),
                     reads=[bC, bBG], writes=[bpx])
                P.op("pe", lambda e, px=px, graw=graw: e.matmul(px[:, 16:24], TRI[:], graw, start=True, stop=True),
                     reads=[bC, bBG], writes=[bpx])
                P.op("pe", lambda e, px=px, graw=graw: e.matmul(px[:, 24:32], SAME[:], graw, start=True, stop=True),
                     reads=[bC, bBG], writes=[bpx])
                P.op("act", lambda e, px=px: e.activation(out=SM[:, 0:24], in_=px[:, 0:24], func=AF.Exp), reads=[bpx], writes=[bSM])
                P.op("act", lambda e, px=px: e.copy(SM[:, 56:64], px[:, 16:24]), reads=[bpx, bSM], writes=[bSM])
                P.op("dve", lambda e, px=px: e.tensor_tensor(out=SM[:, 32:40], in0=px[:, 24:32], in1=SM[:, 56:64], op=ALU.subtract),
                     reads=[bpx, bSM], writes=[bSM])
                P.op("act", lambda e: e.activation(out=SM[:, 24:32], in_=SM[:, 32:40], func=AF.Exp), reads=[bSM], writes=[bSM])
                for g in range(2):
                    pg, bpg = pF.next()
                    rv = flat(Rt, g)
                    P.op("pe", lambda e, pg=pg, rv=rv: e.matmul(pg[:], LGT[:], rv, start=True, stop=False), reads=[bC, bRt], writes=[bpg])
                    P.op("pe", lambda e, pg=pg, g=g: e.matmul(pg[:], identF[:], flat(NEGM8, g), start=False, stop=True),
                         reads=[bC, bidF], writes=[bpg])
                    P.op("act", lambda e, pg=pg, g=g: e.activation(out=flat(ET, g), in_=pg[:].rearrange("p (h c) -> p h c", h=4), func=AF.Exp),
                         reads=[bpg], writes=[bET])
                    pg2, bpg2 = pF.next()
                    P.op("pe", lambda e, pg2=pg2, rv=rv: e.matmul(pg2[:], ONES[:], rv, start=True, stop=True), reads=[bC, bRt], writes=[bpg2])
                    P.op("act", lambda e, pg2=pg2, g=g: e.activation(out=flat(EGB, g), in_=pg2[:].rearrange("p (h c) -> p h c", h=4), func=AF.Exp),
                         reads=[bpg2], writes=[bEGB])
                P.op("pool", lambda e: e.tensor_tensor(out=ETS[:], in0=ET[:], in1=MSU8[:], op=ALU.mult), reads=[bET, bC], writes=[bETS])
                yield
                cur = 0
                for g in range(2):
                    pk, bpk = pF.next()
                    pq, bpq = pF.next()
                    for hh in range(4):
                        h = 4 * g + hh
                        P.op("pe", lambda e, pk=pk, h=h, hh=hh, QKV=QKV, tk=tk: e.matmul(
                            pk[:, hh * 128:(hh + 1) * 128], QKV[:, 8 + h, tk], QKV[:, 8 + h, tk], start=True, stop=True),
                            reads=[bQKV], writes=[bpk])
                        P.op("pe", lambda e, pq=pq, h=h, hh=hh, QKV=QKV, tk=tk: e.matmul(
                            pq[:, hh * 128:(hh + 1) * 128], QKV[:, 8 + h, tk], QKV[:, h, tk], start=True, stop=True),
                            reads=[bQKV], writes=[bpq])
                    for hh in range(4):
                        h = 4 * g + hh
                        P.op("dve", lambda e, pk=pk, h=h, hh=hh: e.scalar_tensor_tensor(
                            out=YR[0][:, h, 0:128], in0=pk[:, hh * 128:(hh + 1) * 128], scalar=NBG[:, h:h + 1],
                            in1=ETS[:, h, :], op0=ALU.mult, op1=ALU.mult), reads=[bpk, bNBG, bETS], writes=[bYR[0][g]])
                    P.op("dve", lambda e, pq=pq, g=g: e.tensor_tensor(out=flat(AIT, g), in0=pq[:].rearrange("p (h c) -> p h c", h=4),
                                                                     in1=flat(ET, g), op=ALU.mult), reads=[bpq, bET], writes=[bAIT])
                yield
                for g in range(2):
                    P.op("pool", lambda e, g=g: e.tensor_tensor(out=YR[1][:, 4 * g:4 * g + 4, 128:256], in0=YR[0][:, 4 * g:4 * g + 4, 0:128],
                                                                in1=flat(ID8, g), op=ALU.add), reads=[bYR[0][g], bC], writes=[bYR[1][g]])
                    if SD == BF16:
                        pz, bpz = pH.next()
                    else:
                        pz, bpz = pF.next()
                    for hh in range(4):
                        h = 4 * g + hh
                        P.op("pe", lambda e, pz=pz, h=h, hh=hh: e.transpose(pz[:, hh * 128:(hh + 1) * 128], YR[0][:, h, 0:128], identS[:]),
                             reads=[bYR[0][g], bidS], writes=[bpz])
                    P.op("act", lambda e, pz=pz, g=g: e.copy(flat(Z[0], g), pz[:, 0:512].rearrange("p (h c) -> p h c", h=4)),
                         reads=[bpz], writes=[bZ[0][g]])
                for k in range(6):
                    yield
                    a, b_ = k % 2, (k + 1) % 2
                    for g in range(2):
                        last = (k == 5)
                        if not last:
                            pz, bpz = pF.next()
                            for hh in range(4):
                                h = 4 * g + hh
                                P.op("pe", lambda e, pz=pz, h=h, hh=hh, a=a: e.matmul(
                                    pz[:, hh * 128:(hh + 1) * 128], YR[a][:, h, 0:128], Z[a][:, h, :], start=True, stop=True),
                                    reads=[bYR[a][g], bZ[a][g]], writes=[bpz])
                            P.op("act", lambda e, pz=pz, g=g, b_=b_: e.copy(flat(Z[b_], g), pz[:].rearrange("p (h c) -> p h c", h=4)),
                                 reads=[bpz], writes=[bZ[b_][g]])
                        if k == 0:
                            py, bpy = pF.next()
                            for hh in range(4):
                                h = 4 * g + hh
                                P.op("pe", lambda e, py=py, h=h, hh=hh: e.matmul(
                                    py[:, hh * 128:(hh + 1) * 128], Z[0][:, h, :], YR[0][:, h, 0:128], start=True, stop=True),
                                    reads=[bYR[0][g], bZ[0][g]], writes=[bpy])
                            P.op("act", lambda e, py=py, g=g: e.copy(YR[1][:, 4 * g:4 * g + 4, 0:128], py[:].rearrange("p (h c) -> p h c", h=4)),
                                 reads=[bpy], writes=[bYR[1][g]])
                        elif not last:
                            for half in range(2):
                                py, bpy = pF.next()
                                for hh in range(2):
                                    h = 4 * g + 2 * half + hh
                                    P.op("pe", lambda e, py=py, h=h, hh=hh, a=a: e.matmul(
                                        py[:, hh * 256:(hh + 1) * 256], Z[a][:, h, :], YR[a][:, h, :], start=True, stop=True),
                                        reads=[bYR[a][g], bZ[a][g]], writes=[bpy])
                                h0 = 4 * g + 2 * half
                                pv = py[:].rearrange("p (h c) -> p h c", h=2)
                                P.op("act", lambda e, pv=pv, h0=h0, b_=b_: e.copy(YR[b_][:, h0:h0 + 2, 0:128], pv[:, :, 0:128]),
                                     reads=[bpy], writes=[bYR[b_][g]])
                                P.op("dve", lambda e, pv=pv, h0=h0, a=a, b_=b_: e.tensor_tensor(
                                    out=YR[b_][:, h0:h0 + 2, 128:256], in0=pv[:, :, 128:256], in1=YR[a][:, h0:h0 + 2, 128:256], op=ALU.add),
                                    reads=[bpy, bYR[a][g]], writes=[bYR[b_][g]])
                        else:
                            py, bpy = pF.next()
                            for hh in range(4):
                                h = 4 * g + hh
                                P.op("pe", lambda e, py=py, h=h, hh=hh, a=a: e.matmul(
                                    py[:, hh * 128:(hh + 1) * 128], Z[a][:, h, :], YR[a][:, h, 128:256], start=True, stop=True),
                                    reads=[bYR[a][g], bZ[a][g]], writes=[bpy])
                            P.op("dve", lambda e, py=py, g=g, a=a: e.tensor_tensor(
                                out=flat(XS, g), in0=py[:].rearrange("p (h c) -> p h c", h=4), in1=YR[a][:, 4 * g:4 * g + 4, 128:256], op=ALU.add),
                                reads=[bpy, bYR[a][g]], writes=[bXS[g]])
                yield
                pkt, bpkt = pH.next()
                for h in range(8 if (M3SUB & 1) else 0):
                    P.op("pe", lambda e, pkt=pkt, h=h, QKV=QKV, tk=tk: e.transpose(pkt[:, h * 128:(h + 1) * 128], QKV[:, 8 + h, tk], identB[:]),
                         reads=[bQKV, bidB], writes=[bpkt])
                P.op("act", lambda e, pkt=pkt: e.copy(KTOK[:].rearrange("p h c -> p (h c)"), pkt[:]), reads=[bpkt], writes=[bKTOK])
                for h in range(8):
                    P.op("dve", lambda e, h=h: e.tensor_scalar(KG[:, h, :], KTOK[:, h, :], SM[:, 16 + h:17 + h], None, ALU.mult),
                         reads=[bKTOK, bSM], writes=[bKG])
                    P.op("act", lambda e, h=h: e.activation(out=KT2[:, h, :], in_=KTOK[:, h, :], func=AF.Identity, scale=SM[:, 24 + h:25 + h]),
                         reads=[bKTOK, bSM], writes=[bKT2])
                pvt, bpvt = pH.next()
                for h in range(8 if (M3SUB & 2) else 0):
                    P.op("pe", lambda e, pvt=pvt, h=h, QKV=QKV, tk=tk: e.transpose(pvt[:, h * 128:(h + 1) * 128], QKV[:, 16 + h, tk], identB[:]),
                         reads=[bQKV, bidB], writes=[bpvt])
                if M3SUB & 2:
                    P.op("act", lambda e, pvt=pvt: e.copy(Vt[:].rearrange("p h c -> p (h c)"), pvt[:]), reads=[bpvt], writes=[bVt])
                if M3SUB & 4:
                    P.op("pool", lambda e, QKV=QKV, tk=tk: e.tensor_tensor(out=QD[:], in0=QKV[:, 0:8, tk], in1=EGB[:], op=ALU.mult),
                         reads=[bQKV, bEGB], writes=[bQD])
                yield
                for g in range(2):
                    pu, bpu = pF.next()
                    pw, bpw = pF.next()
                    for hh in range(4):
                        h = 4 * g + hh
                        P.op("pe", lambda e, pu=pu, h=h, hh=hh: e.matmul(pu[:, hh * 128:(hh + 1) * 128], XS[:, h, :], Vt[:, h, :], start=True, stop=True),
                             reads=[bXS[g], bVt], writes=[bpu])
                        P.op("pe", lambda e, pw=pw, h=h, hh=hh: e.matmul(pw[:, hh * 128:(hh + 1) * 128], KG[:, h, :], XS[:, h, :], start=True, stop=True),
                             reads=[bXS[g], bKG], writes=[bpw])
                    for hh in range(4):
                        h = 4 * g + hh
                        P.op("act", lambda e, pu=pu, h=h, hh=hh, beta=beta: e.activation(out=UB[:, h, :], in_=pu[:, hh * 128:(hh + 1) * 128], func=AF.Identity,
                                                                                      scale=beta[:, h:h + 1]), reads=[bpu, bBG], writes=[bUB[h]])
                    P.op("act", lambda e, pw=pw, g=g: e.copy(flat(WT, g), pw[:].rearrange("p (h c) -> p h c", h=4)), reads=[bpw], writes=[bWT[g]])

            def rec(cx):
                tk, s, QKV, bQKV, BGt, bBG, OGT, bOGT = cx["tk"], cx["s"], cx["QKV"], cx["bQKV"], cx["BGt"], cx["bBG"], cx["OGT"], cx["bOGT"]
                NBG, bNBG, SM, bSM, AIT, bAIT, KT2, bKT2, QD, bQD, UB, bUB, WT, bWT = cx["H"]
                beta = BGt[:, s, 0:8]
                graw = BGt[:, s, 8:16]
                if cx["tile_first"]:
                    for sb4 in range(4):
                        Xs_, bXs_ = XSr.next()
                        P.dma("sp", Xs_[:], src[cx["t0"] + sb4 * 128:cx["t0"] + (sb4 + 1) * 128, :], writes=[bXs_])
                        self.transpose_x([(Xs_, bXs_)], xT1[:, :, sb4 * 128:(sb4 + 1) * 128], bxT1, pF, identF, bidF)
                    for sb4 in range(4):
                        for hf in range(2):
                            pz_, bpz_ = pF.next()
                            for kc in range(8):
                                P.op("pe", lambda e, pz_=pz_, kc=kc, hf=hf, sb4=sb4: e.matmul(pz_[:], xT1[:, kc, sb4 * 128:(sb4 + 1) * 128], WZ[:, kc, hf * 512:(hf + 1) * 512],
                                                                                           start=(kc == 0), stop=(kc == 7)),
                                     reads=[bxT1[kc], bWZ[kc]], writes=[bpz_])
                            P.op("act", lambda e, pz_=pz_, hf=hf, sb4=sb4: e.activation(out=ZG4[:, sb4, 4 * hf:4 * hf + 4, :], in_=pz_[:].rearrange("p (h c) -> p h c", h=4), func=AF.Silu),
                                 reads=[bpz_], writes=[bZG])
                        P.op("pool", lambda e, sb4=sb4: e.tensor_tensor(out=ZG4[:, sb4, :, :], in0=ZG4[:, sb4, :, :], in1=DNGb[:], op=ALU.mult), reads=[bZG, bDNG], writes=[bZG])
                        yield
                if cx["seq_reset"]:
                    P.op("pool", lambda e: e.memset(S[:], 0.0), writes=bS)
                    P.op("pool", lambda e: e.memset(SB[:], 0.0), writes=bSB)
                for ck in range(2):
                    rows = slice(ck * 64, (ck + 1) * 64)
                    for g in range(2):
                        yield
                        pv_, bpv = pF.next()
                        for hh in range(4):
                            h = 4 * g + hh
                            P.op("pe", lambda e, pv_=pv_, h=h, hh=hh: e.matmul(pv_[:, hh * 128:(hh + 1) * 128], WT[:, h, :], SB[:, h, :], start=True, stop=True),
                                 reads=[bWT[g], bSB[h]], writes=[bpv])
                        for hh in range(4):
                            h = 4 * g + hh
                            P.op("dve", lambda e, pv_=pv_, h=h, hh=hh, rows=rows: e.scalar_tensor_tensor(
                                out=VN[rows, h, :], in0=pv_[rows, hh * 128:(hh + 1) * 128], scalar=NBG[rows, h:h + 1], in1=UB[rows, h, :],
                                op0=ALU.mult, op1=ALU.add), reads=[bpv, bNBG, bUB[h]], writes=[bVN[h]])
                        po, bpo = pF.next()
                        for hh in range(4):
                            h = 4 * g + hh
                            P.op("pe", lambda e, po=po, h=h, hh=hh: e.matmul(po[:, hh * 128:(hh + 1) * 128], QD[:, h, :], SB[:, h, :], start=True, stop=False),
                                 reads=[bQD, bSB[h]], writes=[bpo])
                            P.op("pe", lambda e, po=po, h=h, hh=hh: e.matmul(po[:, hh * 128:(hh + 1) * 128], AIT[:, h, :], VN[:, h, :], start=False, stop=True),
                                 reads=[bAIT, bVN[h]], writes=[bpo])
                        P.op("act", lambda e, po=po, g=g, rows=rows: e.copy(O[rows, 4 * g:4 * g + 4, :], po[rows, :].rearrange("p (h c) -> p h c", h=4)),
                             reads=[bpo], writes=[bO])
                        ps_, bps = pF.next()
                        for hh in range(4):
                            h = 4 * g + hh
                            P.op("pe", lambda e, ps_=ps_, h=h, hh=hh, rows=rows: e.matmul(ps_[:, hh * 128:(hh + 1) * 128], KT2[rows, h, :], VN[rows, h, :], start=True, stop=True),
                                 reads=[bKT2, bVN[h]], writes=[bps])
                        for hh in range(4):
                            h = 4 * g + hh
                            P.op("dve", lambda e, ps_=ps_, h=h, hh=hh, ck=ck: e.scalar_tensor_tensor(
                                out=S[:, h, :], in0=S[:, h, :], scalar=SM[:, ck * 8 + h:ck * 8 + h + 1], in1=ps_[:, hh * 128:(hh + 1) * 128],
                                op0=ALU.mult, op1=ALU.add), reads=[bps, bSM, bS[h]], writes=[bS[h]])
                            P.op("act", lambda e, h=h: e.copy(SB[:, h, :], S[:, h, :]), reads=[bS[h]], writes=[bSB[h]])
                yield
                P.op("pool", lambda e: e.tensor_tensor(out=SQ[:], in0=O[:], in1=O[:], op=ALU.mult), reads=[bO], writes=[bSQ])
                P.op("dve", lambda e: e.tensor_reduce(out=SM[:, 40:48], in_=SQ[:], axis=AX.X, op=ALU.add), reads=[bSQ, bSM], writes=[bSM])
                P.op("dve", lambda e: e.tensor_scalar(SM[:, 48:56], SM[:, 40:48], 1.0 / 128.0, float(NORM_EPS), ALU.mult, ALU.add), reads=[bSM], writes=[bSM])
                P.op("act", lambda e: e.activation(out=SM[:, 48:56], in_=SM[:, 48:56], func=AF.Ln), reads=[bSM], writes=[bSM])
                P.op("act", lambda e: e.activation(out=SM[:, 48:56], in_=SM[:, 48:56], func=AF.Exp, scale=-0.5), reads=[bSM], writes=[bSM])
                for h in range(8):
                    P.op("dve", lambda e, h=h: e.scalar_tensor_tensor(out=OG[:, h, :], in0=O[:, h, :], scalar=SM[:, 48 + h:49 + h], in1=ZG4[:, s, h, :],
                                                                      op0=ALU.mult, op1=ALU.mult), reads=[bO, bSM, bZG], writes=[bOG])
                pt_, bpt = pH.next()
                for h in range(8):
                    P.op("pe", lambda e, pt_=pt_, h=h: e.transpose(pt_[:, h * 128:(h + 1) * 128], OG[:, h, :], identB[:]),
                         reads=[bOG, bidB], writes=[bpt])
                P.op("act", lambda e, pt_=pt_, OGT=OGT, tk=tk: e.copy(OGT[:, :, tk], pt_[:].rearrange("p (h c) -> p h c", h=8)),
                     reads=[bpt], writes=[bOGT])
                if cx["tile_last"]:
                    P.dma("pool", self.OGT[:, cx["t0"]:cx["t0"] + 512].rearrange("(c p) t -> p c t", p=128), OGT[:], reads=[bOGT])

            def interleave(g1, g2):
                gens = [g for g in (g1, g2) if g is not None]
                while gens:
                    for g in list(gens):
                        try:
                            next(g)
                        except StopIteration:
                            gens.remove(g)

            ntl = T // 512
            tl = {0: m3_loads(0)}
            prev = None
            for ti in range(ntl):
                t0 = ti * 512
                QKV, bQKV, BGt, bBG = tl.pop(ti)
                if ti + 1 < ntl:
                    tl[ti + 1] = m3_loads(t0 + 512)
                OGT, bOGT = OGTr.next()
                for s in range(4):
                    gb = ti * 4 + s
                    cx = dict(tk=slice(s * 128, (s + 1) * 128), s=s, QKV=QKV, bQKV=bQKV, BGt=BGt, bBG=bBG, OGT=OGT, bOGT=bOGT,
                              H=HH[gb % 2], t0=t0, tile_first=(s == 0), tile_last=(s == 3), seq_reset=(s == 0 and t0 % L == 0))
                    interleave(prep(cx), rec(prev) if prev is not None else None)
                    prev = cx
            interleave(rec(prev), None)

    def m4_phase(self, layer, src, dst):
        P = self.P
        T = self.T
        with self.phase() as ph:
            WA = ph.sb([128, 8, D], BF16)
            WB = ph.sb([128, 8, D], BF16)
            WO = ph.sb([128, 8, D], BF16)
            WG = ph.sb([128, 8, 2 * D], BF16)
            bWA, bWB, bWO, bWG = bufs(8), bufs(8), bufs(8), bufs(8)
            self.load_w(WA, bWA, self.w_a[layer], 8, 0, D, piece=1024)
            self.load_w(WB, bWB, self.w_b[layer], 8, 0, D, piece=1024)
            self.load_w(WG, bWG, self.w_in[layer], 8, C_GATE, C_GATE + 2 * D, piece=1024)
            self.load_w(WO, bWO, self.w_o[layer], 8, 0, D, piece=1024)
            ident, bid = self.make_ident(ph, F32)
            lnc = self.ln_consts(ph, layer, 1)
            NB = 2
            XSr = Rot([ph.sb([128, D], F32) for _ in range(4)])
            XRr = Rot([ph.sb([128, D], F32) for _ in range(2)])
            xTr = [ph.sb([128, 8, 512], BF16) for _ in range(NB)]
            bxTr = [bufs(8) for _ in range(NB)]
            ATr = Rot([ph.sb([128, 8, 512], BF16) for _ in range(2)])
            OGr = Rot([ph.sb([128, 8, 512], BF16) for _ in range(2)])
            MT = ph.sb([128, 8, 512], BF16)
            bMT = bufs(8)
            SGr = Rot([ph.sb([128, 512], F32) for _ in range(4)])
            T1r = Rot([ph.sb([128, 512], F32) for _ in range(2)])
            T2r = Rot([ph.sb([128, 512], F32) for _ in range(2)])
            small = Rot([ph.sb([128, 16], F32) for _ in range(2)])
            pT = Rot([ph.ps() for _ in range(2)])
            pM = Rot([ph.ps() for _ in range(4)])
            pY = [ph.ps() for _ in range(2)]
            bpY = bufs(2)
            def m4_loads(t0):
                xs = self.issue_x(src, t0, XSr)
                At, bA = ATr.next()
                Og, bOg = OGr.next()
                P.dma("sp", At[:], self.AT[:, t0:t0 + 512].rearrange("(c p) t -> p c t", p=128), writes=[bA])
                P.dma("sp", Og[:], self.OGT[:, t0:t0 + 512].rearrange("(c p) t -> p c t", p=128), writes=[bOg])
                return xs, At, bA, Og, bOg

            nxt = m4_loads(0)
            self.transpose_x(nxt[0], xTr[0], bxTr[0], pT, ident, bid)
            for ti in range(T // 512):
                t0 = ti * 512
                xT, bxT = xTr[ti % NB], bxTr[ti % NB]
                _, At, bA, Og, bOg = nxt
                if ti + 1 < T // 512:
                    nxt = m4_loads(t0 + 512)
                for n in range(8):
                    ns = slice(n * 128, (n + 1) * 128)
                    pa, bpa = pM.next()
                    pga, bpga = pM.next()
                    for kc in range(8):
                        P.op("pe", lambda e, pa=pa, kc=kc, ns=ns, At=At: e.matmul(pa[:], WA[:, kc, ns], At[:, kc, :], start=(kc == 0), stop=(kc == 7)),
                             reads=[bWA[kc], bA], writes=[bpa])
                    for kc in range(8):
                        P.op("pe", lambda e, pga=pga, kc=kc, ns=ns, xT=xT: e.matmul(pga[:], WG[:, kc, ns], xT[:, kc, :], start=(kc == 0), stop=(kc == 7)),
                             reads=[bWG[kc], bxT[kc]], writes=[bpga])
                    sga, bsga = SGr.next()
                    P.op("act", lambda e, sga=sga, pga=pga: e.activation(out=sga[:], in_=pga[:], func=AF.Sigmoid), reads=[bpga], writes=[bsga])
                    t1, bt1 = T1r.next()
                    P.op("dve", lambda e, t1=t1, pa=pa, sga=sga: e.tensor_tensor(out=t1[:], in0=pa[:], in1=sga[:], op=ALU.mult),
                         reads=[bpa, bsga], writes=[bt1])
                    pb, bpb = pM.next()
                    pgb, bpgb = pM.next()
                    for kc in range(8):
                        P.op("pe", lambda e, pb=pb, kc=kc, ns=ns, Og=Og: e.matmul(pb[:], WB[:, kc, ns], Og[:, kc, :], start=(kc == 0), stop=(kc == 7)),
                             reads=[bWB[kc], bOg], writes=[bpb])
                    for kc in range(8):
                        P.op("pe", lambda e, pgb=pgb, kc=kc, n=n, xT=xT: e.matmul(pgb[:], WG[:, kc, D + n * 128:D + (n + 1) * 128], xT[:, kc, :],
                                                                               start=(kc == 0), stop=(kc == 7)),
                             reads=[bWG[kc], bxT[kc]], writes=[bpgb])
                    sgb, bsgb = SGr.next()
                    P.op("act", lambda e, sgb=sgb, pgb=pgb: e.activation(out=sgb[:], in_=pgb[:], func=AF.Sigmoid), reads=[bpgb], writes=[bsgb])
                    t2, bt2 = T2r.next()
                    P.op("dve", lambda e, t2=t2, pb=pb, sgb=sgb: e.tensor_tensor(out=t2[:], in0=pb[:], in1=sgb[:], op=ALU.mult),
                         reads=[bpb, bsgb], writes=[bt2])
                    P.op("pool", lambda e, t1=t1, t2=t2, n=n: e.tensor_tensor(out=MT[:, n, :], in0=t1[:], in1=t2[:], op=ALU.add),
                         reads=[bt1, bt2], writes=[bMT[n]])
                if ti + 1 < T // 512:
                    self.transpose_x(nxt[0], xTr[(ti + 1) % NB], bxTr[(ti + 1) % NB], pT, ident, bid)
                for s in range(4):
                    for hf in range(2):
                        for kc in range(8):
                            P.op("pe", lambda e, hf=hf, kc=kc, s=s: e.matmul(pY[hf][:], MT[:, kc, s * 128:(s + 1) * 128], WO[:, kc, hf * 512:(hf + 1) * 512],
                                                                            start=(kc == 0), stop=(kc == 7)),
                                 reads=[bMT[kc], bWO[kc]], writes=[bpY[hf]])
                    self.ln_epilogue(pY, bpY, src, dst, t0 + s * 128, 1.0 / DN_ALPHA, lnc, XRr, small)

    def build(self, upto=99):
        cur = self.x
        n = 0
        for layer in range(self.depth):
            last = (layer == self.depth - 1)
            steps = [
                lambda: self.ffn_phase(layer, 0, cur, self.R[0], 0),
                lambda: self.m1_phase(layer, self.R[0]),
                lambda: self.m2_phase(layer),
                lambda: self.m3_phase(layer, self.R[0]),
                lambda: self.m4_phase(layer, self.R[0], self.R[1]),
                lambda: self.ffn_phase(layer, 1, self.R[1], self.out if last else self.R[0], 2),
            ]
            for st in steps:
                if n < upto:
                    st()
                n += 1
            cur = self.R[0]
        self.top.close()
        return self.nc


def host_pos_tables(rel_bias):
    s = np.arange(128)[:, None]
    q = np.arange(128)[None, :]
    out_b = np.zeros((128, 2, 16, 128), np.float32)
    out_m = np.zeros((128, 2, 16, 128), np.float32)
    for blk in range(2):
        j = s + 128 * blk
        rel = q + 128 - j
        valid = (rel >= 0) & (rel < 128)
        n = np.maximum(rel, 0)
        nf = np.maximum(n, 1).astype(np.float32)
        large = 16 + (np.log(nf / np.float32(16)) / np.float32(np.log(128 / 16)) * np.float32(16)).astype(np.int32)
        large = np.minimum(large, 31)
        bucket = np.where(n < 16, n, large)
        bucket = np.where(valid, bucket, 0)
        g = rel_bias[bucket]
        out_b[:, blk] = np.transpose(g, (0, 2, 1))
        out_m[:, blk] = np.broadcast_to(valid[:, None, :], (128, 16, 128))
    return out_b.reshape(128, -1), out_m.reshape(128, -1)


_NC_CACHE = {}


def kernel(x, rel_bias, ln_g, ln_b, ffn_w13, ffn_w2, w_in, conv_w, a_log, dt_bias,
           dn_norm_g, sinks, w_branch_a, w_branch_b, w_out):
    x = np.asarray(x, np.float32)
    B, L, _ = x.shape
    nseq = B // NCORES
    key = (nseq, L)
    if key not in _NC_CACHE:
        _NC_CACHE[key] = KB(nseq, L, DEPTH).build()
    nc = _NC_CACHE[key]
    pb, pm = host_pos_tables(np.asarray(rel_bias, np.float32))
    f = lambda a: np.ascontiguousarray(np.asarray(a, np.float32))
    shared = dict(ln_g=f(ln_g), ln_b=f(ln_b), ffn_w13=f(ffn_w13), ffn_w2=f(ffn_w2), w_in=f(w_in), conv_w=f(conv_w),
                  a_log=f(a_log), dt_bias=f(dt_bias), dn_norm_g=f(dn_norm_g), sinks=f(sinks),
                  w_branch_a=f(w_branch_a), w_branch_b=f(w_branch_b), w_out=f(w_out), pbias=pb, pmask=pm)
    in_maps = []
    for c in range(NCORES):
        m = dict(shared)
        m["x"] = np.ascontiguousarray(x[c * nseq:(c + 1) * nseq].reshape(nseq * L, D))
        in_maps.append(m)
    res = run_bass_kernel_spmd(nc, in_maps, core_ids=list(range(NCORES)))
    outs = [np.asarray(r["out"], np.float32).reshape(nseq, L, D) for r in res.results]
    return np.concatenate(outs, axis=0)
```

```python
import contextlib
import numpy as np
import concourse.bass as bass
import concourse.mybir as mybir
from concourse.bass_utils import run_bass_kernel_spmd

F32 = mybir.dt.float32
BF16 = mybir.dt.bfloat16
AF = mybir.ActivationFunctionType
ALU = mybir.AluOpType
AX = mybir.AxisListType

D = 1024
DFF = 2816
NIN = 7696
DEPTH = 4
SEQ = 4096
NCORES = 8
LN_EPS = 1e-5
NORM_EPS = 1e-6
DN_ALPHA = (2 * DEPTH) ** 0.25
C_Q0, C_KA, C_VA, C_QKVB, C_BETA, C_DT, C_Z, C_GATE = 0, 1024, 1280, 1536, 4608, 4616, 4624, 5648
SOLVE_DT = BF16
import os
M3STOP = int(os.environ.get("M3STOP", "99"))
M3SUB = int(os.environ.get("M3SUB", "7"))

NDMASEM = 16
ENGS = ("pe", "act", "dve", "pool", "sp")


class Buf:
    __slots__ = ("lw", "rd")

    def __init__(self):
        self.lw = None
        self.rd = []


def bufs(n):
    return [Buf() for _ in range(n)]


class Prog:
    def __init__(self, nc, stack):
        self.nc = nc
        self.ops = {e: [] for e in ENGS}
        self.cnt = {e: 0 for e in ("pe", "act", "dve", "pool")}
        self.seen = {e: {} for e in ENGS}
        self.dq_n = {"sp": 0, "pool": 0}
        self.esem = {e: stack.enter_context(nc.semaphore("s_" + e)) for e in ("pe", "act", "dve", "pool")}
        self.dsem = {}
        for q in ("sp", "pool"):
            for k in range(NDMASEM):
                self.dsem[(q, k)] = stack.enter_context(nc.semaphore("d_%s%d" % (q, k)))
        self.ninstr = 0

    def _kv(self, tok):
        if tok[0] == "e":
            return ("e", tok[1]), tok[2]
        q, i = tok[1], tok[2]
        return ("d", q, i % NDMASEM), 16 * (i // NDMASEM + 1)

    def _deps(self, eng, reads, writes):
        need = {}

        def add(tok):
            if tok is None:
                return
            if tok[0] == "e" and tok[1] == "pe" and eng == "pe":
                return
            k, v = self._kv(tok)
            if need.get(k, 0) < v:
                need[k] = v

        for b in reads:
            add(b.lw)
        for b in writes:
            add(b.lw)
            for t in b.rd:
                add(t)
        out = []
        s = self.seen[eng]
        for k, v in need.items():
            if s.get(k, 0) < v:
                s[k] = v
                out.append((k, v))
        return out

    def _commit(self, tok, reads, writes):
        for b in reads:
            b.rd.append(tok)
            if len(b.rd) > 32:
                best = {}
                for t in b.rd:
                    k, v = self._kv(t)
                    if k not in best or best[k][0] < v:
                        best[k] = (v, t)
                b.rd = [t for (_, t) in best.values()]
        for b in writes:
            b.lw = tok
            b.rd = []

    def op(self, eng, fn, reads=(), writes=()):
        waits = self._deps(eng, reads, writes)
        self.cnt[eng] += 1
        tok = ("e", eng, self.cnt[eng])
        self.ops[eng].append((waits, fn, None))
        self._commit(tok, reads, writes)

    def dma(self, q, out_ap, in_ap, reads=(), writes=(), slow=False):
        i = self.dq_n[q]
        self.dq_n[q] += 1
        tok = ("d", q, i)
        waits = self._deps(q, reads, writes)
        if i >= NDMASEM:
            k, v = self._kv(("d", q, i - NDMASEM))
            if self.seen[q].get(k, 0) < v:
                self.seen[q][k] = v
                waits.append((k, v))
        self.ops[q].append((waits, (out_ap, in_ap, slow), tok))
        self._commit(tok, reads, writes)

    def barrier(self):
        allk = []
        for e, c in self.cnt.items():
            if c:
                allk.append((("e", e), c))
        for q, n in self.dq_n.items():
            for k in range(min(NDMASEM, n)):
                last = ((n - 1 - k) // NDMASEM) * NDMASEM + k
                allk.append((("d", q, k), 16 * (last // NDMASEM + 1)))
        for e in ENGS:
            s = self.seen[e]
            waits = []
            for k, v in allk:
                if k == ("e", "pe") and e == "pe":
                    continue
                if s.get(k, 0) < v:
                    s[k] = v
                    waits.append((k, v))
            if waits:
                self.ops[e].append((waits, None, None))

    def emit(self):
        nc = self.nc

        def semof(k):
            return self.esem[k[1]] if k[0] == "e" else self.dsem[(k[1], k[2])]

        def run(engname, e):
            for waits, fn, tok in self.ops[engname]:
                for k, v in waits:
                    e.wait_ge(semof(k), v)
                if fn is None:
                    continue
                self.ninstr += 1
                if tok is None:
                    fn(e).then_inc(self.esem[engname], 1)
                else:
                    o, i, slow = fn
                    if slow:
                        ins = e.dma_start(out=o, in_=i, allow_slow_non_contiguous=True)
                    else:
                        ins = e.dma_start(out=o, in_=i)
                    ins.then_inc(self.dsem[(tok[1], tok[2] % NDMASEM)], 16)
            self.ops[engname] = []

        with nc.Block() as block:
            @block.tensor
            def _(e):
                run("pe", e)

            @block.scalar
            def _(e):
                run("act", e)

            @block.vector
            def _(e):
                run("dve", e)

            @block.gpsimd
            def _(e):
                run("pool", e)

            @block.sync
            def _(e):
                run("sp", e)


class Phase:
    def __init__(self, kb):
        self.kb = kb
        self.st = contextlib.ExitStack()
        self.n = 0

    def __enter__(self):
        self.st.__enter__()
        return self

    def __exit__(self, *a):
        self.kb.P.barrier()
        self.kb.P.emit()
        return self.st.__exit__(*a)

    def sb(self, shape, dt):
        self.n += 1
        return self.st.enter_context(self.kb.nc.sbuf_tensor("t%d_%d" % (self.kb.phase_id, self.n), list(shape), dt))

    def ps(self, dt=F32):
        self.n += 1
        cols = 512 if dt == F32 else 1024
        return self.st.enter_context(self.kb.nc.psum_tensor("p%d_%d" % (self.kb.phase_id, self.n), [128, cols], dt))


class Rot:
    def __init__(self, tiles):
        self.t = tiles
        self.b = bufs(len(tiles))
        self.i = 0

    def next(self):
        k = self.i % len(self.t)
        self.i += 1
        return self.t[k], self.b[k]


class KB:
    def __init__(self, nseq, L, depth, debug=False):
        self.nseq, self.L, self.depth = nseq, L, depth
        self.T = nseq * L
        self.debug = debug
        self.nc = bass.Bass("TRN2", target_bir_lowering=False)
        self.top = contextlib.ExitStack()
        self.phase_id = 0
        nc = self.nc
        T = self.T
        ext = lambda n, s: nc.dram_tensor(n, list(s), F32, kind="ExternalInput").ap()
        self.x = ext("x", [T, D])
        self.ln_g = ext("ln_g", [DEPTH, 3, D])
        self.ln_b = ext("ln_b", [DEPTH, 3, D])
        self.w13 = ext("ffn_w13", [DEPTH, 2, D, 2 * DFF])
        self.w2 = ext("ffn_w2", [DEPTH, 2, DFF, D])
        self.w_in = ext("w_in", [DEPTH, D, NIN])
        self.conv_w = ext("conv_w", [DEPTH, 4, 3072])
        self.a_log = ext("a_log", [DEPTH, 8])
        self.dt_bias = ext("dt_bias", [DEPTH, 8])
        self.dn_g = ext("dn_norm_g", [DEPTH, 128])
        self.sinks = ext("sinks", [DEPTH, 16])
        self.w_a = ext("w_branch_a", [DEPTH, D, D])
        self.w_b = ext("w_branch_b", [DEPTH, D, D])
        self.w_o = ext("w_out", [DEPTH, D, D])
        self.pbias = ext("pbias", [128, 2 * 16 * 128])
        self.pmask = ext("pmask", [128, 2 * 16 * 128])
        self.out = nc.dram_tensor("out", [T, D], F32, kind="ExternalOutput").ap()
        kind = "ExternalOutput" if debug else "Internal"
        scr = lambda n, s, dt: nc.dram_tensor(n, list(s), dt, kind=kind).ap()
        self.R = [scr("res%d" % i, [T, D], F32) for i in range(2)]
        self.QT = scr("QT", [1024, T], BF16)
        self.KT = scr("KT", [256, T], BF16)
        self.VA = scr("VA", [T, 256], BF16)
        self.QKVB = scr("QKVB", [3072, T], BF16)
        self.BG = scr("BG", [T, 16], F32)
        self.AT = scr("AT", [1024, T], BF16)
        self.OGT = scr("OGT", [1024, T], BF16)
        self.P = Prog(nc, self.top)

    def phase(self):
        self.phase_id += 1
        return Phase(self)

    def make_ident(self, ph, dt):
        P = self.P
        t = ph.sb([128, 128], dt)
        b = Buf()
        P.op("pool", lambda e: e.memset(t[:], 0.0), writes=[b])
        P.op("pool", lambda e: e.affine_select(out=t[:], in_=t[:], pattern=[[-1, 128]], compare_op=ALU.not_equal,
                                               fill=1.0, base=0, channel_multiplier=1), reads=[b], writes=[b])
        return t, b

    def load_w(self, dst, dbufs, src, kcs, c0, c1, piece=2048):
        v = src.rearrange("(kc p) n -> p kc n", p=128)
        for kc in range(kcs):
            a = c0
            while a < c1:
                b = min(c1, a + piece)
                self.P.dma("pool", dst[:, kc, a - c0:b - c0], v[:, kc, a:b], writes=[dbufs[kc]])
                a = b

    def issue_x(self, src, t0, XS4, nsub=4):
        out = []
        for s in range(nsub):
            Xs, bXs = XS4.next()
            self.P.dma("sp", Xs[:], src[t0 + s * 128:t0 + (s + 1) * 128, :], writes=[bXs])
            out.append((Xs, bXs))
        return out

    def transpose_x(self, xs, xT, bxT, pT, ident, bid):
        P = self.P
        for s, (Xs, bXs) in enumerate(xs):
            for half in range(2):
                pt, bpt = pT.next()
                for k4 in range(4):
                    kc = half * 4 + k4
                    P.op("pe", lambda e, pt=pt, k4=k4, kc=kc, Xs=Xs: e.transpose(pt[:, k4 * 128:(k4 + 1) * 128],
                                                                              Xs[:, kc * 128:(kc + 1) * 128], ident[:]),
                         reads=[bXs, bid], writes=[bpt])
                wb = [bxT[half * 4 + k4] for k4 in range(4)]
                dst = xT[:, half * 4:half * 4 + 4, s * 128:(s + 1) * 128]
                srcv = pt[:].rearrange("p (k c) -> p k c", k=4)
                if half == 0:
                    P.op("act", lambda e, dst=dst, srcv=srcv: e.copy(dst, srcv), reads=[bpt], writes=wb)
                else:
                    P.op("dve", lambda e, dst=dst, srcv=srcv: e.tensor_copy(dst, srcv), reads=[bpt], writes=wb)

    def ln_consts(self, ph, layer, idx):
        P = self.P
        G = ph.sb([128, D], F32)
        B = ph.sb([128, D], F32)
        bg, bb = Buf(), Buf()
        P.dma("sp", G[:], self.ln_g[layer, idx, :].partition_broadcast(128), writes=[bg])
        P.dma("sp", B[:], self.ln_b[layer, idx, :].partition_broadcast(128), writes=[bb])
        return (G, bg, B, bb)

    def ln_epilogue(self, py, bpy, src, dst, r0, c, lnc, XRr, small):
        P = self.P
        G, bg, B, bb = lnc
        Rt, bR = XRr.next()
        st, bst = small.next()
        P.dma("sp", Rt[:], src[r0:r0 + 128, :], writes=[bR])
        for hf in range(2):
            P.op("dve", lambda e, hf=hf, Rt=Rt: e.scalar_tensor_tensor(
                out=Rt[:, hf * 512:(hf + 1) * 512], in0=py[hf][:], scalar=float(c),
                in1=Rt[:, hf * 512:(hf + 1) * 512], op0=ALU.mult, op1=ALU.add),
                reads=[bpy[hf], bR], writes=[bR])
        for hf in range(2):
            P.op("dve", lambda e, hf=hf, Rt=Rt, st=st: e.bn_stats(st[:, hf * 6:(hf + 1) * 6], Rt[:, hf * 512:(hf + 1) * 512]),
                 reads=[bR], writes=[bst])
        P.op("dve", lambda e, st=st: e.bn_aggr(st[:, 12:14], st[:, 0:12]), reads=[bst], writes=[bst])
        eps = LN_EPS / (DN_ALPHA ** 2)
        P.op("dve", lambda e, st=st: e.tensor_scalar(st[:, 13:14], st[:, 13:14], float(eps), None, ALU.add),
             reads=[bst], writes=[bst])
        P.op("act", lambda e, st=st: e.activation(out=st[:, 14:15], in_=st[:, 13:14], func=AF.Ln),
             reads=[bst], writes=[bst])
        P.op("act", lambda e, st=st: e.activation(out=st[:, 14:15], in_=st[:, 14:15], func=AF.Exp, scale=-0.5),
             reads=[bst], writes=[bst])
        P.op("dve", lambda e, st=st: e.scalar_tensor_tensor(out=st[:, 15:16], in0=st[:, 12:13], scalar=-1.0,
                                                            in1=st[:, 14:15], op0=ALU.mult, op1=ALU.mult),
             reads=[bst], writes=[bst])
        P.op("act", lambda e, st=st, Rt=Rt: e.activation(out=Rt[:], in_=Rt[:], func=AF.Identity,
                                                        bias=st[:, 15:16], scale=st[:, 14:15]),
             reads=[bst, bR], writes=[bR])
        P.op("pool", lambda e, Rt=Rt: e.tensor_tensor(out=Rt[:], in0=Rt[:], in1=G[:], op=ALU.mult),
             reads=[bR, bg], writes=[bR])
        P.op("pool", lambda e, Rt=Rt: e.tensor_tensor(out=Rt[:], in0=Rt[:], in1=B[:], op=ALU.add),
             reads=[bR, bb], writes=[bR])
        P.dma("pool", dst[r0:r0 + 128, :], Rt[:], reads=[bR])

    def ffn_phase(self, layer, which, src, dst, ln_idx):
        P = self.P
        T = self.T
        with self.phase() as ph:
            W13 = ph.sb([128, 8, 2 * DFF], BF16)
            W2 = ph.sb([128, 22, D], BF16)
            bW13, bW2 = bufs(8), bufs(22)
            self.load_w(W13, bW13, self.w13[layer, which], 8, 0, 2 * DFF, piece=1408)
            self.load_w(W2, bW2, self.w2[layer, which], 22, 0, D, piece=1024)
            ident, bid = self.make_ident(ph, F32)
            lnc = self.ln_consts(ph, layer, ln_idx)
            NB = 2
            XSr = Rot([ph.sb([128, D], F32) for _ in range(4)])
            XRr = Rot([ph.sb([128, D], F32) for _ in range(2)])
            xTr = [ph.sb([128, 8, 512], BF16) for _ in range(NB)]
            bxTr = [bufs(8) for _ in range(NB)]
            HT = ph.sb([128, 22, 512], BF16)
            bHT = bufs(22)
            SG = Rot([ph.sb([128, 512], F32) for _ in range(2)])
            small = Rot([ph.sb([128, 16], F32) for _ in range(2)])
            pT = Rot([ph.ps() for _ in range(2)])
            pGU = Rot([ph.ps() for _ in range(4)])
            pY = [ph.ps() for _ in range(2)]
            bpY = bufs(2)
            ntiles = T // 512
            xs_next = self.issue_x(src, 0, XSr)
            self.transpose_x(xs_next, xTr[0], bxTr[0], pT, ident, bid)
            for ti in range(ntiles):
                t0 = ti * 512
                xT, bxT = xTr[ti % NB], bxTr[ti % NB]
                if ti + 1 < ntiles:
                    xs_next = self.issue_x(src, t0 + 512, XSr)
                for j in range(22):
                    pg, bpg = pGU.next()
                    pu, bpu = pGU.next()
                    for kc in range(8):
                        P.op("pe", lambda e, pg=pg, kc=kc, j=j, xT=xT: e.matmul(
                            pg[:], W13[:, kc, j * 128:(j + 1) * 128], xT[:, kc, :], start=(kc == 0), stop=(kc == 7)),
                            reads=[bW13[kc], bxT[kc]], writes=[bpg])
                    for kc in range(8):
                        P.op("pe", lambda e, pu=pu, kc=kc, j=j, xT=xT: e.matmul(
                            pu[:], W13[:, kc, DFF + j * 128:DFF + (j + 1) * 128], xT[:, kc, :], start=(kc == 0), stop=(kc == 7)),
                            reads=[bW13[kc], bxT[kc]], writes=[bpu])
                    sg, bsg = SG.next()
                    P.op("act", lambda e, sg=sg, pg=pg: e.activation(out=sg[:], in_=pg[:], func=AF.Silu),
                         reads=[bpg], writes=[bsg])
                    P.op("dve", lambda e, sg=sg, pu=pu, j=j: e.tensor_tensor(out=HT[:, j, :], in0=pu[:], in1=sg[:], op=ALU.mult),
                         reads=[bpu, bsg], writes=[bHT[j]])
                if ti + 1 < ntiles:
                    self.transpose_x(xs_next, xTr[(ti + 1) % NB], bxTr[(ti + 1) % NB], pT, ident, bid)
                for s in range(4):
                    for hf in range(2):
                        for j in range(22):
                            P.op("pe", lambda e, hf=hf, j=j, s=s: e.matmul(
                                pY[hf][:], HT[:, j, s * 128:(s + 1) * 128], W2[:, j, hf * 512:(hf + 1) * 512],
                                start=(j == 0), stop=(j == 21)),
                                reads=[bHT[j], bW2[j]], writes=[bpY[hf]])
                    self.ln_epilogue(pY, bpY, src, dst, t0 + s * 128, 0.5 / DN_ALPHA, lnc, XRr, small)

    def m1_phase(self, layer, src):
        P = self.P
        T, L = self.T, self.L
        NW = C_Z
        with self.phase() as ph:
            W = ph.sb([128, 8, NW], BF16)
            bW = bufs(8)
            self.load_w(W, bW, self.w_in[layer], 8, 0, NW, piece=1156)
            ident, bid = self.make_ident(ph, F32)
            ones = ph.sb([128, 128], BF16)
            bones = Buf()
            P.op("pool", lambda e: e.memset(ones[:], 1.0), writes=[bones])
            CW = ph.sb([128, 4, 24], F32)
            bCW = Buf()
            for j in range(4):
                P.dma("sp", CW[:, j, :], self.conv_w[layer, j, :].rearrange("(c p) -> p c", p=128), writes=[bCW], slow=True)
            DTB = ph.sb([128, 8], F32)
            NEGA = ph.sb([128, 8], F32)
            bDTB, bNEGA = Buf(), Buf()
            P.dma("sp", DTB[:], self.dt_bias[layer, :].partition_broadcast(128), writes=[bDTB])
            P.dma("sp", NEGA[:], self.a_log[layer, :].partition_broadcast(128), writes=[bNEGA])
            P.op("act", lambda e: e.activation(out=NEGA[:], in_=NEGA[:], func=AF.Exp), reads=[bNEGA], writes=[bNEGA])
            P.op("dve", lambda e: e.tensor_scalar(NEGA[:], NEGA[:], -1.0, None, ALU.mult), reads=[bNEGA], writes=[bNEGA])
            CAR = ph.sb([128, 24, 3], F32)
            ZERO3 = ph.sb([128, 3], F32)
            bZ3 = Buf()
            P.op("pool", lambda e: e.memset(ZERO3[:], 0.0), writes=[bZ3])
            bCAR = bufs(24)
            NB = 2
            XSr = Rot([ph.sb([128, D], F32) for _ in range(4)])
            xTr = [ph.sb([128, 8, 512], BF16) for _ in range(NB)]
            bxTr = [bufs(8) for _ in range(NB)]
            QAr = Rot([ph.sb([128, 10, 512], BF16) for _ in range(2)])
            VAr = Rot([ph.sb([128, 4, 256], BF16) for _ in range(2)])
            BGr = Rot([ph.sb([128, 4, 16], F32) for _ in range(2)])
            TMP = Rot([ph.sb([128, 56], F32) for _ in range(2)])
            Ur = Rot([ph.sb([128, 515], F32) for _ in range(3)])
            ACr = Rot([ph.sb([128, 512], F32) for _ in range(3)])
            Y8 = ph.sb([128, 8, 512], F32)
            SQ8 = ph.sb([128, 8, 512], BF16)
            bY8, bSQ8 = bufs(8), bufs(8)
            RS8 = ph.sb([128, 8, 512], F32)
            bRS8 = bufs(8)
            OCr = Rot([ph.sb([128, 8, 512], BF16) for _ in range(2)])
            pT = Rot([ph.ps() for _ in range(2)])
            pA = Rot([ph.ps() for _ in range(3)])
            pB = Rot([ph.ps() for _ in range(1)])
            pS = Rot([ph.ps() for _ in range(2)])
            ntiles = T // 512
            xs_next = self.issue_x(src, 0, XSr)
            self.transpose_x(xs_next, xTr[0], bxTr[0], pT, ident, bid)
            for ti in range(ntiles):
                t0 = ti * 512
                seq_start = (t0 % L == 0)
                xT, bxT = xTr[ti % NB], bxTr[ti % NB]
                if ti + 1 < ntiles:
                    xs_next = self.issue_x(src, t0 + 512, XSr)
                QA, bQA = QAr.next()
                for c in range(10):
                    pa, bpa = pA.next()
                    for kc in range(8):
                        P.op("pe", lambda e, pa=pa, kc=kc, c=c, xT=xT: e.matmul(
                            pa[:], W[:, kc, c * 128:(c + 1) * 128], xT[:, kc, :], start=(kc == 0), stop=(kc == 7)),
                            reads=[bW[kc], bxT[kc]], writes=[bpa])
                    sc = 0.125 if c < 8 else 1.0
                    P.op("act", lambda e, pa=pa, c=c, QA=QA, sc=sc: e.activation(out=QA[:, c, :], in_=pa[:], func=AF.Copy, scale=sc),
                         reads=[bpa], writes=[bQA])
                P.dma("pool", self.QT[:, t0:t0 + 512].rearrange("(c p) t -> p c t", p=128), QA[:, 0:8, :], reads=[bQA])
                P.dma("pool", self.KT[:, t0:t0 + 512].rearrange("(c p) t -> p c t", p=128), QA[:, 8:10, :], reads=[bQA])
                VAt, bVA = VAr.next()
                BGt, bBG = BGr.next()
                for s in range(4):
                    pb, bpb = pB.next()
                    for kc in range(8):
                        P.op("pe", lambda e, pb=pb, kc=kc, s=s, xT=xT: e.matmul(
                            pb[:, 0:256], xT[:, kc, s * 128:(s + 1) * 128], W[:, kc, C_VA:C_VA + 256],
                            start=(kc == 0), stop=(kc == 7)), reads=[bW[kc], bxT[kc]], writes=[bpb])
                    for kc in range(8):
                        P.op("pe", lambda e, pb=pb, kc=kc, s=s, xT=xT: e.matmul(
                            pb[:, 256:272], xT[:, kc, s * 128:(s + 1) * 128], W[:, kc, C_BETA:C_BETA + 16],
                            start=(kc == 0), stop=(kc == 7)), reads=[bW[kc], bxT[kc]], writes=[bpb])
                    P.op("act", lambda e, pb=pb, s=s, VAt=VAt: e.copy(VAt[:, s, :], pb[:, 0:256]), reads=[bpb], writes=[bVA])
                    tm, btm = TMP.next()
                    P.op("act", lambda e, pb=pb, tm=tm: e.copy(tm[:, 40:56], pb[:, 256:272]), reads=[bpb], writes=[btm])
                    P.op("act", lambda e, tm=tm, s=s, BGt=BGt: e.activation(out=BGt[:, s, 0:8], in_=tm[:, 40:48], func=AF.Exp, scale=-1.0),
                         reads=[btm], writes=[bBG])
                    P.op("dve", lambda e, s=s, BGt=BGt: e.tensor_scalar(BGt[:, s, 0:8], BGt[:, s, 0:8], 1.0, None, ALU.add),
                         reads=[bBG], writes=[bBG])
                    P.op("dve", lambda e, s=s, BGt=BGt: e.reciprocal(BGt[:, s, 0:8], BGt[:, s, 0:8]), reads=[bBG], writes=[bBG])
                    P.op("dve", lambda e, tm=tm: e.tensor_tensor(out=tm[:, 0:8], in0=tm[:, 48:56], in1=DTB[:], op=ALU.add),
                         reads=[btm, bDTB], writes=[btm])
                    P.op("dve", lambda e, tm=tm: e.tensor_scalar(tm[:, 8:16], tm[:, 0:8], -1.0, None, ALU.mult),
                         reads=[btm], writes=[btm])
                    P.op("dve", lambda e, tm=tm: e.tensor_tensor(out=tm[:, 8:16], in0=tm[:, 8:16], in1=tm[:, 0:8], op=ALU.max),
                         reads=[btm], writes=[btm])
                    P.op("act", lambda e, tm=tm: e.activation(out=tm[:, 16:24], in_=tm[:, 8:16], func=AF.Exp, scale=-1.0),
                         reads=[btm], writes=[btm])
                    P.op("dve", lambda e, tm=tm: e.tensor_scalar(tm[:, 16:24], tm[:, 16:24], 1.0, None, ALU.add), reads=[btm], writes=[btm])
                    P.op("act", lambda e, tm=tm: e.activation(out=tm[:, 24:32], in_=tm[:, 16:24], func=AF.Ln),
                         reads=[btm], writes=[btm])
                    P.op("dve", lambda e, tm=tm: e.scalar_tensor_tensor(out=tm[:, 32:40], in0=tm[:, 0:8], scalar=0.0,
                                                                        in1=tm[:, 24:32], op0=ALU.max, op1=ALU.add),
                         reads=[btm], writes=[btm])
                    P.op("dve", lambda e, tm=tm, s=s, BGt=BGt: e.tensor_tensor(out=BGt[:, s, 8:16], in0=tm[:, 32:40], in1=NEGA[:], op=ALU.mult),
                         reads=[btm, bNEGA], writes=[bBG])
                P.dma("pool", self.VA[t0:t0 + 512, :].rearrange("(s p) f -> p s f", p=128), VAt[:], reads=[bVA])
                P.dma("pool", self.BG[t0:t0 + 512, :].rearrange("(s p) f -> p s f", p=128), BGt[:], reads=[bBG])
                for grp in range(3):
                    OC, bOC = OCr.next()
                    for cc in range(8):
                        c = grp * 8 + cc
                        pa, bpa = pA.next()
                        col = C_QKVB + c * 128
                        for kc in range(8):
                            P.op("pe", lambda e, pa=pa, kc=kc, col=col, xT=xT: e.matmul(
                                pa[:], W[:, kc, col:col + 128], xT[:, kc, :], start=(kc == 0), stop=(kc == 7)),
                                reads=[bW[kc], bxT[kc]], writes=[bpa])
                        U, bU = Ur.next()
                        if seq_start:
                            P.op("act", lambda e, U=U: e.copy(U[:, 0:3], ZERO3[:]), reads=[bZ3], writes=[bU])
                        else:
                            P.op("act", lambda e, U=U, c=c: e.copy(U[:, 0:3], CAR[:, c, :]), reads=[bCAR[c]], writes=[bU])
                        P.op("act", lambda e, U=U, pa=pa: e.copy(U[:, 3:515], pa[:]), reads=[bpa], writes=[bU])
                        P.op("act", lambda e, U=U, c=c: e.copy(CAR[:, c, :], U[:, 512:515]), reads=[bU], writes=[bCAR[c]])
                        A1, bA1 = ACr.next()
                        P.op("act", lambda e, U=U, A1=A1, c=c: e.activation(out=A1[:], in_=U[:, 0:512], func=AF.Identity, scale=CW[:, 0, c:c + 1]),
                             reads=[bU, bCW], writes=[bA1])
                        P.op("act", lambda e, U=U, cc=cc, c=c: e.activation(out=Y8[:, cc, :], in_=U[:, 2:514], func=AF.Identity, scale=CW[:, 2, c:c + 1]),
                             reads=[bU, bCW], writes=[bY8[cc]])
                        P.op("dve", lambda e, U=U, A1=A1, c=c: e.scalar_tensor_tensor(out=A1[:], in0=U[:, 1:513], scalar=CW[:, 1, c:c + 1],
                                                                                    in1=A1[:], op0=ALU.mult, op1=ALU.add),
                             reads=[bU, bCW, bA1], writes=[bA1])
                        P.op("dve", lambda e, U=U, cc=cc, c=c: e.scalar_tensor_tensor(out=Y8[:, cc, :], in0=U[:, 3:515], scalar=CW[:, 3, c:c + 1],
                                                                                    in1=Y8[:, cc, :], op0=ALU.mult, op1=ALU.add),
                             reads=[bU, bCW, bY8[cc]], writes=[bY8[cc]])
                        P.op("pool", lambda e, A1=A1, cc=cc: e.tensor_tensor(out=Y8[:, cc, :], in0=A1[:], in1=Y8[:, cc, :], op=ALU.add),
                             reads=[bA1, bY8[cc]], writes=[bY8[cc]])
                    for cc in range(8):
                        if grp == 2:
                            P.op("act", lambda e, cc=cc, OC=OC: e.activation(out=OC[:, cc, :], in_=Y8[:, cc, :], func=AF.Silu), reads=[bY8[cc]], writes=[bOC])
                        else:
                            P.op("act", lambda e, cc=cc: e.activation(out=Y8[:, cc, :], in_=Y8[:, cc, :], func=AF.Silu), reads=[bY8[cc]], writes=[bY8[cc]])
                    if grp < 2:
                        for cc in range(8):
                            P.op("dve", lambda e, cc=cc: e.tensor_tensor(out=SQ8[:, cc, :], in0=Y8[:, cc, :], in1=Y8[:, cc, :], op=ALU.mult),
                                 reads=[bY8[cc]], writes=[bSQ8[cc]])
                    if grp < 2:
                        for cc in range(8):
                            ps_, bps = pS.next()
                            P.op("pe", lambda e, ps_=ps_, cc=cc: e.matmul(ps_[:], ones[:], SQ8[:, cc, :], start=True, stop=True),
                                 reads=[bones, bSQ8[cc]], writes=[bps])
                            P.op("dve", lambda e, ps_=ps_, cc=cc: e.tensor_scalar(RS8[:, cc, :], ps_[:], float(NORM_EPS), None, ALU.add),
                                 reads=[bps], writes=[bRS8[cc]])
                        for cc in range(8):
                            P.op("act", lambda e, cc=cc: e.activation(out=RS8[:, cc, :], in_=RS8[:, cc, :], func=AF.Ln), reads=[bRS8[cc]], writes=[bRS8[cc]])
                        for cc in range(8):
                            P.op("act", lambda e, cc=cc: e.activation(out=RS8[:, cc, :], in_=RS8[:, cc, :], func=AF.Exp, scale=-0.5),
                                 reads=[bRS8[cc]], writes=[bRS8[cc]])
                        qs = (128.0 ** -0.5) if grp == 0 else 1.0
                        for cc in range(8):
                            P.op("dve", lambda e, OC=OC, cc=cc, qs=qs: e.scalar_tensor_tensor(
                                out=OC[:, cc, :], in0=Y8[:, cc, :], scalar=float(qs), in1=RS8[:, cc, :], op0=ALU.mult, op1=ALU.mult),
                                reads=[bY8[cc], bRS8[cc]], writes=[bOC])
                    P.dma("pool", self.QKVB[grp * 1024:(grp + 1) * 1024, t0:t0 + 512].rearrange("(c p) t -> p c t", p=128),
                          OC[:], reads=[bOC])
                if ti + 1 < ntiles:
                    self.transpose_x(xs_next, xTr[(ti + 1) % NB], bxTr[(ti + 1) % NB], pT, ident, bid)

    def m2_phase(self, layer):
        P = self.P
        T, L = self.T, self.L
        with self.phase() as ph:
            EB = ph.sb([128, 2 * 16 * 128], BF16)
            bEB = Buf()
            identF2, bidF2 = self.make_ident(ph, F32)
            identB2 = ph.sb([128, 128], BF16)
            bidB2 = Buf()
            P.op("pool", lambda e: e.tensor_copy(identB2[:], identF2[:]), reads=[bidF2], writes=[bidB2])
            for q4 in range(4):
                sl = slice(q4 * 1024, (q4 + 1) * 1024)
                stg, bstg = Buf(), None
                STG = ph.sb([128, 1024], F32)
                bSTG = Buf()
                P.dma("sp", STG[:], self.pbias[:, sl], writes=[bSTG])
                P.op("dve", lambda e, STG=STG, sl=sl: e.tensor_copy(EB[:, sl], STG[:]), reads=[bSTG], writes=[bEB])
            SK = ph.sb([1, 16], F32)
            SKR = ph.sb([1, 16, 128], BF16)
            ONE1 = ph.sb([1, 128], F32)
            bSK, bSKR = Buf(), Buf()
            P.dma("sp", SK[:], self.sinks[layer:layer + 1, :], writes=[bSK])
            P.op("act", lambda e: e.activation(out=SK[:], in_=SK[:], func=AF.Exp), reads=[bSK], writes=[bSK])
            P.op("dve", lambda e: e.memset(ONE1[:], 1.0), writes=[bSKR])
            for h in range(16):
                P.op("dve", lambda e, h=h: e.tensor_scalar(SKR[0:1, h, :], ONE1[0:1, :], SK[0:1, h:h + 1], None, ALU.mult),
                     reads=[bSK, bSKR], writes=[bSKR])
            ones = ph.sb([128, 64], BF16)
            bones = Buf()
            P.op("pool", lambda e: e.memset(ones[:], 1.0), writes=[bones])
            Qr = Rot([ph.sb([64, 16, 512], BF16) for _ in range(2)])
            Kr = Rot([ph.sb([64, 4, 640], BF16) for _ in range(2)])
            Vr = Rot([ph.sb([128, 5, 256], BF16) for _ in range(2)])
            ATr = Rot([ph.sb([64, 16, 512], BF16) for _ in range(2)])
            Pr = Rot([ph.sb([128, 512], BF16) for _ in range(4)])
            Dr = Rot([ph.sb([64, 512], F32) for _ in range(2)])
            pSr = Rot([ph.ps() for _ in range(4)])
            pOr = Rot([ph.ps() for _ in range(2)])
            pDr = Rot([ph.ps() for _ in range(2)])
            EBv = EB[:].rearrange("p (b h q) -> p b h q", b=2, h=16)
            def m2_loads(t0):
                Qt, bQ = Qr.next()
                Kt, bK = Kr.next()
                Vt, bV = Vr.next()
                P.dma("sp", Qt[:], self.QT[:, t0:t0 + 512].rearrange("(h d) t -> d h t", d=64), writes=[bQ])
                if t0 % L == 0:
                    P.dma("sp", Kt[:, :, 128:640], self.KT[:, t0:t0 + 512].rearrange("(g d) t -> d g t", d=64), writes=[bK])
                    P.dma("sp", Vt[:, 1:5, :], self.VA[t0:t0 + 512, :].rearrange("(b p) c -> p b c", p=128), writes=[bV])
                else:
                    P.dma("sp", Kt[:], self.KT[:, t0 - 128:t0 + 512].rearrange("(g d) t -> d g t", d=64), writes=[bK])
                    P.dma("sp", Vt[:], self.VA[t0 - 128:t0 + 512, :].rearrange("(b p) c -> p b c", p=128), writes=[bV])
                return Qt, bQ, Kt, bK, Vt, bV

            nxt = m2_loads(0)
            for ti in range(T // 512):
                t0 = ti * 512
                seq_start = (t0 % L == 0)
                Qt, bQ, Kt, bK, Vt, bV = nxt
                if ti + 1 < T // 512:
                    nxt = m2_loads(t0 + 512)
                At, bA = ATr.next()
                for i in range(4):
                    first = seq_start and i == 0
                    sbl = [1] if first else [0, 1]
                    for g in range(4):
                        Pb = {}
                        for sb_ in sbl:
                            ps_, bps = pSr.next()
                            P.op("pe", lambda e, ps_=ps_, g=g, i=i, sb_=sb_, Kt=Kt, Qt=Qt: e.matmul(
                                ps_[:], Kt[:, g, (i + sb_) * 128:(i + sb_ + 1) * 128], Qt[:, 4 * g:4 * g + 4, i * 128:(i + 1) * 128],
                                start=True, stop=False), reads=[bK, bQ], writes=[bps])
                            P.op("pe", lambda e, ps_=ps_, g=g, sb_=sb_: e.matmul(
                                ps_[:], identB2[:], EBv[:, sb_, 4 * g:4 * g + 4, :], start=False, stop=True),
                                reads=[bidB2, bEB], writes=[bps])
                            Pt, bP = Pr.next()
                            P.op("act", lambda e, Pt=Pt, ps_=ps_: e.activation(out=Pt[:], in_=ps_[:], func=AF.Exp),
                                 reads=[bps], writes=[bP])
                            Pb[sb_] = (Pt, bP)
                        po, bpo = pOr.next()
                        pd, bpd = pDr.next()
                        for n_, sb_ in enumerate(sbl):
                            Pt, bP = Pb[sb_]
                            P.op("pe", lambda e, po=po, Pt=Pt, sb_=sb_, g=g, i=i, Vt=Vt, n_=n_: e.matmul(
                                po[0:64, :], Vt[:, i + sb_, g * 64:(g + 1) * 64], Pt[:], start=(n_ == 0), stop=(n_ == len(sbl) - 1)),
                                reads=[bV, bP], writes=[bpo])
                        for n_, sb_ in enumerate(sbl):
                            Pt, bP = Pb[sb_]
                            P.op("pe", lambda e, pd=pd, Pt=Pt, n_=n_: e.matmul(
                                pd[0:64, :], ones[:], Pt[:], start=(n_ == 0), stop=False),
                                reads=[bones, bP], writes=[bpd])
                        P.op("pe", lambda e, pd=pd, g=g: e.matmul(
                            pd[0:64, :], ones[0:1, :], SKR[0:1, 4 * g:4 * g + 4, :], start=False, stop=True),
                            reads=[bones, bSKR], writes=[bpd])
                        Dt, bD = Dr.next()
                        P.op("dve", lambda e, Dt=Dt, pd=pd: e.reciprocal(Dt[:], pd[0:64, :]), reads=[bpd], writes=[bD])
                        P.op("dve", lambda e, Dt=Dt, po=po, At=At, g=g, i=i: e.tensor_tensor(
                            out=At[:, 4 * g:4 * g + 4, i * 128:(i + 1) * 128], in0=po[0:64, :].rearrange("p (h q) -> p h q", h=4),
                            in1=Dt[:].rearrange("p (h q) -> p h q", h=4), op=ALU.mult), reads=[bpo, bD], writes=[bA])
                P.dma("pool", self.AT[:, t0:t0 + 512].rearrange("(h d) t -> d h t", d=64), At[:], reads=[bA])

    def m3_phase(self, layer, src):
        P = self.P
        T, L = self.T, self.L
        SD = SOLVE_DT
        with self.phase() as ph:
            WZ = ph.sb([128, 8, 1024], BF16)
            bWZ = bufs(8)
            self.load_w(WZ, bWZ, self.w_in[layer], 8, C_Z, C_Z + 1024, piece=1024)
            identF, bidF = self.make_ident(ph, F32)
            identB = ph.sb([128, 128], BF16)
            bidB = Buf()
            P.op("pool", lambda e: e.tensor_copy(identB[:], identF[:]), reads=[bidF], writes=[bidB])
            if SD == BF16:
                identS, bidS = identB, bidB
            else:
                identS, bidS = identF, bidF
            bC = Buf()

            def mk(shape=(128, 128)):
                return ph.sb(list(shape), F32)

            TRI, LGT, MSU, ONES, SELA, SELB, SAME = mk(), mk(), mk(), mk(), mk(), mk(), mk()
            P.op("pool", lambda e: e.memset(TRI[:], 1.0), writes=[bC])
            P.op("pool", lambda e: e.affine_select(out=TRI[:], in_=TRI[:], pattern=[[1, 128]], compare_op=ALU.is_ge,
                                                   fill=0.0, base=0, channel_multiplier=-1), reads=[bC], writes=[bC])
            P.op("pool", lambda e: e.memset(TRI[0:64, 64:128], 0.0), reads=[bC], writes=[bC])
            P.op("pool", lambda e: e.memset(LGT[:], 1.0), reads=[bC], writes=[bC])
            P.op("pool", lambda e: e.affine_select(out=LGT[:], in_=LGT[:], pattern=[[-1, 128]], compare_op=ALU.is_gt,
                                                   fill=0.0, base=0, channel_multiplier=1), reads=[bC], writes=[bC])
            P.op("pool", lambda e: e.memset(LGT[64:128, 0:64], 0.0), reads=[bC], writes=[bC])
            P.op("pool", lambda e: e.memset(MSU[:], 1.0), reads=[bC], writes=[bC])
            P.op("pool", lambda e: e.affine_select(out=MSU[:], in_=MSU[:], pattern=[[1, 128]], compare_op=ALU.is_gt,
                                                   fill=0.0, base=0, channel_multiplier=-1), reads=[bC], writes=[bC])
            P.op("pool", lambda e: e.memset(MSU[0:64, 64:128], 0.0), reads=[bC], writes=[bC])
            P.op("pool", lambda e: e.memset(ONES[:], 1.0), reads=[bC], writes=[bC])
            P.op("pool", lambda e: e.memset(SELA[:], 0.0), reads=[bC], writes=[bC])
            P.op("pool", lambda e: e.memset(SELA[0:64, :], 1.0), reads=[bC], writes=[bC])
            P.op("pool", lambda e: e.memset(SELB[:], 0.0), reads=[bC], writes=[bC])
            P.op("pool", lambda e: e.memset(SELB[64:128, :], 1.0), reads=[bC], writes=[bC])
            P.op("pool", lambda e: e.memset(SAME[:], 0.0), reads=[bC], writes=[bC])
            P.op("pool", lambda e: e.memset(SAME[0:64, 0:64], 1.0), reads=[bC], writes=[bC])
            P.op("pool", lambda e: e.memset(SAME[64:128, 64:128], 1.0), reads=[bC], writes=[bC])
            NEGM8 = mk((128, 8, 128))
            MSU8 = mk((128, 8, 128))
            ID8 = ph.sb([128, 8, 128], SD)
            for h in range(8):
                P.op("pool", lambda e, h=h: e.tensor_copy(NEGM8[:, h, :], TRI[:]), reads=[bC], writes=[bC])
                P.op("pool", lambda e, h=h: e.tensor_copy(MSU8[:, h, :], MSU[:]), reads=[bC], writes=[bC])
                P.op("pool", lambda e, h=h: e.tensor_copy(ID8[:, h, :], identF[:]), reads=[bC, bidF], writes=[bC])
            DNG = mk((128, 8, 128))
            bDNG = Buf()
            for h in range(8):
                P.dma("sp", DNG[:, h, :], self.dn_g[layer, :].partition_broadcast(128), writes=[bDNG])
            S = ph.sb([128, 8, 128], F32)
            SB = ph.sb([128, 8, 128], BF16)
            bS, bSB = bufs(8), bufs(8)
            VN = ph.sb([128, 8, 128], BF16)
            bVN = bufs(8)
            P.op("pool", lambda e: e.memset(VN[:], 0.0), writes=bVN)
            XSr = Rot([ph.sb([128, D], F32) for _ in range(2)])
            xT1 = ph.sb([128, 8, 512], BF16)
            bxT1 = bufs(8)
            QKVr = Rot([ph.sb([128, 24, 512], BF16) for _ in range(2)])
            BGr = Rot([ph.sb([128, 4, 16], F32) for _ in range(2)])
            OGTr = Rot([ph.sb([128, 8, 512], BF16) for _ in range(2)])
            Rt = ph.sb([128, 8, 128], F32)
            bRt = Buf()
            ET = ph.sb([128, 8, 128], F32)
            ETS = ph.sb([128, 8, 128], F32)
            EGB = ph.sb([128, 8, 128], F32)
            bET, bETS, bEGB = Buf(), Buf(), Buf()
            YR = [ph.sb([128, 8, 256], SD) for _ in range(2)]
            bYR = [bufs(2), bufs(2)]
            Z = [ph.sb([128, 8, 128], SD) for _ in range(2)]
            bZ = [bufs(2), bufs(2)]
            XS = ph.sb([128, 8, 128], BF16)
            bXS = bufs(2)
            KG = ph.sb([128, 8, 128], BF16)
            KTOK = ph.sb([128, 8, 128], BF16)
            bKTOK = Buf()
            Vt = ph.sb([128, 8, 128], BF16)
            bKG, bVt = Buf(), Buf()
            O = ph.sb([128, 8, 128], F32)
            bO = Buf()
            SQ = ph.sb([128, 8, 128], F32)
            OG = ph.sb([128, 8, 128], BF16)
            bSQ, bZG, bOG = Buf(), Buf(), Buf()
            HH = []
            for _ in range(2):
                HH.append((ph.sb([128, 8], F32), Buf(), ph.sb([128, 64], F32), Buf(), ph.sb([128, 8, 128], BF16), Buf(),
                           ph.sb([128, 8, 128], BF16), Buf(), ph.sb([128, 8, 128], BF16), Buf(),
                           ph.sb([128, 8, 128], F32), bufs(8), ph.sb([128, 8, 128], BF16), bufs(2)))
            ZG4 = ph.sb([128, 4, 8, 128], BF16)
            DNGb = DNG
            pF = Rot([ph.ps() for _ in range(6)])
            pH = Rot([ph.ps(BF16) for _ in range(2)])
            YRv = [y[:].rearrange("p h c -> p (h c)") for y in YR]

            def flat(t, g):
                return t[:, 4 * g:4 * g + 4, :]

            def m3_loads(t0):
                QKV, bQKV = QKVr.next()
                for grp in range(3):
                    P.dma("sp", QKV[:, grp * 8:(grp + 1) * 8, :],
                          self.QKVB[grp * 1024:(grp + 1) * 1024, t0:t0 + 512].rearrange("(c p) t -> p c t", p=128), writes=[bQKV])
                BGt, bBG = BGr.next()
                P.dma("sp", BGt[:], self.BG[t0:t0 + 512, :].rearrange("(s p) f -> p s f", p=128), writes=[bBG])
                return QKV, bQKV, BGt, bBG

            def prep(cx):
                tk, s, QKV, bQKV, BGt, bBG, OGT, bOGT = cx["tk"], cx["s"], cx["QKV"], cx["bQKV"], cx["BGt"], cx["bBG"], cx["OGT"], cx["bOGT"]
                NBG, bNBG, SM, bSM, AIT, bAIT, KT2, bKT2, QD, bQD, UB, bUB, WT, bWT = cx["H"]
                beta = BGt[:, s, 0:8]
                graw = BGt[:, s, 8:16]
                P.op("dve", lambda e, beta=beta: e.tensor_scalar(NBG[:], beta, -1.0, None, ALU.mult), reads=[bBG], writes=[bNBG])
                for h in range(8):
                    P.op("dve", lambda e, h=h, graw=graw: e.tensor_scalar(Rt[:, h, :], TRI[:], graw[:, h:h + 1], None, ALU.mult),
                         reads=[bC, bBG], writes=[bRt])
                px, bpx = pF.next()
                P.op("pe", lambda e, px=px, graw=graw: e.matmul(px[:, 0:8], SELA[:], graw, start=True, stop=True),
                     reads=[bC, bBG], writes=[bpx])
                P.op("pe", lambda e, px=px, graw=graw: e.matmul(px[:, 8:16], SELB[:], graw, start=True, stop=True),
                     reads=[bC, bBG], writes=[bpx])
                P.op("pe", lambda e, px=px, graw=graw: e.matmul(px[:, 16:24], TRI[:], graw, start=True, stop=True),
                     reads=[bC, bBG], writes=[bpx])
                P.op("pe", lambda e, px=px, graw=graw: e.matmul(px[:, 24:32], SAME[:], graw, start=True, stop=True),
                     reads=[bC, bBG], writes=[bpx])
                P.op("act", lambda e, px=px: e.activation(out=SM[:, 0:24], in_=px[:, 0:24], func=AF.Exp), reads=[bpx], writes=[bSM])
                P.op("act", lambda e, px=px: e.copy(SM[:, 56:64], px[:, 16:24]), reads=[bpx, bSM], writes=[bSM])
                P.op("dve", lambda e, px=px: e.tensor_tensor(out=SM[:, 32:40], in0=px[:, 24:32], in1=SM[:, 56:64], op=ALU.subtract),
                     reads=[bpx, bSM], writes=[bSM])
                P.op("act", lambda e: e.activation(out=SM[:, 24:32], in_=SM[:, 32:40], func=AF.Exp), reads=[bSM], writes=[bSM])
                for g in range(2):
                    pg, bpg = pF.next()
                    rv = flat(Rt, g)
                    P.op("pe", lambda e, pg=pg, rv=rv: e.matmul(pg[:], LGT[:], rv, start=True, stop=True), reads=[bC, bRt], writes=[bpg])
                    P.op("act", lambda e, pg=pg, g=g: e.activation(out=flat(ET, g), in_=pg[:].rearrange("p (h c) -> p h c", h=4), func=AF.Exp),
                         reads=[bpg], writes=[bET])
                    pg2, bpg2 = pF.next()
                    P.op("pe", lambda e, pg2=pg2, rv=rv: e.matmul(pg2[:], ONES[:], rv, start=True, stop=True), reads=[bC, bRt], writes=[bpg2])
                    P.op("act", lambda e, pg2=pg2, g=g: e.activation(out=flat(EGB, g), in_=pg2[:].rearrange("p (h c) -> p h c", h=4), func=AF.Exp),
                         reads=[bpg2], writes=[bEGB])
                P.op("pool", lambda e: e.tensor_tensor(out=ETS[:], in0=ET[:], in1=MSU8[:], op=ALU.mult), reads=[bET, bC], writes=[bETS])
                P.op("pool", lambda e: e.tensor_tensor(out=ET[:], in0=ET[:], in1=NEGM8[:], op=ALU.mult), reads=[bET, bC], writes=[bET])
                yield
                cur = 0
                for g in range(2):
                    pk, bpk = pF.next()
                    pq, bpq = pF.next()
                    for hh in range(4):
                        h = 4 * g + hh
                        P.op("pe", lambda e, pk=pk, h=h, hh=hh, QKV=QKV, tk=tk: e.matmul(
                            pk[:, hh * 128:(hh + 1) * 128], QKV[:, 8 + h, tk], QKV[:, 8 + h, tk], start=True, stop=True),
                            reads=[bQKV], writes=[bpk])
                        P.op("pe", lambda e, pq=pq, h=h, hh=hh, QKV=QKV, tk=tk: e.matmul(
                            pq[:, hh * 128:(hh + 1) * 128], QKV[:, 8 + h, tk], QKV[:, h, tk], start=True, stop=True),
                            reads=[bQKV], writes=[bpq])
                    for hh in range(4):
                        h = 4 * g + hh
                        P.op("dve", lambda e, pk=pk, h=h, hh=hh: e.scalar_tensor_tensor(
                            out=YR[0][:, h, 0:128], in0=pk[:, hh * 128:(hh + 1) * 128], scalar=NBG[:, h:h + 1],
                            in1=ETS[:, h, :], op0=ALU.mult, op1=ALU.mult), reads=[bpk, bNBG, bETS], writes=[bYR[0][g]])
                    P.op("dve", lambda e, pq=pq, g=g: e.tensor_tensor(out=flat(AIT, g), in0=pq[:].rearrange("p (h c) -> p h c", h=4),
                                                                     in1=flat(ET, g), op=ALU.mult), reads=[bpq, bET], writes=[bAIT])
                yield
                for g in range(2):
                    P.op("pool", lambda e, g=g: e.tensor_tensor(out=YR[1][:, 4 * g:4 * g + 4, 128:256], in0=YR[0][:, 4 * g:4 * g + 4, 0:128],
                                                                in1=flat(ID8, g), op=ALU.add), reads=[bYR[0][g], bC], writes=[bYR[1][g]])
                    if SD == BF16:
                        pz, bpz = pH.next()
                    else:
                        pz, bpz = pF.next()
                    for hh in range(4):
                        h = 4 * g + hh
                        P.op("pe", lambda e, pz=pz, h=h, hh=hh: e.transpose(pz[:, hh * 128:(hh + 1) * 128], YR[0][:, h, 0:128], identS[:]),
                             reads=[bYR[0][g], bidS], writes=[bpz])
                    P.op("act", lambda e, pz=pz, g=g: e.copy(flat(Z[0], g), pz[:, 0:512].rearrange("p (h c) -> p h c", h=4)),
                         reads=[bpz], writes=[bZ[0][g]])
                for k in range(6):
                    yield
                    a, b_ = k % 2, (k + 1) % 2
                    for g in range(2):
                        last = (k == 5)
                        if not last:
                            pz, bpz = pF.next()
                            for hh in range(4):
                                h = 4 * g + hh
                                P.op("pe", lambda e, pz=pz, h=h, hh=hh, a=a: e.matmul(
                                    pz[:, hh * 128:(hh + 1) * 128], YR[a][:, h, 0:128], Z[a][:, h, :], start=True, stop=True),
                                    reads=[bYR[a][g], bZ[a][g]], writes=[bpz])
                            P.op("act", lambda e, pz=pz, g=g, b_=b_: e.copy(flat(Z[b_], g), pz[:].rearrange("p (h c) -> p h c", h=4)),
                                 reads=[bpz], writes=[bZ[b_][g]])
                        if k == 0:
                            py, bpy = pF.next()
                            for hh in range(4):
                                h = 4 * g + hh
                                P.op("pe", lambda e, py=py, h=h, hh=hh: e.matmul(
                                    py[:, hh * 128:(hh + 1) * 128], Z[0][:, h, :], YR[0][:, h, 0:128], start=True, stop=True),
                                    reads=[bYR[0][g], bZ[0][g]], writes=[bpy])
                            P.op("act", lambda e, py=py, g=g: e.copy(YR[1][:, 4 * g:4 * g + 4, 0:128], py[:].rearrange("p (h c) -> p h c", h=4)),
                                 reads=[bpy], writes=[bYR[1][g]])
                        elif not last:
                            for half in range(2):
                                py, bpy = pF.next()
                                for hh in range(2):
                                    h = 4 * g + 2 * half + hh
                                    P.op("pe", lambda e, py=py, h=h, hh=hh, a=a: e.matmul(
                                        py[:, hh * 256:(hh + 1) * 256], Z[a][:, h, :], YR[a][:, h, :], start=True, stop=True),
                                        reads=[bYR[a][g], bZ[a][g]], writes=[bpy])
                                h0 = 4 * g + 2 * half
                                pv = py[:].rearrange("p (h c) -> p h c", h=2)
                                P.op("act", lambda e, pv=pv, h0=h0, b_=b_: e.copy(YR[b_][:, h0:h0 + 2, 0:128], pv[:, :, 0:128]),
                                     reads=[bpy], writes=[bYR[b_][g]])
                                P.op("dve", lambda e, pv=pv, h0=h0, a=a, b_=b_: e.tensor_tensor(
                                    out=YR[b_][:, h0:h0 + 2, 128:256], in0=pv[:, :, 128:256], in1=YR[a][:, h0:h0 + 2, 128:256], op=ALU.add),
                                    reads=[bpy, bYR[a][g]], writes=[bYR[b_][g]])
                        else:
                            py, bpy = pF.next()
                            for hh in range(4):
                                h = 4 * g + hh
                                P.op("pe", lambda e, py=py, h=h, hh=hh, a=a: e.matmul(
                                    py[:, hh * 128:(hh + 1) * 128], Z[a][:, h, :], YR[a][:, h, 128:256], start=True, stop=True),
                                    reads=[bYR[a][g], bZ[a][g]], writes=[bpy])
                            P.op("dve", lambda e, py=py, g=g, a=a: e.tensor_tensor(
                                out=flat(XS, g), in0=py[:].rearrange("p (h c) -> p h c", h=4), in1=YR[a][:, 4 * g:4 * g + 4, 128:256], op=ALU.add),
                                reads=[bpy, bYR[a][g]], writes=[bXS[g]])
                yield
                pkt, bpkt = pH.next()
                for h in range(8 if (M3SUB & 1) else 0):
                    P.op("pe", lambda e, pkt=pkt, h=h, QKV=QKV, tk=tk: e.transpose(pkt[:, h * 128:(h + 1) * 128], QKV[:, 8 + h, tk], identB[:]),
                         reads=[bQKV, bidB], writes=[bpkt])
                P.op("act", lambda e, pkt=pkt: e.copy(KTOK[:].rearrange("p h c -> p (h c)"), pkt[:]), reads=[bpkt], writes=[bKTOK])
                for h in range(8):
                    P.op("dve", lambda e, h=h: e.tensor_scalar(KG[:, h, :], KTOK[:, h, :], SM[:, 16 + h:17 + h], None, ALU.mult),
                         reads=[bKTOK, bSM], writes=[bKG])
                    P.op("act", lambda e, h=h: e.activation(out=KT2[:, h, :], in_=KTOK[:, h, :], func=AF.Identity, scale=SM[:, 24 + h:25 + h]),
                         reads=[bKTOK, bSM], writes=[bKT2])
                pvt, bpvt = pH.next()
                for h in range(8 if (M3SUB & 2) else 0):
                    P.op("pe", lambda e, pvt=pvt, h=h, QKV=QKV, tk=tk: e.transpose(pvt[:, h * 128:(h + 1) * 128], QKV[:, 16 + h, tk], identB[:]),
                         reads=[bQKV, bidB], writes=[bpvt])
                if M3SUB & 2:
                    P.op("act", lambda e, pvt=pvt: e.copy(Vt[:].rearrange("p h c -> p (h c)"), pvt[:]), reads=[bpvt], writes=[bVt])
                if M3SUB & 4:
                    P.op("pool", lambda e, QKV=QKV, tk=tk: e.tensor_tensor(out=QD[:], in0=QKV[:, 0:8, tk], in1=EGB[:], op=ALU.mult),
                         reads=[bQKV, bEGB], writes=[bQD])
                yield
                for g in range(2):
                    pu, bpu = pF.next()
                    pw, bpw = pF.next()
                    for hh in range(4):
                        h = 4 * g + hh
                        P.op("pe", lambda e, pu=pu, h=h, hh=hh: e.matmul(pu[:, hh * 128:(hh + 1) * 128], XS[:, h, :], Vt[:, h, :], start=True, stop=True),
                             reads=[bXS[g], bVt], writes=[bpu])
                        P.op("pe", lambda e, pw=pw, h=h, hh=hh: e.matmul(pw[:, hh * 128:(hh + 1) * 128], KG[:, h, :], XS[:, h, :], start=True, stop=True),
                             reads=[bXS[g], bKG], writes=[bpw])
                    for hh in range(4):
                        h = 4 * g + hh
                        P.op("act", lambda e, pu=pu, h=h, hh=hh, beta=beta: e.activation(out=UB[:, h, :], in_=pu[:, hh * 128:(hh + 1) * 128], func=AF.Identity,
                                                                                      scale=beta[:, h:h + 1]), reads=[bpu, bBG], writes=[bUB[h]])
                    P.op("act", lambda e, pw=pw, g=g: e.copy(flat(WT, g), pw[:].rearrange("p (h c) -> p h c", h=4)), reads=[bpw], writes=[bWT[g]])

            def rec(cx):
                tk, s, QKV, bQKV, BGt, bBG, OGT, bOGT = cx["tk"], cx["s"], cx["QKV"], cx["bQKV"], cx["BGt"], cx["bBG"], cx["OGT"], cx["bOGT"]
                NBG, bNBG, SM, bSM, AIT, bAIT, KT2, bKT2, QD, bQD, UB, bUB, WT, bWT = cx["H"]
                beta = BGt[:, s, 0:8]
                graw = BGt[:, s, 8:16]
                if cx["tile_first"]:
                    for sb4 in range(4):
                        Xs_, bXs_ = XSr.next()
                        P.dma("sp", Xs_[:], src[cx["t0"] + sb4 * 128:cx["t0"] + (sb4 + 1) * 128, :], writes=[bXs_])
                        self.transpose_x([(Xs_, bXs_)], xT1[:, :, sb4 * 128:(sb4 + 1) * 128], bxT1, pF, identF, bidF)
                    for sb4 in range(4):
                        for hf in range(2):
                            pz_, bpz_ = pF.next()
                            for kc in range(8):
                                P.op("pe", lambda e, pz_=pz_, kc=kc, hf=hf, sb4=sb4: e.matmul(pz_[:], xT1[:, kc, sb4 * 128:(sb4 + 1) * 128], WZ[:, kc, hf * 512:(hf + 1) * 512],
                                                                                           start=(kc == 0), stop=(kc == 7)),
                                     reads=[bxT1[kc], bWZ[kc]], writes=[bpz_])
                            P.op("act", lambda e, pz_=pz_, hf=hf, sb4=sb4: e.activation(out=ZG4[:, sb4, 4 * hf:4 * hf + 4, :], in_=pz_[:].rearrange("p (h c) -> p h c", h=4), func=AF.Silu),
                                 reads=[bpz_], writes=[bZG])
                        P.op("pool", lambda e, sb4=sb4: e.tensor_tensor(out=ZG4[:, sb4, :, :], in0=ZG4[:, sb4, :, :], in1=DNGb[:], op=ALU.mult), reads=[bZG, bDNG], writes=[bZG])
                        yield
                if cx["seq_reset"]:
                    P.op("pool", lambda e: e.memset(S[:], 0.0), writes=bS)
                    P.op("pool", lambda e: e.memset(SB[:], 0.0), writes=bSB)
                for ck in range(2):
                    rows = slice(ck * 64, (ck + 1) * 64)
                    for g in range(2):
                        yield
                        pv_, bpv = pF.next()
                        for hh in range(4):
                            h = 4 * g + hh
                            P.op("pe", lambda e, pv_=pv_, h=h, hh=hh: e.matmul(pv_[:, hh * 128:(hh + 1) * 128], WT[:, h, :], SB[:, h, :], start=True, stop=True),
                                 reads=[bWT[g], bSB[h]], writes=[bpv])
                        for hh in range(4):
                            h = 4 * g + hh
                            P.op("dve", lambda e, pv_=pv_, h=h, hh=hh, rows=rows: e.scalar_tensor_tensor(
                                out=VN[rows, h, :], in0=pv_[rows, hh * 128:(hh + 1) * 128], scalar=NBG[rows, h:h + 1], in1=UB[rows, h, :],
                                op0=ALU.mult, op1=ALU.add), reads=[bpv, bNBG, bUB[h]], writes=[bVN[h]])
                        po, bpo = pF.next()
                        for hh in range(4):
                            h = 4 * g + hh
                            P.op("pe", lambda e, po=po, h=h, hh=hh: e.matmul(po[:, hh * 128:(hh + 1) * 128], QD[:, h, :], SB[:, h, :], start=True, stop=False),
                                 reads=[bQD, bSB[h]], writes=[bpo])
                            P.op("pe", lambda e, po=po, h=h, hh=hh: e.matmul(po[:, hh * 128:(hh + 1) * 128], AIT[:, h, :], VN[:, h, :], start=False, stop=True),
                                 reads=[bAIT, bVN[h]], writes=[bpo])
                        P.op("act", lambda e, po=po, g=g, rows=rows: e.copy(O[rows, 4 * g:4 * g + 4, :], po[rows, :].rearrange("p (h c) -> p h c", h=4)),
                             reads=[bpo], writes=[bO])
                        ps_, bps = pF.next()
                        for hh in range(4):
                            h = 4 * g + hh
                            P.op("pe", lambda e, ps_=ps_, h=h, hh=hh, rows=rows: e.matmul(ps_[:, hh * 128:(hh + 1) * 128], KT2[rows, h, :], VN[rows, h, :], start=True, stop=True),
                                 reads=[bKT2, bVN[h]], writes=[bps])
                        for hh in range(4):
                            h = 4 * g + hh
                            P.op("dve", lambda e, ps_=ps_, h=h, hh=hh, ck=ck: e.scalar_tensor_tensor(
                                out=S[:, h, :], in0=S[:, h, :], scalar=SM[:, ck * 8 + h:ck * 8 + h + 1], in1=ps_[:, hh * 128:(hh + 1) * 128],
                                op0=ALU.mult, op1=ALU.add), reads=[bps, bSM, bS[h]], writes=[bS[h]])
                            P.op("act", lambda e, h=h: e.copy(SB[:, h, :], S[:, h, :]), reads=[bS[h]], writes=[bSB[h]])
                yield
                P.op("pool", lambda e: e.tensor_tensor(out=SQ[:], in0=O[:], in1=O[:], op=ALU.mult), reads=[bO], writes=[bSQ])
                P.op("dve", lambda e: e.tensor_reduce(out=SM[:, 40:48], in_=SQ[:], axis=AX.X, op=ALU.add), reads=[bSQ, bSM], writes=[bSM])
                P.op("dve", lambda e: e.tensor_scalar(SM[:, 48:56], SM[:, 40:48], 1.0 / 128.0, float(NORM_EPS), ALU.mult, ALU.add), reads=[bSM], writes=[bSM])
                P.op("act", lambda e: e.activation(out=SM[:, 48:56], in_=SM[:, 48:56], func=AF.Ln), reads=[bSM], writes=[bSM])
                P.op("act", lambda e: e.activation(out=SM[:, 48:56], in_=SM[:, 48:56], func=AF.Exp, scale=-0.5), reads=[bSM], writes=[bSM])
                for h in range(8):
                    P.op("dve", lambda e, h=h: e.scalar_tensor_tensor(out=OG[:, h, :], in0=O[:, h, :], scalar=SM[:, 48 + h:49 + h], in1=ZG4[:, s, h, :],
                                                                      op0=ALU.mult, op1=ALU.mult), reads=[bO, bSM, bZG], writes=[bOG])
                pt_, bpt = pH.next()
                for h in range(8):
                    P.op("pe", lambda e, pt_=pt_, h=h: e.transpose(pt_[:, h * 128:(h + 1) * 128], OG[:, h, :], identB[:]),
                         reads=[bOG, bidB], writes=[bpt])
                P.op("act", lambda e, pt_=pt_, OGT=OGT, tk=tk: e.copy(OGT[:, :, tk], pt_[:].rearrange("p (h c) -> p h c", h=8)),
                     reads=[bpt], writes=[bOGT])
                if cx["tile_last"]:
                    P.dma("pool", self.OGT[:, cx["t0"]:cx["t0"] + 512].rearrange("(c p) t -> p c t", p=128), OGT[:], reads=[bOGT])

            def interleave(g1, g2):
                gens = [g for g in (g1, g2) if g is not None]
                while gens:
                    for g in list(gens):
                        try:
                            next(g)
                        except StopIteration:
                            gens.remove(g)

            ntl = T // 512
            tl = {0: m3_loads(0)}
            prev = None
            for ti in range(ntl):
                t0 = ti * 512
                QKV, bQKV, BGt, bBG = tl.pop(ti)
                if ti + 1 < ntl:
                    tl[ti + 1] = m3_loads(t0 + 512)
                OGT, bOGT = OGTr.next()
                for s in range(4):
                    gb = ti * 4 + s
                    cx = dict(tk=slice(s * 128, (s + 1) * 128), s=s, QKV=QKV, bQKV=bQKV, BGt=BGt, bBG=bBG, OGT=OGT, bOGT=bOGT,
                              H=HH[gb % 2], t0=t0, tile_first=(s == 0), tile_last=(s == 3), seq_reset=(s == 0 and t0 % L == 0))
                    interleave(prep(cx), rec(prev) if prev is not None else None)
                    prev = cx
            interleave(rec(prev), None)

    def m4_phase(self, layer, src, dst):
        P = self.P
        T = self.T
        with self.phase() as ph:
            WA = ph.sb([128, 8, D], BF16)
            WB = ph.sb([128, 8, D], BF16)
            WO = ph.sb([128, 8, D], BF16)
            WG = ph.sb([128, 8, 2 * D], BF16)
            bWA, bWB, bWO, bWG = bufs(8), bufs(8), bufs(8), bufs(8)
            self.load_w(WA, bWA, self.w_a[layer], 8, 0, D, piece=1024)
            self.load_w(WB, bWB, self.w_b[layer], 8, 0, D, piece=1024)
            self.load_w(WG, bWG, self.w_in[layer], 8, C_GATE, C_GATE + 2 * D, piece=1024)
            self.load_w(WO, bWO, self.w_o[layer], 8, 0, D, piece=1024)
            ident, bid = self.make_ident(ph, F32)
            lnc = self.ln_consts(ph, layer, 1)
            NB = 2
            XSr = Rot([ph.sb([128, D], F32) for _ in range(4)])
            XRr = Rot([ph.sb([128, D], F32) for _ in range(2)])
            xTr = [ph.sb([128, 8, 512], BF16) for _ in range(NB)]
            bxTr = [bufs(8) for _ in range(NB)]
            ATr = Rot([ph.sb([128, 8, 512], BF16) for _ in range(2)])
            OGr = Rot([ph.sb([128, 8, 512], BF16) for _ in range(2)])
            MT = ph.sb([128, 8, 512], BF16)
            bMT = bufs(8)
            SGr = Rot([ph.sb([128, 512], F32) for _ in range(4)])
            T1r = Rot([ph.sb([128, 512], F32) for _ in range(2)])
            T2r = Rot([ph.sb([128, 512], F32) for _ in range(2)])
            small = Rot([ph.sb([128, 16], F32) for _ in range(2)])
            pT = Rot([ph.ps() for _ in range(2)])
            pM = Rot([ph.ps() for _ in range(4)])
            pY = [ph.ps() for _ in range(2)]
            bpY = bufs(2)
            def m4_loads(t0):
                xs = self.issue_x(src, t0, XSr)
                At, bA = ATr.next()
                Og, bOg = OGr.next()
                P.dma("sp", At[:], self.AT[:, t0:t0 + 512].rearrange("(c p) t -> p c t", p=128), writes=[bA])
                P.dma("sp", Og[:], self.OGT[:, t0:t0 + 512].rearrange("(c p) t -> p c t", p=128), writes=[bOg])
                return xs, At, bA, Og, bOg

            nxt = m4_loads(0)
            self.transpose_x(nxt[0], xTr[0], bxTr[0], pT, ident, bid)
            for ti in range(T // 512):
                t0 = ti * 512
                xT, bxT = xTr[ti % NB], bxTr[ti % NB]
                _, At, bA, Og, bOg = nxt
                if ti + 1 < T // 512:
                    nxt = m4_loads(t0 + 512)
                for n in range(8):
                    ns = slice(n * 128, (n + 1) * 128)
                    pa, bpa = pM.next()
                    pga, bpga = pM.next()
                    for kc in range(8):
                        P.op("pe", lambda e, pa=pa, kc=kc, ns=ns, At=At: e.matmul(pa[:], WA[:, kc, ns], At[:, kc, :], start=(kc == 0), stop=(kc == 7)),
                             reads=[bWA[kc], bA], writes=[bpa])
                    for kc in range(8):
                        P.op("pe", lambda e, pga=pga, kc=kc, ns=ns, xT=xT: e.matmul(pga[:], WG[:, kc, ns], xT[:, kc, :], start=(kc == 0), stop=(kc == 7)),
                             reads=[bWG[kc], bxT[kc]], writes=[bpga])
                    sga, bsga = SGr.next()
                    P.op("act", lambda e, sga=sga, pga=pga: e.activation(out=sga[:], in_=pga[:], func=AF.Sigmoid), reads=[bpga], writes=[bsga])
                    t1, bt1 = T1r.next()
                    P.op("dve", lambda e, t1=t1, pa=pa, sga=sga: e.tensor_tensor(out=t1[:], in0=pa[:], in1=sga[:], op=ALU.mult),
                         reads=[bpa, bsga], writes=[bt1])
                    pb, bpb = pM.next()
                    pgb, bpgb = pM.next()
                    for kc in range(8):
                        P.op("pe", lambda e, pb=pb, kc=kc, ns=ns, Og=Og: e.matmul(pb[:], WB[:, kc, ns], Og[:, kc, :], start=(kc == 0), stop=(kc == 7)),
                             reads=[bWB[kc], bOg], writes=[bpb])
                    for kc in range(8):
                        P.op("pe", lambda e, pgb=pgb, kc=kc, n=n, xT=xT: e.matmul(pgb[:], WG[:, kc, D + n * 128:D + (n + 1) * 128], xT[:, kc, :],
                                                                               start=(kc == 0), stop=(kc == 7)),
                             reads=[bWG[kc], bxT[kc]], writes=[bpgb])
                    sgb, bsgb = SGr.next()
                    P.op("act", lambda e, sgb=sgb, pgb=pgb: e.activation(out=sgb[:], in_=pgb[:], func=AF.Sigmoid), reads=[bpgb], writes=[bsgb])
                    t2, bt2 = T2r.next()
                    P.op("dve", lambda e, t2=t2, pb=pb, sgb=sgb: e.tensor_tensor(out=t2[:], in0=pb[:], in1=sgb[:], op=ALU.mult),
                         reads=[bpb, bsgb], writes=[bt2])
                    P.op("pool", lambda e, t1=t1, t2=t2, n=n: e.tensor_tensor(out=MT[:, n, :], in0=t1[:], in1=t2[:], op=ALU.add),
                         reads=[bt1, bt2], writes=[bMT[n]])
                if ti + 1 < T // 512:
                    self.transpose_x(nxt[0], xTr[(ti + 1) % NB], bxTr[(ti + 1) % NB], pT, ident, bid)
                for s in range(4):
                    for hf in range(2):
                        for kc in range(8):
                            P.op("pe", lambda e, hf=hf, kc=kc, s=s: e.matmul(pY[hf][:], MT[:, kc, s * 128:(s + 1) * 128], WO[:, kc, hf * 512:(hf + 1) * 512],
                                                                            start=(kc == 0), stop=(kc == 7)),
                                 reads=[bMT[kc], bWO[kc]], writes=[bpY[hf]])
                    self.ln_epilogue(pY, bpY, src, dst, t0 + s * 128, 1.0 / DN_ALPHA, lnc, XRr, small)

    def build(self, upto=99):
        cur = self.x
        n = 0
        for layer in range(self.depth):
            last = (layer == self.depth - 1)
            steps = [
                lambda: self.ffn_phase(layer, 0, cur, self.R[0], 0),
                lambda: self.m1_phase(layer, self.R[0]),
                lambda: self.m2_phase(layer),
                lambda: self.m3_phase(layer, self.R[0]),
                lambda: self.m4_phase(layer, self.R[0], self.R[1]),
                lambda: self.ffn_phase(layer, 1, self.R[1], self.out if last else self.R[0], 2),
            ]
            for st in steps:
                if n < upto:
                    st()
                n += 1
            cur = self.R[0]
        self.top.close()
        return self.nc


def host_pos_tables(rel_bias):
    s = np.arange(128)[:, None]
    q = np.arange(128)[None, :]
    out_b = np.zeros((128, 2, 16, 128), np.float32)
    out_m = np.zeros((128, 2, 16, 128), np.float32)
    for blk in range(2):
        j = s + 128 * blk
        rel = q + 128 - j
        valid = (rel >= 0) & (rel < 128)
        n = np.maximum(rel, 0)
        nf = np.maximum(n, 1).astype(np.float32)
        large = 16 + (np.log(nf / np.float32(16)) / np.float32(np.log(128 / 16)) * np.float32(16)).astype(np.int32)
        large = np.minimum(large, 31)
        bucket = np.where(n < 16, n, large)
        bucket = np.where(valid, bucket, 0)
        g = rel_bias[bucket]
        vm = np.broadcast_to(valid[:, None, :], (128, 16, 128))
        out_b[:, blk] = np.where(vm, np.transpose(g, (0, 2, 1)), np.float32(-1e30))
        out_m[:, blk] = vm
    return out_b.reshape(128, -1), out_m.reshape(128, -1)


_NC_CACHE = {}


def kernel(x, rel_bias, ln_g, ln_b, ffn_w13, ffn_w2, w_in, conv_w, a_log, dt_bias,
           dn_norm_g, sinks, w_branch_a, w_branch_b, w_out):
    x = np.asarray(x, np.float32)
    B, L, _ = x.shape
    nseq = B // NCORES
    key = (nseq, L)
    if key not in _NC_CACHE:
        _NC_CACHE[key] = KB(nseq, L, DEPTH).build()
    nc = _NC_CACHE[key]
    pb, pm = host_pos_tables(np.asarray(rel_bias, np.float32))
    f = lambda a: np.ascontiguousarray(np.asarray(a, np.float32))
    shared = dict(ln_g=f(ln_g), ln_b=f(ln_b), ffn_w13=f(ffn_w13), ffn_w2=f(ffn_w2), w_in=f(w_in), conv_w=f(conv_w),
                  a_log=f(a_log), dt_bias=f(dt_bias), dn_norm_g=f(dn_norm_g), sinks=f(sinks),
                  w_branch_a=f(w_branch_a), w_branch_b=f(w_branch_b), w_out=f(w_out), pbias=pb, pmask=pm)
    in_maps = []
    for c in range(NCORES):
        m = dict(shared)
        m["x"] = np.ascontiguousarray(x[c * nseq:(c + 1) * nseq].reshape(nseq * L, D))
        in_maps.append(m)
    res = run_bass_kernel_spmd(nc, in_maps, core_ids=list(range(NCORES)))
    outs = [np.asarray(r["out"], np.float32).reshape(nseq, L, D) for r in res.results]
    return np.concatenate(outs, axis=0)
```

```python
import contextlib
import numpy as np
import concourse.bass as bass
import concourse.mybir as mybir
from concourse.bass_utils import run_bass_kernel_spmd

F32 = mybir.dt.float32
BF16 = mybir.dt.bfloat16
AF = mybir.ActivationFunctionType
ALU = mybir.AluOpType
AX = mybir.AxisListType

D = 1024
DFF = 2816
NIN = 7696
DEPTH = 4
SEQ = 4096
NCORES = 8
LN_EPS = 1e-5
NORM_EPS = 1e-6
DN_ALPHA = (2 * DEPTH) ** 0.25
C_Q0, C_KA, C_VA, C_QKVB, C_BETA, C_DT, C_Z, C_GATE = 0, 1024, 1280, 1536, 4608, 4616, 4624, 5648
SOLVE_DT = BF16
import os
M3STOP = int(os.environ.get("M3STOP", "99"))
M3SUB = int(os.environ.get("M3SUB", "7"))

NDMASEM = 16
ENGS = ("pe", "act", "dve", "pool", "sp")


class Buf:
    __slots__ = ("lw", "rd")

    def __init__(self):
        self.lw = None
        self.rd = []


def bufs(n):
    return [Buf() for _ in range(n)]


class Prog:
    def __init__(self, nc, stack):
        self.nc = nc
        self.ops = {e: [] for e in ENGS}
        self.cnt = {e: 0 for e in ("pe", "act", "dve", "pool")}
        self.seen = {e: {} for e in ENGS}
        self.dq_n = {"sp": 0, "pool": 0}
        self.esem = {e: stack.enter_context(nc.semaphore("s_" + e)) for e in ("pe", "act", "dve", "pool")}
        self.dsem = {}
        for q in ("sp", "pool"):
            for k in range(NDMASEM):
                self.dsem[(q, k)] = stack.enter_context(nc.semaphore("d_%s%d" % (q, k)))
        self.ninstr = 0

    def _kv(self, tok):
        if tok[0] == "e":
            return ("e", tok[1]), tok[2]
        q, i = tok[1], tok[2]
        return ("d", q, i % NDMASEM), 16 * (i // NDMASEM + 1)

    def _deps(self, eng, reads, writes):
        need = {}

        def add(tok):
            if tok is None:
                return
            if tok[0] == "e" and tok[1] == "pe" and eng == "pe":
                return
            k, v = self._kv(tok)
            if need.get(k, 0) < v:
                need[k] = v

        for b in reads:
            add(b.lw)
        for b in writes:
            add(b.lw)
            for t in b.rd:
                add(t)
        out = []
        s = self.seen[eng]
        for k, v in need.items():
            if s.get(k, 0) < v:
                s[k] = v
                out.append((k, v))
        return out

    def _commit(self, tok, reads, writes):
        for b in reads:
            b.rd.append(tok)
            if len(b.rd) > 32:
                best = {}
                for t in b.rd:
                    k, v = self._kv(t)
                    if k not in best or best[k][0] < v:
                        best[k] = (v, t)
                b.rd = [t for (_, t) in best.values()]
        for b in writes:
            b.lw = tok
            b.rd = []

    def op(self, eng, fn, reads=(), writes=()):
        waits = self._deps(eng, reads, writes)
        self.cnt[eng] += 1
        tok = ("e", eng, self.cnt[eng])
        self.ops[eng].append((waits, fn, None))
        self._commit(tok, reads, writes)

    def dma(self, q, out_ap, in_ap, reads=(), writes=(), slow=False):
        i = self.dq_n[q]
        self.dq_n[q] += 1
        tok = ("d", q, i)
        waits = self._deps(q, reads, writes)
        if i >= NDMASEM:
            k, v = self._kv(("d", q, i - NDMASEM))
            if self.seen[q].get(k, 0) < v:
                self.seen[q][k] = v
                waits.append((k, v))
        self.ops[q].append((waits, (out_ap, in_ap, slow), tok))
        self._commit(tok, reads, writes)

    def barrier(self):
        allk = []
        for e, c in self.cnt.items():
            if c:
                allk.append((("e", e), c))
        for q, n in self.dq_n.items():
            for k in range(min(NDMASEM, n)):
                last = ((n - 1 - k) // NDMASEM) * NDMASEM + k
                allk.append((("d", q, k), 16 * (last // NDMASEM + 1)))
        for e in ENGS:
            s = self.seen[e]
            waits = []
            for k, v in allk:
                if k == ("e", "pe") and e == "pe":
                    continue
                if s.get(k, 0) < v:
                    s[k] = v
                    waits.append((k, v))
            if waits:
                self.ops[e].append((waits, None, None))

    def emit(self):
        nc = self.nc

        def semof(k):
            return self.esem[k[1]] if k[0] == "e" else self.dsem[(k[1], k[2])]

        def run(engname, e):
            for waits, fn, tok in self.ops[engname]:
                for k, v in waits:
                    e.wait_ge(semof(k), v)
                if fn is None:
                    continue
                self.ninstr += 1
                if tok is None:
                    fn(e).then_inc(self.esem[engname], 1)
                else:
                    o, i, slow = fn
                    if slow:
                        ins = e.dma_start(out=o, in_=i, allow_slow_non_contiguous=True)
                    else:
                        ins = e.dma_start(out=o, in_=i)
                    ins.then_inc(self.dsem[(tok[1], tok[2] % NDMASEM)], 16)
            self.ops[engname] = []

        with nc.Block() as block:
            @block.tensor
            def _(e):
                run("pe", e)

            @block.scalar
            def _(e):
                run("act", e)

            @block.vector
            def _(e):
                run("dve", e)

            @block.gpsimd
            def _(e):
                run("pool", e)

            @block.sync
            def _(e):
                run("sp", e)


class Phase:
    def __init__(self, kb):
        self.kb = kb
        self.st = contextlib.ExitStack()
        self.n = 0

    def __enter__(self):
        self.st.__enter__()
        return self

    def __exit__(self, *a):
        self.kb.P.barrier()
        self.kb.P.emit()
        return self.st.__exit__(*a)

    def sb(self, shape, dt):
        self.n += 1
        return self.st.enter_context(self.kb.nc.sbuf_tensor("t%d_%d" % (self.kb.phase_id, self.n), list(shape), dt))

    def ps(self, dt=F32):
        self.n += 1
        cols = 512 if dt == F32 else 1024
        return self.st.enter_context(self.kb.nc.psum_tensor("p%d_%d" % (self.kb.phase_id, self.n), [128, cols], dt))


class Rot:
    def __init__(self, tiles):
        self.t = tiles
        self.b = bufs(len(tiles))
        self.i = 0

    def next(self):
        k = self.i % len(self.t)
        self.i += 1
        return self.t[k], self.b[k]


class KB:
    def __init__(self, nseq, L, depth, debug=False):
        self.nseq, self.L, self.depth = nseq, L, depth
        self.T = nseq * L
        self.debug = debug
        self.nc = bass.Bass("TRN2", target_bir_lowering=False)
        self.top = contextlib.ExitStack()
        self.phase_id = 0
        nc = self.nc
        T = self.T
        ext = lambda n, s: nc.dram_tensor(n, list(s), F32, kind="ExternalInput").ap()
        self.x = ext("x", [T, D])
        self.ln_g = ext("ln_g", [DEPTH, 3, D])
        self.ln_b = ext("ln_b", [DEPTH, 3, D])
        self.w13 = ext("ffn_w13", [DEPTH, 2, D, 2 * DFF])
        self.w2 = ext("ffn_w2", [DEPTH, 2, DFF, D])
        self.w_in = ext("w_in", [DEPTH, D, NIN])
        self.conv_w = ext("conv_w", [DEPTH, 4, 3072])
        self.a_log = ext("a_log", [DEPTH, 8])
        self.dt_bias = ext("dt_bias", [DEPTH, 8])
        self.dn_g = ext("dn_norm_g", [DEPTH, 128])
        self.sinks = ext("sinks", [DEPTH, 16])
        self.w_a = ext("w_branch_a", [DEPTH, D, D])
        self.w_b = ext("w_branch_b", [DEPTH, D, D])
        self.w_o = ext("w_out", [DEPTH, D, D])
        self.pbias = ext("pbias", [128, 2 * 16 * 128])
        self.pmask = ext("pmask", [128, 2 * 16 * 128])
        self.out = nc.dram_tensor("out", [T, D], F32, kind="ExternalOutput").ap()
        kind = "ExternalOutput" if debug else "Internal"
        scr = lambda n, s, dt: nc.dram_tensor(n, list(s), dt, kind=kind).ap()
        self.R = [scr("res%d" % i, [T, D], F32) for i in range(2)]
        self.QT = scr("QT", [1024, T], BF16)
        self.KT = scr("KT", [256, T], BF16)
        self.VA = scr("VA", [T, 256], BF16)
        self.QKVB = scr("QKVB", [3072, T], BF16)
        self.BG = scr("BG", [T, 16], F32)
        self.AT = scr("AT", [1024, T], BF16)
        self.OGT = scr("OGT", [1024, T], BF16)
        self.P = Prog(nc, self.top)

    def phase(self):
        self.phase_id += 1
        return Phase(self)

    def make_ident(self, ph, dt):
        P = self.P
        t = ph.sb([128, 128], dt)
        b = Buf()
        P.op("pool", lambda e: e.memset(t[:], 0.0), writes=[b])
        P.op("pool", lambda e: e.affine_select(out=t[:], in_=t[:], pattern=[[-1, 128]], compare_op=ALU.not_equal,
                                               fill=1.0, base=0, channel_multiplier=1), reads=[b], writes=[b])
        return t, b

    def load_w(self, dst, dbufs, src, kcs, c0, c1, piece=2048):
        v = src.rearrange("(kc p) n -> p kc n", p=128)
        for kc in range(kcs):
            a = c0
            while a < c1:
                b = min(c1, a + piece)
                self.P.dma("pool", dst[:, kc, a - c0:b - c0], v[:, kc, a:b], writes=[dbufs[kc]])
                a = b

    def issue_x(self, src, t0, XS4, nsub=4):
        out = []
        for s in range(nsub):
            Xs, bXs = XS4.next()
            self.P.dma("sp", Xs[:], src[t0 + s * 128:t0 + (s + 1) * 128, :], writes=[bXs])
            out.append((Xs, bXs))
        return out

    def transpose_x(self, xs, xT, bxT, pT, ident, bid):
        P = self.P
        for s, (Xs, bXs) in enumerate(xs):
            for half in range(2):
                pt, bpt = pT.next()
                for k4 in range(4):
                    kc = half * 4 + k4
                    P.op("pe", lambda e, pt=pt, k4=k4, kc=kc, Xs=Xs: e.transpose(pt[:, k4 * 128:(k4 + 1) * 128],
                                                                              Xs[:, kc * 128:(kc + 1) * 128], ident[:]),
                         reads=[bXs, bid], writes=[bpt])
                wb = [bxT[half * 4 + k4] for k4 in range(4)]
                dst = xT[:, half * 4:half * 4 + 4, s * 128:(s + 1) * 128]
                srcv = pt[:].rearrange("p (k c) -> p k c", k=4)
                if half == 0:
                    P.op("act", lambda e, dst=dst, srcv=srcv: e.copy(dst, srcv), reads=[bpt], writes=wb)
                else:
                    P.op("dve", lambda e, dst=dst, srcv=srcv: e.tensor_copy(dst, srcv), reads=[bpt], writes=wb)

    def ln_consts(self, ph, layer, idx):
        P = self.P
        G = ph.sb([128, D], F32)
        B = ph.sb([128, D], F32)
        bg, bb = Buf(), Buf()
        P.dma("sp", G[:], self.ln_g[layer, idx, :].partition_broadcast(128), writes=[bg])
        P.dma("sp", B[:], self.ln_b[layer, idx, :].partition_broadcast(128), writes=[bb])
        return (G, bg, B, bb)

    def ln_epilogue(self, py, bpy, src, dst, r0, c, lnc, XRr, small):
        P = self.P
        G, bg, B, bb = lnc
        Rt, bR = XRr.next()
        st, bst = small.next()
        P.dma("sp", Rt[:], src[r0:r0 + 128, :], writes=[bR])
        for hf in range(2):
            P.op("dve", lambda e, hf=hf, Rt=Rt: e.scalar_tensor_tensor(
                out=Rt[:, hf * 512:(hf + 1) * 512], in0=py[hf][:], scalar=float(c),
                in1=Rt[:, hf * 512:(hf + 1) * 512], op0=ALU.mult, op1=ALU.add),
                reads=[bpy[hf], bR], writes=[bR])
        for hf in range(2):
            P.op("dve", lambda e, hf=hf, Rt=Rt, st=st: e.bn_stats(st[:, hf * 6:(hf + 1) * 6], Rt[:, hf * 512:(hf + 1) * 512]),
                 reads=[bR], writes=[bst])
        P.op("dve", lambda e, st=st: e.bn_aggr(st[:, 12:14], st[:, 0:12]), reads=[bst], writes=[bst])
        eps = LN_EPS / (DN_ALPHA ** 2)
        P.op("dve", lambda e, st=st: e.tensor_scalar(st[:, 13:14], st[:, 13:14], float(eps), None, ALU.add),
             reads=[bst], writes=[bst])
        P.op("act", lambda e, st=st: e.activation(out=st[:, 14:15], in_=st[:, 13:14], func=AF.Ln),
             reads=[bst], writes=[bst])
        P.op("act", lambda e, st=st: e.activation(out=st[:, 14:15], in_=st[:, 14:15], func=AF.Exp, scale=-0.5),
             reads=[bst], writes=[bst])
        P.op("dve", lambda e, st=st: e.scalar_tensor_tensor(out=st[:, 15:16], in0=st[:, 12:13], scalar=-1.0,
                                                            in1=st[:, 14:15], op0=ALU.mult, op1=ALU.mult),
             reads=[bst], writes=[bst])
        P.op("act", lambda e, st=st, Rt=Rt: e.activation(out=Rt[:], in_=Rt[:], func=AF.Identity,
                                                        bias=st[:, 15:16], scale=st[:, 14:15]),
             reads=[bst, bR], writes=[bR])
        P.op("pool", lambda e, Rt=Rt: e.tensor_tensor(out=Rt[:], in0=Rt[:], in1=G[:], op=ALU.mult),
             reads=[bR, bg], writes=[bR])
        P.op("pool", lambda e, Rt=Rt: e.tensor_tensor(out=Rt[:], in0=Rt[:], in1=B[:], op=ALU.add),
             reads=[bR, bb], writes=[bR])
        P.dma("pool", dst[r0:r0 + 128, :], Rt[:], reads=[bR])

    def ffn_phase(self, layer, which, src, dst, ln_idx):
        P = self.P
        T = self.T
        with self.phase() as ph:
            W13 = ph.sb([128, 8, 2 * DFF], BF16)
            W2 = ph.sb([128, 22, D], BF16)
            bW13, bW2 = bufs(8), bufs(22)
            self.load_w(W13, bW13, self.w13[layer, which], 8, 0, 2 * DFF, piece=1408)
            self.load_w(W2, bW2, self.w2[layer, which], 22, 0, D, piece=1024)
            ident, bid = self.make_ident(ph, F32)
            lnc = self.ln_consts(ph, layer, ln_idx)
            NB = 2
            XSr = Rot([ph.sb([128, D], F32) for _ in range(4)])
            XRr = Rot([ph.sb([128, D], F32) for _ in range(2)])
            xTr = [ph.sb([128, 8, 512], BF16) for _ in range(NB)]
            bxTr = [bufs(8) for _ in range(NB)]
            HT = ph.sb([128, 22, 512], BF16)
            bHT = bufs(22)
            SG = Rot([ph.sb([128, 512], F32) for _ in range(2)])
            small = Rot([ph.sb([128, 16], F32) for _ in range(2)])
            pT = Rot([ph.ps() for _ in range(2)])
            pGU = Rot([ph.ps() for _ in range(4)])
            pY = [ph.ps() for _ in range(2)]
            bpY = bufs(2)
            ntiles = T // 512
            xs_next = self.issue_x(src, 0, XSr)
            self.transpose_x(xs_next, xTr[0], bxTr[0], pT, ident, bid)
            for ti in range(ntiles):
                t0 = ti * 512
                xT, bxT = xTr[ti % NB], bxTr[ti % NB]
                if ti + 1 < ntiles:
                    xs_next = self.issue_x(src, t0 + 512, XSr)
                for j in range(22):
                    pg, bpg = pGU.next()
                    pu, bpu = pGU.next()
                    for kc in range(8):
                        P.op("pe", lambda e, pg=pg, kc=kc, j=j, xT=xT: e.matmul(
                            pg[:], W13[:, kc, j * 128:(j + 1) * 128], xT[:, kc, :], start=(kc == 0), stop=(kc == 7)),
                            reads=[bW13[kc], bxT[kc]], writes=[bpg])
                    for kc in range(8):
                        P.op("pe", lambda e, pu=pu, kc=kc, j=j, xT=xT: e.matmul(
                            pu[:], W13[:, kc, DFF + j * 128:DFF + (j + 1) * 128], xT[:, kc, :], start=(kc == 0), stop=(kc == 7)),
                            reads=[bW13[kc], bxT[kc]], writes=[bpu])
                    sg, bsg = SG.next()
                    P.op("act", lambda e, sg=sg, pg=pg: e.activation(out=sg[:], in_=pg[:], func=AF.Silu),
                         reads=[bpg], writes=[bsg])
                    P.op("dve", lambda e, sg=sg, pu=pu, j=j: e.tensor_tensor(out=HT[:, j, :], in0=pu[:], in1=sg[:], op=ALU.mult),
                         reads=[bpu, bsg], writes=[bHT[j]])
                if ti + 1 < ntiles:
                    self.transpose_x(xs_next, xTr[(ti + 1) % NB], bxTr[(ti + 1) % NB], pT, ident, bid)
                for s in range(4):
                    for hf in range(2):
                        for j in range(22):
                            P.op("pe", lambda e, hf=hf, j=j, s=s: e.matmul(
                                pY[hf][:], HT[:, j, s * 128:(s + 1) * 128], W2[:, j, hf * 512:(hf + 1) * 512],
                                start=(j == 0), stop=(j == 21)),
                                reads=[bHT[j], bW2[j]], writes=[bpY[hf]])
                    self.ln_epilogue(pY, bpY, src, dst, t0 + s * 128, 0.5 / DN_ALPHA, lnc, XRr, small)

    def m1_phase(self, layer, src):
        P = self.P
        T, L = self.T, self.L
        NW = C_Z
        with self.phase() as ph:
            W = ph.sb([128, 8, NW], BF16)
            bW = bufs(8)
            self.load_w(W, bW, self.w_in[layer], 8, 0, NW, piece=1156)
            ident, bid = self.make_ident(ph, F32)
            ones = ph.sb([128, 128], BF16)
            bones = Buf()
            P.op("pool", lambda e: e.memset(ones[:], 1.0), writes=[bones])
            CW = ph.sb([128, 4, 24], F32)
            bCW = Buf()
            for j in range(4):
                P.dma("sp", CW[:, j, :], self.conv_w[layer, j, :].rearrange("(c p) -> p c", p=128), writes=[bCW], slow=True)
            DTB = ph.sb([128, 8], F32)
            NEGA = ph.sb([128, 8], F32)
            bDTB, bNEGA = Buf(), Buf()
            P.dma("sp", DTB[:], self.dt_bias[layer, :].partition_broadcast(128), writes=[bDTB])
            P.dma("sp", NEGA[:], self.a_log[layer, :].partition_broadcast(128), writes=[bNEGA])
            P.op("act", lambda e: e.activation(out=NEGA[:], in_=NEGA[:], func=AF.Exp), reads=[bNEGA], writes=[bNEGA])
            P.op("dve", lambda e: e.tensor_scalar(NEGA[:], NEGA[:], -1.0, None, ALU.mult), reads=[bNEGA], writes=[bNEGA])
            CAR = ph.sb([128, 24, 3], F32)
            ZERO3 = ph.sb([128, 3], F32)
            bZ3 = Buf()
            P.op("pool", lambda e: e.memset(ZERO3[:], 0.0), writes=[bZ3])
            bCAR = bufs(24)
            NB = 2
            XSr = Rot([ph.sb([128, D], F32) for _ in range(4)])
            xTr = [ph.sb([128, 8, 512], BF16) for _ in range(NB)]
            bxTr = [bufs(8) for _ in range(NB)]
            QAr = Rot([ph.sb([128, 10, 512], BF16) for _ in range(2)])
            VAr = Rot([ph.sb([128, 4, 256], BF16) for _ in range(2)])
            BGr = Rot([ph.sb([128, 4, 16], F32) for _ in range(2)])
            TMP = Rot([ph.sb([128, 56], F32) for _ in range(2)])
            Ur = Rot([ph.sb([128, 515], F32) for _ in range(3)])
            ACr = Rot([ph.sb([128, 512], F32) for _ in range(3)])
            Y8 = ph.sb([128, 8, 512], F32)
            SQ8 = ph.sb([128, 8, 512], BF16)
            bY8, bSQ8 = bufs(8), bufs(8)
            RS8 = ph.sb([128, 8, 512], F32)
            bRS8 = bufs(8)
            OCr = Rot([ph.sb([128, 8, 512], BF16) for _ in range(2)])
            pT = Rot([ph.ps() for _ in range(2)])
            pA = Rot([ph.ps() for _ in range(3)])
            pB = Rot([ph.ps() for _ in range(1)])
            pS = Rot([ph.ps() for _ in range(2)])
            ntiles = T // 512
            xs_next = self.issue_x(src, 0, XSr)
            self.transpose_x(xs_next, xTr[0], bxTr[0], pT, ident, bid)
            for ti in range(ntiles):
                t0 = ti * 512
                seq_start = (t0 % L == 0)
                xT, bxT = xTr[ti % NB], bxTr[ti % NB]
                if ti + 1 < ntiles:
                    xs_next = self.issue_x(src, t0 + 512, XSr)
                QA, bQA = QAr.next()
                for c in range(10):
                    pa, bpa = pA.next()
                    for kc in range(8):
                        P.op("pe", lambda e, pa=pa, kc=kc, c=c, xT=xT: e.matmul(
                            pa[:], W[:, kc, c * 128:(c + 1) * 128], xT[:, kc, :], start=(kc == 0), stop=(kc == 7)),
                            reads=[bW[kc], bxT[kc]], writes=[bpa])
                    sc = 0.125 if c < 8 else 1.0
                    P.op("act", lambda e, pa=pa, c=c, QA=QA, sc=sc: e.activation(out=QA[:, c, :], in_=pa[:], func=AF.Copy, scale=sc),
                         reads=[bpa], writes=[bQA])
                P.dma("pool", self.QT[:, t0:t0 + 512].rearrange("(c p) t -> p c t", p=128), QA[:, 0:8, :], reads=[bQA])
                P.dma("pool", self.KT[:, t0:t0 + 512].rearrange("(c p) t -> p c t", p=128), QA[:, 8:10, :], reads=[bQA])
                VAt, bVA = VAr.next()
                BGt, bBG = BGr.next()
                for s in range(4):
                    pb, bpb = pB.next()
                    for kc in range(8):
                        P.op("pe", lambda e, pb=pb, kc=kc, s=s, xT=xT: e.matmul(
                            pb[:, 0:256], xT[:, kc, s * 128:(s + 1) * 128], W[:, kc, C_VA:C_VA + 256],
                            start=(kc == 0), stop=(kc == 7)), reads=[bW[kc], bxT[kc]], writes=[bpb])
                    for kc in range(8):
                        P.op("pe", lambda e, pb=pb, kc=kc, s=s, xT=xT: e.matmul(
                            pb[:, 256:272], xT[:, kc, s * 128:(s + 1) * 128], W[:, kc, C_BETA:C_BETA + 16],
                            start=(kc == 0), stop=(kc == 7)), reads=[bW[kc], bxT[kc]], writes=[bpb])
                    P.op("act", lambda e, pb=pb, s=s, VAt=VAt: e.copy(VAt[:, s, :], pb[:, 0:256]), reads=[bpb], writes=[bVA])
                    tm, btm = TMP.next()
                    P.op("act", lambda e, pb=pb, tm=tm: e.copy(tm[:, 40:56], pb[:, 256:272]), reads=[bpb], writes=[btm])
                    P.op("act", lambda e, tm=tm, s=s, BGt=BGt: e.activation(out=BGt[:, s, 0:8], in_=tm[:, 40:48], func=AF.Exp, scale=-1.0),
                         reads=[btm], writes=[bBG])
                    P.op("dve", lambda e, s=s, BGt=BGt: e.tensor_scalar(BGt[:, s, 0:8], BGt[:, s, 0:8], 1.0, None, ALU.add),
                         reads=[bBG], writes=[bBG])
                    P.op("dve", lambda e, s=s, BGt=BGt: e.reciprocal(BGt[:, s, 0:8], BGt[:, s, 0:8]), reads=[bBG], writes=[bBG])
                    P.op("dve", lambda e, tm=tm: e.tensor_tensor(out=tm[:, 0:8], in0=tm[:, 48:56], in1=DTB[:], op=ALU.add),
                         reads=[btm, bDTB], writes=[btm])
                    P.op("dve", lambda e, tm=tm: e.tensor_scalar(tm[:, 8:16], tm[:, 0:8], -1.0, None, ALU.mult),
                         reads=[btm], writes=[btm])
                    P.op("dve", lambda e, tm=tm: e.tensor_tensor(out=tm[:, 8:16], in0=tm[:, 8:16], in1=tm[:, 0:8], op=ALU.max),
                         reads=[btm], writes=[btm])
                    P.op("act", lambda e, tm=tm: e.activation(out=tm[:, 16:24], in_=tm[:, 8:16], func=AF.Exp, scale=-1.0),
                         reads=[btm], writes=[btm])
                    P.op("dve", lambda e, tm=tm: e.tensor_scalar(tm[:, 16:24], tm[:, 16:24], 1.0, None, ALU.add), reads=[btm], writes=[btm])
                    P.op("act", lambda e, tm=tm: e.activation(out=tm[:, 24:32], in_=tm[:, 16:24], func=AF.Ln),
                         reads=[btm], writes=[btm])
                    P.op("dve", lambda e, tm=tm: e.scalar_tensor_tensor(out=tm[:, 32:40], in0=tm[:, 0:8], scalar=0.0,
                                                                        in1=tm[:, 24:32], op0=ALU.max, op1=ALU.add),
                         reads=[btm], writes=[btm])
                    P.op("dve", lambda e, tm=tm, s=s, BGt=BGt: e.tensor_tensor(out=BGt[:, s, 8:16], in0=tm[:, 32:40], in1=NEGA[:], op=ALU.mult),
                         reads=[btm, bNEGA], writes=[bBG])
                P.dma("pool", self.VA[t0:t0 + 512, :].rearrange("(s p) f -> p s f", p=128), VAt[:], reads=[bVA])
                P.dma("pool", self.BG[t0:t0 + 512, :].rearrange("(s p) f -> p s f", p=128), BGt[:], reads=[bBG])
                for grp in range(3):
                    OC, bOC = OCr.next()
                    for cc in range(8):
                        c = grp * 8 + cc
                        pa, bpa = pA.next()
                        col = C_QKVB + c * 128
                        for kc in range(8):
                            P.op("pe", lambda e, pa=pa, kc=kc, col=col, xT=xT: e.matmul(
                                pa[:], W[:, kc, col:col + 128], xT[:, kc, :], start=(kc == 0), stop=(kc == 7)),
                                reads=[bW[kc], bxT[kc]], writes=[bpa])
                        U, bU = Ur.next()
                        if seq_start:
                            P.op("act", lambda e, U=U: e.copy(U[:, 0:3], ZERO3[:]), reads=[bZ3], writes=[bU])
                        else:
                            P.op("act", lambda e, U=U, c=c: e.copy(U[:, 0:3], CAR[:, c, :]), reads=[bCAR[c]], writes=[bU])
                        P.op("act", lambda e, U=U, pa=pa: e.copy(U[:, 3:515], pa[:]), reads=[bpa], writes=[bU])
                        P.op("act", lambda e, U=U, c=c: e.copy(CAR[:, c, :], U[:, 512:515]), reads=[bU], writes=[bCAR[c]])
                        A1, bA1 = ACr.next()
                        P.op("act", lambda e, U=U, A1=A1, c=c: e.activation(out=A1[:], in_=U[:, 0:512], func=AF.Identity, scale=CW[:, 0, c:c + 1]),
                             reads=[bU, bCW], writes=[bA1])
                        P.op("act", lambda e, U=U, cc=cc, c=c: e.activation(out=Y8[:, cc, :], in_=U[:, 2:514], func=AF.Identity, scale=CW[:, 2, c:c + 1]),
                             reads=[bU, bCW], writes=[bY8[cc]])
                        P.op("dve", lambda e, U=U, A1=A1, c=c: e.scalar_tensor_tensor(out=A1[:], in0=U[:, 1:513], scalar=CW[:, 1, c:c + 1],
                                                                                    in1=A1[:], op0=ALU.mult, op1=ALU.add),
                             reads=[bU, bCW, bA1], writes=[bA1])
                        P.op("dve", lambda e, U=U, cc=cc, c=c: e.scalar_tensor_tensor(out=Y8[:, cc, :], in0=U[:, 3:515], scalar=CW[:, 3, c:c + 1],
                                                                                    in1=Y8[:, cc, :], op0=ALU.mult, op1=ALU.add),
                             reads=[bU, bCW, bY8[cc]], writes=[bY8[cc]])
                        P.op("pool", lambda e, A1=A1, cc=cc: e.tensor_tensor(out=Y8[:, cc, :], in0=A1[:], in1=Y8[:, cc, :], op=ALU.add),
                             reads=[bA1, bY8[cc]], writes=[bY8[cc]])
                    for cc in range(8):
                        if grp == 2:
                            P.op("act", lambda e, cc=cc, OC=OC: e.activation(out=OC[:, cc, :], in_=Y8[:, cc, :], func=AF.Silu), reads=[bY8[cc]], writes=[bOC])
                        else:
                            P.op("act", lambda e, cc=cc: e.activation(out=Y8[:, cc, :], in_=Y8[:, cc, :], func=AF.Silu), reads=[bY8[cc]], writes=[bY8[cc]])
                    if grp < 2:
                        for cc in range(8):
                            P.op("dve", lambda e, cc=cc: e.tensor_tensor(out=SQ8[:, cc, :], in0=Y8[:, cc, :], in1=Y8[:, cc, :], op=ALU.mult),
                                 reads=[bY8[cc]], writes=[bSQ8[cc]])
                    if grp < 2:
                        for cc in range(8):
                            ps_, bps = pS.next()
                            P.op("pe", lambda e, ps_=ps_, cc=cc: e.matmul(ps_[:], ones[:], SQ8[:, cc, :], start=True, stop=True),
                                 reads=[bones, bSQ8[cc]], writes=[bps])
                            P.op("dve", lambda e, ps_=ps_, cc=cc: e.tensor_scalar(RS8[:, cc, :], ps_[:], float(NORM_EPS), None, ALU.add),
                                 reads=[bps], writes=[bRS8[cc]])
                        for cc in range(8):
                            P.op("act", lambda e, cc=cc: e.activation(out=RS8[:, cc, :], in_=RS8[:, cc, :], func=AF.Ln), reads=[bRS8[cc]], writes=[bRS8[cc]])
                        for cc in range(8):
                            P.op("act", lambda e, cc=cc: e.activation(out=RS8[:, cc, :], in_=RS8[:, cc, :], func=AF.Exp, scale=-0.5),
                                 reads=[bRS8[cc]], writes=[bRS8[cc]])
                        qs = (128.0 ** -0.5) if grp == 0 else 1.0
                        for cc in range(8):
                            P.op("dve", lambda e, OC=OC, cc=cc, qs=qs: e.scalar_tensor_tensor(
                                out=OC[:, cc, :], in0=Y8[:, cc, :], scalar=float(qs), in1=RS8[:, cc, :], op0=ALU.mult, op1=ALU.mult),
                                reads=[bY8[cc], bRS8[cc]], writes=[bOC])
                    P.dma("pool", self.QKVB[grp * 1024:(grp + 1) * 1024, t0:t0 + 512].rearrange("(c p) t -> p c t", p=128),
                          OC[:], reads=[bOC])
                if ti + 1 < ntiles:
                    self.transpose_x(xs_next, xTr[(ti + 1) % NB], bxTr[(ti + 1) % NB], pT, ident, bid)

    def m2_phase(self, layer):
        P = self.P
        T, L = self.T, self.L
        with self.phase() as ph:
            EB = ph.sb([128, 2 * 16 * 128], BF16)
            bEB = Buf()
            identF2, bidF2 = self.make_ident(ph, F32)
            identB2 = ph.sb([128, 128], BF16)
            bidB2 = Buf()
            P.op("pool", lambda e: e.tensor_copy(identB2[:], identF2[:]), reads=[bidF2], writes=[bidB2])
            for q4 in range(4):
                sl = slice(q4 * 1024, (q4 + 1) * 1024)
                stg, bstg = Buf(), None
                STG = ph.sb([128, 1024], F32)
                bSTG = Buf()
                P.dma("sp", STG[:], self.pbias[:, sl], writes=[bSTG])
                P.op("dve", lambda e, STG=STG, sl=sl: e.tensor_copy(EB[:, sl], STG[:]), reads=[bSTG], writes=[bEB])
            SK = ph.sb([1, 16], F32)
            SKR = ph.sb([1, 16, 128], BF16)
            ONE1 = ph.sb([1, 128], F32)
            bSK, bSKR = Buf(), Buf()
            P.dma("sp", SK[:], self.sinks[layer:layer + 1, :], writes=[bSK])
            P.op("act", lambda e: e.activation(out=SK[:], in_=SK[:], func=AF.Exp), reads=[bSK], writes=[bSK])
            P.op("dve", lambda e: e.memset(ONE1[:], 1.0), writes=[bSKR])
            for h in range(16):
                P.op("dve", lambda e, h=h: e.tensor_scalar(SKR[0:1, h, :], ONE1[0:1, :], SK[0:1, h:h + 1], None, ALU.mult),
                     reads=[bSK, bSKR], writes=[bSKR])
            ones = ph.sb([128, 64], BF16)
            bones = Buf()
            P.op("pool", lambda e: e.memset(ones[:], 1.0), writes=[bones])
            Qr = Rot([ph.sb([64, 16, 512], BF16) for _ in range(2)])
            Kr = Rot([ph.sb([64, 4, 640], BF16) for _ in range(2)])
            Vr = Rot([ph.sb([128, 5, 256], BF16) for _ in range(2)])
            ATr = Rot([ph.sb([64, 16, 512], BF16) for _ in range(2)])
            Pr = Rot([ph.sb([128, 512], BF16) for _ in range(4)])
            Dr = Rot([ph.sb([64, 512], F32) for _ in range(2)])
            pSr = Rot([ph.ps() for _ in range(4)])
            pOr = Rot([ph.ps() for _ in range(2)])
            pDr = Rot([ph.ps() for _ in range(2)])
            EBv = EB[:].rearrange("p (b h q) -> p b h q", b=2, h=16)
            def m2_loads(t0):
                Qt, bQ = Qr.next()
                Kt, bK = Kr.next()
                Vt, bV = Vr.next()
                P.dma("sp", Qt[:], self.QT[:, t0:t0 + 512].rearrange("(h d) t -> d h t", d=64), writes=[bQ])
                if t0 % L == 0:
                    P.dma("sp", Kt[:, :, 128:640], self.KT[:, t0:t0 + 512].rearrange("(g d) t -> d g t", d=64), writes=[bK])
                    P.dma("sp", Vt[:, 1:5, :], self.VA[t0:t0 + 512, :].rearrange("(b p) c -> p b c", p=128), writes=[bV])
                else:
                    P.dma("sp", Kt[:], self.KT[:, t0 - 128:t0 + 512].rearrange("(g d) t -> d g t", d=64), writes=[bK])
                    P.dma("sp", Vt[:], self.VA[t0 - 128:t0 + 512, :].rearrange("(b p) c -> p b c", p=128), writes=[bV])
                return Qt, bQ, Kt, bK, Vt, bV

            nxt = m2_loads(0)
            for ti in range(T // 512):
                t0 = ti * 512
                seq_start = (t0 % L == 0)
                Qt, bQ, Kt, bK, Vt, bV = nxt
                if ti + 1 < T // 512:
                    nxt = m2_loads(t0 + 512)
                At, bA = ATr.next()
                for i in range(4):
                    first = seq_start and i == 0
                    sbl = [1] if first else [0, 1]
                    for g in range(4):
                        Pb = {}
                        for sb_ in sbl:
                            ps_, bps = pSr.next()
                            P.op("pe", lambda e, ps_=ps_, g=g, i=i, sb_=sb_, Kt=Kt, Qt=Qt: e.matmul(
                                ps_[:], Kt[:, g, (i + sb_) * 128:(i + sb_ + 1) * 128], Qt[:, 4 * g:4 * g + 4, i * 128:(i + 1) * 128],
                                start=True, stop=False), reads=[bK, bQ], writes=[bps])
                            P.op("pe", lambda e, ps_=ps_, g=g, sb_=sb_: e.matmul(
                                ps_[:], identB2[:], EBv[:, sb_, 4 * g:4 * g + 4, :], start=False, stop=True),
                                reads=[bidB2, bEB], writes=[bps])
                            Pt, bP = Pr.next()
                            P.op("act", lambda e, Pt=Pt, ps_=ps_: e.activation(out=Pt[:], in_=ps_[:], func=AF.Exp),
                                 reads=[bps], writes=[bP])
                            Pb[sb_] = (Pt, bP)
                        po, bpo = pOr.next()
                        pd, bpd = pDr.next()
                        for n_, sb_ in enumerate(sbl):
                            Pt, bP = Pb[sb_]
                            P.op("pe", lambda e, po=po, Pt=Pt, sb_=sb_, g=g, i=i, Vt=Vt, n_=n_: e.matmul(
                                po[0:64, :], Vt[:, i + sb_, g * 64:(g + 1) * 64], Pt[:], start=(n_ == 0), stop=(n_ == len(sbl) - 1)),
                                reads=[bV, bP], writes=[bpo])
                        for n_, sb_ in enumerate(sbl):
                            Pt, bP = Pb[sb_]
                            P.op("pe", lambda e, pd=pd, Pt=Pt, n_=n_: e.matmul(
                                pd[0:64, :], ones[:], Pt[:], start=(n_ == 0), stop=False),
                                reads=[bones, bP], writes=[bpd])
                        P.op("pe", lambda e, pd=pd, g=g: e.matmul(
                            pd[0:64, :], ones[0:1, :], SKR[0:1, 4 * g:4 * g + 4, :], start=False, stop=True),
                            reads=[bones, bSKR], writes=[bpd])
                        Dt, bD = Dr.next()
                        P.op("act", lambda e, Dt=Dt, pd=pd: e.activation(out=Dt[:], in_=pd[0:64, :], func=AF.Ln), reads=[bpd], writes=[bD])
                        P.op("act", lambda e, Dt=Dt: e.activation(out=Dt[:], in_=Dt[:], func=AF.Exp, scale=-1.0), reads=[bD], writes=[bD])
                        P.op("dve", lambda e, Dt=Dt, po=po, At=At, g=g, i=i: e.tensor_tensor(
                            out=At[:, 4 * g:4 * g + 4, i * 128:(i + 1) * 128], in0=po[0:64, :].rearrange("p (h q) -> p h q", h=4),
                            in1=Dt[:].rearrange("p (h q) -> p h q", h=4), op=ALU.mult), reads=[bpo, bD], writes=[bA])
                P.dma("pool", self.AT[:, t0:t0 + 512].rearrange("(h d) t -> d h t", d=64), At[:], reads=[bA])

    def m3_phase(self, layer, src):
        P = self.P
        T, L = self.T, self.L
        SD = SOLVE_DT
        with self.phase() as ph:
            WZ = ph.sb([128, 8, 1024], BF16)
            bWZ = bufs(8)
            self.load_w(WZ, bWZ, self.w_in[layer], 8, C_Z, C_Z + 1024, piece=1024)
            identF, bidF = self.make_ident(ph, F32)
            identB = ph.sb([128, 128], BF16)
            bidB = Buf()
            P.op("pool", lambda e: e.tensor_copy(identB[:], identF[:]), reads=[bidF], writes=[bidB])
            if SD == BF16:
                identS, bidS = identB, bidB
            else:
                identS, bidS = identF, bidF
            bC = Buf()

            def mk(shape=(128, 128)):
                return ph.sb(list(shape), F32)

            TRI, LGT, MSU, ONES, SELA, SELB, SAME = mk(), mk(), mk(), mk(), mk(), mk(), mk()
            P.op("pool", lambda e: e.memset(TRI[:], 1.0), writes=[bC])
            P.op("pool", lambda e: e.affine_select(out=TRI[:], in_=TRI[:], pattern=[[1, 128]], compare_op=ALU.is_ge,
                                                   fill=0.0, base=0, channel_multiplier=-1), reads=[bC], writes=[bC])
            P.op("pool", lambda e: e.memset(TRI[0:64, 64:128], 0.0), reads=[bC], writes=[bC])
            P.op("pool", lambda e: e.memset(LGT[:], 1.0), reads=[bC], writes=[bC])
            P.op("pool", lambda e: e.affine_select(out=LGT[:], in_=LGT[:], pattern=[[-1, 128]], compare_op=ALU.is_gt,
                                                   fill=0.0, base=0, channel_multiplier=1), reads=[bC], writes=[bC])
            P.op("pool", lambda e: e.memset(LGT[64:128, 0:64], 0.0), reads=[bC], writes=[bC])
            P.op("pool", lambda e: e.memset(MSU[:], 1.0), reads=[bC], writes=[bC])
            P.op("pool", lambda e: e.affine_select(out=MSU[:], in_=MSU[:], pattern=[[1, 128]], compare_op=ALU.is_gt,
                                                   fill=0.0, base=0, channel_multiplier=-1), reads=[bC], writes=[bC])
            P.op("pool", lambda e: e.memset(MSU[0:64, 64:128], 0.0), reads=[bC], writes=[bC])
            P.op("pool", lambda e: e.memset(ONES[:], 1.0), reads=[bC], writes=[bC])
            P.op("pool", lambda e: e.memset(SELA[:], 0.0), reads=[bC], writes=[bC])
            P.op("pool", lambda e: e.memset(SELA[0:64, :], 1.0), reads=[bC], writes=[bC])
            P.op("pool", lambda e: e.memset(SELB[:], 0.0), reads=[bC], writes=[bC])
            P.op("pool", lambda e: e.memset(SELB[64:128, :], 1.0), reads=[bC], writes=[bC])
            P.op("pool", lambda e: e.memset(SAME[:], 0.0), reads=[bC], writes=[bC])
            P.op("pool", lambda e: e.memset(SAME[0:64, 0:64], 1.0), reads=[bC], writes=[bC])
            P.op("pool", lambda e: e.memset(SAME[64:128, 64:128], 1.0), reads=[bC], writes=[bC])
            NEGM8 = mk((128, 8, 128))
            MSU8 = mk((128, 8, 128))
            ID8 = ph.sb([128, 8, 128], SD)
            for h in range(8):
                P.op("pool", lambda e, h=h: e.tensor_copy(NEGM8[:, h, :], TRI[:]), reads=[bC], writes=[bC])
                P.op("pool", lambda e, h=h: e.tensor_copy(MSU8[:, h, :], MSU[:]), reads=[bC], writes=[bC])
                P.op("pool", lambda e, h=h: e.tensor_copy(ID8[:, h, :], identF[:]), reads=[bC, bidF], writes=[bC])
            DNG = mk((128, 8, 128))
            bDNG = Buf()
            for h in range(8):
                P.dma("sp", DNG[:, h, :], self.dn_g[layer, :].partition_broadcast(128), writes=[bDNG])
            S = ph.sb([128, 8, 128], F32)
            SB = ph.sb([128, 8, 128], BF16)
            bS, bSB = bufs(8), bufs(8)
            VN = ph.sb([128, 8, 128], BF16)
            bVN = bufs(8)
            P.op("pool", lambda e: e.memset(VN[:], 0.0), writes=bVN)
            XSr = Rot([ph.sb([128, D], F32) for _ in range(2)])
            xT1 = ph.sb([128, 8, 512], BF16)
            bxT1 = bufs(8)
            QKVr = Rot([ph.sb([128, 24, 512], BF16) for _ in range(2)])
            BGr = Rot([ph.sb([128, 4, 16], F32) for _ in range(2)])
            OGTr = Rot([ph.sb([128, 8, 512], BF16) for _ in range(2)])
            Rt = ph.sb([128, 8, 128], F32)
            bRt = Buf()
            ET = ph.sb([128, 8, 128], F32)
            ETS = ph.sb([128, 8, 128], F32)
            EGB = ph.sb([128, 8, 128], F32)
            bET, bETS, bEGB = Buf(), Buf(), Buf()
            YR = [ph.sb([128, 8, 256], SD) for _ in range(2)]
            bYR = [bufs(2), bufs(2)]
            Z = [ph.sb([128, 8, 128], SD) for _ in range(2)]
            bZ = [bufs(2), bufs(2)]
            XS = ph.sb([128, 8, 128], BF16)
            bXS = bufs(2)
            KG = ph.sb([128, 8, 128], BF16)
            KTOK = ph.sb([128, 8, 128], BF16)
            bKTOK = Buf()
            Vt = ph.sb([128, 8, 128], BF16)
            bKG, bVt = Buf(), Buf()
            O = ph.sb([128, 8, 128], F32)
            bO = Buf()
            SQ = ph.sb([128, 8, 128], F32)
            OG = ph.sb([128, 8, 128], BF16)
            bSQ, bZG, bOG = Buf(), Buf(), Buf()
            HH = []
            for _ in range(2):
                HH.append((ph.sb([128, 8], F32), Buf(), ph.sb([128, 64], F32), Buf(), ph.sb([128, 8, 128], BF16), Buf(),
                           ph.sb([128, 8, 128], BF16), Buf(), ph.sb([128, 8, 128], BF16), Buf(),
                           ph.sb([128, 8, 128], F32), bufs(8), ph.sb([128, 8, 128], BF16), bufs(2)))
            ZG4 = ph.sb([128, 4, 8, 128], BF16)
            DNGb = DNG
            pF = Rot([ph.ps() for _ in range(6)])
            pH = Rot([ph.ps(BF16) for _ in range(2)])
            YRv = [y[:].rearrange("p h c -> p (h c)") for y in YR]

            def flat(t, g):
                return t[:, 4 * g:4 * g + 4, :]

            def m3_loads(t0):
                QKV, bQKV = QKVr.next()
                for grp in range(3):
                    P.dma("sp", QKV[:, grp * 8:(grp + 1) * 8, :],
                          self.QKVB[grp * 1024:(grp + 1) * 1024, t0:t0 + 512].rearrange("(c p) t -> p c t", p=128), writes=[bQKV])
                BGt, bBG = BGr.next()
                P.dma("sp", BGt[:], self.BG[t0:t0 + 512, :].rearrange("(s p) f -> p s f", p=128), writes=[bBG])
                return QKV, bQKV, BGt, bBG

            def prep(cx):
                tk, s, QKV, bQKV, BGt, bBG, OGT, bOGT = cx["tk"], cx["s"], cx["QKV"], cx["bQKV"], cx["BGt"], cx["bBG"], cx["OGT"], cx["bOGT"]
                NBG, bNBG, SM, bSM, AIT, bAIT, KT2, bKT2, QD, bQD, UB, bUB, WT, bWT = cx["H"]
                beta = BGt[:, s, 0:8]
                graw = BGt[:, s, 8:16]
                P.op("dve", lambda e, beta=beta: e.tensor_scalar(NBG[:], beta, -1.0, None, ALU.mult), reads=[bBG], writes=[bNBG])
                for h in range(8):
                    P.op("dve", lambda e, h=h, graw=graw: e.tensor_scalar(Rt[:, h, :], TRI[:], graw[:, h:h + 1], None, ALU.mult),
                         reads=[bC, bBG], writes=[bRt])
                px, bpx = pF.next()
                P.op("pe", lambda e, px=px, graw=graw: e.matmul(px[:, 0:8], SELA[:], graw, start=True, stop=True),
                     reads=[bC, bBG], writes=[bpx])
                P.op("pe", lambda e, px=px, graw=graw: e.matmul(px[:, 8:16], SELB[:], graw, start=True, stop=True),
                     reads=[bC, bBG], writes=[bpx])
                P.op("pe", lambda e, px=px, graw=graw: e.matmul(px[:, 16:24], TRI[:], graw, start=True, stop=True),
                     reads=[bC, bBG], writes=[bpx])
                P.op("pe", lambda e, px=px, graw=graw: e.matmul(px[:, 24:32], SAME[:], graw, start=True, stop=True),
                     reads=[bC, bBG], writes=[bpx])
                P.op("act", lambda e, px=px: e.activation(out=SM[:, 0:24], in_=px[:, 0:24], func=AF.Exp), reads=[bpx], writes=[bSM])
                P.op("act", lambda e, px=px: e.copy(SM[:, 56:64], px[:, 16:24]), reads=[bpx, bSM], writes=[bSM])
                P.op("dve", lambda e, px=px: e.tensor_tensor(out=SM[:, 32:40], in0=px[:, 24:32], in1=SM[:, 56:64], op=ALU.subtract),
                     reads=[bpx, bSM], writes=[bSM])
                P.op("act", lambda e: e.activation(out=SM[:, 24:32], in_=SM[:, 32:40], func=AF.Exp), reads=[bSM], writes=[bSM])
                for g in range(2):
                    pg, bpg = pF.next()
                    rv = flat(Rt, g)
                    P.op("pe", lambda e, pg=pg, rv=rv: e.matmul(pg[:], LGT[:], rv, start=True, stop=True), reads=[bC, bRt], writes=[bpg])
                    P.op("act", lambda e, pg=pg, g=g: e.activation(out=flat(ET, g), in_=pg[:].rearrange("p (h c) -> p h c", h=4), func=AF.Exp),
                         reads=[bpg], writes=[bET])
                    pg2, bpg2 = pF.next()
                    P.op("pe", lambda e, pg2=pg2, rv=rv: e.matmul(pg2[:], ONES[:], rv, start=True, stop=True), reads=[bC, bRt], writes=[bpg2])
                    P.op("act", lambda e, pg2=pg2, g=g: e.activation(out=flat(EGB, g), in_=pg2[:].rearrange("p (h c) -> p h c", h=4), func=AF.Exp),
                         reads=[bpg2], writes=[bEGB])
                P.op("pool", lambda e: e.tensor_tensor(out=ETS[:], in0=ET[:], in1=MSU8[:], op=ALU.mult), reads=[bET, bC], writes=[bETS])
                P.op("pool", lambda e: e.tensor_tensor(out=ET[:], in0=ET[:], in1=NEGM8[:], op=ALU.mult), reads=[bET, bC], writes=[bET])
                yield
                cur = 0
                for g in range(2):
                    pk, bpk = pF.next()
                    pq, bpq = pF.next()
                    for hh in range(4):
                        h = 4 * g + hh
                        P.op("pe", lambda e, pk=pk, h=h, hh=hh, QKV=QKV, tk=tk: e.matmul(
                            pk[:, hh * 128:(hh + 1) * 128], QKV[:, 8 + h, tk], QKV[:, 8 + h, tk], start=True, stop=True),
                            reads=[bQKV], writes=[bpk])
                        P.op("pe", lambda e, pq=pq, h=h, hh=hh, QKV=QKV, tk=tk: e.matmul(
                            pq[:, hh * 128:(hh + 1) * 128], QKV[:, 8 + h, tk], QKV[:, h, tk], start=True, stop=True),
                            reads=[bQKV], writes=[bpq])
                    for hh in range(4):
                        h = 4 * g + hh
                        P.op("dve", lambda e, pk=pk, h=h, hh=hh: e.scalar_tensor_tensor(
                            out=YR[0][:, h, 0:128], in0=pk[:, hh * 128:(hh + 1) * 128], scalar=NBG[:, h:h + 1],
                            in1=ETS[:, h, :], op0=ALU.mult, op1=ALU.mult), reads=[bpk, bNBG, bETS], writes=[bYR[0][g]])
                    P.op("dve", lambda e, pq=pq, g=g: e.tensor_tensor(out=flat(AIT, g), in0=pq[:].rearrange("p (h c) -> p h c", h=4),
                                                                     in1=flat(ET, g), op=ALU.mult), reads=[bpq, bET], writes=[bAIT])
                yield
                for g in range(2):
                    P.op("pool", lambda e, g=g: e.tensor_tensor(out=YR[1][:, 4 * g:4 * g + 4, 128:256], in0=YR[0][:, 4 * g:4 * g + 4, 0:128],
                                                                in1=flat(ID8, g), op=ALU.add), reads=[bYR[0][g], bC], writes=[bYR[1][g]])
                    if SD == BF16:
                        pz, bpz = pH.next()
                    else:
                        pz, bpz = pF.next()
                    for hh in range(4):
                        h = 4 * g + hh
                        P.op("pe", lambda e, pz=pz, h=h, hh=hh: e.transpose(pz[:, hh * 128:(hh + 1) * 128], YR[0][:, h, 0:128], identS[:]),
                             reads=[bYR[0][g], bidS], writes=[bpz])
                    P.op("act", lambda e, pz=pz, g=g: e.copy(flat(Z[0], g), pz[:, 0:512].rearrange("p (h c) -> p h c", h=4)),
                         reads=[bpz], writes=[bZ[0][g]])
                for k in range(6):
                    yield
                    a, b_ = k % 2, (k + 1) % 2
                    for g in range(2):
                        last = (k == 5)
                        if not last:
                            pz, bpz = pF.next()
                            for hh in range(4):
                                h = 4 * g + hh
                                P.op("pe", lambda e, pz=pz, h=h, hh=hh, a=a: e.matmul(
                                    pz[:, hh * 128:(hh + 1) * 128], YR[a][:, h, 0:128], Z[a][:, h, :], start=True, stop=True),
                                    reads=[bYR[a][g], bZ[a][g]], writes=[bpz])
                            P.op("act", lambda e, pz=pz, g=g, b_=b_: e.copy(flat(Z[b_], g), pz[:].rearrange("p (h c) -> p h c", h=4)),
                                 reads=[bpz], writes=[bZ[b_][g]])
                        if k == 0:
                            py, bpy = pF.next()
                            for hh in range(4):
                                h = 4 * g + hh
                                P.op("pe", lambda e, py=py, h=h, hh=hh: e.matmul(
                                    py[:, hh * 128:(hh + 1) * 128], Z[0][:, h, :], YR[0][:, h, 0:128], start=True, stop=True),
                                    reads=[bYR[0][g], bZ[0][g]], writes=[bpy])
                            P.op("act", lambda e, py=py, g=g: e.copy(YR[1][:, 4 * g:4 * g + 4, 0:128], py[:].rearrange("p (h c) -> p h c", h=4)),
                                 reads=[bpy], writes=[bYR[1][g]])
                        elif not last:
                            for half in range(2):
                                py, bpy = pF.next()
                                for hh in range(2):
                                    h = 4 * g + 2 * half + hh
                                    P.op("pe", lambda e, py=py, h=h, hh=hh, a=a: e.matmul(
                                        py[:, hh * 256:(hh + 1) * 256], Z[a][:, h, :], YR[a][:, h, :], start=True, stop=True),
                                        reads=[bYR[a][g], bZ[a][g]], writes=[bpy])
                                h0 = 4 * g + 2 * half
                                pv = py[:].rearrange("p (h c) -> p h c", h=2)
                                P.op("act", lambda e, pv=pv, h0=h0, b_=b_: e.copy(YR[b_][:, h0:h0 + 2, 0:128], pv[:, :, 0:128]),
                                     reads=[bpy], writes=[bYR[b_][g]])
                                P.op("dve", lambda e, pv=pv, h0=h0, a=a, b_=b_: e.tensor_tensor(
                                    out=YR[b_][:, h0:h0 + 2, 128:256], in0=pv[:, :, 128:256], in1=YR[a][:, h0:h0 + 2, 128:256], op=ALU.add),
                                    reads=[bpy, bYR[a][g]], writes=[bYR[b_][g]])
                        else:
                            py, bpy = pF.next()
                            for hh in range(4):
                                h = 4 * g + hh
                                P.op("pe", lambda e, py=py, h=h, hh=hh, a=a: e.matmul(
                                    py[:, hh * 128:(hh + 1) * 128], Z[a][:, h, :], YR[a][:, h, 128:256], start=True, stop=True),
                                    reads=[bYR[a][g], bZ[a][g]], writes=[bpy])
                            P.op("dve", lambda e, py=py, g=g, a=a: e.tensor_tensor(
                                out=flat(XS, g), in0=py[:].rearrange("p (h c) -> p h c", h=4), in1=YR[a][:, 4 * g:4 * g + 4, 128:256], op=ALU.add),
                                reads=[bpy, bYR[a][g]], writes=[bXS[g]])
                yield
                pkt, bpkt = pH.next()
                for h in range(8 if (M3SUB & 1) else 0):
                    P.op("pe", lambda e, pkt=pkt, h=h, QKV=QKV, tk=tk: e.transpose(pkt[:, h * 128:(h + 1) * 128], QKV[:, 8 + h, tk], identB[:]),
                         reads=[bQKV, bidB], writes=[bpkt])
                P.op("act", lambda e, pkt=pkt: e.copy(KTOK[:].rearrange("p h c -> p (h c)"), pkt[:]), reads=[bpkt], writes=[bKTOK])
                for h in range(8):
                    P.op("dve", lambda e, h=h: e.tensor_scalar(KG[:, h, :], KTOK[:, h, :], SM[:, 16 + h:17 + h], None, ALU.mult),
                         reads=[bKTOK, bSM], writes=[bKG])
                    P.op("act", lambda e, h=h: e.activation(out=KT2[:, h, :], in_=KTOK[:, h, :], func=AF.Identity, scale=SM[:, 24 + h:25 + h]),
                         reads=[bKTOK, bSM], writes=[bKT2])
                pvt, bpvt = pH.next()
                for h in range(8 if (M3SUB & 2) else 0):
                    P.op("pe", lambda e, pvt=pvt, h=h, QKV=QKV, tk=tk: e.transpose(pvt[:, h * 128:(h + 1) * 128], QKV[:, 16 + h, tk], identB[:]),
                         reads=[bQKV, bidB], writes=[bpvt])
                if M3SUB & 2:
                    P.op("act", lambda e, pvt=pvt: e.copy(Vt[:].rearrange("p h c -> p (h c)"), pvt[:]), reads=[bpvt], writes=[bVt])
                if M3SUB & 4:
                    P.op("pool", lambda e, QKV=QKV, tk=tk: e.tensor_tensor(out=QD[:], in0=QKV[:, 0:8, tk], in1=EGB[:], op=ALU.mult),
                         reads=[bQKV, bEGB], writes=[bQD])
                yield
                for g in range(2):
                    pu, bpu = pF.next()
                    pw, bpw = pF.next()
                    for hh in range(4):
                        h = 4 * g + hh
                        P.op("pe", lambda e, pu=pu, h=h, hh=hh: e.matmul(pu[:, hh * 128:(hh + 1) * 128], XS[:, h, :], Vt[:, h, :], start=True, stop=True),
                             reads=[bXS[g], bVt], writes=[bpu])
                        P.op("pe", lambda e, pw=pw, h=h, hh=hh: e.matmul(pw[:, hh * 128:(hh + 1) * 128], KG[:, h, :], XS[:, h, :], start=True, stop=True),
                             reads=[bXS[g], bKG], writes=[bpw])
                    for hh in range(4):
                        h = 4 * g + hh
                        P.op("act", lambda e, pu=pu, h=h, hh=hh, beta=beta: e.activation(out=UB[:, h, :], in_=pu[:, hh * 128:(hh + 1) * 128], func=AF.Identity,
                                                                                      scale=beta[:, h:h + 1]), reads=[bpu, bBG], writes=[bUB[h]])
                    P.op("act", lambda e, pw=pw, g=g: e.copy(flat(WT, g), pw[:].rearrange("p (h c) -> p h c", h=4)), reads=[bpw], writes=[bWT[g]])

            def rec(cx):
                tk, s, QKV, bQKV, BGt, bBG, OGT, bOGT = cx["tk"], cx["s"], cx["QKV"], cx["bQKV"], cx["BGt"], cx["bBG"], cx["OGT"], cx["bOGT"]
                NBG, bNBG, SM, bSM, AIT, bAIT, KT2, bKT2, QD, bQD, UB, bUB, WT, bWT = cx["H"]
                beta = BGt[:, s, 0:8]
                graw = BGt[:, s, 8:16]
                if cx["tile_first"]:
                    for sb4 in range(4):
                        Xs_, bXs_ = XSr.next()
                        P.dma("sp", Xs_[:], src[cx["t0"] + sb4 * 128:cx["t0"] + (sb4 + 1) * 128, :], writes=[bXs_])
                        self.transpose_x([(Xs_, bXs_)], xT1[:, :, sb4 * 128:(sb4 + 1) * 128], bxT1, pF, identF, bidF)
                    for sb4 in range(4):
                        for hf in range(2):
                            pz_, bpz_ = pF.next()
                            for kc in range(8):
                                P.op("pe", lambda e, pz_=pz_, kc=kc, hf=hf, sb4=sb4: e.matmul(pz_[:], xT1[:, kc, sb4 * 128:(sb4 + 1) * 128], WZ[:, kc, hf * 512:(hf + 1) * 512],
                                                                                           start=(kc == 0), stop=(kc == 7)),
                                     reads=[bxT1[kc], bWZ[kc]], writes=[bpz_])
                            P.op("act", lambda e, pz_=pz_, hf=hf, sb4=sb4: e.activation(out=ZG4[:, sb4, 4 * hf:4 * hf + 4, :], in_=pz_[:].rearrange("p (h c) -> p h c", h=4), func=AF.Silu),
                                 reads=[bpz_], writes=[bZG])
                        P.op("pool", lambda e, sb4=sb4: e.tensor_tensor(out=ZG4[:, sb4, :, :], in0=ZG4[:, sb4, :, :], in1=DNGb[:], op=ALU.mult), reads=[bZG, bDNG], writes=[bZG])
                        yield
                if cx["seq_reset"]:
                    P.op("pool", lambda e: e.memset(S[:], 0.0), writes=bS)
                    P.op("pool", lambda e: e.memset(SB[:], 0.0), writes=bSB)
                for ck in range(2):
                    rows = slice(ck * 64, (ck + 1) * 64)
                    for g in range(2):
                        yield
                        pv_, bpv = pF.next()
                        for hh in range(4):
                            h = 4 * g + hh
                            P.op("pe", lambda e, pv_=pv_, h=h, hh=hh: e.matmul(pv_[:, hh * 128:(hh + 1) * 128], WT[:, h, :], SB[:, h, :], start=True, stop=True),
                                 reads=[bWT[g], bSB[h]], writes=[bpv])
                        for hh in range(4):
                            h = 4 * g + hh
                            P.op("dve", lambda e, pv_=pv_, h=h, hh=hh, rows=rows: e.scalar_tensor_tensor(
                                out=VN[rows, h, :], in0=pv_[rows, hh * 128:(hh + 1) * 128], scalar=NBG[rows, h:h + 1], in1=UB[rows, h, :],
                                op0=ALU.mult, op1=ALU.add), reads=[bpv, bNBG, bUB[h]], writes=[bVN[h]])
                        po, bpo = pF.next()
                        for hh in range(4):
                            h = 4 * g + hh
                            P.op("pe", lambda e, po=po, h=h, hh=hh: e.matmul(po[:, hh * 128:(hh + 1) * 128], QD[:, h, :], SB[:, h, :], start=True, stop=False),
                                 reads=[bQD, bSB[h]], writes=[bpo])
                            P.op("pe", lambda e, po=po, h=h, hh=hh: e.matmul(po[:, hh * 128:(hh + 1) * 128], AIT[:, h, :], VN[:, h, :], start=False, stop=True),
                                 reads=[bAIT, bVN[h]], writes=[bpo])
                        P.op("act", lambda e, po=po, g=g, rows=rows: e.copy(O[rows, 4 * g:4 * g + 4, :], po[rows, :].rearrange("p (h c) -> p h c", h=4)),
                             reads=[bpo], writes=[bO])
                        ps_, bps = pF.next()
                        for hh in range(4):
                            h = 4 * g + hh
                            P.op("pe", lambda e, ps_=ps_, h=h, hh=hh, rows=rows: e.matmul(ps_[:, hh * 128:(hh + 1) * 128], KT2[rows, h, :], VN[rows, h, :], start=True, stop=True),
                                 reads=[bKT2, bVN[h]], writes=[bps])
                        for hh in range(4):
                            h = 4 * g + hh
                            P.op("dve", lambda e, ps_=ps_, h=h, hh=hh, ck=ck: e.scalar_tensor_tensor(
                                out=S[:, h, :], in0=S[:, h, :], scalar=SM[:, ck * 8 + h:ck * 8 + h + 1], in1=ps_[:, hh * 128:(hh + 1) * 128],
                                op0=ALU.mult, op1=ALU.add), reads=[bps, bSM, bS[h]], writes=[bS[h]])
                            P.op("act", lambda e, h=h: e.copy(SB[:, h, :], S[:, h, :]), reads=[bS[h]], writes=[bSB[h]])
                yield
                P.op("pool", lambda e: e.tensor_tensor(out=SQ[:], in0=O[:], in1=O[:], op=ALU.mult), reads=[bO], writes=[bSQ])
                P.op("dve", lambda e: e.tensor_reduce(out=SM[:, 40:48], in_=SQ[:], axis=AX.X, op=ALU.add), reads=[bSQ, bSM], writes=[bSM])
                P.op("dve", lambda e: e.tensor_scalar(SM[:, 48:56], SM[:, 40:48], 1.0 / 128.0, float(NORM_EPS), ALU.mult, ALU.add), reads=[bSM], writes=[bSM])
                P.op("act", lambda e: e.activation(out=SM[:, 48:56], in_=SM[:, 48:56], func=AF.Ln), reads=[bSM], writes=[bSM])
                P.op("act", lambda e: e.activation(out=SM[:, 48:56], in_=SM[:, 48:56], func=AF.Exp, scale=-0.5), reads=[bSM], writes=[bSM])
                for h in range(8):
                    P.op("dve", lambda e, h=h: e.scalar_tensor_tensor(out=OG[:, h, :], in0=O[:, h, :], scalar=SM[:, 48 + h:49 + h], in1=ZG4[:, s, h, :],
                                                                      op0=ALU.mult, op1=ALU.mult), reads=[bO, bSM, bZG], writes=[bOG])
                pt_, bpt = pH.next()
                for h in range(8):
                    P.op("pe", lambda e, pt_=pt_, h=h: e.transpose(pt_[:, h * 128:(h + 1) * 128], OG[:, h, :], identB[:]),
                         reads=[bOG, bidB], writes=[bpt])
                P.op("act", lambda e, pt_=pt_, OGT=OGT, tk=tk: e.copy(OGT[:, :, tk], pt_[:].rearrange("p (h c) -> p h c", h=8)),
                     reads=[bpt], writes=[bOGT])
                if cx["tile_last"]:
                    P.dma("pool", self.OGT[:, cx["t0"]:cx["t0"] + 512].rearrange("(c p) t -> p c t", p=128), OGT[:], reads=[bOGT])

            def interleave(g1, g2):
                gens = [g for g in (g1, g2) if g is not None]
                while gens:
                    for g in list(gens):
                        try:
                            next(g)
                        except StopIteration:
                            gens.remove(g)

            ntl = T // 512
            tl = {0: m3_loads(0)}
            prev = None
            for ti in range(ntl):
                t0 = ti * 512
                QKV, bQKV, BGt, bBG = tl.pop(ti)
                if ti + 1 < ntl:
                    tl[ti + 1] = m3_loads(t0 + 512)
                OGT, bOGT = OGTr.next()
                for s in range(4):
                    gb = ti * 4 + s
                    cx = dict(tk=slice(s * 128, (s + 1) * 128), s=s, QKV=QKV, bQKV=bQKV, BGt=BGt, bBG=bBG, OGT=OGT, bOGT=bOGT,
                              H=HH[gb % 2], t0=t0, tile_first=(s == 0), tile_last=(s == 3), seq_reset=(s == 0 and t0 % L == 0))
                    interleave(prep(cx), rec(prev) if prev is not None else None)
                    prev = cx
            interleave(rec(prev), None)

    def m4_phase(self, layer, src, dst):
        P = self.P
        T = self.T
        with self.phase() as ph:
            WA = ph.sb([128, 8, D], BF16)
            WB = ph.sb([128, 8, D], BF16)
            WO = ph.sb([128, 8, D], BF16)
            WG = ph.sb([128, 8, 2 * D], BF16)
            bWA, bWB, bWO, bWG = bufs(8), bufs(8), bufs(8), bufs(8)
            self.load_w(WA, bWA, self.w_a[layer], 8, 0, D, piece=1024)
            self.load_w(WB, bWB, self.w_b[layer], 8, 0, D, piece=1024)
            self.load_w(WG, bWG, self.w_in[layer], 8, C_GATE, C_GATE + 2 * D, piece=1024)
            self.load_w(WO, bWO, self.w_o[layer], 8, 0, D, piece=1024)
            ident, bid = self.make_ident(ph, F32)
            lnc = self.ln_consts(ph, layer, 1)
            NB = 2
            XSr = Rot([ph.sb([128, D], F32) for _ in range(4)])
            XRr = Rot([ph.sb([128, D], F32) for _ in range(2)])
            xTr = [ph.sb([128, 8, 512], BF16) for _ in range(NB)]
            bxTr = [bufs(8) for _ in range(NB)]
            ATr = Rot([ph.sb([128, 8, 512], BF16) for _ in range(2)])
            OGr = Rot([ph.sb([128, 8, 512], BF16) for _ in range(2)])
            MT = ph.sb([128, 8, 512], BF16)
            bMT = bufs(8)
            SGr = Rot([ph.sb([128, 512], F32) for _ in range(4)])
            T1r = Rot([ph.sb([128, 512], F32) for _ in range(2)])
            T2r = Rot([ph.sb([128, 512], F32) for _ in range(2)])
            small = Rot([ph.sb([128, 16], F32) for _ in range(2)])
            pT = Rot([ph.ps() for _ in range(2)])
            pM = Rot([ph.ps() for _ in range(4)])
            pY = [ph.ps() for _ in range(2)]
            bpY = bufs(2)
            def m4_loads(t0):
                xs = self.issue_x(src, t0, XSr)
                At, bA = ATr.next()
                Og, bOg = OGr.next()
                P.dma("sp", At[:], self.AT[:, t0:t0 + 512].rearrange("(c p) t -> p c t", p=128), writes=[bA])
                P.dma("sp", Og[:], self.OGT[:, t0:t0 + 512].rearrange("(c p) t -> p c t", p=128), writes=[bOg])
                return xs, At, bA, Og, bOg

            nxt = m4_loads(0)
            self.transpose_x(nxt[0], xTr[0], bxTr[0], pT, ident, bid)
            for ti in range(T // 512):
                t0 = ti * 512
                xT, bxT = xTr[ti % NB], bxTr[ti % NB]
                _, At, bA, Og, bOg = nxt
                if ti + 1 < T // 512:
                    nxt = m4_loads(t0 + 512)
                for n in range(8):
                    ns = slice(n * 128, (n + 1) * 128)
                    pa, bpa = pM.next()
                    pga, bpga = pM.next()
                    for kc in range(8):
                        P.op("pe", lambda e, pa=pa, kc=kc, ns=ns, At=At: e.matmul(pa[:], WA[:, kc, ns], At[:, kc, :], start=(kc == 0), stop=(kc == 7)),
                             reads=[bWA[kc], bA], writes=[bpa])
                    for kc in range(8):
                        P.op("pe", lambda e, pga=pga, kc=kc, ns=ns, xT=xT: e.matmul(pga[:], WG[:, kc, ns], xT[:, kc, :], start=(kc == 0), stop=(kc == 7)),
                             reads=[bWG[kc], bxT[kc]], writes=[bpga])
                    sga, bsga = SGr.next()
                    P.op("act", lambda e, sga=sga, pga=pga: e.activation(out=sga[:], in_=pga[:], func=AF.Sigmoid), reads=[bpga], writes=[bsga])
                    t1, bt1 = T1r.next()
                    P.op("dve", lambda e, t1=t1, pa=pa, sga=sga: e.tensor_tensor(out=t1[:], in0=pa[:], in1=sga[:], op=ALU.mult),
                         reads=[bpa, bsga], writes=[bt1])
                    pb, bpb = pM.next()
                    pgb, bpgb = pM.next()
                    for kc in range(8):
                        P.op("pe", lambda e, pb=pb, kc=kc, ns=ns, Og=Og: e.matmul(pb[:], WB[:, kc, ns], Og[:, kc, :], start=(kc == 0), stop=(kc == 7)),
                             reads=[bWB[kc], bOg], writes=[bpb])
                    for kc in range(8):
                        P.op("pe", lambda e, pgb=pgb, kc=kc, n=n, xT=xT: e.matmul(pgb[:], WG[:, kc, D + n * 128:D + (n + 1) * 128], xT[:, kc, :],
                                                                               start=(kc == 0), stop=(kc == 7)),
                             reads=[bWG[kc], bxT[kc]], writes=[bpgb])
                    sgb, bsgb = SGr.next()
                    P.op("act", lambda e, sgb=sgb, pgb=pgb: e.activation(out=sgb[:], in_=pgb[:], func=AF.Sigmoid), reads=[bpgb], writes=[bsgb])
                    t2, bt2 = T2r.next()
                    P.op("dve", lambda e, t2=t2, pb=pb, sgb=sgb: e.tensor_tensor(out=t2[:], in0=pb[:], in1=sgb[:], op=ALU.mult),
                         reads=[bpb, bsgb], writes=[bt2])
                    P.op("pool", lambda e, t1=t1, t2=t2, n=n: e.tensor_tensor(out=MT[:, n, :], in0=t1[:], in1=t2[:], op=ALU.add),
                         reads=[bt1, bt2], writes=[bMT[n]])
                if ti + 1 < T // 512:
                    self.transpose_x(nxt[0], xTr[(ti + 1) % NB], bxTr[(ti + 1) % NB], pT, ident, bid)
                for s in range(4):
                    for hf in range(2):
                        for kc in range(8):
                            P.op("pe", lambda e, hf=hf, kc=kc, s=s: e.matmul(pY[hf][:], MT[:, kc, s * 128:(s + 1) * 128], WO[:, kc, hf * 512:(hf + 1) * 512],
                                                                            start=(kc == 0), stop=(kc == 7)),
                                 reads=[bMT[kc], bWO[kc]], writes=[bpY[hf]])
                    self.ln_epilogue(pY, bpY, src, dst, t0 + s * 128, 1.0 / DN_ALPHA, lnc, XRr, small)

    def build(self, upto=99):
        cur = self.x
        n = 0
        for layer in range(self.depth):
            last = (layer == self.depth - 1)
            steps = [
                lambda: self.ffn_phase(layer, 0, cur, self.R[0], 0),
                lambda: self.m1_phase(layer, self.R[0]),
                lambda: self.m2_phase(layer),
                lambda: self.m3_phase(layer, self.R[0]),
                lambda: self.m4_phase(layer, self.R[0], self.R[1]),
                lambda: self.ffn_phase(layer, 1, self.R[1], self.out if last else self.R[0], 2),
            ]
            for st in steps:
                if n < upto:
                    st()
                n += 1
            cur = self.R[0]
        self.top.close()
        return self.nc


def host_pos_tables(rel_bias):
    s = np.arange(128)[:, None]
    q = np.arange(128)[None, :]
    out_b = np.zeros((128, 2, 16, 128), np.float32)
    out_m = np.zeros((128, 2, 16, 128), np.float32)
    for blk in range(2):
        j = s + 128 * blk
        rel = q + 128 - j
        valid = (rel >= 0) & (rel < 128)
        n = np.maximum(rel, 0)
        nf = np.maximum(n, 1).astype(np.float32)
        large = 16 + (np.log(nf / np.float32(16)) / np.float32(np.log(128 / 16)) * np.float32(16)).astype(np.int32)
        large = np.minimum(large, 31)
        bucket = np.where(n < 16, n, large)
        bucket = np.where(valid, bucket, 0)
        g = rel_bias[bucket]
        vm = np.broadcast_to(valid[:, None, :], (128, 16, 128))
        out_b[:, blk] = np.where(vm, np.transpose(g, (0, 2, 1)), np.float32(-1e30))
        out_m[:, blk] = vm
    return out_b.reshape(128, -1), out_m.reshape(128, -1)


_NC_CACHE = {}


def kernel(x, rel_bias, ln_g, ln_b, ffn_w13, ffn_w2, w_in, conv_w, a_log, dt_bias,
           dn_norm_g, sinks, w_branch_a, w_branch_b, w_out):
    x = np.asarray(x, np.float32)
    B, L, _ = x.shape
    nseq = B // NCORES
    key = (nseq, L)
    if key not in _NC_CACHE:
        _NC_CACHE[key] = KB(nseq, L, DEPTH).build()
    nc = _NC_CACHE[key]
    pb, pm = host_pos_tables(np.asarray(rel_bias, np.float32))
    f = lambda a: np.ascontiguousarray(np.asarray(a, np.float32))
    shared = dict(ln_g=f(ln_g), ln_b=f(ln_b), ffn_w13=f(ffn_w13), ffn_w2=f(ffn_w2), w_in=f(w_in), conv_w=f(conv_w),
                  a_log=f(a_log), dt_bias=f(dt_bias), dn_norm_g=f(dn_norm_g), sinks=f(sinks),
                  w_branch_a=f(w_branch_a), w_branch_b=f(w_branch_b), w_out=f(w_out), pbias=pb, pmask=pm)
    in_maps = []
    for c in range(NCORES):
        m = dict(shared)
        m["x"] = np.ascontiguousarray(x[c * nseq:(c + 1) * nseq].reshape(nseq * L, D))
        in_maps.append(m)
    res = run_bass_kernel_spmd(nc, in_maps, core_ids=list(range(NCORES)))
    outs = [np.asarray(r["out"], np.float32).reshape(nseq, L, D) for r in res.results]
    return np.concatenate(outs, axis=0)
```

```python
import contextlib
import numpy as np
import concourse.bass as bass
import concourse.mybir as mybir
from concourse.bass_utils import run_bass_kernel_spmd

F32 = mybir.dt.float32
BF16 = mybir.dt.bfloat16
AF = mybir.ActivationFunctionType
ALU = mybir.AluOpType
AX = mybir.AxisListType

D = 1024
DFF = 2816
NIN = 7696
DEPTH = 4
SEQ = 4096
NCORES = 8
LN_EPS = 1e-5
NORM_EPS = 1e-6
DN_ALPHA = (2 * DEPTH) ** 0.25
C_Q0, C_KA, C_VA, C_QKVB, C_BETA, C_DT, C_Z, C_GATE = 0, 1024, 1280, 1536, 4608, 4616, 4624, 5648
SOLVE_DT = BF16
import os
M3STOP = int(os.environ.get("M3STOP", "99"))
M3SUB = int(os.environ.get("M3SUB", "7"))

NDMASEM = 16
ENGS = ("pe", "act", "dve", "pool", "sp")


class Buf:
    __slots__ = ("lw", "rd")

    def __init__(self):
        self.lw = None
        self.rd = []


def bufs(n):
    return [Buf() for _ in range(n)]


class Prog:
    def __init__(self, nc, stack):
        self.nc = nc
        self.ops = {e: [] for e in ENGS}
        self.cnt = {e: 0 for e in ("pe", "act", "dve", "pool")}
        self.seen = {e: {} for e in ENGS}
        self.dq_n = {"sp": 0, "pool": 0}
        self.esem = {e: stack.enter_context(nc.semaphore("s_" + e)) for e in ("pe", "act", "dve", "pool")}
        self.dsem = {}
        for q in ("sp", "pool"):
            for k in range(NDMASEM):
                self.dsem[(q, k)] = stack.enter_context(nc.semaphore("d_%s%d" % (q, k)))
        self.ninstr = 0

    def _kv(self, tok):
        if tok[0] == "e":
            return ("e", tok[1]), tok[2]
        q, i = tok[1], tok[2]
        return ("d", q, i % NDMASEM), 16 * (i // NDMASEM + 1)

    def _deps(self, eng, reads, writes):
        need = {}

        def add(tok):
            if tok is None:
                return
            if tok[0] == "e" and tok[1] == "pe" and eng == "pe":
                return
            k, v = self._kv(tok)
            if need.get(k, 0) < v:
                need[k] = v

        for b in reads:
            add(b.lw)
        for b in writes:
            add(b.lw)
            for t in b.rd:
                add(t)
        out = []
        s = self.seen[eng]
        for k, v in need.items():
            if s.get(k, 0) < v:
                s[k] = v
                out.append((k, v))
        return out

    def _commit(self, tok, reads, writes):
        for b in reads:
            b.rd.append(tok)
            if len(b.rd) > 32:
                best = {}
                for t in b.rd:
                    k, v = self._kv(t)
                    if k not in best or best[k][0] < v:
                        best[k] = (v, t)
                b.rd = [t for (_, t) in best.values()]
        for b in writes:
            b.lw = tok
            b.rd = []

    def op(self, eng, fn, reads=(), writes=()):
        waits = self._deps(eng, reads, writes)
        self.cnt[eng] += 1
        tok = ("e", eng, self.cnt[eng])
        self.ops[eng].append((waits, fn, None))
        self._commit(tok, reads, writes)

    def dma(self, q, out_ap, in_ap, reads=(), writes=(), slow=False):
        i = self.dq_n[q]
        self.dq_n[q] += 1
        tok = ("d", q, i)
        waits = self._deps(q, reads, writes)
        if i >= NDMASEM:
            k, v = self._kv(("d", q, i - NDMASEM))
            if self.seen[q].get(k, 0) < v:
                self.seen[q][k] = v
                waits.append((k, v))
        self.ops[q].append((waits, (out_ap, in_ap, slow), tok))
        self._commit(tok, reads, writes)

    def barrier(self):
        allk = []
        for e, c in self.cnt.items():
            if c:
                allk.append((("e", e), c))
        for q, n in self.dq_n.items():
            for k in range(min(NDMASEM, n)):
                last = ((n - 1 - k) // NDMASEM) * NDMASEM + k
                allk.append((("d", q, k), 16 * (last // NDMASEM + 1)))
        for e in ENGS:
            s = self.seen[e]
            waits = []
            for k, v in allk:
                if k == ("e", "pe") and e == "pe":
                    continue
                if s.get(k, 0) < v:
                    s[k] = v
                    waits.append((k, v))
            if waits:
                self.ops[e].append((waits, None, None))

    def emit(self):
        nc = self.nc

        def semof(k):
            return self.esem[k[1]] if k[0] == "e" else self.dsem[(k[1], k[2])]

        def run(engname, e):
            for waits, fn, tok in self.ops[engname]:
                for k, v in waits:
                    e.wait_ge(semof(k), v)
                if fn is None:
                    continue
                self.ninstr += 1
                if tok is None:
                    fn(e).then_inc(self.esem[engname], 1)
                else:
                    o, i, slow = fn
                    if slow:
                        ins = e.dma_start(out=o, in_=i, allow_slow_non_contiguous=True)
                    else:
                        ins = e.dma_start(out=o, in_=i)
                    ins.then_inc(self.dsem[(tok[1], tok[2] % NDMASEM)], 16)
            self.ops[engname] = []

        with nc.Block() as block:
            @block.tensor
            def _(e):
                run("pe", e)

            @block.scalar
            def _(e):
                run("act", e)

            @block.vector
            def _(e):
                run("dve", e)

            @block.gpsimd
            def _(e):
                run("pool", e)

            @block.sync
            def _(e):
                run("sp", e)


class Phase:
    def __init__(self, kb):
        self.kb = kb
        self.st = contextlib.ExitStack()
        self.n = 0

    def __enter__(self):
        self.st.__enter__()
        return self

    def __exit__(self, *a):
        self.kb.P.barrier()
        self.kb.P.emit()
        return self.st.__exit__(*a)

    def sb(self, shape, dt):
        self.n += 1
        return self.st.enter_context(self.kb.nc.sbuf_tensor("t%d_%d" % (self.kb.phase_id, self.n), list(shape), dt))

    def ps(self, dt=F32):
        self.n += 1
        cols = 512 if dt == F32 else 1024
        return self.st.enter_context(self.kb.nc.psum_tensor("p%d_%d" % (self.kb.phase_id, self.n), [128, cols], dt))


class Rot:
    def __init__(self, tiles):
        self.t = tiles
        self.b = bufs(len(tiles))
        self.i = 0

    def next(self):
        k = self.i % len(self.t)
        self.i += 1
        return self.t[k], self.b[k]


class KB:
    def __init__(self, nseq, L, depth, debug=False):
        self.nseq, self.L, self.depth = nseq, L, depth
        self.T = nseq * L
        self.debug = debug
        self.nc = bass.Bass("TRN2", target_bir_lowering=False)
        self.top = contextlib.ExitStack()
        self.phase_id = 0
        nc = self.nc
        T = self.T
        ext = lambda n, s: nc.dram_tensor(n, list(s), F32, kind="ExternalInput").ap()
        self.x = ext("x", [T, D])
        self.ln_g = ext("ln_g", [DEPTH, 3, D])
        self.ln_b = ext("ln_b", [DEPTH, 3, D])
        self.w13 = ext("ffn_w13", [DEPTH, 2, D, 2 * DFF])
        self.w2 = ext("ffn_w2", [DEPTH, 2, DFF, D])
        self.w_in = ext("w_in", [DEPTH, D, NIN])
        self.conv_w = ext("conv_w", [DEPTH, 4, 3072])
        self.a_log = ext("a_log", [DEPTH, 8])
        self.dt_bias = ext("dt_bias", [DEPTH, 8])
        self.dn_g = ext("dn_norm_g", [DEPTH, 128])
        self.sinks = ext("sinks", [DEPTH, 16])
        self.w_a = ext("w_branch_a", [DEPTH, D, D])
        self.w_b = ext("w_branch_b", [DEPTH, D, D])
        self.w_o = ext("w_out", [DEPTH, D, D])
        self.pbias = ext("pbias", [128, 2 * 16 * 128])
        self.pmask = ext("pmask", [128, 2 * 16 * 128])
        self.out = nc.dram_tensor("out", [T, D], F32, kind="ExternalOutput").ap()
        kind = "ExternalOutput" if debug else "Internal"
        scr = lambda n, s, dt: nc.dram_tensor(n, list(s), dt, kind=kind).ap()
        self.R = [scr("res%d" % i, [T, D], F32) for i in range(2)]
        self.QT = scr("QT", [1024, T], BF16)
        self.KT = scr("KT", [256, T], BF16)
        self.VA = scr("VA", [T, 256], BF16)
        self.QKVB = scr("QKVB", [3072, T], BF16)
        self.BG = scr("BG", [T, 16], F32)
        self.AT = scr("AT", [1024, T], BF16)
        self.OGT = scr("OGT", [1024, T], BF16)
        self.P = Prog(nc, self.top)

    def phase(self):
        self.phase_id += 1
        return Phase(self)

    def make_ident(self, ph, dt):
        P = self.P
        t = ph.sb([128, 128], dt)
        b = Buf()
        P.op("pool", lambda e: e.memset(t[:], 0.0), writes=[b])
        P.op("pool", lambda e: e.affine_select(out=t[:], in_=t[:], pattern=[[-1, 128]], compare_op=ALU.not_equal,
                                               fill=1.0, base=0, channel_multiplier=1), reads=[b], writes=[b])
        return t, b

    def load_w(self, dst, dbufs, src, kcs, c0, c1, piece=2048):
        v = src.rearrange("(kc p) n -> p kc n", p=128)
        for kc in range(kcs):
            a = c0
            while a < c1:
                b = min(c1, a + piece)
                self.P.dma("pool", dst[:, kc, a - c0:b - c0], v[:, kc, a:b], writes=[dbufs[kc]])
                a = b

    def issue_x(self, src, t0, XS4, nsub=4):
        out = []
        for s in range(nsub):
            Xs, bXs = XS4.next()
            self.P.dma("sp", Xs[:], src[t0 + s * 128:t0 + (s + 1) * 128, :], writes=[bXs])
            out.append((Xs, bXs))
        return out

    def transpose_x(self, xs, xT, bxT, pT, ident, bid):
        P = self.P
        for s, (Xs, bXs) in enumerate(xs):
            for half in range(2):
                pt, bpt = pT.next()
                for k4 in range(4):
                    kc = half * 4 + k4
                    P.op("pe", lambda e, pt=pt, k4=k4, kc=kc, Xs=Xs: e.transpose(pt[:, k4 * 128:(k4 + 1) * 128],
                                                                              Xs[:, kc * 128:(kc + 1) * 128], ident[:]),
                         reads=[bXs, bid], writes=[bpt])
                wb = [bxT[half * 4 + k4] for k4 in range(4)]
                dst = xT[:, half * 4:half * 4 + 4, s * 128:(s + 1) * 128]
                srcv = pt[:].rearrange("p (k c) -> p k c", k=4)
                if half == 0:
                    P.op("act", lambda e, dst=dst, srcv=srcv: e.copy(dst, srcv), reads=[bpt], writes=wb)
                else:
                    P.op("dve", lambda e, dst=dst, srcv=srcv: e.tensor_copy(dst, srcv), reads=[bpt], writes=wb)

    def ln_consts(self, ph, layer, idx):
        P = self.P
        G = ph.sb([128, D], F32)
        B = ph.sb([128, D], F32)
        bg, bb = Buf(), Buf()
        P.dma("sp", G[:], self.ln_g[layer, idx, :].partition_broadcast(128), writes=[bg])
        P.dma("sp", B[:], self.ln_b[layer, idx, :].partition_broadcast(128), writes=[bb])
        return (G, bg, B, bb)

    def ln_epilogue(self, py, bpy, src, dst, r0, c, lnc, XRr, small):
        P = self.P
        G, bg, B, bb = lnc
        Rt, bR = XRr.next()
        st, bst = small.next()
        P.dma("sp", Rt[:], src[r0:r0 + 128, :], writes=[bR])
        for hf in range(2):
            P.op("dve", lambda e, hf=hf, Rt=Rt: e.scalar_tensor_tensor(
                out=Rt[:, hf * 512:(hf + 1) * 512], in0=py[hf][:], scalar=float(c),
                in1=Rt[:, hf * 512:(hf + 1) * 512], op0=ALU.mult, op1=ALU.add),
                reads=[bpy[hf], bR], writes=[bR])
        for hf in range(2):
            P.op("dve", lambda e, hf=hf, Rt=Rt, st=st: e.bn_stats(st[:, hf * 6:(hf + 1) * 6], Rt[:, hf * 512:(hf + 1) * 512]),
                 reads=[bR], writes=[bst])
        P.op("dve", lambda e, st=st: e.bn_aggr(st[:, 12:14], st[:, 0:12]), reads=[bst], writes=[bst])
        eps = LN_EPS / (DN_ALPHA ** 2)
        P.op("dve", lambda e, st=st: e.tensor_scalar(st[:, 13:14], st[:, 13:14], float(eps), None, ALU.add),
             reads=[bst], writes=[bst])
        P.op("act", lambda e, st=st: e.activation(out=st[:, 14:15], in_=st[:, 13:14], func=AF.Ln),
             reads=[bst], writes=[bst])
        P.op("act", lambda e, st=st: e.activation(out=st[:, 14:15], in_=st[:, 14:15], func=AF.Exp, scale=-0.5),
             reads=[bst], writes=[bst])
        P.op("dve", lambda e, st=st: e.scalar_tensor_tensor(out=st[:, 15:16], in0=st[:, 12:13], scalar=-1.0,
                                                            in1=st[:, 14:15], op0=ALU.mult, op1=ALU.mult),
             reads=[bst], writes=[bst])
        P.op("act", lambda e, st=st, Rt=Rt: e.activation(out=Rt[:], in_=Rt[:], func=AF.Identity,
                                                        bias=st[:, 15:16], scale=st[:, 14:15]),
             reads=[bst, bR], writes=[bR])
        P.op("pool", lambda e, Rt=Rt: e.tensor_tensor(out=Rt[:], in0=Rt[:], in1=G[:], op=ALU.mult),
             reads=[bR, bg], writes=[bR])
        P.op("pool", lambda e, Rt=Rt: e.tensor_tensor(out=Rt[:], in0=Rt[:], in1=B[:], op=ALU.add),
             reads=[bR, bb], writes=[bR])
        P.dma("pool", dst[r0:r0 + 128, :], Rt[:], reads=[bR])

    def ffn_phase(self, layer, which, src, dst, ln_idx):
        P = self.P
        T = self.T
        with self.phase() as ph:
            W13 = ph.sb([128, 8, 2 * DFF], BF16)
            W2 = ph.sb([128, 22, D], BF16)
            bW13, bW2 = bufs(8), bufs(22)
            self.load_w(W13, bW13, self.w13[layer, which], 8, 0, 2 * DFF, piece=1408)
            self.load_w(W2, bW2, self.w2[layer, which], 22, 0, D, piece=1024)
            ident, bid = self.make_ident(ph, F32)
            lnc = self.ln_consts(ph, layer, ln_idx)
            NB = 2
            XSr = Rot([ph.sb([128, D], F32) for _ in range(4)])
            XRr = Rot([ph.sb([128, D], F32) for _ in range(2)])
            xTr = [ph.sb([128, 8, 512], BF16) for _ in range(NB)]
            bxTr = [bufs(8) for _ in range(NB)]
            HT = ph.sb([128, 22, 512], BF16)
            bHT = bufs(22)
            SG = Rot([ph.sb([128, 512], F32) for _ in range(2)])
            small = Rot([ph.sb([128, 16], F32) for _ in range(2)])
            pGU = Rot([ph.ps() for _ in range(4)])
            pT = Rot([ph.ps() for _ in range(4)])
            pYi = [0]
            ntiles = T // 512
            xs_next = self.issue_x(src, 0, XSr)
            self.transpose_x(xs_next, xTr[0], bxTr[0], pT, ident, bid)
            for ti in range(ntiles):
                t0 = ti * 512
                xT, bxT = xTr[ti % NB], bxTr[ti % NB]
                if ti + 1 < ntiles:
                    xs_next = self.issue_x(src, t0 + 512, XSr)
                for j in range(22):
                    pg, bpg = pGU.next()
                    pu, bpu = pGU.next()
                    for kc in range(8):
                        P.op("pe", lambda e, pg=pg, kc=kc, j=j, xT=xT: e.matmul(
                            pg[:], W13[:, kc, j * 128:(j + 1) * 128], xT[:, kc, :], start=(kc == 0), stop=(kc == 7)),
                            reads=[bW13[kc], bxT[kc]], writes=[bpg])
                    for kc in range(8):
                        P.op("pe", lambda e, pu=pu, kc=kc, j=j, xT=xT: e.matmul(
                            pu[:], W13[:, kc, DFF + j * 128:DFF + (j + 1) * 128], xT[:, kc, :], start=(kc == 0), stop=(kc == 7)),
                            reads=[bW13[kc], bxT[kc]], writes=[bpu])
                    sg, bsg = SG.next()
                    P.op("act", lambda e, sg=sg, pg=pg: e.activation(out=sg[:], in_=pg[:], func=AF.Silu),
                         reads=[bpg], writes=[bsg])
                    P.op("dve", lambda e, sg=sg, pu=pu, j=j: e.tensor_tensor(out=HT[:, j, :], in0=pu[:], in1=sg[:], op=ALU.mult),
                         reads=[bpu, bsg], writes=[bHT[j]])
                if ti + 1 < ntiles:
                    self.transpose_x(xs_next, xTr[(ti + 1) % NB], bxTr[(ti + 1) % NB], pT, ident, bid)
                for s in range(4):
                    k2 = 2 * (pYi[0] % 2)
                    pYi[0] += 1
                    pY = [pT.t[k2], pT.t[k2 + 1]]
                    bpY = [pT.b[k2], pT.b[k2 + 1]]
                    for hf in range(2):
                        for j in range(22):
                            P.op("pe", lambda e, hf=hf, j=j, s=s, pY=pY: e.matmul(
                                pY[hf][:], HT[:, j, s * 128:(s + 1) * 128], W2[:, j, hf * 512:(hf + 1) * 512],
                                start=(j == 0), stop=(j == 21)),
                                reads=[bHT[j], bW2[j]], writes=[bpY[hf]])
                    self.ln_epilogue(pY, bpY, src, dst, t0 + s * 128, 0.5 / DN_ALPHA, lnc, XRr, small)

    def m1_phase(self, layer, src):
        P = self.P
        T, L = self.T, self.L
        NW = C_Z
        with self.phase() as ph:
            W = ph.sb([128, 8, NW], BF16)
            bW = bufs(8)
            self.load_w(W, bW, self.w_in[layer], 8, 0, NW, piece=1156)
            ident, bid = self.make_ident(ph, F32)
            ones = ph.sb([128, 128], BF16)
            bones = Buf()
            P.op("pool", lambda e: e.memset(ones[:], 1.0), writes=[bones])
            CW = ph.sb([128, 4, 24], F32)
            bCW = Buf()
            for j in range(4):
                P.dma("sp", CW[:, j, :], self.conv_w[layer, j, :].rearrange("(c p) -> p c", p=128), writes=[bCW], slow=True)
            DTB = ph.sb([128, 8], F32)
            NEGA = ph.sb([128, 8], F32)
            bDTB, bNEGA = Buf(), Buf()
            P.dma("sp", DTB[:], self.dt_bias[layer, :].partition_broadcast(128), writes=[bDTB])
            P.dma("sp", NEGA[:], self.a_log[layer, :].partition_broadcast(128), writes=[bNEGA])
            P.op("act", lambda e: e.activation(out=NEGA[:], in_=NEGA[:], func=AF.Exp), reads=[bNEGA], writes=[bNEGA])
            P.op("dve", lambda e: e.tensor_scalar(NEGA[:], NEGA[:], -1.0, None, ALU.mult), reads=[bNEGA], writes=[bNEGA])
            CAR = ph.sb([128, 24, 3], F32)
            ZERO3 = ph.sb([128, 3], F32)
            bZ3 = Buf()
            P.op("pool", lambda e: e.memset(ZERO3[:], 0.0), writes=[bZ3])
            bCAR = bufs(24)
            NB = 2
            XSr = Rot([ph.sb([128, D], F32) for _ in range(4)])
            xTr = [ph.sb([128, 8, 512], BF16) for _ in range(NB)]
            bxTr = [bufs(8) for _ in range(NB)]
            QAr = Rot([ph.sb([128, 10, 512], BF16) for _ in range(2)])
            VAr = Rot([ph.sb([128, 4, 256], BF16) for _ in range(2)])
            BGr = Rot([ph.sb([128, 4, 16], F32) for _ in range(2)])
            TMP = Rot([ph.sb([128, 56], F32) for _ in range(2)])
            Ur = Rot([ph.sb([128, 515], F32) for _ in range(3)])
            ACr = Rot([ph.sb([128, 512], F32) for _ in range(3)])
            Y8 = ph.sb([128, 8, 512], F32)
            SQ8 = ph.sb([128, 8, 512], BF16)
            bY8, bSQ8 = bufs(8), bufs(8)
            RS8 = ph.sb([128, 8, 512], F32)
            bRS8 = bufs(8)
            OCr = Rot([ph.sb([128, 8, 512], BF16) for _ in range(2)])
            pT = Rot([ph.ps() for _ in range(2)])
            pA = Rot([ph.ps() for _ in range(3)])
            pB = Rot([ph.ps() for _ in range(1)])
            pS = Rot([ph.ps() for _ in range(2)])
            ntiles = T // 512
            xs_next = self.issue_x(src, 0, XSr)
            self.transpose_x(xs_next, xTr[0], bxTr[0], pT, ident, bid)
            for ti in range(ntiles):
                t0 = ti * 512
                seq_start = (t0 % L == 0)
                xT, bxT = xTr[ti % NB], bxTr[ti % NB]
                if ti + 1 < ntiles:
                    xs_next = self.issue_x(src, t0 + 512, XSr)
                QA, bQA = QAr.next()
                for c in range(10):
                    pa, bpa = pA.next()
                    for kc in range(8):
                        P.op("pe", lambda e, pa=pa, kc=kc, c=c, xT=xT: e.matmul(
                            pa[:], W[:, kc, c * 128:(c + 1) * 128], xT[:, kc, :], start=(kc == 0), stop=(kc == 7)),
                            reads=[bW[kc], bxT[kc]], writes=[bpa])
                    sc = 0.125 if c < 8 else 1.0
                    P.op("act", lambda e, pa=pa, c=c, QA=QA, sc=sc: e.activation(out=QA[:, c, :], in_=pa[:], func=AF.Copy, scale=sc),
                         reads=[bpa], writes=[bQA])
                P.dma("pool", self.QT[:, t0:t0 + 512].rearrange("(c p) t -> p c t", p=128), QA[:, 0:8, :], reads=[bQA])
                P.dma("pool", self.KT[:, t0:t0 + 512].rearrange("(c p) t -> p c t", p=128), QA[:, 8:10, :], reads=[bQA])
                VAt, bVA = VAr.next()
                BGt, bBG = BGr.next()
                for s in range(4):
                    pb, bpb = pB.next()
                    for kc in range(8):
                        P.op("pe", lambda e, pb=pb, kc=kc, s=s, xT=xT: e.matmul(
                            pb[:, 0:256], xT[:, kc, s * 128:(s + 1) * 128], W[:, kc, C_VA:C_VA + 256],
                            start=(kc == 0), stop=(kc == 7)), reads=[bW[kc], bxT[kc]], writes=[bpb])
                    for kc in range(8):
                        P.op("pe", lambda e, pb=pb, kc=kc, s=s, xT=xT: e.matmul(
                            pb[:, 256:272], xT[:, kc, s * 128:(s + 1) * 128], W[:, kc, C_BETA:C_BETA + 16],
                            start=(kc == 0), stop=(kc == 7)), reads=[bW[kc], bxT[kc]], writes=[bpb])
                    P.op("act", lambda e, pb=pb, s=s, VAt=VAt: e.copy(VAt[:, s, :], pb[:, 0:256]), reads=[bpb], writes=[bVA])
                    tm, btm = TMP.next()
                    P.op("act", lambda e, pb=pb, tm=tm: e.copy(tm[:, 40:56], pb[:, 256:272]), reads=[bpb], writes=[btm])
                    P.op("act", lambda e, tm=tm, s=s, BGt=BGt: e.activation(out=BGt[:, s, 0:8], in_=tm[:, 40:48], func=AF.Exp, scale=-1.0),
                         reads=[btm], writes=[bBG])
                    P.op("dve", lambda e, s=s, BGt=BGt: e.tensor_scalar(BGt[:, s, 0:8], BGt[:, s, 0:8], 1.0, None, ALU.add),
                         reads=[bBG], writes=[bBG])
                    P.op("dve", lambda e, s=s, BGt=BGt: e.reciprocal(BGt[:, s, 0:8], BGt[:, s, 0:8]), reads=[bBG], writes=[bBG])
                    P.op("dve", lambda e, tm=tm: e.tensor_tensor(out=tm[:, 0:8], in0=tm[:, 48:56], in1=DTB[:], op=ALU.add),
                         reads=[btm, bDTB], writes=[btm])
                    P.op("dve", lambda e, tm=tm: e.tensor_scalar(tm[:, 8:16], tm[:, 0:8], -1.0, None, ALU.mult),
                         reads=[btm], writes=[btm])
                    P.op("dve", lambda e, tm=tm: e.tensor_tensor(out=tm[:, 8:16], in0=tm[:, 8:16], in1=tm[:, 0:8], op=ALU.max),
                         reads=[btm], writes=[btm])
                    P.op("act", lambda e, tm=tm: e.activation(out=tm[:, 16:24], in_=tm[:, 8:16], func=AF.Exp, scale=-1.0),
                         reads=[btm], writes=[btm])
                    P.op("dve", lambda e, tm=tm: e.tensor_scalar(tm[:, 16:24], tm[:, 16:24], 1.0, None, ALU.add), reads=[btm], writes=[btm])
                    P.op("act", lambda e, tm=tm: e.activation(out=tm[:, 24:32], in_=tm[:, 16:24], func=AF.Ln),
                         reads=[btm], writes=[btm])
                    P.op("dve", lambda e, tm=tm: e.scalar_tensor_tensor(out=tm[:, 32:40], in0=tm[:, 0:8], scalar=0.0,
                                                                        in1=tm[:, 24:32], op0=ALU.max, op1=ALU.add),
                         reads=[btm], writes=[btm])
                    P.op("dve", lambda e, tm=tm, s=s, BGt=BGt: e.tensor_tensor(out=BGt[:, s, 8:16], in0=tm[:, 32:40], in1=NEGA[:], op=ALU.mult),
                         reads=[btm, bNEGA], writes=[bBG])
                P.dma("pool", self.VA[t0:t0 + 512, :].rearrange("(s p) f -> p s f", p=128), VAt[:], reads=[bVA])
                P.dma("pool", self.BG[t0:t0 + 512, :].rearrange("(s p) f -> p s f", p=128), BGt[:], reads=[bBG])
                for grp in range(3):
                    OC, bOC = OCr.next()
                    for cc in range(8):
                        c = grp * 8 + cc
                        pa, bpa = pA.next()
                        col = C_QKVB + c * 128
                        for kc in range(8):
                            P.op("pe", lambda e, pa=pa, kc=kc, col=col, xT=xT: e.matmul(
                                pa[:], W[:, kc, col:col + 128], xT[:, kc, :], start=(kc == 0), stop=(kc == 7)),
                                reads=[bW[kc], bxT[kc]], writes=[bpa])
                        U, bU = Ur.next()
                        if seq_start:
                            P.op("act", lambda e, U=U: e.copy(U[:, 0:3], ZERO3[:]), reads=[bZ3], writes=[bU])
                        else:
                            P.op("act", lambda e, U=U, c=c: e.copy(U[:, 0:3], CAR[:, c, :]), reads=[bCAR[c]], writes=[bU])
                        P.op("act", lambda e, U=U, pa=pa: e.copy(U[:, 3:515], pa[:]), reads=[bpa], writes=[bU])
                        P.op("act", lambda e, U=U, c=c: e.copy(CAR[:, c, :], U[:, 512:515]), reads=[bU], writes=[bCAR[c]])
                        A1, bA1 = ACr.next()
                        P.op("act", lambda e, U=U, A1=A1, c=c: e.activation(out=A1[:], in_=U[:, 0:512], func=AF.Identity, scale=CW[:, 0, c:c + 1]),
                             reads=[bU, bCW], writes=[bA1])
                        P.op("act", lambda e, U=U, cc=cc, c=c: e.activation(out=Y8[:, cc, :], in_=U[:, 2:514], func=AF.Identity, scale=CW[:, 2, c:c + 1]),
                             reads=[bU, bCW], writes=[bY8[cc]])
                        P.op("dve", lambda e, U=U, A1=A1, c=c: e.scalar_tensor_tensor(out=A1[:], in0=U[:, 1:513], scalar=CW[:, 1, c:c + 1],
                                                                                    in1=A1[:], op0=ALU.mult, op1=ALU.add),
                             reads=[bU, bCW, bA1], writes=[bA1])
                        P.op("dve", lambda e, U=U, cc=cc, c=c: e.scalar_tensor_tensor(out=Y8[:, cc, :], in0=U[:, 3:515], scalar=CW[:, 3, c:c + 1],
                                                                                    in1=Y8[:, cc, :], op0=ALU.mult, op1=ALU.add),
                             reads=[bU, bCW, bY8[cc]], writes=[bY8[cc]])
                        P.op("pool", lambda e, A1=A1, cc=cc: e.tensor_tensor(out=Y8[:, cc, :], in0=A1[:], in1=Y8[:, cc, :], op=ALU.add),
                             reads=[bA1, bY8[cc]], writes=[bY8[cc]])
                    for cc in range(8):
                        if grp == 2:
                            P.op("act", lambda e, cc=cc, OC=OC: e.activation(out=OC[:, cc, :], in_=Y8[:, cc, :], func=AF.Silu), reads=[bY8[cc]], writes=[bOC])
                        else:
                            P.op("act", lambda e, cc=cc: e.activation(out=Y8[:, cc, :], in_=Y8[:, cc, :], func=AF.Silu), reads=[bY8[cc]], writes=[bY8[cc]])
                    if grp < 2:
                        for cc in range(8):
                            P.op("dve", lambda e, cc=cc: e.tensor_tensor(out=SQ8[:, cc, :], in0=Y8[:, cc, :], in1=Y8[:, cc, :], op=ALU.mult),
                                 reads=[bY8[cc]], writes=[bSQ8[cc]])
                    if grp < 2:
                        for cc in range(8):
                            ps_, bps = pS.next()
                            P.op("pe", lambda e, ps_=ps_, cc=cc: e.matmul(ps_[:], ones[:], SQ8[:, cc, :], start=True, stop=True),
                                 reads=[bones, bSQ8[cc]], writes=[bps])
                            P.op("dve", lambda e, ps_=ps_, cc=cc: e.tensor_scalar(RS8[:, cc, :], ps_[:], float(NORM_EPS), None, ALU.add),
                                 reads=[bps], writes=[bRS8[cc]])
                        for cc in range(8):
                            P.op("act", lambda e, cc=cc: e.activation(out=RS8[:, cc, :], in_=RS8[:, cc, :], func=AF.Ln), reads=[bRS8[cc]], writes=[bRS8[cc]])
                        for cc in range(8):
                            P.op("act", lambda e, cc=cc: e.activation(out=RS8[:, cc, :], in_=RS8[:, cc, :], func=AF.Exp, scale=-0.5),
                                 reads=[bRS8[cc]], writes=[bRS8[cc]])
                        qs = (128.0 ** -0.5) if grp == 0 else 1.0
                        for cc in range(8):
                            P.op("dve", lambda e, OC=OC, cc=cc, qs=qs: e.scalar_tensor_tensor(
                                out=OC[:, cc, :], in0=Y8[:, cc, :], scalar=float(qs), in1=RS8[:, cc, :], op0=ALU.mult, op1=ALU.mult),
                                reads=[bY8[cc], bRS8[cc]], writes=[bOC])
                    P.dma("pool", self.QKVB[grp * 1024:(grp + 1) * 1024, t0:t0 + 512].rearrange("(c p) t -> p c t", p=128),
                          OC[:], reads=[bOC])
                if ti + 1 < ntiles:
                    self.transpose_x(xs_next, xTr[(ti + 1) % NB], bxTr[(ti + 1) % NB], pT, ident, bid)

    def m2_phase(self, layer):
        P = self.P
        T, L = self.T, self.L
        with self.phase() as ph:
            EB = ph.sb([128, 2 * 16 * 128], BF16)
            bEB = Buf()
            identF2, bidF2 = self.make_ident(ph, F32)
            identB2 = ph.sb([128, 128], BF16)
            bidB2 = Buf()
            P.op("pool", lambda e: e.tensor_copy(identB2[:], identF2[:]), reads=[bidF2], writes=[bidB2])
            for q4 in range(4):
                sl = slice(q4 * 1024, (q4 + 1) * 1024)
                stg, bstg = Buf(), None
                STG = ph.sb([128, 1024], F32)
                bSTG = Buf()
                P.dma("sp", STG[:], self.pbias[:, sl], writes=[bSTG])
                P.op("dve", lambda e, STG=STG, sl=sl: e.tensor_copy(EB[:, sl], STG[:]), reads=[bSTG], writes=[bEB])
            SK = ph.sb([1, 16], F32)
            SKR = ph.sb([1, 16, 128], BF16)
            ONE1 = ph.sb([1, 128], F32)
            bSK, bSKR = Buf(), Buf()
            P.dma("sp", SK[:], self.sinks[layer:layer + 1, :], writes=[bSK])
            P.op("act", lambda e: e.activation(out=SK[:], in_=SK[:], func=AF.Exp), reads=[bSK], writes=[bSK])
            P.op("dve", lambda e: e.memset(ONE1[:], 1.0), writes=[bSKR])
            for h in range(16):
                P.op("dve", lambda e, h=h: e.tensor_scalar(SKR[0:1, h, :], ONE1[0:1, :], SK[0:1, h:h + 1], None, ALU.mult),
                     reads=[bSK, bSKR], writes=[bSKR])
            ones = ph.sb([128, 64], BF16)
            bones = Buf()
            P.op("pool", lambda e: e.memset(ones[:], 1.0), writes=[bones])
            Qr = Rot([ph.sb([64, 16, 512], BF16) for _ in range(2)])
            Kr = Rot([ph.sb([64, 4, 640], BF16) for _ in range(2)])
            Vr = Rot([ph.sb([128, 5, 256], BF16) for _ in range(2)])
            ATr = Rot([ph.sb([64, 16, 512], BF16) for _ in range(2)])
            Pr = Rot([ph.sb([128, 512], BF16) for _ in range(4)])
            Dr = Rot([ph.sb([64, 512], F32) for _ in range(2)])
            pSr = Rot([ph.ps() for _ in range(4)])
            pOr = Rot([ph.ps() for _ in range(2)])
            pDr = Rot([ph.ps() for _ in range(2)])
            EBv = EB[:].rearrange("p (b h q) -> p b h q", b=2, h=16)
            def m2_loads(t0):
                Qt, bQ = Qr.next()
                Kt, bK = Kr.next()
                Vt, bV = Vr.next()
                P.dma("sp", Qt[:], self.QT[:, t0:t0 + 512].rearrange("(h d) t -> d h t", d=64), writes=[bQ])
                if t0 % L == 0:
                    P.dma("sp", Kt[:, :, 128:640], self.KT[:, t0:t0 + 512].rearrange("(g d) t -> d g t", d=64), writes=[bK])
                    P.dma("sp", Vt[:, 1:5, :], self.VA[t0:t0 + 512, :].rearrange("(b p) c -> p b c", p=128), writes=[bV])
                else:
                    P.dma("sp", Kt[:], self.KT[:, t0 - 128:t0 + 512].rearrange("(g d) t -> d g t", d=64), writes=[bK])
                    P.dma("sp", Vt[:], self.VA[t0 - 128:t0 + 512, :].rearrange("(b p) c -> p b c", p=128), writes=[bV])
                return Qt, bQ, Kt, bK, Vt, bV

            nxt = m2_loads(0)
            for ti in range(T // 512):
                t0 = ti * 512
                seq_start = (t0 % L == 0)
                Qt, bQ, Kt, bK, Vt, bV = nxt
                if ti + 1 < T // 512:
                    nxt = m2_loads(t0 + 512)
                At, bA = ATr.next()
                for i in range(4):
                    first = seq_start and i == 0
                    sbl = [1] if first else [0, 1]
                    for g in range(4):
                        Pb = {}
                        for sb_ in sbl:
                            ps_, bps = pSr.next()
                            P.op("pe", lambda e, ps_=ps_, g=g, i=i, sb_=sb_, Kt=Kt, Qt=Qt: e.matmul(
                                ps_[:], Kt[:, g, (i + sb_) * 128:(i + sb_ + 1) * 128], Qt[:, 4 * g:4 * g + 4, i * 128:(i + 1) * 128],
                                start=True, stop=False), reads=[bK, bQ], writes=[bps])
                            P.op("pe", lambda e, ps_=ps_, g=g, sb_=sb_: e.matmul(
                                ps_[:], identB2[:], EBv[:, sb_, 4 * g:4 * g + 4, :], start=False, stop=True),
                                reads=[bidB2, bEB], writes=[bps])
                            Pt, bP = Pr.next()
                            P.op("act", lambda e, Pt=Pt, ps_=ps_: e.activation(out=Pt[:], in_=ps_[:], func=AF.Exp),
                                 reads=[bps], writes=[bP])
                            Pb[sb_] = (Pt, bP)
                        po, bpo = pOr.next()
                        pd, bpd = pDr.next()
                        for n_, sb_ in enumerate(sbl):
                            Pt, bP = Pb[sb_]
                            P.op("pe", lambda e, po=po, Pt=Pt, sb_=sb_, g=g, i=i, Vt=Vt, n_=n_: e.matmul(
                                po[0:64, :], Vt[:, i + sb_, g * 64:(g + 1) * 64], Pt[:], start=(n_ == 0), stop=(n_ == len(sbl) - 1)),
                                reads=[bV, bP], writes=[bpo])
                        for n_, sb_ in enumerate(sbl):
                            Pt, bP = Pb[sb_]
                            P.op("pe", lambda e, pd=pd, Pt=Pt, n_=n_: e.matmul(
                                pd[0:64, :], ones[:], Pt[:], start=(n_ == 0), stop=False),
                                reads=[bones, bP], writes=[bpd])
                        P.op("pe", lambda e, pd=pd, g=g: e.matmul(
                            pd[0:64, :], ones[0:1, :], SKR[0:1, 4 * g:4 * g + 4, :], start=False, stop=True),
                            reads=[bones, bSKR], writes=[bpd])
                        Dt, bD = Dr.next()
                        P.op("act", lambda e, Dt=Dt, pd=pd: e.activation(out=Dt[:], in_=pd[0:64, :], func=AF.Ln), reads=[bpd], writes=[bD])
                        P.op("act", lambda e, Dt=Dt: e.activation(out=Dt[:], in_=Dt[:], func=AF.Exp, scale=-1.0), reads=[bD], writes=[bD])
                        P.op("dve", lambda e, Dt=Dt, po=po, At=At, g=g, i=i: e.tensor_tensor(
                            out=At[:, 4 * g:4 * g + 4, i * 128:(i + 1) * 128], in0=po[0:64, :].rearrange("p (h q) -> p h q", h=4),
                            in1=Dt[:].rearrange("p (h q) -> p h q", h=4), op=ALU.mult), reads=[bpo, bD], writes=[bA])
                P.dma("pool", self.AT[:, t0:t0 + 512].rearrange("(h d) t -> d h t", d=64), At[:], reads=[bA])

    def m3_phase(self, layer, src):
        P = self.P
        T, L = self.T, self.L
        SD = SOLVE_DT
        with self.phase() as ph:
            WZ = ph.sb([128, 8, 1024], BF16)
            bWZ = bufs(8)
            self.load_w(WZ, bWZ, self.w_in[layer], 8, C_Z, C_Z + 1024, piece=1024)
            identF, bidF = self.make_ident(ph, F32)
            identB = ph.sb([128, 128], BF16)
            bidB = Buf()
            P.op("pool", lambda e: e.tensor_copy(identB[:], identF[:]), reads=[bidF], writes=[bidB])
            if SD == BF16:
                identS, bidS = identB, bidB
            else:
                identS, bidS = identF, bidF
            bC = Buf()

            def mk(shape=(128, 128)):
                return ph.sb(list(shape), F32)

            TRI, LGT, MSU, ONES, SELA, SELB, SAME = mk(), mk(), mk(), mk(), mk(), mk(), mk()
            P.op("pool", lambda e: e.memset(TRI[:], 1.0), writes=[bC])
            P.op("pool", lambda e: e.affine_select(out=TRI[:], in_=TRI[:], pattern=[[1, 128]], compare_op=ALU.is_ge,
                                                   fill=0.0, base=0, channel_multiplier=-1), reads=[bC], writes=[bC])
            P.op("pool", lambda e: e.memset(TRI[0:64, 64:128], 0.0), reads=[bC], writes=[bC])
            P.op("pool", lambda e: e.memset(LGT[:], 1.0), reads=[bC], writes=[bC])
            P.op("pool", lambda e: e.affine_select(out=LGT[:], in_=LGT[:], pattern=[[-1, 128]], compare_op=ALU.is_gt,
                                                   fill=0.0, base=0, channel_multiplier=1), reads=[bC], writes=[bC])
            P.op("pool", lambda e: e.memset(LGT[64:128, 0:64], 0.0), reads=[bC], writes=[bC])
            P.op("pool", lambda e: e.memset(MSU[:], 1.0), reads=[bC], writes=[bC])
            P.op("pool", lambda e: e.affine_select(out=MSU[:], in_=MSU[:], pattern=[[1, 128]], compare_op=ALU.is_gt,
                                                   fill=0.0, base=0, channel_multiplier=-1), reads=[bC], writes=[bC])
            P.op("pool", lambda e: e.memset(MSU[0:64, 64:128], 0.0), reads=[bC], writes=[bC])
            P.op("pool", lambda e: e.memset(ONES[:], 1.0), reads=[bC], writes=[bC])
            P.op("pool", lambda e: e.memset(SELA[:], 0.0), reads=[bC], writes=[bC])
            P.op("pool", lambda e: e.memset(SELA[0:64, :], 1.0), reads=[bC], writes=[bC])
            P.op("pool", lambda e: e.memset(SELB[:], 0.0), reads=[bC], writes=[bC])
            P.op("pool", lambda e: e.memset(SELB[64:128, :], 1.0), reads=[bC], writes=[bC])
            P.op("pool", lambda e: e.memset(SAME[:], 0.0), reads=[bC], writes=[bC])
            P.op("pool", lambda e: e.memset(SAME[0:64, 0:64], 1.0), reads=[bC], writes=[bC])
            P.op("pool", lambda e: e.memset(SAME[64:128, 64:128], 1.0), reads=[bC], writes=[bC])
            NEGM8 = mk((128, 8, 128))
            MSU8 = mk((128, 8, 128))
            ID8 = ph.sb([128, 8, 128], SD)
            for h in range(8):
                P.op("pool", lambda e, h=h: e.tensor_copy(NEGM8[:, h, :], TRI[:]), reads=[bC], writes=[bC])
                P.op("pool", lambda e, h=h: e.tensor_copy(MSU8[:, h, :], MSU[:]), reads=[bC], writes=[bC])
                P.op("pool", lambda e, h=h: e.tensor_copy(ID8[:, h, :], identF[:]), reads=[bC, bidF], writes=[bC])
            DNG = mk((128, 8, 128))
            bDNG = Buf()
            for h in range(8):
                P.dma("sp", DNG[:, h, :], self.dn_g[layer, :].partition_broadcast(128), writes=[bDNG])
            S = ph.sb([128, 8, 128], F32)
            SB = ph.sb([128, 8, 128], BF16)
            bS, bSB = bufs(8), bufs(8)
            VN = ph.sb([128, 8, 128], BF16)
            bVN = bufs(8)
            P.op("pool", lambda e: e.memset(VN[:], 0.0), writes=bVN)
            XSr = Rot([ph.sb([128, D], F32) for _ in range(2)])
            xT1 = ph.sb([128, 8, 512], BF16)
            bxT1 = bufs(8)
            QKVr = Rot([ph.sb([128, 24, 512], BF16) for _ in range(2)])
            BGr = Rot([ph.sb([128, 4, 16], F32) for _ in range(2)])
            OGTr = Rot([ph.sb([128, 8, 512], BF16) for _ in range(2)])
            Rt = ph.sb([128, 8, 128], F32)
            bRt = Buf()
            ET = ph.sb([128, 8, 128], F32)
            ETS = ph.sb([128, 8, 128], F32)
            EGB = ph.sb([128, 8, 128], F32)
            bET, bETS, bEGB = Buf(), Buf(), Buf()
            YR = [ph.sb([128, 8, 256], SD) for _ in range(2)]
            bYR = [bufs(2), bufs(2)]
            Z = [ph.sb([128, 8, 128], SD) for _ in range(2)]
            bZ = [bufs(2), bufs(2)]
            XS = ph.sb([128, 8, 128], BF16)
            bXS = bufs(2)
            KG = ph.sb([128, 8, 128], BF16)
            KTOK = ph.sb([128, 8, 128], BF16)
            bKTOK = Buf()
            Vt = ph.sb([128, 8, 128], BF16)
            bKG, bVt = Buf(), Buf()
            O = ph.sb([128, 8, 128], F32)
            bO = Buf()
            SQ = ph.sb([128, 8, 128], F32)
            OG = ph.sb([128, 8, 128], BF16)
            bSQ, bZG, bOG = Buf(), Buf(), Buf()
            HH = []
            for _ in range(2):
                HH.append((ph.sb([128, 8], F32), Buf(), ph.sb([128, 64], F32), Buf(), ph.sb([128, 8, 128], BF16), Buf(),
                           ph.sb([128, 8, 128], BF16), Buf(), ph.sb([128, 8, 128], BF16), Buf(),
                           ph.sb([128, 8, 128], F32), bufs(8), ph.sb([128, 8, 128], BF16), bufs(2)))
            ZG4 = ph.sb([128, 4, 8, 128], BF16)
            DNGb = DNG
            pF = Rot([ph.ps() for _ in range(6)])
            pH = Rot([ph.ps(BF16) for _ in range(2)])
            YRv = [y[:].rearrange("p h c -> p (h c)") for y in YR]

            def flat(t, g):
                return t[:, 4 * g:4 * g + 4, :]

            def m3_loads(t0):
                QKV, bQKV = QKVr.next()
                for grp in range(3):
                    P.dma("sp", QKV[:, grp * 8:(grp + 1) * 8, :],
                          self.QKVB[grp * 1024:(grp + 1) * 1024, t0:t0 + 512].rearrange("(c p) t -> p c t", p=128), writes=[bQKV])
                BGt, bBG = BGr.next()
                P.dma("sp", BGt[:], self.BG[t0:t0 + 512, :].rearrange("(s p) f -> p s f", p=128), writes=[bBG])
                return QKV, bQKV, BGt, bBG

            def prep(cx):
                tk, s, QKV, bQKV, BGt, bBG, OGT, bOGT = cx["tk"], cx["s"], cx["QKV"], cx["bQKV"], cx["BGt"], cx["bBG"], cx["OGT"], cx["bOGT"]
                NBG, bNBG, SM, bSM, AIT, bAIT, KT2, bKT2, QD, bQD, UB, bUB, WT, bWT = cx["H"]
                beta = BGt[:, s, 0:8]
                graw = BGt[:, s, 8:16]
                P.op("dve", lambda e, beta=beta: e.tensor_scalar(NBG[:], beta, -1.0, None, ALU.mult), reads=[bBG], writes=[bNBG])
                for h in range(8):
                    P.op("dve", lambda e, h=h, graw=graw: e.tensor_scalar(Rt[:, h, :], TRI[:], graw[:, h:h + 1], None, ALU.mult),
                         reads=[bC, bBG], writes=[bRt])
                px, bpx = pF.next()
                P.op("pe", lambda e, px=px, graw=graw: e.matmul(px[:, 0:8], SELA[:], graw, start=True, stop=True),
                     reads=[bC, bBG], writes=[bpx])
                P.op("pe", lambda e, px=px, graw=graw: e.matmul(px[:, 8:16], SELB[:], graw, start=True, stop=True),
                     reads=[bC, bBG], writes=[bpx])
                P.op("pe", lambda e, px=px, graw=graw: e.matmul(px[:, 16:24], TRI[:], graw, start=True, stop=True),
                     reads=[bC, bBG], writes=[bpx])
                P.op("pe", lambda e, px=px, graw=graw: e.matmul(px[:, 24:32], SAME[:], graw, start=True, stop=True),
                     reads=[bC, bBG], writes=[bpx])
                P.op("act", lambda e, px=px: e.activation(out=SM[:, 0:24], in_=px[:, 0:24], func=AF.Exp), reads=[bpx], writes=[bSM])
                P.op("act", lambda e, px=px: e.copy(SM[:, 56:64], px[:, 16:24]), reads=[bpx, bSM], writes=[bSM])
                P.op("dve", lambda e, px=px: e.tensor_tensor(out=SM[:, 32:40], in0=px[:, 24:32], in1=SM[:, 56:64], op=ALU.subtract),
                     reads=[bpx, bSM], writes=[bSM])
                P.op("act", lambda e: e.activation(out=SM[:, 24:32], in_=SM[:, 32:40], func=AF.Exp), reads=[bSM], writes=[bSM])
                for g in range(2):
                    pg, bpg = pF.next()
                    rv = flat(Rt, g)
                    P.op("pe", lambda e, pg=pg, rv=rv: e.matmul(pg[:], LGT[:], rv, start=True, stop=True), reads=[bC, bRt], writes=[bpg])
                    P.op("act", lambda e, pg=pg, g=g: e.activation(out=flat(ET, g), in_=pg[:].rearrange("p (h c) -> p h c", h=4), func=AF.Exp),
                         reads=[bpg], writes=[bET])
                    pg2, bpg2 = pF.next()
                    P.op("pe", lambda e, pg2=pg2, rv=rv: e.matmul(pg2[:], ONES[:], rv, start=True, stop=True), reads=[bC, bRt], writes=[bpg2])
                    P.op("act", lambda e, pg2=pg2, g=g: e.activation(out=flat(EGB, g), in_=pg2[:].rearrange("p (h c) -> p h c", h=4), func=AF.Exp),
                         reads=[bpg2], writes=[bEGB])
                P.op("pool", lambda e: e.tensor_tensor(out=ETS[:], in0=ET[:], in1=MSU8[:], op=ALU.mult), reads=[bET, bC], writes=[bETS])
                P.op("pool", lambda e: e.tensor_tensor(out=ET[:], in0=ET[:], in1=NEGM8[:], op=ALU.mult), reads=[bET, bC], writes=[bET])
                yield
                cur = 0
                for g in range(2):
                    pk, bpk = pF.next()
                    pq, bpq = pF.next()
                    for hh in range(4):
                        h = 4 * g + hh
                        P.op("pe", lambda e, pk=pk, h=h, hh=hh, QKV=QKV, tk=tk: e.matmul(
                            pk[:, hh * 128:(hh + 1) * 128], QKV[:, 8 + h, tk], QKV[:, 8 + h, tk], start=True, stop=True),
                            reads=[bQKV], writes=[bpk])
                        P.op("pe", lambda e, pq=pq, h=h, hh=hh, QKV=QKV, tk=tk: e.matmul(
                            pq[:, hh * 128:(hh + 1) * 128], QKV[:, 8 + h, tk], QKV[:, h, tk], start=True, stop=True),
                            reads=[bQKV], writes=[bpq])
                    for hh in range(4):
                        h = 4 * g + hh
                        P.op("dve", lambda e, pk=pk, h=h, hh=hh: e.scalar_tensor_tensor(
                            out=YR[0][:, h, 0:128], in0=pk[:, hh * 128:(hh + 1) * 128], scalar=NBG[:, h:h + 1],
                            in1=ETS[:, h, :], op0=ALU.mult, op1=ALU.mult), reads=[bpk, bNBG, bETS], writes=[bYR[0][g]])
                    P.op("dve", lambda e, pq=pq, g=g: e.tensor_tensor(out=flat(AIT, g), in0=pq[:].rearrange("p (h c) -> p h c", h=4),
                                                                     in1=flat(ET, g), op=ALU.mult), reads=[bpq, bET], writes=[bAIT])
                yield
                for g in range(2):
                    P.op("pool", lambda e, g=g: e.tensor_tensor(out=YR[1][:, 4 * g:4 * g + 4, 128:256], in0=YR[0][:, 4 * g:4 * g + 4, 0:128],
                                                                in1=flat(ID8, g), op=ALU.add), reads=[bYR[0][g], bC], writes=[bYR[1][g]])
                    if SD == BF16:
                        pz, bpz = pH.next()
                    else:
                        pz, bpz = pF.next()
                    for hh in range(4):
                        h = 4 * g + hh
                        P.op("pe", lambda e, pz=pz, h=h, hh=hh: e.transpose(pz[:, hh * 128:(hh + 1) * 128], YR[0][:, h, 0:128], identS[:]),
                             reads=[bYR[0][g], bidS], writes=[bpz])
                    P.op("act", lambda e, pz=pz, g=g: e.copy(flat(Z[0], g), pz[:, 0:512].rearrange("p (h c) -> p h c", h=4)),
                         reads=[bpz], writes=[bZ[0][g]])
                for k in range(6):
                    yield
                    a, b_ = k % 2, (k + 1) % 2
                    for g in range(2):
                        last = (k == 5)
                        if not last:
                            pz, bpz = pF.next()
                            for hh in range(4):
                                h = 4 * g + hh
                                P.op("pe", lambda e, pz=pz, h=h, hh=hh, a=a: e.matmul(
                                    pz[:, hh * 128:(hh + 1) * 128], YR[a][:, h, 0:128], Z[a][:, h, :], start=True, stop=True),
                                    reads=[bYR[a][g], bZ[a][g]], writes=[bpz])
                            P.op("act", lambda e, pz=pz, g=g, b_=b_: e.copy(flat(Z[b_], g), pz[:].rearrange("p (h c) -> p h c", h=4)),
                                 reads=[bpz], writes=[bZ[b_][g]])
                        if k == 0:
                            py, bpy = pF.next()
                            for hh in range(4):
                                h = 4 * g + hh
                                P.op("pe", lambda e, py=py, h=h, hh=hh: e.matmul(
                                    py[:, hh * 128:(hh + 1) * 128], Z[0][:, h, :], YR[0][:, h, 0:128], start=True, stop=True),
                                    reads=[bYR[0][g], bZ[0][g]], writes=[bpy])
                            P.op("act", lambda e, py=py, g=g: e.copy(YR[1][:, 4 * g:4 * g + 4, 0:128], py[:].rearrange("p (h c) -> p h c", h=4)),
                                 reads=[bpy], writes=[bYR[1][g]])
                        elif not last:
                            for half in range(2):
                                py, bpy = pF.next()
                                for hh in range(2):
                                    h = 4 * g + 2 * half + hh
                                    P.op("pe", lambda e, py=py, h=h, hh=hh, a=a: e.matmul(
                                        py[:, hh * 256:(hh + 1) * 256], Z[a][:, h, :], YR[a][:, h, :], start=True, stop=True),
                                        reads=[bYR[a][g], bZ[a][g]], writes=[bpy])
                                h0 = 4 * g + 2 * half
                                pv = py[:].rearrange("p (h c) -> p h c", h=2)
                                P.op("act", lambda e, pv=pv, h0=h0, b_=b_: e.copy(YR[b_][:, h0:h0 + 2, 0:128], pv[:, :, 0:128]),
                                     reads=[bpy], writes=[bYR[b_][g]])
                                P.op("dve", lambda e, pv=pv, h0=h0, a=a, b_=b_: e.tensor_tensor(
                                    out=YR[b_][:, h0:h0 + 2, 128:256], in0=pv[:, :, 128:256], in1=YR[a][:, h0:h0 + 2, 128:256], op=ALU.add),
                                    reads=[bpy, bYR[a][g]], writes=[bYR[b_][g]])
                        else:
                            py, bpy = pF.next()
                            for hh in range(4):
                                h = 4 * g + hh
                                P.op("pe", lambda e, py=py, h=h, hh=hh, a=a: e.matmul(
                                    py[:, hh * 128:(hh + 1) * 128], Z[a][:, h, :], YR[a][:, h, 128:256], start=True, stop=True),
                                    reads=[bYR[a][g], bZ[a][g]], writes=[bpy])
                            P.op("dve", lambda e, py=py, g=g, a=a: e.tensor_tensor(
                                out=flat(XS, g), in0=py[:].rearrange("p (h c) -> p h c", h=4), in1=YR[a][:, 4 * g:4 * g + 4, 128:256], op=ALU.add),
                                reads=[bpy, bYR[a][g]], writes=[bXS[g]])
                yield
                pkt, bpkt = pH.next()
                for h in range(8 if (M3SUB & 1) else 0):
                    P.op("pe", lambda e, pkt=pkt, h=h, QKV=QKV, tk=tk: e.transpose(pkt[:, h * 128:(h + 1) * 128], QKV[:, 8 + h, tk], identB[:]),
                         reads=[bQKV, bidB], writes=[bpkt])
                P.op("act", lambda e, pkt=pkt: e.copy(KTOK[:].rearrange("p h c -> p (h c)"), pkt[:]), reads=[bpkt], writes=[bKTOK])
                for h in range(8):
                    P.op("dve", lambda e, h=h: e.tensor_scalar(KG[:, h, :], KTOK[:, h, :], SM[:, 16 + h:17 + h], None, ALU.mult),
                         reads=[bKTOK, bSM], writes=[bKG])
                    P.op("act", lambda e, h=h: e.activation(out=KT2[:, h, :], in_=KTOK[:, h, :], func=AF.Identity, scale=SM[:, 24 + h:25 + h]),
                         reads=[bKTOK, bSM], writes=[bKT2])
                pvt, bpvt = pH.next()
                for h in range(8 if (M3SUB & 2) else 0):
                    P.op("pe", lambda e, pvt=pvt, h=h, QKV=QKV, tk=tk: e.transpose(pvt[:, h * 128:(h + 1) * 128], QKV[:, 16 + h, tk], identB[:]),
                         reads=[bQKV, bidB], writes=[bpvt])
                if M3SUB & 2:
                    P.op("act", lambda e, pvt=pvt: e.copy(Vt[:].rearrange("p h c -> p (h c)"), pvt[:]), reads=[bpvt], writes=[bVt])
                if M3SUB & 4:
                    P.op("pool", lambda e, QKV=QKV, tk=tk: e.tensor_tensor(out=QD[:], in0=QKV[:, 0:8, tk], in1=EGB[:], op=ALU.mult),
                         reads=[bQKV, bEGB], writes=[bQD])
                yield
                for g in range(2):
                    pu, bpu = pF.next()
                    pw, bpw = pF.next()
                    for hh in range(4):
                        h = 4 * g + hh
                        P.op("pe", lambda e, pu=pu, h=h, hh=hh: e.matmul(pu[:, hh * 128:(hh + 1) * 128], XS[:, h, :], Vt[:, h, :], start=True, stop=True),
                             reads=[bXS[g], bVt], writes=[bpu])
                        P.op("pe", lambda e, pw=pw, h=h, hh=hh: e.matmul(pw[:, hh * 128:(hh + 1) * 128], KG[:, h, :], XS[:, h, :], start=True, stop=True),
                             reads=[bXS[g], bKG], writes=[bpw])
                    for hh in range(4):
                        h = 4 * g + hh
                        P.op("act", lambda e, pu=pu, h=h, hh=hh, beta=beta: e.activation(out=UB[:, h, :], in_=pu[:, hh * 128:(hh + 1) * 128], func=AF.Identity,
                                                                                      scale=beta[:, h:h + 1]), reads=[bpu, bBG], writes=[bUB[h]])
                    P.op("act", lambda e, pw=pw, g=g: e.copy(flat(WT, g), pw[:].rearrange("p (h c) -> p h c", h=4)), reads=[bpw], writes=[bWT[g]])

            def rec(cx):
                tk, s, QKV, bQKV, BGt, bBG, OGT, bOGT = cx["tk"], cx["s"], cx["QKV"], cx["bQKV"], cx["BGt"], cx["bBG"], cx["OGT"], cx["bOGT"]
                NBG, bNBG, SM, bSM, AIT, bAIT, KT2, bKT2, QD, bQD, UB, bUB, WT, bWT = cx["H"]
                beta = BGt[:, s, 0:8]
                graw = BGt[:, s, 8:16]
                if cx["tile_first"]:
                    for sb4 in range(4):
                        Xs_, bXs_ = XSr.next()
                        P.dma("sp", Xs_[:], src[cx["t0"] + sb4 * 128:cx["t0"] + (sb4 + 1) * 128, :], writes=[bXs_])
                        self.transpose_x([(Xs_, bXs_)], xT1[:, :, sb4 * 128:(sb4 + 1) * 128], bxT1, pF, identF, bidF)
                    for sb4 in range(4):
                        for hf in range(2):
                            pz_, bpz_ = pF.next()
                            for kc in range(8):
                                P.op("pe", lambda e, pz_=pz_, kc=kc, hf=hf, sb4=sb4: e.matmul(pz_[:], xT1[:, kc, sb4 * 128:(sb4 + 1) * 128], WZ[:, kc, hf * 512:(hf + 1) * 512],
                                                                                           start=(kc == 0), stop=(kc == 7)),
                                     reads=[bxT1[kc], bWZ[kc]], writes=[bpz_])
                            P.op("act", lambda e, pz_=pz_, hf=hf, sb4=sb4: e.activation(out=ZG4[:, sb4, 4 * hf:4 * hf + 4, :], in_=pz_[:].rearrange("p (h c) -> p h c", h=4), func=AF.Silu),
                                 reads=[bpz_], writes=[bZG])
                        P.op("pool", lambda e, sb4=sb4: e.tensor_tensor(out=ZG4[:, sb4, :, :], in0=ZG4[:, sb4, :, :], in1=DNGb[:], op=ALU.mult), reads=[bZG, bDNG], writes=[bZG])
                        yield
                if cx["seq_reset"]:
                    P.op("pool", lambda e: e.memset(S[:], 0.0), writes=bS)
                    P.op("pool", lambda e: e.memset(SB[:], 0.0), writes=bSB)
                for ck in range(2):
                    rows = slice(ck * 64, (ck + 1) * 64)
                    for g in range(2):
                        yield
                        pv_, bpv = pF.next()
                        for hh in range(4):
                            h = 4 * g + hh
                            P.op("pe", lambda e, pv_=pv_, h=h, hh=hh: e.matmul(pv_[:, hh * 128:(hh + 1) * 128], WT[:, h, :], SB[:, h, :], start=True, stop=True),
                                 reads=[bWT[g], bSB[h]], writes=[bpv])
                        for hh in range(4):
                            h = 4 * g + hh
                            P.op("dve", lambda e, pv_=pv_, h=h, hh=hh, rows=rows: e.scalar_tensor_tensor(
                                out=VN[rows, h, :], in0=pv_[rows, hh * 128:(hh + 1) * 128], scalar=NBG[rows, h:h + 1], in1=UB[rows, h, :],
                                op0=ALU.mult, op1=ALU.add), reads=[bpv, bNBG, bUB[h]], writes=[bVN[h]])
                        po, bpo = pF.next()
                        for hh in range(4):
                            h = 4 * g + hh
                            P.op("pe", lambda e, po=po, h=h, hh=hh: e.matmul(po[:, hh * 128:(hh + 1) * 128], QD[:, h, :], SB[:, h, :], start=True, stop=False),
                                 reads=[bQD, bSB[h]], writes=[bpo])
                            P.op("pe", lambda e, po=po, h=h, hh=hh: e.matmul(po[:, hh * 128:(hh + 1) * 128], AIT[:, h, :], VN[:, h, :], start=False, stop=True),
                                 reads=[bAIT, bVN[h]], writes=[bpo])
                        P.op("act", lambda e, po=po, g=g, rows=rows: e.copy(O[rows, 4 * g:4 * g + 4, :], po[rows, :].rearrange("p (h c) -> p h c", h=4)),
                             reads=[bpo], writes=[bO])
                        ps_, bps = pF.next()
                        for hh in range(4):
                            h = 4 * g + hh
                            P.op("pe", lambda e, ps_=ps_, h=h, hh=hh, rows=rows: e.matmul(ps_[:, hh * 128:(hh + 1) * 128], KT2[rows, h, :], VN[rows, h, :], start=True, stop=True),
                                 reads=[bKT2, bVN[h]], writes=[bps])
                        for hh in range(4):
                            h = 4 * g + hh
                            P.op("dve", lambda e, ps_=ps_, h=h, hh=hh, ck=ck: e.scalar_tensor_tensor(
                                out=S[:, h, :], in0=S[:, h, :], scalar=SM[:, ck * 8 + h:ck * 8 + h + 1], in1=ps_[:, hh * 128:(hh + 1) * 128],
                                op0=ALU.mult, op1=ALU.add), reads=[bps, bSM, bS[h]], writes=[bS[h]])
                            P.op("act", lambda e, h=h: e.copy(SB[:, h, :], S[:, h, :]), reads=[bS[h]], writes=[bSB[h]])
                yield
                P.op("pool", lambda e: e.tensor_tensor(out=SQ[:], in0=O[:], in1=O[:], op=ALU.mult), reads=[bO], writes=[bSQ])
                P.op("dve", lambda e: e.tensor_reduce(out=SM[:, 40:48], in_=SQ[:], axis=AX.X, op=ALU.add), reads=[bSQ, bSM], writes=[bSM])
                P.op("dve", lambda e: e.tensor_scalar(SM[:, 48:56], SM[:, 40:48], 1.0 / 128.0, float(NORM_EPS), ALU.mult, ALU.add), reads=[bSM], writes=[bSM])
                P.op("act", lambda e: e.activation(out=SM[:, 48:56], in_=SM[:, 48:56], func=AF.Ln), reads=[bSM], writes=[bSM])
                P.op("act", lambda e: e.activation(out=SM[:, 48:56], in_=SM[:, 48:56], func=AF.Exp, scale=-0.5), reads=[bSM], writes=[bSM])
                for h in range(8):
                    P.op("dve", lambda e, h=h: e.scalar_tensor_tensor(out=OG[:, h, :], in0=O[:, h, :], scalar=SM[:, 48 + h:49 + h], in1=ZG4[:, s, h, :],
                                                                      op0=ALU.mult, op1=ALU.mult), reads=[bO, bSM, bZG], writes=[bOG])
                pt_, bpt = pH.next()
                for h in range(8):
                    P.op("pe", lambda e, pt_=pt_, h=h: e.transpose(pt_[:, h * 128:(h + 1) * 128], OG[:, h, :], identB[:]),
                         reads=[bOG, bidB], writes=[bpt])
                P.op("act", lambda e, pt_=pt_, OGT=OGT, tk=tk: e.copy(OGT[:, :, tk], pt_[:].rearrange("p (h c) -> p h c", h=8)),
                     reads=[bpt], writes=[bOGT])
                if cx["tile_last"]:
                    P.dma("pool", self.OGT[:, cx["t0"]:cx["t0"] + 512].rearrange("(c p) t -> p c t", p=128), OGT[:], reads=[bOGT])

            def interleave(g1, g2):
                gens = [g for g in (g1, g2) if g is not None]
                while gens:
                    for g in list(gens):
                        try:
                            next(g)
                        except StopIteration:
                            gens.remove(g)

            ntl = T // 512
            tl = {0: m3_loads(0)}
            prev = None
            for ti in range(ntl):
                t0 = ti * 512
                QKV, bQKV, BGt, bBG = tl.pop(ti)
                if ti + 1 < ntl:
                    tl[ti + 1] = m3_loads(t0 + 512)
                OGT, bOGT = OGTr.next()
                for s in range(4):
                    gb = ti * 4 + s
                    cx = dict(tk=slice(s * 128, (s + 1) * 128), s=s, QKV=QKV, bQKV=bQKV, BGt=BGt, bBG=bBG, OGT=OGT, bOGT=bOGT,
                              H=HH[gb % 2], t0=t0, tile_first=(s == 0), tile_last=(s == 3), seq_reset=(s == 0 and t0 % L == 0))
                    interleave(prep(cx), rec(prev) if prev is not None else None)
                    prev = cx
            interleave(rec(prev), None)

    def m4_phase(self, layer, src, dst):
        P = self.P
        T = self.T
        with self.phase() as ph:
            WA = ph.sb([128, 8, D], BF16)
            WB = ph.sb([128, 8, D], BF16)
            WO = ph.sb([128, 8, D], BF16)
            WG = ph.sb([128, 8, 2 * D], BF16)
            bWA, bWB, bWO, bWG = bufs(8), bufs(8), bufs(8), bufs(8)
            self.load_w(WA, bWA, self.w_a[layer], 8, 0, D, piece=1024)
            self.load_w(WB, bWB, self.w_b[layer], 8, 0, D, piece=1024)
            self.load_w(WG, bWG, self.w_in[layer], 8, C_GATE, C_GATE + 2 * D, piece=1024)
            self.load_w(WO, bWO, self.w_o[layer], 8, 0, D, piece=1024)
            ident, bid = self.make_ident(ph, F32)
            lnc = self.ln_consts(ph, layer, 1)
            NB = 2
            XSr = Rot([ph.sb([128, D], F32) for _ in range(4)])
            XRr = Rot([ph.sb([128, D], F32) for _ in range(2)])
            xTr = [ph.sb([128, 8, 512], BF16) for _ in range(NB)]
            bxTr = [bufs(8) for _ in range(NB)]
            ATr = Rot([ph.sb([128, 8, 512], BF16) for _ in range(2)])
            OGr = Rot([ph.sb([128, 8, 512], BF16) for _ in range(2)])
            MT = ph.sb([128, 8, 512], BF16)
            bMT = bufs(8)
            SGr = Rot([ph.sb([128, 512], F32) for _ in range(4)])
            T1r = Rot([ph.sb([128, 512], F32) for _ in range(2)])
            T2r = Rot([ph.sb([128, 512], F32) for _ in range(2)])
            small = Rot([ph.sb([128, 16], F32) for _ in range(2)])
            pM = Rot([ph.ps() for _ in range(4)])
            pT = Rot([ph.ps() for _ in range(4)])
            pYi = [0]
            def m4_loads(t0):
                xs = self.issue_x(src, t0, XSr)
                At, bA = ATr.next()
                Og, bOg = OGr.next()
                P.dma("sp", At[:], self.AT[:, t0:t0 + 512].rearrange("(c p) t -> p c t", p=128), writes=[bA])
                P.dma("sp", Og[:], self.OGT[:, t0:t0 + 512].rearrange("(c p) t -> p c t", p=128), writes=[bOg])
                return xs, At, bA, Og, bOg

            nxt = m4_loads(0)
            self.transpose_x(nxt[0], xTr[0], bxTr[0], pT, ident, bid)
            for ti in range(T // 512):
                t0 = ti * 512
                xT, bxT = xTr[ti % NB], bxTr[ti % NB]
                _, At, bA, Og, bOg = nxt
                if ti + 1 < T // 512:
                    nxt = m4_loads(t0 + 512)
                for n in range(8):
                    ns = slice(n * 128, (n + 1) * 128)
                    pa, bpa = pM.next()
                    pga, bpga = pM.next()
                    for kc in range(8):
                        P.op("pe", lambda e, pa=pa, kc=kc, ns=ns, At=At: e.matmul(pa[:], WA[:, kc, ns], At[:, kc, :], start=(kc == 0), stop=(kc == 7)),
                             reads=[bWA[kc], bA], writes=[bpa])
                    for kc in range(8):
                        P.op("pe", lambda e, pga=pga, kc=kc, ns=ns, xT=xT: e.matmul(pga[:], WG[:, kc, ns], xT[:, kc, :], start=(kc == 0), stop=(kc == 7)),
                             reads=[bWG[kc], bxT[kc]], writes=[bpga])
                    sga, bsga = SGr.next()
                    P.op("act", lambda e, sga=sga, pga=pga: e.activation(out=sga[:], in_=pga[:], func=AF.Sigmoid), reads=[bpga], writes=[bsga])
                    t1, bt1 = T1r.next()
                    P.op("dve", lambda e, t1=t1, pa=pa, sga=sga: e.tensor_tensor(out=t1[:], in0=pa[:], in1=sga[:], op=ALU.mult),
                         reads=[bpa, bsga], writes=[bt1])
                    pb, bpb = pM.next()
                    pgb, bpgb = pM.next()
                    for kc in range(8):
                        P.op("pe", lambda e, pb=pb, kc=kc, ns=ns, Og=Og: e.matmul(pb[:], WB[:, kc, ns], Og[:, kc, :], start=(kc == 0), stop=(kc == 7)),
                             reads=[bWB[kc], bOg], writes=[bpb])
                    for kc in range(8):
                        P.op("pe", lambda e, pgb=pgb, kc=kc, n=n, xT=xT: e.matmul(pgb[:], WG[:, kc, D + n * 128:D + (n + 1) * 128], xT[:, kc, :],
                                                                               start=(kc == 0), stop=(kc == 7)),
                             reads=[bWG[kc], bxT[kc]], writes=[bpgb])
                    sgb, bsgb = SGr.next()
                    P.op("act", lambda e, sgb=sgb, pgb=pgb: e.activation(out=sgb[:], in_=pgb[:], func=AF.Sigmoid), reads=[bpgb], writes=[bsgb])
                    t2, bt2 = T2r.next()
                    P.op("dve", lambda e, t2=t2, pb=pb, sgb=sgb: e.tensor_tensor(out=t2[:], in0=pb[:], in1=sgb[:], op=ALU.mult),
                         reads=[bpb, bsgb], writes=[bt2])
                    P.op("pool", lambda e, t1=t1, t2=t2, n=n: e.tensor_tensor(out=MT[:, n, :], in0=t1[:], in1=t2[:], op=ALU.add),
                         reads=[bt1, bt2], writes=[bMT[n]])
                if ti + 1 < T // 512:
                    self.transpose_x(nxt[0], xTr[(ti + 1) % NB], bxTr[(ti + 1) % NB], pT, ident, bid)
                for s in range(4):
                    k2 = 2 * (pYi[0] % 2)
                    pYi[0] += 1
                    pY = [pT.t[k2], pT.t[k2 + 1]]
                    bpY = [pT.b[k2], pT.b[k2 + 1]]
                    for hf in range(2):
                        for kc in range(8):
                            P.op("pe", lambda e, hf=hf, kc=kc, s=s, pY=pY: e.matmul(pY[hf][:], MT[:, kc, s * 128:(s + 1) * 128], WO[:, kc, hf * 512:(hf + 1) * 512],
                                                                            start=(kc == 0), stop=(kc == 7)),
                                 reads=[bMT[kc], bWO[kc]], writes=[bpY[hf]])
                    self.ln_epilogue(pY, bpY, src, dst, t0 + s * 128, 1.0 / DN_ALPHA, lnc, XRr, small)

    def build(self, upto=99):
        cur = self.x
        n = 0
        for layer in range(self.depth):
            last = (layer == self.depth - 1)
            steps = [
                lambda: self.ffn_phase(layer, 0, cur, self.R[0], 0),
                lambda: self.m1_phase(layer, self.R[0]),
                lambda: self.m2_phase(layer),
                lambda: self.m3_phase(layer, self.R[0]),
                lambda: self.m4_phase(layer, self.R[0], self.R[1]),
                lambda: self.ffn_phase(layer, 1, self.R[1], self.out if last else self.R[0], 2),
            ]
            for st in steps:
                if n < upto:
                    st()
                n += 1
            cur = self.R[0]
        self.top.close()
        return self.nc


def host_pos_tables(rel_bias):
    s = np.arange(128)[:, None]
    q = np.arange(128)[None, :]
    out_b = np.zeros((128, 2, 16, 128), np.float32)
    out_m = np.zeros((128, 2, 16, 128), np.float32)
    for blk in range(2):
        j = s + 128 * blk
        rel = q + 128 - j
        valid = (rel >= 0) & (rel < 128)
        n = np.maximum(rel, 0)
        nf = np.maximum(n, 1).astype(np.float32)
        large = 16 + (np.log(nf / np.float32(16)) / np.float32(np.log(128 / 16)) * np.float32(16)).astype(np.int32)
        large = np.minimum(large, 31)
        bucket = np.where(n < 16, n, large)
        bucket = np.where(valid, bucket, 0)
        g = rel_bias[bucket]
        vm = np.broadcast_to(valid[:, None, :], (128, 16, 128))
        out_b[:, blk] = np.where(vm, np.transpose(g, (0, 2, 1)), np.float32(-1e30))
        out_m[:, blk] = vm
    return out_b.reshape(128, -1), out_m.reshape(128, -1)


_NC_CACHE = {}


def kernel(x, rel_bias, ln_g, ln_b, ffn_w13, ffn_w2, w_in, conv_w, a_log, dt_bias,
           dn_norm_g, sinks, w_branch_a, w_branch_b, w_out):
    x = np.asarray(x, np.float32)
    B, L, _ = x.shape
    nseq = B // NCORES
    key = (nseq, L)
    if key not in _NC_CACHE:
        _NC_CACHE[key] = KB(nseq, L, DEPTH).build()
    nc = _NC_CACHE[key]
    pb, pm = host_pos_tables(np.asarray(rel_bias, np.float32))
    f = lambda a: np.ascontiguousarray(np.asarray(a, np.float32))
    shared = dict(ln_g=f(ln_g), ln_b=f(ln_b), ffn_w13=f(ffn_w13), ffn_w2=f(ffn_w2), w_in=f(w_in), conv_w=f(conv_w),
                  a_log=f(a_log), dt_bias=f(dt_bias), dn_norm_g=f(dn_norm_g), sinks=f(sinks),
                  w_branch_a=f(w_branch_a), w_branch_b=f(w_branch_b), w_out=f(w_out), pbias=pb, pmask=pm)
    in_maps = []
    for c in range(NCORES):
        m = dict(shared)
        m["x"] = np.ascontiguousarray(x[c * nseq:(c + 1) * nseq].reshape(nseq * L, D))
        in_maps.append(m)
    res = run_bass_kernel_spmd(nc, in_maps, core_ids=list(range(NCORES)))
    outs = [np.asarray(r["out"], np.float32).reshape(nseq, L, D) for r in res.results]
    return np.concatenate(outs, axis=0)
```

```python
import contextlib
import numpy as np
import concourse.bass as bass
import concourse.mybir as mybir
from concourse.bass_utils import run_bass_kernel_spmd

F32 = mybir.dt.float32
BF16 = mybir.dt.bfloat16
AF = mybir.ActivationFunctionType
ALU = mybir.AluOpType
AX = mybir.AxisListType

D = 1024
DFF = 2816
NIN = 7696
DEPTH = 4
SEQ = 4096
NCORES = 8
LN_EPS = 1e-5
NORM_EPS = 1e-6
DN_ALPHA = (2 * DEPTH) ** 0.25
C_Q0, C_KA, C_VA, C_QKVB, C_BETA, C_DT, C_Z, C_GATE = 0, 1024, 1280, 1536, 4608, 4616, 4624, 5648
SOLVE_DT = BF16
import os
M3STOP = int(os.environ.get("M3STOP", "99"))
M3SUB = int(os.environ.get("M3SUB", "7"))

NDMASEM = 16
ENGS = ("pe", "act", "dve", "pool", "sp")


class Buf:
    __slots__ = ("lw", "rd")

    def __init__(self):
        self.lw = None
        self.rd = []


def bufs(n):
    return [Buf() for _ in range(n)]


class Prog:
    def __init__(self, nc, stack):
        self.nc = nc
        self.ops = {e: [] for e in ENGS}
        self.cnt = {e: 0 for e in ("pe", "act", "dve", "pool")}
        self.seen = {e: {} for e in ENGS}
        self.dq_n = {"sp": 0, "pool": 0}
        self.esem = {e: stack.enter_context(nc.semaphore("s_" + e)) for e in ("pe", "act", "dve", "pool")}
        self.dsem = {}
        for q in ("sp", "pool"):
            for k in range(NDMASEM):
                self.dsem[(q, k)] = stack.enter_context(nc.semaphore("d_%s%d" % (q, k)))
        self.ninstr = 0

    def _kv(self, tok):
        if tok[0] == "e":
            return ("e", tok[1]), tok[2]
        q, i = tok[1], tok[2]
        return ("d", q, i % NDMASEM), 16 * (i // NDMASEM + 1)

    def _deps(self, eng, reads, writes):
        need = {}

        def add(tok):
            if tok is None:
                return
            if tok[0] == "e" and tok[1] == "pe" and eng == "pe":
                return
            k, v = self._kv(tok)
            if need.get(k, 0) < v:
                need[k] = v

        for b in reads:
            add(b.lw)
        for b in writes:
            add(b.lw)
            for t in b.rd:
                add(t)
        out = []
        s = self.seen[eng]
        for k, v in need.items():
            if s.get(k, 0) < v:
                s[k] = v
                out.append((k, v))
        return out

    def _commit(self, tok, reads, writes):
        for b in reads:
            b.rd.append(tok)
            if len(b.rd) > 32:
                best = {}
                for t in b.rd:
                    k, v = self._kv(t)
                    if k not in best or best[k][0] < v:
                        best[k] = (v, t)
                b.rd = [t for (_, t) in best.values()]
        for b in writes:
            b.lw = tok
            b.rd = []

    def op(self, eng, fn, reads=(), writes=()):
        waits = self._deps(eng, reads, writes)
        self.cnt[eng] += 1
        tok = ("e", eng, self.cnt[eng])
        self.ops[eng].append((waits, fn, None))
        self._commit(tok, reads, writes)

    def dma(self, q, out_ap, in_ap, reads=(), writes=(), slow=False):
        i = self.dq_n[q]
        self.dq_n[q] += 1
        tok = ("d", q, i)
        waits = self._deps(q, reads, writes)
        if i >= NDMASEM:
            k, v = self._kv(("d", q, i - NDMASEM))
            if self.seen[q].get(k, 0) < v:
                self.seen[q][k] = v
                waits.append((k, v))
        self.ops[q].append((waits, (out_ap, in_ap, slow), tok))
        self._commit(tok, reads, writes)

    def barrier(self):
        allk = []
        for e, c in self.cnt.items():
            if c:
                allk.append((("e", e), c))
        for q, n in self.dq_n.items():
            for k in range(min(NDMASEM, n)):
                last = ((n - 1 - k) // NDMASEM) * NDMASEM + k
                allk.append((("d", q, k), 16 * (last // NDMASEM + 1)))
        for e in ENGS:
            s = self.seen[e]
            waits = []
            for k, v in allk:
                if k == ("e", "pe") and e == "pe":
                    continue
                if s.get(k, 0) < v:
                    s[k] = v
                    waits.append((k, v))
            if waits:
                self.ops[e].append((waits, None, None))

    def emit(self):
        nc = self.nc

        def semof(k):
            return self.esem[k[1]] if k[0] == "e" else self.dsem[(k[1], k[2])]

        def run(engname, e):
            for waits, fn, tok in self.ops[engname]:
                for k, v in waits:
                    e.wait_ge(semof(k), v)
                if fn is None:
                    continue
                self.ninstr += 1
                if tok is None:
                    fn(e).then_inc(self.esem[engname], 1)
                else:
                    o, i, slow = fn
                    if slow:
                        ins = e.dma_start(out=o, in_=i, allow_slow_non_contiguous=True)
                    else:
                        ins = e.dma_start(out=o, in_=i)
                    ins.then_inc(self.dsem[(tok[1], tok[2] % NDMASEM)], 16)
            self.ops[engname] = []

        with nc.Block() as block:
            @block.tensor
            def _(e):
                run("pe", e)

            @block.scalar
            def _(e):
                run("act", e)

            @block.vector
            def _(e):
                run("dve", e)

            @block.gpsimd
            def _(e):
                run("pool", e)

            @block.sync
            def _(e):
                run("sp", e)


class Phase:
    def __init__(self, kb):
        self.kb = kb
        self.st = contextlib.ExitStack()
        self.n = 0

    def __enter__(self):
        self.st.__enter__()
        return self

    def __exit__(self, *a):
        self.kb.P.barrier()
        self.kb.P.emit()
        return self.st.__exit__(*a)

    def sb(self, shape, dt):
        self.n += 1
        return self.st.enter_context(self.kb.nc.sbuf_tensor("t%d_%d" % (self.kb.phase_id, self.n), list(shape), dt))

    def ps(self, dt=F32):
        self.n += 1
        cols = 512 if dt == F32 else 1024
        return self.st.enter_context(self.kb.nc.psum_tensor("p%d_%d" % (self.kb.phase_id, self.n), [128, cols], dt))


class Rot:
    def __init__(self, tiles):
        self.t = tiles
        self.b = bufs(len(tiles))
        self.i = 0

    def next(self):
        k = self.i % len(self.t)
        self.i += 1
        return self.t[k], self.b[k]


class KB:
    def __init__(self, nseq, L, depth, debug=False):
        self.nseq, self.L, self.depth = nseq, L, depth
        self.T = nseq * L
        self.debug = debug
        self.nc = bass.Bass("TRN2", target_bir_lowering=False)
        self.top = contextlib.ExitStack()
        self.phase_id = 0
        nc = self.nc
        T = self.T
        ext = lambda n, s: nc.dram_tensor(n, list(s), F32, kind="ExternalInput").ap()
        self.x = ext("x", [T, D])
        self.ln_g = ext("ln_g", [DEPTH, 3, D])
        self.ln_b = ext("ln_b", [DEPTH, 3, D])
        self.w13 = ext("ffn_w13", [DEPTH, 2, D, 2 * DFF])
        self.w2 = ext("ffn_w2", [DEPTH, 2, DFF, D])
        self.w_in = ext("w_in", [DEPTH, D, NIN])
        self.conv_w = ext("conv_w", [DEPTH, 4, 3072])
        self.a_log = ext("a_log", [DEPTH, 8])
        self.dt_bias = ext("dt_bias", [DEPTH, 8])
        self.dn_g = ext("dn_norm_g", [DEPTH, 128])
        self.sinks = ext("sinks", [DEPTH, 16])
        self.w_a = ext("w_branch_a", [DEPTH, D, D])
        self.w_b = ext("w_branch_b", [DEPTH, D, D])
        self.w_o = ext("w_out", [DEPTH, D, D])
        self.pbias = ext("pbias", [128, 2 * 16 * 128])
        self.pmask = ext("pmask", [128, 2 * 16 * 128])
        self.out = nc.dram_tensor("out", [T, D], F32, kind="ExternalOutput").ap()
        kind = "ExternalOutput" if debug else "Internal"
        scr = lambda n, s, dt: nc.dram_tensor(n, list(s), dt, kind=kind).ap()
        self.R = [scr("res%d" % i, [T, D], F32) for i in range(2)]
        self.QT = scr("QT", [1024, T], BF16)
        self.KT = scr("KT", [256, T], BF16)
        self.VA = scr("VA", [T, 256], BF16)
        self.QKVB = scr("QKVB", [3072, T], BF16)
        self.BG = scr("BG", [T, 16], F32)
        self.AT = scr("AT", [1024, T], BF16)
        self.OGT = scr("OGT", [1024, T], BF16)
        self.P = Prog(nc, self.top)

    def phase(self):
        self.phase_id += 1
        return Phase(self)

    def make_ident(self, ph, dt):
        P = self.P
        t = ph.sb([128, 128], dt)
        b = Buf()
        P.op("pool", lambda e: e.memset(t[:], 0.0), writes=[b])
        P.op("pool", lambda e: e.affine_select(out=t[:], in_=t[:], pattern=[[-1, 128]], compare_op=ALU.not_equal,
                                               fill=1.0, base=0, channel_multiplier=1), reads=[b], writes=[b])
        return t, b

    def load_w(self, dst, dbufs, src, kcs, c0, c1, piece=2048):
        v = src.rearrange("(kc p) n -> p kc n", p=128)
        for kc in range(kcs):
            a = c0
            while a < c1:
                b = min(c1, a + piece)
                self.P.dma("pool", dst[:, kc, a - c0:b - c0], v[:, kc, a:b], writes=[dbufs[kc]])
                a = b

    def issue_x(self, src, t0, XS4, nsub=4):
        out = []
        for s in range(nsub):
            Xs, bXs = XS4.next()
            self.P.dma("sp", Xs[:], src[t0 + s * 128:t0 + (s + 1) * 128, :], writes=[bXs])
            out.append((Xs, bXs))
        return out

    def transpose_x(self, xs, xT, bxT, pT, ident, bid):
        P = self.P
        for s, (Xs, bXs) in enumerate(xs):
            for half in range(2):
                pt, bpt = pT.next()
                for k4 in range(4):
                    kc = half * 4 + k4
                    P.op("pe", lambda e, pt=pt, k4=k4, kc=kc, Xs=Xs: e.transpose(pt[:, k4 * 128:(k4 + 1) * 128],
                                                                              Xs[:, kc * 128:(kc + 1) * 128], ident[:]),
                         reads=[bXs, bid], writes=[bpt])
                wb = [bxT[half * 4 + k4] for k4 in range(4)]
                dst = xT[:, half * 4:half * 4 + 4, s * 128:(s + 1) * 128]
                srcv = pt[:].rearrange("p (k c) -> p k c", k=4)
                if half == 0:
                    P.op("act", lambda e, dst=dst, srcv=srcv: e.copy(dst, srcv), reads=[bpt], writes=wb)
                else:
                    P.op("dve", lambda e, dst=dst, srcv=srcv: e.tensor_copy(dst, srcv), reads=[bpt], writes=wb)

    def ln_consts(self, ph, layer, idx):
        P = self.P
        G = ph.sb([128, D], F32)
        B = ph.sb([128, D], F32)
        bg, bb = Buf(), Buf()
        P.dma("sp", G[:], self.ln_g[layer, idx, :].partition_broadcast(128), writes=[bg])
        P.dma("sp", B[:], self.ln_b[layer, idx, :].partition_broadcast(128), writes=[bb])
        return (G, bg, B, bb)

    def ln_epilogue(self, py, bpy, src, dst, r0, c, lnc, XRr, small):
        P = self.P
        G, bg, B, bb = lnc
        Rt, bR = XRr.next()
        st, bst = small.next()
        P.dma("sp", Rt[:], src[r0:r0 + 128, :], writes=[bR])
        for hf in range(2):
            P.op("dve", lambda e, hf=hf, Rt=Rt: e.scalar_tensor_tensor(
                out=Rt[:, hf * 512:(hf + 1) * 512], in0=py[hf][:], scalar=float(c),
                in1=Rt[:, hf * 512:(hf + 1) * 512], op0=ALU.mult, op1=ALU.add),
                reads=[bpy[hf], bR], writes=[bR])
        for hf in range(2):
            P.op("dve", lambda e, hf=hf, Rt=Rt, st=st: e.bn_stats(st[:, hf * 6:(hf + 1) * 6], Rt[:, hf * 512:(hf + 1) * 512]),
                 reads=[bR], writes=[bst])
        P.op("dve", lambda e, st=st: e.bn_aggr(st[:, 12:14], st[:, 0:12]), reads=[bst], writes=[bst])
        eps = LN_EPS / (DN_ALPHA ** 2)
        P.op("dve", lambda e, st=st: e.tensor_scalar(st[:, 13:14], st[:, 13:14], float(eps), None, ALU.add),
             reads=[bst], writes=[bst])
        P.op("act", lambda e, st=st: e.activation(out=st[:, 14:15], in_=st[:, 13:14], func=AF.Ln),
             reads=[bst], writes=[bst])
        P.op("act", lambda e, st=st: e.activation(out=st[:, 14:15], in_=st[:, 14:15], func=AF.Exp, scale=-0.5),
             reads=[bst], writes=[bst])
        P.op("dve", lambda e, st=st: e.scalar_tensor_tensor(out=st[:, 15:16], in0=st[:, 12:13], scalar=-1.0,
                                                            in1=st[:, 14:15], op0=ALU.mult, op1=ALU.mult),
             reads=[bst], writes=[bst])
        P.op("act", lambda e, st=st, Rt=Rt: e.activation(out=Rt[:], in_=Rt[:], func=AF.Identity,
                                                        bias=st[:, 15:16], scale=st[:, 14:15]),
             reads=[bst, bR], writes=[bR])
        P.op("pool", lambda e, Rt=Rt: e.tensor_tensor(out=Rt[:], in0=Rt[:], in1=G[:], op=ALU.mult),
             reads=[bR, bg], writes=[bR])
        P.op("pool", lambda e, Rt=Rt: e.tensor_tensor(out=Rt[:], in0=Rt[:], in1=B[:], op=ALU.add),
             reads=[bR, bb], writes=[bR])
        P.dma("pool", dst[r0:r0 + 128, :], Rt[:], reads=[bR])

    def ffn_phase(self, layer, which, src, dst, ln_idx):
        P = self.P
        T = self.T
        with self.phase() as ph:
            W13 = ph.sb([128, 8, 2 * DFF], BF16)
            W2 = ph.sb([128, 22, D], BF16)
            bW13, bW2 = [bufs(4) for _ in range(8)], bufs(22)
            w13v = self.w13[layer, which].rearrange("(kc p) n -> p kc n", p=128)
            for pc in (0, 2, 1, 3):
                for kc in range(8):
                    P.dma("pool", W13[:, kc, pc * 1408:(pc + 1) * 1408], w13v[:, kc, pc * 1408:(pc + 1) * 1408], writes=[bW13[kc][pc]])
            self.load_w(W2, bW2, self.w2[layer, which], 22, 0, D, piece=1024)
            ident, bid = self.make_ident(ph, F32)
            lnc = self.ln_consts(ph, layer, ln_idx)
            NB = 2
            XSr = Rot([ph.sb([128, D], F32) for _ in range(4)])
            XRr = Rot([ph.sb([128, D], F32) for _ in range(2)])
            xTr = [ph.sb([128, 8, 512], BF16) for _ in range(NB)]
            bxTr = [bufs(8) for _ in range(NB)]
            HT = ph.sb([128, 22, 512], BF16)
            bHT = bufs(22)
            SG = Rot([ph.sb([128, 512], F32) for _ in range(2)])
            small = Rot([ph.sb([128, 16], F32) for _ in range(2)])
            pGU = Rot([ph.ps() for _ in range(4)])
            pT = Rot([ph.ps() for _ in range(4)])
            pYi = [0]
            ntiles = T // 512
            xs_next = self.issue_x(src, 0, XSr)
            self.transpose_x(xs_next, xTr[0], bxTr[0], pT, ident, bid)
            for ti in range(ntiles):
                t0 = ti * 512
                xT, bxT = xTr[ti % NB], bxTr[ti % NB]
                if ti + 1 < ntiles:
                    xs_next = self.issue_x(src, t0 + 512, XSr)
                for j in range(22):
                    pg, bpg = pGU.next()
                    pu, bpu = pGU.next()
                    for kc in range(8):
                        P.op("pe", lambda e, pg=pg, kc=kc, j=j, xT=xT: e.matmul(
                            pg[:], W13[:, kc, j * 128:(j + 1) * 128], xT[:, kc, :], start=(kc == 0), stop=(kc == 7)),
                            reads=[bW13[kc][j // 11], bxT[kc]], writes=[bpg])
                    for kc in range(8):
                        P.op("pe", lambda e, pu=pu, kc=kc, j=j, xT=xT: e.matmul(
                            pu[:], W13[:, kc, DFF + j * 128:DFF + (j + 1) * 128], xT[:, kc, :], start=(kc == 0), stop=(kc == 7)),
                            reads=[bW13[kc][2 + j // 11], bxT[kc]], writes=[bpu])
                    sg, bsg = SG.next()
                    P.op("act", lambda e, sg=sg, pg=pg: e.activation(out=sg[:], in_=pg[:], func=AF.Silu),
                         reads=[bpg], writes=[bsg])
                    P.op("dve", lambda e, sg=sg, pu=pu, j=j: e.tensor_tensor(out=HT[:, j, :], in0=pu[:], in1=sg[:], op=ALU.mult),
                         reads=[bpu, bsg], writes=[bHT[j]])
                if ti + 1 < ntiles:
                    self.transpose_x(xs_next, xTr[(ti + 1) % NB], bxTr[(ti + 1) % NB], pT, ident, bid)
                for s in range(4):
                    k2 = 2 * (pYi[0] % 2)
                    pYi[0] += 1
                    pY = [pT.t[k2], pT.t[k2 + 1]]
                    bpY = [pT.b[k2], pT.b[k2 + 1]]
                    for hf in range(2):
                        for j in range(22):
                            P.op("pe", lambda e, hf=hf, j=j, s=s, pY=pY: e.matmul(
                                pY[hf][:], HT[:, j, s * 128:(s + 1) * 128], W2[:, j, hf * 512:(hf + 1) * 512],
                                start=(j == 0), stop=(j == 21)),
                                reads=[bHT[j], bW2[j]], writes=[bpY[hf]])
                    self.ln_epilogue(pY, bpY, src, dst, t0 + s * 128, 0.5 / DN_ALPHA, lnc, XRr, small)

    def m1_phase(self, layer, src):
        P = self.P
        T, L = self.T, self.L
        NW = C_Z
        with self.phase() as ph:
            W = ph.sb([128, 8, NW], BF16)
            bW = bufs(8)
            self.load_w(W, bW, self.w_in[layer], 8, 0, NW, piece=1156)
            ident, bid = self.make_ident(ph, F32)
            ones = ph.sb([128, 128], BF16)
            bones = Buf()
            P.op("pool", lambda e: e.memset(ones[:], 1.0), writes=[bones])
            CW = ph.sb([128, 4, 24], F32)
            bCW = Buf()
            for j in range(4):
                P.dma("sp", CW[:, j, :], self.conv_w[layer, j, :].rearrange("(c p) -> p c", p=128), writes=[bCW], slow=True)
            DTB = ph.sb([128, 8], F32)
            NEGA = ph.sb([128, 8], F32)
            bDTB, bNEGA = Buf(), Buf()
            P.dma("sp", DTB[:], self.dt_bias[layer, :].partition_broadcast(128), writes=[bDTB])
            P.dma("sp", NEGA[:], self.a_log[layer, :].partition_broadcast(128), writes=[bNEGA])
            P.op("act", lambda e: e.activation(out=NEGA[:], in_=NEGA[:], func=AF.Exp), reads=[bNEGA], writes=[bNEGA])
            P.op("dve", lambda e: e.tensor_scalar(NEGA[:], NEGA[:], -1.0, None, ALU.mult), reads=[bNEGA], writes=[bNEGA])
            CAR = ph.sb([128, 24, 3], F32)
            ZERO3 = ph.sb([128, 3], F32)
            bZ3 = Buf()
            P.op("pool", lambda e: e.memset(ZERO3[:], 0.0), writes=[bZ3])
            bCAR = bufs(24)
            NB = 2
            XSr = Rot([ph.sb([128, D], F32) for _ in range(4)])
            xTr = [ph.sb([128, 8, 512], BF16) for _ in range(NB)]
            bxTr = [bufs(8) for _ in range(NB)]
            QAr = Rot([ph.sb([128, 10, 512], BF16) for _ in range(2)])
            VAr = Rot([ph.sb([128, 4, 256], BF16) for _ in range(2)])
            BGr = Rot([ph.sb([128, 4, 16], F32) for _ in range(2)])
            TMP = Rot([ph.sb([128, 56], F32) for _ in range(2)])
            Ur = Rot([ph.sb([128, 515], F32) for _ in range(3)])
            ACr = Rot([ph.sb([128, 512], F32) for _ in range(3)])
            Y8 = ph.sb([128, 8, 512], F32)
            SQ8 = ph.sb([128, 8, 512], BF16)
            bY8, bSQ8 = bufs(8), bufs(8)
            RS8 = ph.sb([128, 8, 512], F32)
            bRS8 = bufs(8)
            OCr = Rot([ph.sb([128, 8, 512], BF16) for _ in range(2)])
            pT = Rot([ph.ps() for _ in range(2)])
            pA = Rot([ph.ps() for _ in range(3)])
            pB = Rot([ph.ps() for _ in range(1)])
            pS = Rot([ph.ps() for _ in range(2)])
            ntiles = T // 512
            xs_next = self.issue_x(src, 0, XSr)
            self.transpose_x(xs_next, xTr[0], bxTr[0], pT, ident, bid)
            for ti in range(ntiles):
                t0 = ti * 512
                seq_start = (t0 % L == 0)
                xT, bxT = xTr[ti % NB], bxTr[ti % NB]
                if ti + 1 < ntiles:
                    xs_next = self.issue_x(src, t0 + 512, XSr)
                QA, bQA = QAr.next()
                for c in range(10):
                    pa, bpa = pA.next()
                    for kc in range(8):
                        P.op("pe", lambda e, pa=pa, kc=kc, c=c, xT=xT: e.matmul(
                            pa[:], W[:, kc, c * 128:(c + 1) * 128], xT[:, kc, :], start=(kc == 0), stop=(kc == 7)),
                            reads=[bW[kc], bxT[kc]], writes=[bpa])
                    sc = 0.125 if c < 8 else 1.0
                    P.op("act", lambda e, pa=pa, c=c, QA=QA, sc=sc: e.activation(out=QA[:, c, :], in_=pa[:], func=AF.Copy, scale=sc),
                         reads=[bpa], writes=[bQA])
                P.dma("pool", self.QT[:, t0:t0 + 512].rearrange("(c p) t -> p c t", p=128), QA[:, 0:8, :], reads=[bQA])
                P.dma("pool", self.KT[:, t0:t0 + 512].rearrange("(c p) t -> p c t", p=128), QA[:, 8:10, :], reads=[bQA])
                VAt, bVA = VAr.next()
                BGt, bBG = BGr.next()
                for s in range(4):
                    pb, bpb = pB.next()
                    for kc in range(8):
                        P.op("pe", lambda e, pb=pb, kc=kc, s=s, xT=xT: e.matmul(
                            pb[:, 0:256], xT[:, kc, s * 128:(s + 1) * 128], W[:, kc, C_VA:C_VA + 256],
                            start=(kc == 0), stop=(kc == 7)), reads=[bW[kc], bxT[kc]], writes=[bpb])
                    for kc in range(8):
                        P.op("pe", lambda e, pb=pb, kc=kc, s=s, xT=xT: e.matmul(
                            pb[:, 256:272], xT[:, kc, s * 128:(s + 1) * 128], W[:, kc, C_BETA:C_BETA + 16],
                            start=(kc == 0), stop=(kc == 7)), reads=[bW[kc], bxT[kc]], writes=[bpb])
                    P.op("act", lambda e, pb=pb, s=s, VAt=VAt: e.copy(VAt[:, s, :], pb[:, 0:256]), reads=[bpb], writes=[bVA])
                    tm, btm = TMP.next()
                    P.op("act", lambda e, pb=pb, tm=tm: e.copy(tm[:, 40:56], pb[:, 256:272]), reads=[bpb], writes=[btm])
                    P.op("act", lambda e, tm=tm, s=s, BGt=BGt: e.activation(out=BGt[:, s, 0:8], in_=tm[:, 40:48], func=AF.Exp, scale=-1.0),
                         reads=[btm], writes=[bBG])
                    P.op("dve", lambda e, s=s, BGt=BGt: e.tensor_scalar(BGt[:, s, 0:8], BGt[:, s, 0:8], 1.0, None, ALU.add),
                         reads=[bBG], writes=[bBG])
                    P.op("dve", lambda e, s=s, BGt=BGt: e.reciprocal(BGt[:, s, 0:8], BGt[:, s, 0:8]), reads=[bBG], writes=[bBG])
                    P.op("dve", lambda e, tm=tm: e.tensor_tensor(out=tm[:, 0:8], in0=tm[:, 48:56], in1=DTB[:], op=ALU.add),
                         reads=[btm, bDTB], writes=[btm])
                    P.op("dve", lambda e, tm=tm: e.tensor_scalar(tm[:, 8:16], tm[:, 0:8], -1.0, None, ALU.mult),
                         reads=[btm], writes=[btm])
                    P.op("dve", lambda e, tm=tm: e.tensor_tensor(out=tm[:, 8:16], in0=tm[:, 8:16], in1=tm[:, 0:8], op=ALU.max),
                         reads=[btm], writes=[btm])
                    P.op("act", lambda e, tm=tm: e.activation(out=tm[:, 16:24], in_=tm[:, 8:16], func=AF.Exp, scale=-1.0),
                         reads=[btm], writes=[btm])
                    P.op("dve", lambda e, tm=tm: e.tensor_scalar(tm[:, 16:24], tm[:, 16:24], 1.0, None, ALU.add), reads=[btm], writes=[btm])
                    P.op("act", lambda e, tm=tm: e.activation(out=tm[:, 24:32], in_=tm[:, 16:24], func=AF.Ln),
                         reads=[btm], writes=[btm])
                    P.op("dve", lambda e, tm=tm: e.scalar_tensor_tensor(out=tm[:, 32:40], in0=tm[:, 0:8], scalar=0.0,
                                                                        in1=tm[:, 24:32], op0=ALU.max, op1=ALU.add),
                         reads=[btm], writes=[btm])
                    P.op("dve", lambda e, tm=tm, s=s, BGt=BGt: e.tensor_tensor(out=BGt[:, s, 8:16], in0=tm[:, 32:40], in1=NEGA[:], op=ALU.mult),
                         reads=[btm, bNEGA], writes=[bBG])
                P.dma("pool", self.VA[t0:t0 + 512, :].rearrange("(s p) f -> p s f", p=128), VAt[:], reads=[bVA])
                P.dma("pool", self.BG[t0:t0 + 512, :].rearrange("(s p) f -> p s f", p=128), BGt[:], reads=[bBG])
                for grp in range(3):
                    OC, bOC = OCr.next()
                    for cc in range(8):
                        c = grp * 8 + cc
                        pa, bpa = pA.next()
                        col = C_QKVB + c * 128
                        for kc in range(8):
                            P.op("pe", lambda e, pa=pa, kc=kc, col=col, xT=xT: e.matmul(
                                pa[:], W[:, kc, col:col + 128], xT[:, kc, :], start=(kc == 0), stop=(kc == 7)),
                                reads=[bW[kc], bxT[kc]], writes=[bpa])
                        U, bU = Ur.next()
                        if seq_start:
                            P.op("act", lambda e, U=U: e.copy(U[:, 0:3], ZERO3[:]), reads=[bZ3], writes=[bU])
                        else:
                            P.op("act", lambda e, U=U, c=c: e.copy(U[:, 0:3], CAR[:, c, :]), reads=[bCAR[c]], writes=[bU])
                        P.op("act", lambda e, U=U, pa=pa: e.copy(U[:, 3:515], pa[:]), reads=[bpa], writes=[bU])
                        P.op("act", lambda e, U=U, c=c: e.copy(CAR[:, c, :], U[:, 512:515]), reads=[bU], writes=[bCAR[c]])
                        A1, bA1 = ACr.next()
                        P.op("act", lambda e, U=U, A1=A1, c=c: e.activation(out=A1[:], in_=U[:, 0:512], func=AF.Identity, scale=CW[:, 0, c:c + 1]),
                             reads=[bU, bCW], writes=[bA1])
                        P.op("act", lambda e, U=U, cc=cc, c=c: e.activation(out=Y8[:, cc, :], in_=U[:, 2:514], func=AF.Identity, scale=CW[:, 2, c:c + 1]),
                             reads=[bU, bCW], writes=[bY8[cc]])
                        P.op("dve", lambda e, U=U, A1=A1, c=c: e.scalar_tensor_tensor(out=A1[:], in0=U[:, 1:513], scalar=CW[:, 1, c:c + 1],
                                                                                    in1=A1[:], op0=ALU.mult, op1=ALU.add),
                             reads=[bU, bCW, bA1], writes=[bA1])
                        P.op("dve", lambda e, U=U, cc=cc, c=c: e.scalar_tensor_tensor(out=Y8[:, cc, :], in0=U[:, 3:515], scalar=CW[:, 3, c:c + 1],
                                                                                    in1=Y8[:, cc, :], op0=ALU.mult, op1=ALU.add),
                             reads=[bU, bCW, bY8[cc]], writes=[bY8[cc]])
                        P.op("pool", lambda e, A1=A1, cc=cc: e.tensor_tensor(out=Y8[:, cc, :], in0=A1[:], in1=Y8[:, cc, :], op=ALU.add),
                             reads=[bA1, bY8[cc]], writes=[bY8[cc]])
                    for cc in range(8):
                        if grp == 2:
                            P.op("act", lambda e, cc=cc, OC=OC: e.activation(out=OC[:, cc, :], in_=Y8[:, cc, :], func=AF.Silu), reads=[bY8[cc]], writes=[bOC])
                        else:
                            P.op("act", lambda e, cc=cc: e.activation(out=Y8[:, cc, :], in_=Y8[:, cc, :], func=AF.Silu), reads=[bY8[cc]], writes=[bY8[cc]])
                    if grp < 2:
                        for cc in range(8):
                            P.op("dve", lambda e, cc=cc: e.tensor_tensor(out=SQ8[:, cc, :], in0=Y8[:, cc, :], in1=Y8[:, cc, :], op=ALU.mult),
                                 reads=[bY8[cc]], writes=[bSQ8[cc]])
                    if grp < 2:
                        for cc in range(8):
                            ps_, bps = pS.next()
                            P.op("pe", lambda e, ps_=ps_, cc=cc: e.matmul(ps_[:], ones[:], SQ8[:, cc, :], start=True, stop=True),
                                 reads=[bones, bSQ8[cc]], writes=[bps])
                            P.op("dve", lambda e, ps_=ps_, cc=cc: e.tensor_scalar(RS8[:, cc, :], ps_[:], float(NORM_EPS), None, ALU.add),
                                 reads=[bps], writes=[bRS8[cc]])
                        for cc in range(8):
                            P.op("act", lambda e, cc=cc: e.activation(out=RS8[:, cc, :], in_=RS8[:, cc, :], func=AF.Ln), reads=[bRS8[cc]], writes=[bRS8[cc]])
                        for cc in range(8):
                            P.op("act", lambda e, cc=cc: e.activation(out=RS8[:, cc, :], in_=RS8[:, cc, :], func=AF.Exp, scale=-0.5),
                                 reads=[bRS8[cc]], writes=[bRS8[cc]])
                        qs = (128.0 ** -0.5) if grp == 0 else 1.0
                        for cc in range(8):
                            P.op("dve", lambda e, OC=OC, cc=cc, qs=qs: e.scalar_tensor_tensor(
                                out=OC[:, cc, :], in0=Y8[:, cc, :], scalar=float(qs), in1=RS8[:, cc, :], op0=ALU.mult, op1=ALU.mult),
                                reads=[bY8[cc], bRS8[cc]], writes=[bOC])
                    P.dma("pool", self.QKVB[grp * 1024:(grp + 1) * 1024, t0:t0 + 512].rearrange("(c p) t -> p c t", p=128),
                          OC[:], reads=[bOC])
                if ti + 1 < ntiles:
                    self.transpose_x(xs_next, xTr[(ti + 1) % NB], bxTr[(ti + 1) % NB], pT, ident, bid)

    def m2_phase(self, layer):
        P = self.P
        T, L = self.T, self.L
        with self.phase() as ph:
            EB = ph.sb([128, 2 * 16 * 128], BF16)
            bEB = Buf()
            identF2, bidF2 = self.make_ident(ph, F32)
            identB2 = ph.sb([128, 128], BF16)
            bidB2 = Buf()
            P.op("pool", lambda e: e.tensor_copy(identB2[:], identF2[:]), reads=[bidF2], writes=[bidB2])
            for q4 in range(4):
                sl = slice(q4 * 1024, (q4 + 1) * 1024)
                stg, bstg = Buf(), None
                STG = ph.sb([128, 1024], F32)
                bSTG = Buf()
                P.dma("sp", STG[:], self.pbias[:, sl], writes=[bSTG])
                P.op("dve", lambda e, STG=STG, sl=sl: e.tensor_copy(EB[:, sl], STG[:]), reads=[bSTG], writes=[bEB])
            SK = ph.sb([1, 16], F32)
            SKR = ph.sb([1, 16, 128], BF16)
            ONE1 = ph.sb([1, 128], F32)
            bSK, bSKR = Buf(), Buf()
            P.dma("sp", SK[:], self.sinks[layer:layer + 1, :], writes=[bSK])
            P.op("act", lambda e: e.activation(out=SK[:], in_=SK[:], func=AF.Exp), reads=[bSK], writes=[bSK])
            P.op("dve", lambda e: e.memset(ONE1[:], 1.0), writes=[bSKR])
            for h in range(16):
                P.op("dve", lambda e, h=h: e.tensor_scalar(SKR[0:1, h, :], ONE1[0:1, :], SK[0:1, h:h + 1], None, ALU.mult),
                     reads=[bSK, bSKR], writes=[bSKR])
            ones = ph.sb([128, 64], BF16)
            bones = Buf()
            P.op("pool", lambda e: e.memset(ones[:], 1.0), writes=[bones])
            Qr = Rot([ph.sb([64, 16, 512], BF16) for _ in range(2)])
            Kr = Rot([ph.sb([64, 4, 640], BF16) for _ in range(2)])
            Vr = Rot([ph.sb([128, 5, 256], BF16) for _ in range(2)])
            ATr = Rot([ph.sb([64, 16, 512], BF16) for _ in range(2)])
            Pr = Rot([ph.sb([128, 512], BF16) for _ in range(4)])
            Dr = Rot([ph.sb([64, 512], F32) for _ in range(2)])
            pSr = Rot([ph.ps() for _ in range(4)])
            pOr = Rot([ph.ps() for _ in range(2)])
            pDr = Rot([ph.ps() for _ in range(2)])
            EBv = EB[:].rearrange("p (b h q) -> p b h q", b=2, h=16)
            def m2_loads(t0):
                Qt, bQ = Qr.next()
                Kt, bK = Kr.next()
                Vt, bV = Vr.next()
                P.dma("sp", Qt[:], self.QT[:, t0:t0 + 512].rearrange("(h d) t -> d h t", d=64), writes=[bQ])
                if t0 % L == 0:
                    P.dma("sp", Kt[:, :, 128:640], self.KT[:, t0:t0 + 512].rearrange("(g d) t -> d g t", d=64), writes=[bK])
                    P.dma("sp", Vt[:, 1:5, :], self.VA[t0:t0 + 512, :].rearrange("(b p) c -> p b c", p=128), writes=[bV])
                else:
                    P.dma("sp", Kt[:], self.KT[:, t0 - 128:t0 + 512].rearrange("(g d) t -> d g t", d=64), writes=[bK])
                    P.dma("sp", Vt[:], self.VA[t0 - 128:t0 + 512, :].rearrange("(b p) c -> p b c", p=128), writes=[bV])
                return Qt, bQ, Kt, bK, Vt, bV

            nxt = m2_loads(0)
            for ti in range(T // 512):
                t0 = ti * 512
                seq_start = (t0 % L == 0)
                Qt, bQ, Kt, bK, Vt, bV = nxt
                if ti + 1 < T // 512:
                    nxt = m2_loads(t0 + 512)
                At, bA = ATr.next()
                for i in range(4):
                    first = seq_start and i == 0
                    sbl = [1] if first else [0, 1]
                    for g in range(4):
                        Pb = {}
                        for sb_ in sbl:
                            ps_, bps = pSr.next()
                            P.op("pe", lambda e, ps_=ps_, g=g, i=i, sb_=sb_, Kt=Kt, Qt=Qt: e.matmul(
                                ps_[:], Kt[:, g, (i + sb_) * 128:(i + sb_ + 1) * 128], Qt[:, 4 * g:4 * g + 4, i * 128:(i + 1) * 128],
                                start=True, stop=False), reads=[bK, bQ], writes=[bps])
                            P.op("pe", lambda e, ps_=ps_, g=g, sb_=sb_: e.matmul(
                                ps_[:], identB2[:], EBv[:, sb_, 4 * g:4 * g + 4, :], start=False, stop=True),
                                reads=[bidB2, bEB], writes=[bps])
                            Pt, bP = Pr.next()
                            P.op("act", lambda e, Pt=Pt, ps_=ps_: e.activation(out=Pt[:], in_=ps_[:], func=AF.Exp),
                                 reads=[bps], writes=[bP])
                            Pb[sb_] = (Pt, bP)
                        po, bpo = pOr.next()
                        pd, bpd = pDr.next()
                        for n_, sb_ in enumerate(sbl):
                            Pt, bP = Pb[sb_]
                            P.op("pe", lambda e, po=po, Pt=Pt, sb_=sb_, g=g, i=i, Vt=Vt, n_=n_: e.matmul(
                                po[0:64, :], Vt[:, i + sb_, g * 64:(g + 1) * 64], Pt[:], start=(n_ == 0), stop=(n_ == len(sbl) - 1)),
                                reads=[bV, bP], writes=[bpo])
                        for n_, sb_ in enumerate(sbl):
                            Pt, bP = Pb[sb_]
                            P.op("pe", lambda e, pd=pd, Pt=Pt, n_=n_: e.matmul(
                                pd[0:64, :], ones[:], Pt[:], start=(n_ == 0), stop=False),
                                reads=[bones, bP], writes=[bpd])
                        P.op("pe", lambda e, pd=pd, g=g: e.matmul(
                            pd[0:64, :], ones[0:1, :], SKR[0:1, 4 * g:4 * g + 4, :], start=False, stop=True),
                            reads=[bones, bSKR], writes=[bpd])
                        Dt, bD = Dr.next()
                        P.op("act", lambda e, Dt=Dt, pd=pd: e.activation(out=Dt[:], in_=pd[0:64, :], func=AF.Ln), reads=[bpd], writes=[bD])
                        P.op("act", lambda e, Dt=Dt: e.activation(out=Dt[:], in_=Dt[:], func=AF.Exp, scale=-1.0), reads=[bD], writes=[bD])
                        P.op("dve", lambda e, Dt=Dt, po=po, At=At, g=g, i=i: e.tensor_tensor(
                            out=At[:, 4 * g:4 * g + 4, i * 128:(i + 1) * 128], in0=po[0:64, :].rearrange("p (h q) -> p h q", h=4),
                            in1=Dt[:].rearrange("p (h q) -> p h q", h=4), op=ALU.mult), reads=[bpo, bD], writes=[bA])
                P.dma("pool", self.AT[:, t0:t0 + 512].rearrange("(h d) t -> d h t", d=64), At[:], reads=[bA])

    def m3_phase(self, layer, src):
        P = self.P
        T, L = self.T, self.L
        SD = SOLVE_DT
        with self.phase() as ph:
            WZ = ph.sb([128, 8, 1024], BF16)
            bWZ = bufs(8)
            self.load_w(WZ, bWZ, self.w_in[layer], 8, C_Z, C_Z + 1024, piece=1024)
            identF, bidF = self.make_ident(ph, F32)
            identB = ph.sb([128, 128], BF16)
            bidB = Buf()
            P.op("pool", lambda e: e.tensor_copy(identB[:], identF[:]), reads=[bidF], writes=[bidB])
            if SD == BF16:
                identS, bidS = identB, bidB
            else:
                identS, bidS = identF, bidF
            bC = Buf()

            def mk(shape=(128, 128)):
                return ph.sb(list(shape), F32)

            TRI, LGT, MSU, ONES, SELA, SELB, SAME = mk(), mk(), mk(), mk(), mk(), mk(), mk()
            P.op("pool", lambda e: e.memset(TRI[:], 1.0), writes=[bC])
            P.op("pool", lambda e: e.affine_select(out=TRI[:], in_=TRI[:], pattern=[[1, 128]], compare_op=ALU.is_ge,
                                                   fill=0.0, base=0, channel_multiplier=-1), reads=[bC], writes=[bC])
            P.op("pool", lambda e: e.memset(TRI[0:64, 64:128], 0.0), reads=[bC], writes=[bC])
            P.op("pool", lambda e: e.memset(LGT[:], 1.0), reads=[bC], writes=[bC])
            P.op("pool", lambda e: e.affine_select(out=LGT[:], in_=LGT[:], pattern=[[-1, 128]], compare_op=ALU.is_gt,
                                                   fill=0.0, base=0, channel_multiplier=1), reads=[bC], writes=[bC])
            P.op("pool", lambda e: e.memset(LGT[64:128, 0:64], 0.0), reads=[bC], writes=[bC])
            P.op("pool", lambda e: e.memset(MSU[:], 1.0), reads=[bC], writes=[bC])
            P.op("pool", lambda e: e.affine_select(out=MSU[:], in_=MSU[:], pattern=[[1, 128]], compare_op=ALU.is_gt,
                                                   fill=0.0, base=0, channel_multiplier=-1), reads=[bC], writes=[bC])
            P.op("pool", lambda e: e.memset(MSU[0:64, 64:128], 0.0), reads=[bC], writes=[bC])
            P.op("pool", lambda e: e.memset(ONES[:], 1.0), reads=[bC], writes=[bC])
            P.op("pool", lambda e: e.memset(SELA[:], 0.0), reads=[bC], writes=[bC])
            P.op("pool", lambda e: e.memset(SELA[0:64, :], 1.0), reads=[bC], writes=[bC])
            P.op("pool", lambda e: e.memset(SELB[:], 0.0), reads=[bC], writes=[bC])
            P.op("pool", lambda e: e.memset(SELB[64:128, :], 1.0), reads=[bC], writes=[bC])
            P.op("pool", lambda e: e.memset(SAME[:], 0.0), reads=[bC], writes=[bC])
            P.op("pool", lambda e: e.memset(SAME[0:64, 0:64], 1.0), reads=[bC], writes=[bC])
            P.op("pool", lambda e: e.memset(SAME[64:128, 64:128], 1.0), reads=[bC], writes=[bC])
            NEGM8 = mk((128, 8, 128))
            MSU8 = mk((128, 8, 128))
            ID8 = ph.sb([128, 8, 128], SD)
            for h in range(8):
                P.op("pool", lambda e, h=h: e.tensor_copy(NEGM8[:, h, :], TRI[:]), reads=[bC], writes=[bC])
                P.op("pool", lambda e, h=h: e.tensor_copy(MSU8[:, h, :], MSU[:]), reads=[bC], writes=[bC])
                P.op("pool", lambda e, h=h: e.tensor_copy(ID8[:, h, :], identF[:]), reads=[bC, bidF], writes=[bC])
            DNG = mk((128, 8, 128))
            bDNG = Buf()
            for h in range(8):
                P.dma("sp", DNG[:, h, :], self.dn_g[layer, :].partition_broadcast(128), writes=[bDNG])
            S = ph.sb([128, 8, 128], F32)
            SB = ph.sb([128, 8, 128], BF16)
            bS, bSB = bufs(8), bufs(8)
            VN = ph.sb([128, 8, 128], BF16)
            bVN = bufs(8)
            P.op("pool", lambda e: e.memset(VN[:], 0.0), writes=bVN)
            XSr = Rot([ph.sb([128, D], F32) for _ in range(2)])
            xT1 = ph.sb([128, 8, 512], BF16)
            bxT1 = bufs(8)
            QKVr = Rot([ph.sb([128, 24, 512], BF16) for _ in range(2)])
            BGr = Rot([ph.sb([128, 4, 16], F32) for _ in range(2)])
            OGTr = Rot([ph.sb([128, 8, 512], BF16) for _ in range(2)])
            Rt = ph.sb([128, 8, 128], F32)
            bRt = Buf()
            ET = ph.sb([128, 8, 128], F32)
            ETS = ph.sb([128, 8, 128], F32)
            EGB = ph.sb([128, 8, 128], F32)
            bET, bETS, bEGB = Buf(), Buf(), Buf()
            YR = [ph.sb([128, 8, 256], SD) for _ in range(2)]
            bYR = [bufs(2), bufs(2)]
            Z = [ph.sb([128, 8, 128], SD) for _ in range(2)]
            bZ = [bufs(2), bufs(2)]
            XS = ph.sb([128, 8, 128], BF16)
            bXS = bufs(2)
            KG = ph.sb([128, 8, 128], BF16)
            KTOK = ph.sb([128, 8, 128], BF16)
            bKTOK = Buf()
            Vt = ph.sb([128, 8, 128], BF16)
            bKG, bVt = Buf(), Buf()
            O = ph.sb([128, 8, 128], F32)
            bO = Buf()
            SQ = ph.sb([128, 8, 128], F32)
            OG = ph.sb([128, 8, 128], BF16)
            bSQ, bZG, bOG = Buf(), Buf(), Buf()
            HH = []
            for _ in range(2):
                HH.append((ph.sb([128, 8], F32), Buf(), ph.sb([128, 64], F32), Buf(), ph.sb([128, 8, 128], BF16), Buf(),
                           ph.sb([128, 8, 128], BF16), Buf(), ph.sb([128, 8, 128], BF16), Buf(),
                           ph.sb([128, 8, 128], F32), bufs(8), ph.sb([128, 8, 128], BF16), bufs(2)))
            ZG4 = ph.sb([128, 4, 8, 128], BF16)
            DNGb = DNG
            pF = Rot([ph.ps() for _ in range(6)])
            pH = Rot([ph.ps(BF16) for _ in range(2)])
            YRv = [y[:].rearrange("p h c -> p (h c)") for y in YR]

            def flat(t, g):
                return t[:, 4 * g:4 * g + 4, :]

            def m3_loads(t0):
                QKV, bQKV = QKVr.next()
                for grp in range(3):
                    P.dma("sp", QKV[:, grp * 8:(grp + 1) * 8, :],
                          self.QKVB[grp * 1024:(grp + 1) * 1024, t0:t0 + 512].rearrange("(c p) t -> p c t", p=128), writes=[bQKV])
                BGt, bBG = BGr.next()
                P.dma("sp", BGt[:], self.BG[t0:t0 + 512, :].rearrange("(s p) f -> p s f", p=128), writes=[bBG])
                return QKV, bQKV, BGt, bBG

            def prep(cx):
                tk, s, QKV, bQKV, BGt, bBG, OGT, bOGT = cx["tk"], cx["s"], cx["QKV"], cx["bQKV"], cx["BGt"], cx["bBG"], cx["OGT"], cx["bOGT"]
                NBG, bNBG, SM, bSM, AIT, bAIT, KT2, bKT2, QD, bQD, UB, bUB, WT, bWT = cx["H"]
                beta = BGt[:, s, 0:8]
                graw = BGt[:, s, 8:16]
                P.op("dve", lambda e, beta=beta: e.tensor_scalar(NBG[:], beta, -1.0, None, ALU.mult), reads=[bBG], writes=[bNBG])
                for h in range(8):
                    P.op("dve", lambda e, h=h, graw=graw: e.tensor_scalar(Rt[:, h, :], TRI[:], graw[:, h:h + 1], None, ALU.mult),
                         reads=[bC, bBG], writes=[bRt])
                px, bpx = pF.next()
                P.op("pe", lambda e, px=px, graw=graw: e.matmul(px[:, 0:8], SELA[:], graw, start=True, stop=True),
                     reads=[bC, bBG], writes=[bpx])
                P.op("pe", lambda e, px=px, graw=graw: e.matmul(px[:, 8:16], SELB[:], graw, start=True, stop=True),
                     reads=[bC, bBG], writes=[bpx])
                P.op("pe", lambda e, px=px, graw=graw: e.matmul(px[:, 16:24], TRI[:], graw, start=True, stop=True),
                     reads=[bC, bBG], writes=[bpx])
                P.op("pe", lambda e, px=px, graw=graw: e.matmul(px[:, 24:32], SAME[:], graw, start=True, stop=True),
                     reads=[bC, bBG], writes=[bpx])
                P.op("act", lambda e, px=px: e.activation(out=SM[:, 0:24], in_=px[:, 0:24], func=AF.Exp), reads=[bpx], writes=[bSM])
                P.op("act", lambda e, px=px: e.copy(SM[:, 56:64], px[:, 16:24]), reads=[bpx, bSM], writes=[bSM])
                P.op("dve", lambda e, px=px: e.tensor_tensor(out=SM[:, 32:40], in0=px[:, 24:32], in1=SM[:, 56:64], op=ALU.subtract),
                     reads=[bpx, bSM], writes=[bSM])
                P.op("act", lambda e: e.activation(out=SM[:, 24:32], in_=SM[:, 32:40], func=AF.Exp), reads=[bSM], writes=[bSM])
                for g in range(2):
                    pg, bpg = pF.next()
                    rv = flat(Rt, g)
                    P.op("pe", lambda e, pg=pg, rv=rv: e.matmul(pg[:], LGT[:], rv, start=True, stop=True), reads=[bC, bRt], writes=[bpg])
                    P.op("act", lambda e, pg=pg, g=g: e.activation(out=flat(ET, g), in_=pg[:].rearrange("p (h c) -> p h c", h=4), func=AF.Exp),
                         reads=[bpg], writes=[bET])
                    pg2, bpg2 = pF.next()
                    P.op("pe", lambda e, pg2=pg2, rv=rv: e.matmul(pg2[:], ONES[:], rv, start=True, stop=True), reads=[bC, bRt], writes=[bpg2])
                    P.op("act", lambda e, pg2=pg2, g=g: e.activation(out=flat(EGB, g), in_=pg2[:].rearrange("p (h c) -> p h c", h=4), func=AF.Exp),
                         reads=[bpg2], writes=[bEGB])
                P.op("pool", lambda e: e.tensor_tensor(out=ETS[:], in0=ET[:], in1=MSU8[:], op=ALU.mult), reads=[bET, bC], writes=[bETS])
                P.op("pool", lambda e: e.tensor_tensor(out=ET[:], in0=ET[:], in1=NEGM8[:], op=ALU.mult), reads=[bET, bC], writes=[bET])
                yield
                cur = 0
                for g in range(2):
                    pk, bpk = pF.next()
                    pq, bpq = pF.next()
                    for hh in range(4):
                        h = 4 * g + hh
                        P.op("pe", lambda e, pk=pk, h=h, hh=hh, QKV=QKV, tk=tk: e.matmul(
                            pk[:, hh * 128:(hh + 1) * 128], QKV[:, 8 + h, tk], QKV[:, 8 + h, tk], start=True, stop=True),
                            reads=[bQKV], writes=[bpk])
                        P.op("pe", lambda e, pq=pq, h=h, hh=hh, QKV=QKV, tk=tk: e.matmul(
                            pq[:, hh * 128:(hh + 1) * 128], QKV[:, 8 + h, tk], QKV[:, h, tk], start=True, stop=True),
                            reads=[bQKV], writes=[bpq])
                    for hh in range(4):
                        h = 4 * g + hh
                        P.op("dve", lambda e, pk=pk, h=h, hh=hh: e.scalar_tensor_tensor(
                            out=YR[0][:, h, 0:128], in0=pk[:, hh * 128:(hh + 1) * 128], scalar=NBG[:, h:h + 1],
                            in1=ETS[:, h, :], op0=ALU.mult, op1=ALU.mult), reads=[bpk, bNBG, bETS], writes=[bYR[0][g]])
                    P.op("dve", lambda e, pq=pq, g=g: e.tensor_tensor(out=flat(AIT, g), in0=pq[:].rearrange("p (h c) -> p h c", h=4),
                                                                     in1=flat(ET, g), op=ALU.mult), reads=[bpq, bET], writes=[bAIT])
                yield
                for g in range(2):
                    P.op("pool", lambda e, g=g: e.tensor_tensor(out=YR[1][:, 4 * g:4 * g + 4, 128:256], in0=YR[0][:, 4 * g:4 * g + 4, 0:128],
                                                                in1=flat(ID8, g), op=ALU.add), reads=[bYR[0][g], bC], writes=[bYR[1][g]])
                    if SD == BF16:
                        pz, bpz = pH.next()
                    else:
                        pz, bpz = pF.next()
                    for hh in range(4):
                        h = 4 * g + hh
                        P.op("pe", lambda e, pz=pz, h=h, hh=hh: e.transpose(pz[:, hh * 128:(hh + 1) * 128], YR[0][:, h, 0:128], identS[:]),
                             reads=[bYR[0][g], bidS], writes=[bpz])
                    P.op("act", lambda e, pz=pz, g=g: e.copy(flat(Z[0], g), pz[:, 0:512].rearrange("p (h c) -> p h c", h=4)),
                         reads=[bpz], writes=[bZ[0][g]])
                for k in range(6):
                    yield
                    a, b_ = k % 2, (k + 1) % 2
                    for g in range(2):
                        last = (k == 5)
                        if not last:
                            pz, bpz = pF.next()
                            for hh in range(4):
                                h = 4 * g + hh
                                P.op("pe", lambda e, pz=pz, h=h, hh=hh, a=a: e.matmul(
                                    pz[:, hh * 128:(hh + 1) * 128], YR[a][:, h, 0:128], Z[a][:, h, :], start=True, stop=True),
                                    reads=[bYR[a][g], bZ[a][g]], writes=[bpz])
                            P.op("act", lambda e, pz=pz, g=g, b_=b_: e.copy(flat(Z[b_], g), pz[:].rearrange("p (h c) -> p h c", h=4)),
                                 reads=[bpz], writes=[bZ[b_][g]])
                        if k == 0:
                            py, bpy = pF.next()
                            for hh in range(4):
                                h = 4 * g + hh
                                P.op("pe", lambda e, py=py, h=h, hh=hh: e.matmul(
                                    py[:, hh * 128:(hh + 1) * 128], Z[0][:, h, :], YR[0][:, h, 0:128], start=True, stop=True),
                                    reads=[bYR[0][g], bZ[0][g]], writes=[bpy])
                            P.op("act", lambda e, py=py, g=g: e.copy(YR[1][:, 4 * g:4 * g + 4, 0:128], py[:].rearrange("p (h c) -> p h c", h=4)),
                                 reads=[bpy], writes=[bYR[1][g]])
                        elif not last:
                            for half in range(2):
                                py, bpy = pF.next()
                                for hh in range(2):
                                    h = 4 * g + 2 * half + hh
                                    P.op("pe", lambda e, py=py, h=h, hh=hh, a=a: e.matmul(
                                        py[:, hh * 256:(hh + 1) * 256], Z[a][:, h, :], YR[a][:, h, :], start=True, stop=True),
                                        reads=[bYR[a][g], bZ[a][g]], writes=[bpy])
                                h0 = 4 * g + 2 * half
                                pv = py[:].rearrange("p (h c) -> p h c", h=2)
                                P.op("act", lambda e, pv=pv, h0=h0, b_=b_: e.copy(YR[b_][:, h0:h0 + 2, 0:128], pv[:, :, 0:128]),
                                     reads=[bpy], writes=[bYR[b_][g]])
                                P.op("dve", lambda e, pv=pv, h0=h0, a=a, b_=b_: e.tensor_tensor(
                                    out=YR[b_][:, h0:h0 + 2, 128:256], in0=pv[:, :, 128:256], in1=YR[a][:, h0:h0 + 2, 128:256], op=ALU.add),
                                    reads=[bpy, bYR[a][g]], writes=[bYR[b_][g]])
                        else:
                            py, bpy = pF.next()
                            for hh in range(4):
                                h = 4 * g + hh
                                P.op("pe", lambda e, py=py, h=h, hh=hh, a=a: e.matmul(
                                    py[:, hh * 128:(hh + 1) * 128], Z[a][:, h, :], YR[a][:, h, 128:256], start=True, stop=True),
                                    reads=[bYR[a][g], bZ[a][g]], writes=[bpy])
                            P.op("dve", lambda e, py=py, g=g, a=a: e.tensor_tensor(
                                out=flat(XS, g), in0=py[:].rearrange("p (h c) -> p h c", h=4), in1=YR[a][:, 4 * g:4 * g + 4, 128:256], op=ALU.add),
                                reads=[bpy, bYR[a][g]], writes=[bXS[g]])
                yield
                pkt, bpkt = pH.next()
                for h in range(8 if (M3SUB & 1) else 0):
                    P.op("pe", lambda e, pkt=pkt, h=h, QKV=QKV, tk=tk: e.transpose(pkt[:, h * 128:(h + 1) * 128], QKV[:, 8 + h, tk], identB[:]),
                         reads=[bQKV, bidB], writes=[bpkt])
                P.op("act", lambda e, pkt=pkt: e.copy(KTOK[:].rearrange("p h c -> p (h c)"), pkt[:]), reads=[bpkt], writes=[bKTOK])
                for h in range(8):
                    P.op("dve", lambda e, h=h: e.tensor_scalar(KG[:, h, :], KTOK[:, h, :], SM[:, 16 + h:17 + h], None, ALU.mult),
                         reads=[bKTOK, bSM], writes=[bKG])
                    P.op("act", lambda e, h=h: e.activation(out=KT2[:, h, :], in_=KTOK[:, h, :], func=AF.Identity, scale=SM[:, 24 + h:25 + h]),
                         reads=[bKTOK, bSM], writes=[bKT2])
                pvt, bpvt = pH.next()
                for h in range(8 if (M3SUB & 2) else 0):
                    P.op("pe", lambda e, pvt=pvt, h=h, QKV=QKV, tk=tk: e.transpose(pvt[:, h * 128:(h + 1) * 128], QKV[:, 16 + h, tk], identB[:]),
                         reads=[bQKV, bidB], writes=[bpvt])
                if M3SUB & 2:
                    P.op("act", lambda e, pvt=pvt: e.copy(Vt[:].rearrange("p h c -> p (h c)"), pvt[:]), reads=[bpvt], writes=[bVt])
                if M3SUB & 4:
                    P.op("pool", lambda e, QKV=QKV, tk=tk: e.tensor_tensor(out=QD[:], in0=QKV[:, 0:8, tk], in1=EGB[:], op=ALU.mult),
                         reads=[bQKV, bEGB], writes=[bQD])
                yield
                for g in range(2):
                    pu, bpu = pF.next()
                    pw, bpw = pF.next()
                    for hh in range(4):
                        h = 4 * g + hh
                        P.op("pe", lambda e, pu=pu, h=h, hh=hh: e.matmul(pu[:, hh * 128:(hh + 1) * 128], XS[:, h, :], Vt[:, h, :], start=True, stop=True),
                             reads=[bXS[g], bVt], writes=[bpu])
                        P.op("pe", lambda e, pw=pw, h=h, hh=hh: e.matmul(pw[:, hh * 128:(hh + 1) * 128], KG[:, h, :], XS[:, h, :], start=True, stop=True),
                             reads=[bXS[g], bKG], writes=[bpw])
                    for hh in range(4):
                        h = 4 * g + hh
                        P.op("act", lambda e, pu=pu, h=h, hh=hh, beta=beta: e.activation(out=UB[:, h, :], in_=pu[:, hh * 128:(hh + 1) * 128], func=AF.Identity,
                                                                                      scale=beta[:, h:h + 1]), reads=[bpu, bBG], writes=[bUB[h]])
                    P.op("act", lambda e, pw=pw, g=g: e.copy(flat(WT, g), pw[:].rearrange("p (h c) -> p h c", h=4)), reads=[bpw], writes=[bWT[g]])

            def rec(cx):
                tk, s, QKV, bQKV, BGt, bBG, OGT, bOGT = cx["tk"], cx["s"], cx["QKV"], cx["bQKV"], cx["BGt"], cx["bBG"], cx["OGT"], cx["bOGT"]
                NBG, bNBG, SM, bSM, AIT, bAIT, KT2, bKT2, QD, bQD, UB, bUB, WT, bWT = cx["H"]
                beta = BGt[:, s, 0:8]
                graw = BGt[:, s, 8:16]
                if cx["tile_first"]:
                    for sb4 in range(4):
                        Xs_, bXs_ = XSr.next()
                        P.dma("sp", Xs_[:], src[cx["t0"] + sb4 * 128:cx["t0"] + (sb4 + 1) * 128, :], writes=[bXs_])
                        self.transpose_x([(Xs_, bXs_)], xT1[:, :, sb4 * 128:(sb4 + 1) * 128], bxT1, pF, identF, bidF)
                    for sb4 in range(4):
                        for hf in range(2):
                            pz_, bpz_ = pF.next()
                            for kc in range(8):
                                P.op("pe", lambda e, pz_=pz_, kc=kc, hf=hf, sb4=sb4: e.matmul(pz_[:], xT1[:, kc, sb4 * 128:(sb4 + 1) * 128], WZ[:, kc, hf * 512:(hf + 1) * 512],
                                                                                           start=(kc == 0), stop=(kc == 7)),
                                     reads=[bxT1[kc], bWZ[kc]], writes=[bpz_])
                            P.op("act", lambda e, pz_=pz_, hf=hf, sb4=sb4: e.activation(out=ZG4[:, sb4, 4 * hf:4 * hf + 4, :], in_=pz_[:].rearrange("p (h c) -> p h c", h=4), func=AF.Silu),
                                 reads=[bpz_], writes=[bZG])
                        P.op("pool", lambda e, sb4=sb4: e.tensor_tensor(out=ZG4[:, sb4, :, :], in0=ZG4[:, sb4, :, :], in1=DNGb[:], op=ALU.mult), reads=[bZG, bDNG], writes=[bZG])
                        yield
                if cx["seq_reset"]:
                    P.op("pool", lambda e: e.memset(S[:], 0.0), writes=bS)
                    P.op("pool", lambda e: e.memset(SB[:], 0.0), writes=bSB)
                for ck in range(2):
                    rows = slice(ck * 64, (ck + 1) * 64)
                    for g in range(2):
                        yield
                        pv_, bpv = pF.next()
                        for hh in range(4):
                            h = 4 * g + hh
                            P.op("pe", lambda e, pv_=pv_, h=h, hh=hh: e.matmul(pv_[:, hh * 128:(hh + 1) * 128], WT[:, h, :], SB[:, h, :], start=True, stop=True),
                                 reads=[bWT[g], bSB[h]], writes=[bpv])
                        for hh in range(4):
                            h = 4 * g + hh
                            P.op("dve", lambda e, pv_=pv_, h=h, hh=hh, rows=rows: e.scalar_tensor_tensor(
                                out=VN[rows, h, :], in0=pv_[rows, hh * 128:(hh + 1) * 128], scalar=NBG[rows, h:h + 1], in1=UB[rows, h, :],
                                op0=ALU.mult, op1=ALU.add), reads=[bpv, bNBG, bUB[h]], writes=[bVN[h]])
                        po, bpo = pF.next()
                        for hh in range(4):
                            h = 4 * g + hh
                            P.op("pe", lambda e, po=po, h=h, hh=hh: e.matmul(po[:, hh * 128:(hh + 1) * 128], QD[:, h, :], SB[:, h, :], start=True, stop=False),
                                 reads=[bQD, bSB[h]], writes=[bpo])
                            P.op("pe", lambda e, po=po, h=h, hh=hh: e.matmul(po[:, hh * 128:(hh + 1) * 128], AIT[:, h, :], VN[:, h, :], start=False, stop=True),
                                 reads=[bAIT, bVN[h]], writes=[bpo])
                        P.op("act", lambda e, po=po, g=g, rows=rows: e.copy(O[rows, 4 * g:4 * g + 4, :], po[rows, :].rearrange("p (h c) -> p h c", h=4)),
                             reads=[bpo], writes=[bO])
                        ps_, bps = pF.next()
                        for hh in range(4):
                            h = 4 * g + hh
                            P.op("pe", lambda e, ps_=ps_, h=h, hh=hh, rows=rows: e.matmul(ps_[:, hh * 128:(hh + 1) * 128], KT2[rows, h, :], VN[rows, h, :], start=True, stop=True),
                                 reads=[bKT2, bVN[h]], writes=[bps])
                        for hh in range(4):
                            h = 4 * g + hh
                            P.op("dve", lambda e, ps_=ps_, h=h, hh=hh, ck=ck: e.scalar_tensor_tensor(
                                out=S[:, h, :], in0=S[:, h, :], scalar=SM[:, ck * 8 + h:ck * 8 + h + 1], in1=ps_[:, hh * 128:(hh + 1) * 128],
                                op0=ALU.mult, op1=ALU.add), reads=[bps, bSM, bS[h]], writes=[bS[h]])
                            P.op("act", lambda e, h=h: e.copy(SB[:, h, :], S[:, h, :]), reads=[bS[h]], writes=[bSB[h]])
                yield
                P.op("pool", lambda e: e.tensor_tensor(out=SQ[:], in0=O[:], in1=O[:], op=ALU.mult), reads=[bO], writes=[bSQ])
                P.op("dve", lambda e: e.tensor_reduce(out=SM[:, 40:48], in_=SQ[:], axis=AX.X, op=ALU.add), reads=[bSQ, bSM], writes=[bSM])
                P.op("dve", lambda e: e.tensor_scalar(SM[:, 48:56], SM[:, 40:48], 1.0 / 128.0, float(NORM_EPS), ALU.mult, ALU.add), reads=[bSM], writes=[bSM])
                P.op("act", lambda e: e.activation(out=SM[:, 48:56], in_=SM[:, 48:56], func=AF.Ln), reads=[bSM], writes=[bSM])
                P.op("act", lambda e: e.activation(out=SM[:, 48:56], in_=SM[:, 48:56], func=AF.Exp, scale=-0.5), reads=[bSM], writes=[bSM])
                for h in range(8):
                    P.op("dve", lambda e, h=h: e.scalar_tensor_tensor(out=OG[:, h, :], in0=O[:, h, :], scalar=SM[:, 48 + h:49 + h], in1=ZG4[:, s, h, :],
                                                                      op0=ALU.mult, op1=ALU.mult), reads=[bO, bSM, bZG], writes=[bOG])
                pt_, bpt = pH.next()
                for h in range(8):
                    P.op("pe", lambda e, pt_=pt_, h=h: e.transpose(pt_[:, h * 128:(h + 1) * 128], OG[:, h, :], identB[:]),
                         reads=[bOG, bidB], writes=[bpt])
                P.op("act", lambda e, pt_=pt_, OGT=OGT, tk=tk: e.copy(OGT[:, :, tk], pt_[:].rearrange("p (h c) -> p h c", h=8)),
                     reads=[bpt], writes=[bOGT])
                if cx["tile_last"]:
                    P.dma("pool", self.OGT[:, cx["t0"]:cx["t0"] + 512].rearrange("(c p) t -> p c t", p=128), OGT[:], reads=[bOGT])

            def interleave(g1, g2):
                gens = [g for g in (g1, g2) if g is not None]
                while gens:
                    for g in list(gens):
                        try:
                            next(g)
                        except StopIteration:
                            gens.remove(g)

            ntl = T // 512
            tl = {0: m3_loads(0)}
            prev = None
            for ti in range(ntl):
                t0 = ti * 512
                QKV, bQKV, BGt, bBG = tl.pop(ti)
                if ti + 1 < ntl:
                    tl[ti + 1] = m3_loads(t0 + 512)
                OGT, bOGT = OGTr.next()
                for s in range(4):
                    gb = ti * 4 + s
                    cx = dict(tk=slice(s * 128, (s + 1) * 128), s=s, QKV=QKV, bQKV=bQKV, BGt=BGt, bBG=bBG, OGT=OGT, bOGT=bOGT,
                              H=HH[gb % 2], t0=t0, tile_first=(s == 0), tile_last=(s == 3), seq_reset=(s == 0 and t0 % L == 0))
                    interleave(prep(cx), rec(prev) if prev is not None else None)
                    prev = cx
            interleave(rec(prev), None)

    def m4_phase(self, layer, src, dst):
        P = self.P
        T = self.T
        with self.phase() as ph:
            WA = ph.sb([128, 8, D], BF16)
            WB = ph.sb([128, 8, D], BF16)
            WO = ph.sb([128, 8, D], BF16)
            WG = ph.sb([128, 8, 2 * D], BF16)
            bWA, bWB, bWO, bWG = bufs(8), bufs(8), bufs(8), bufs(8)
            self.load_w(WA, bWA, self.w_a[layer], 8, 0, D, piece=1024)
            self.load_w(WB, bWB, self.w_b[layer], 8, 0, D, piece=1024)
            self.load_w(WG, bWG, self.w_in[layer], 8, C_GATE, C_GATE + 2 * D, piece=1024)
            self.load_w(WO, bWO, self.w_o[layer], 8, 0, D, piece=1024)
            ident, bid = self.make_ident(ph, F32)
            lnc = self.ln_consts(ph, layer, 1)
            NB = 2
            XSr = Rot([ph.sb([128, D], F32) for _ in range(4)])
            XRr = Rot([ph.sb([128, D], F32) for _ in range(2)])
            xTr = [ph.sb([128, 8, 512], BF16) for _ in range(NB)]
            bxTr = [bufs(8) for _ in range(NB)]
            ATr = Rot([ph.sb([128, 8, 512], BF16) for _ in range(2)])
            OGr = Rot([ph.sb([128, 8, 512], BF16) for _ in range(2)])
            MT = ph.sb([128, 8, 512], BF16)
            bMT = bufs(8)
            SGr = Rot([ph.sb([128, 512], F32) for _ in range(4)])
            T1r = Rot([ph.sb([128, 512], F32) for _ in range(2)])
            T2r = Rot([ph.sb([128, 512], F32) for _ in range(2)])
            small = Rot([ph.sb([128, 16], F32) for _ in range(2)])
            pM = Rot([ph.ps() for _ in range(4)])
            pT = Rot([ph.ps() for _ in range(4)])
            pYi = [0]
            def m4_loads(t0):
                xs = self.issue_x(src, t0, XSr)
                At, bA = ATr.next()
                Og, bOg = OGr.next()
                P.dma("sp", At[:], self.AT[:, t0:t0 + 512].rearrange("(c p) t -> p c t", p=128), writes=[bA])
                P.dma("sp", Og[:], self.OGT[:, t0:t0 + 512].rearrange("(c p) t -> p c t", p=128), writes=[bOg])
                return xs, At, bA, Og, bOg

            nxt = m4_loads(0)
            self.transpose_x(nxt[0], xTr[0], bxTr[0], pT, ident, bid)
            for ti in range(T // 512):
                t0 = ti * 512
                xT, bxT = xTr[ti % NB], bxTr[ti % NB]
                _, At, bA, Og, bOg = nxt
                if ti + 1 < T // 512:
                    nxt = m4_loads(t0 + 512)
                for n in range(8):
                    ns = slice(n * 128, (n + 1) * 128)
                    pa, bpa = pM.next()
                    pga, bpga = pM.next()
                    for kc in range(8):
                        P.op("pe", lambda e, pa=pa, kc=kc, ns=ns, At=At: e.matmul(pa[:], WA[:, kc, ns], At[:, kc, :], start=(kc == 0), stop=(kc == 7)),
                             reads=[bWA[kc], bA], writes=[bpa])
                    for kc in range(8):
                        P.op("pe", lambda e, pga=pga, kc=kc, ns=ns, xT=xT: e.matmul(pga[:], WG[:, kc, ns], xT[:, kc, :], start=(kc == 0), stop=(kc == 7)),
                             reads=[bWG[kc], bxT[kc]], writes=[bpga])
                    sga, bsga = SGr.next()
                    P.op("act", lambda e, sga=sga, pga=pga: e.activation(out=sga[:], in_=pga[:], func=AF.Sigmoid), reads=[bpga], writes=[bsga])
                    t1, bt1 = T1r.next()
                    P.op("dve", lambda e, t1=t1, pa=pa, sga=sga: e.tensor_tensor(out=t1[:], in0=pa[:], in1=sga[:], op=ALU.mult),
                         reads=[bpa, bsga], writes=[bt1])
                    pb, bpb = pM.next()
                    pgb, bpgb = pM.next()
                    for kc in range(8):
                        P.op("pe", lambda e, pb=pb, kc=kc, ns=ns, Og=Og: e.matmul(pb[:], WB[:, kc, ns], Og[:, kc, :], start=(kc == 0), stop=(kc == 7)),
                             reads=[bWB[kc], bOg], writes=[bpb])
                    for kc in range(8):
                        P.op("pe", lambda e, pgb=pgb, kc=kc, n=n, xT=xT: e.matmul(pgb[:], WG[:, kc, D + n * 128:D + (n + 1) * 128], xT[:, kc, :],
                                                                               start=(kc == 0), stop=(kc == 7)),
                             reads=[bWG[kc], bxT[kc]], writes=[bpgb])
                    sgb, bsgb = SGr.next()
                    P.op("act", lambda e, sgb=sgb, pgb=pgb: e.activation(out=sgb[:], in_=pgb[:], func=AF.Sigmoid), reads=[bpgb], writes=[bsgb])
                    t2, bt2 = T2r.next()
                    P.op("dve", lambda e, t2=t2, pb=pb, sgb=sgb: e.tensor_tensor(out=t2[:], in0=pb[:], in1=sgb[:], op=ALU.mult),
                         reads=[bpb, bsgb], writes=[bt2])
                    P.op("pool", lambda e, t1=t1, t2=t2, n=n: e.tensor_tensor(out=MT[:, n, :], in0=t1[:], in1=t2[:], op=ALU.add),
                         reads=[bt1, bt2], writes=[bMT[n]])
                if ti + 1 < T // 512:
                    self.transpose_x(nxt[0], xTr[(ti + 1) % NB], bxTr[(ti + 1) % NB], pT, ident, bid)
                for s in range(4):
                    k2 = 2 * (pYi[0] % 2)
                    pYi[0] += 1
                    pY = [pT.t[k2], pT.t[k2 + 1]]
                    bpY = [pT.b[k2], pT.b[k2 + 1]]
                    for hf in range(2):
                        for kc in range(8):
                            P.op("pe", lambda e, hf=hf, kc=kc, s=s, pY=pY: e.matmul(pY[hf][:], MT[:, kc, s * 128:(s + 1) * 128], WO[:, kc, hf * 512:(hf + 1) * 512],
                                                                            start=(kc == 0), stop=(kc == 7)),
                                 reads=[bMT[kc], bWO[kc]], writes=[bpY[hf]])
                    self.ln_epilogue(pY, bpY, src, dst, t0 + s * 128, 1.0 / DN_ALPHA, lnc, XRr, small)

    def build(self, upto=99):
        cur = self.x
        n = 0
        for layer in range(self.depth):
            last = (layer == self.depth - 1)
            steps = [
                lambda: self.ffn_phase(layer, 0, cur, self.R[0], 0),
                lambda: self.m1_phase(layer, self.R[0]),
                lambda: self.m2_phase(layer),
                lambda: self.m3_phase(layer, self.R[0]),
                lambda: self.m4_phase(layer, self.R[0], self.R[1]),
                lambda: self.ffn_phase(layer, 1, self.R[1], self.out if last else self.R[0], 2),
            ]
            for st in steps:
                if n < upto:
                    st()
                n += 1
            cur = self.R[0]
        self.top.close()
        return self.nc


def host_pos_tables(rel_bias):
    s = np.arange(128)[:, None]
    q = np.arange(128)[None, :]
    out_b = np.zeros((128, 2, 16, 128), np.float32)
    out_m = np.zeros((128, 2, 16, 128), np.float32)
    for blk in range(2):
        j = s + 128 * blk
        rel = q + 128 - j
        valid = (rel >= 0) & (rel < 128)
        n = np.maximum(rel, 0)
        nf = np.maximum(n, 1).astype(np.float32)
        large = 16 + (np.log(nf / np.float32(16)) / np.float32(np.log(128 / 16)) * np.float32(16)).astype(np.int32)
        large = np.minimum(large, 31)
        bucket = np.where(n < 16, n, large)
        bucket = np.where(valid, bucket, 0)
        g = rel_bias[bucket]
        vm = np.broadcast_to(valid[:, None, :], (128, 16, 128))
        out_b[:, blk] = np.where(vm, np.transpose(g, (0, 2, 1)), np.float32(-1e30))
        out_m[:, blk] = vm
    return out_b.reshape(128, -1), out_m.reshape(128, -1)


_NC_CACHE = {}


def kernel(x, rel_bias, ln_g, ln_b, ffn_w13, ffn_w2, w_in, conv_w, a_log, dt_bias,
           dn_norm_g, sinks, w_branch_a, w_branch_b, w_out):
    x = np.asarray(x, np.float32)
    B, L, _ = x.shape
    nseq = B // NCORES
    key = (nseq, L)
    if key not in _NC_CACHE:
        _NC_CACHE[key] = KB(nseq, L, DEPTH).build()
    nc = _NC_CACHE[key]
    pb, pm = host_pos_tables(np.asarray(rel_bias, np.float32))
    f = lambda a: np.ascontiguousarray(np.asarray(a, np.float32))
    shared = dict(ln_g=f(ln_g), ln_b=f(ln_b), ffn_w13=f(ffn_w13), ffn_w2=f(ffn_w2), w_in=f(w_in), conv_w=f(conv_w),
                  a_log=f(a_log), dt_bias=f(dt_bias), dn_norm_g=f(dn_norm_g), sinks=f(sinks),
                  w_branch_a=f(w_branch_a), w_branch_b=f(w_branch_b), w_out=f(w_out), pbias=pb, pmask=pm)
    in_maps = []
    for c in range(NCORES):
        m = dict(shared)
        m["x"] = np.ascontiguousarray(x[c * nseq:(c + 1) * nseq].reshape(nseq * L, D))
        in_maps.append(m)
    res = run_bass_kernel_spmd(nc, in_maps, core_ids=list(range(NCORES)))
    outs = [np.asarray(r["out"], np.float32).reshape(nseq, L, D) for r in res.results]
    return np.concatenate(outs, axis=0)
```
